# Optimizing a Trainium2 kernel written in Bass

```python
import jax
import jax.numpy as jnp
from jax import lax
import numpy as np

D_MODEL = 1024
BATCH = 16
SEQ = 256
DEPTH = 4
DEC_BATCH = 4
DEC_SEQ = 1024
PAST_LEN = 256

GRID_W = 64
A_HEADS = 4
A_NOPE = 64
A_ROPE = 32
A_V = 64
A_QK = A_NOPE + A_ROPE
A_Q_RANK = 256
A_KV_RANK = 128
B_HEADS = 4
B_KV_HEADS = 2
B_HEAD_DIM = 64
B_WINDOW = 128
B_BLOCK = 128
C_GROUPS = 4
C_GROUP_DIM = 64
C_WIDTH = C_GROUPS * C_GROUP_DIM
D_HEADS = 4
D_HEAD_DIM = 64
NA_ROWS = 8
NA_COLS = 16

MIX_WIDTH = A_HEADS * A_V + B_HEADS * B_HEAD_DIM + C_WIDTH + D_HEADS * D_HEAD_DIM
IN_SIZES = (A_Q_RANK, A_KV_RANK, A_ROPE,
            B_HEADS * B_HEAD_DIM, B_KV_HEADS * B_HEAD_DIM, B_KV_HEADS * B_HEAD_DIM,
            C_WIDTH,
            D_HEADS * D_HEAD_DIM, D_HEADS * D_HEAD_DIM, D_HEADS * D_HEAD_DIM)
IN_WIDTH = sum(IN_SIZES)
D_FF = 2816
N_MOD = 9
ROPE_BASE = 10000.0
EPS = 1e-6
DENSE_QBLOCK = 128
DENSE_SWEEP_KEYS = 2048
F32 = jnp.float32

kernel_name = 'hybrid_prefix_diffusion_step'


def rms_norm(x, g):
    xf = x.astype(F32)
    y = xf * lax.rsqrt(jnp.mean(xf * xf, axis=-1, keepdims=True) + EPS)
    return (y * g.astype(F32)).astype(x.dtype)


def swiglu(x, w_gate, w_up, w_down):
    return (jax.nn.silu(x @ w_gate) * (x @ w_up)) @ w_down


def split_cols(x, sizes):
    out, start = [], 0
    for size in sizes:
        out.append(x[..., start:start + size])
        start += size
    return out


def axial_rope(x):
    s, r = x.shape[1], x.shape[-1]
    half = r // 2
    t = jnp.arange(s)
    inv = ROPE_BASE ** (-jnp.arange(0, half, 2, dtype=F32) / half)

    def rot(xa, pos):
        ang = pos.astype(F32)[:, None] * inv[None, :]
        cos = jnp.cos(ang)[None, :, None, :]
        sin = jnp.sin(ang)[None, :, None, :]
        x1, x2 = jnp.split(xa.astype(F32), 2, axis=-1)
        return jnp.concatenate([x1 * cos - x2 * sin, x2 * cos + x1 * sin], axis=-1)

    out = jnp.concatenate([rot(x[..., :half], t // GRID_W), rot(x[..., half:], t % GRID_W)], axis=-1)
    return out.astype(x.dtype)


def dense_attend(q, k, v, sink=None):
    b, lq, h, d = q.shape
    hk, dv = k.shape[2], v.shape[-1]
    g = h // hk
    scale = d ** -0.5
    kf, vf = k.astype(F32), v.astype(F32)

    def attend(qb):
        s = jnp.einsum('bqkgd,bskd->bkgqs', qb.astype(F32), kf) * scale
        if sink is not None:
            sk = jnp.broadcast_to(sink.astype(F32).reshape(1, hk, g, 1, 1), s.shape[:-1] + (1,))
            p = jax.nn.softmax(jnp.concatenate([s, sk], axis=-1), axis=-1)[..., :-1]
        else:
            p = jax.nn.softmax(s, axis=-1)
        return jnp.einsum('bkgqs,bskd->bqkgd', p, vf)

    qg = q.reshape(b, lq, hk, g, d)
    if k.shape[1] >= DENSE_SWEEP_KEYS and lq % DENSE_QBLOCK == 0:
        qs = jnp.moveaxis(qg.reshape(b, lq // DENSE_QBLOCK, DENSE_QBLOCK, hk, g, d), 1, 0)
        o = jnp.moveaxis(lax.map(attend, qs), 0, 1)
    else:
        o = attend(qg)
    return o.reshape(b, lq, h, dv).astype(q.dtype)


def window_attend(q, k, v, kc, vc, sink):
    b, s, h, d = q.shape
    hk = k.shape[2]
    g = h // hk
    nb = s // B_BLOCK
    scale = d ** -0.5
    qb = q.reshape(b, nb, B_BLOCK, hk, g, d).astype(F32)
    pad = ((0, 0), (B_BLOCK, B_BLOCK), (0, 0), (0, 0))
    kp = jnp.pad(k.astype(F32), pad).reshape(b, nb + 2, B_BLOCK, hk, d)
    vp = jnp.pad(v.astype(F32), pad).reshape(b, nb + 2, B_BLOCK, hk, d)
    kw = jnp.concatenate([kp[:, :nb], kp[:, 1:nb + 1], kp[:, 2:]], axis=2)
    vw = jnp.concatenate([vp[:, :nb], vp[:, 1:nb + 1], vp[:, 2:]], axis=2)
    qpos = jnp.arange(s).reshape(nb, B_BLOCK)
    kpos = (jnp.arange(nb)[:, None] - 1) * B_BLOCK + jnp.arange(3 * B_BLOCK)[None, :]
    valid = ((kpos[:, None, :] >= 0) & (kpos[:, None, :] < s)
             & (jnp.abs(qpos[:, :, None] - kpos[:, None, :]) <= B_WINDOW))
    s_loc = jnp.einsum('bnqkgd,bnskd->bnkgqs', qb, kw) * scale
    s_loc = jnp.where(valid[None, :, None, None], s_loc, -jnp.inf)
    s_ctx = jnp.einsum('bnqkgd,bckd->bnkgqc', qb, kc.astype(F32)) * scale
    sk = jnp.broadcast_to(sink.astype(F32).reshape(1, 1, hk, g, 1, 1), s_loc.shape[:-1] + (1,))
    p = jax.nn.softmax(jnp.concatenate([s_loc, s_ctx, sk], axis=-1), axis=-1)
    n_loc, n_ctx = 3 * B_BLOCK, kc.shape[1]
    o = (jnp.einsum('bnkgqs,bnskd->bnqkgd', p[..., :n_loc], vw)
         + jnp.einsum('bnkgqc,bckd->bnqkgd', p[..., n_loc:n_loc + n_ctx], vc.astype(F32)))
    return o.reshape(b, s, h, d).astype(q.dtype)


def neighbourhood_attend(q, k, v, kc, vc, rpb):
    b, s, h, d = q.shape
    rows = s // GRID_W
    kr = min(NA_ROWS, rows)
    scale = d ** -0.5
    qg = q.reshape(b, rows, GRID_W, h, d).astype(F32)
    kg = k.reshape(b, rows, GRID_W, h, d).astype(F32)
    vg = v.reshape(b, rows, GRID_W, h, d).astype(F32)
    r = jnp.arange(rows)
    row_idx = jnp.clip(r - kr // 2, 0, rows - kr)[:, None] + jnp.arange(kr)[None, :]
    k_blk = kg[:, row_idx]
    v_blk = vg[:, row_idx]
    col = jnp.arange(GRID_W)
    col_start = jnp.clip(col - NA_COLS // 2, 0, GRID_W - NA_COLS)
    col_ok = (col[None, :] >= col_start[:, None]) & (col[None, :] < col_start[:, None] + NA_COLS)
    dr_idx = row_idx - r[:, None] + NA_ROWS - 1
    dc_idx = jnp.clip(col[None, :] - col[:, None] + NA_COLS - 1, 0, 2 * NA_COLS - 2)
    bias = rpb.astype(F32)[:, dr_idx[:, None, :, None], dc_idx[None, :, None, :]]
    s_loc = jnp.einsum('brqhd,brkwhd->bhrqkw', qg, k_blk) * scale + bias[None]
    s_loc = jnp.where(col_ok[:, None, :], s_loc, -jnp.inf)
    n_loc = kr * GRID_W
    s_loc = s_loc.reshape(b, h, rows, GRID_W, n_loc)
    s_ctx = jnp.einsum('brqhd,bchd->bhrqc', qg, kc.astype(F32)) * scale
    p = jax.nn.softmax(jnp.concatenate([s_loc, s_ctx], axis=-1), axis=-1)
    p_loc = p[..., :n_loc].reshape(b, h, rows, GRID_W, kr, GRID_W)
    o = (jnp.einsum('bhrqkw,brkwhd->brqhd', p_loc, v_blk)
         + jnp.einsum('bhrqc,bchd->brqhd', p[..., n_loc:], vc.astype(F32)))
    return o.reshape(b, s, h, d).astype(q.dtype)


def fourier_mix(x):
    b, l, _ = x.shape
    xg = x.reshape(b, l, C_GROUPS, C_GROUP_DIM).astype(F32)
    y = jnp.fft.fft2(xg, axes=(1, 3), norm='ortho').real
    return y.reshape(b, l, C_WIDTH).astype(x.dtype)


def mla_queries(cq, lp, rope):
    b, l, _ = cq.shape
    q = (rms_norm(cq, lp['g_qa']) @ lp['w_uq']).reshape(b, l, A_HEADS, A_QK)
    q = rms_norm(q, lp['qn_a'])
    if rope:
        q = jnp.concatenate([q[..., :A_NOPE], axial_rope(q[..., A_NOPE:])], axis=-1)
    return q


def mla_keys_values(ckv_n, krope, lp, rope):
    b, l, _ = ckv_n.shape
    kv = (ckv_n @ lp['w_ukv']).reshape(b, l, A_HEADS, A_NOPE + A_V)
    k = jnp.concatenate([kv[..., :A_NOPE],
                         jnp.broadcast_to(krope[:, :, None, :], (b, l, A_HEADS, A_ROPE))], axis=-1)
    k = rms_norm(k, lp['kn_a'])
    if rope:
        k = jnp.concatenate([k[..., :A_NOPE], axial_rope(k[..., A_NOPE:])], axis=-1)
    return k, kv[..., A_NOPE:]


def merge_heads(oa, ob, oc, od):
    b, l = oc.shape[:2]
    return jnp.concatenate([oa.reshape(b, l, -1), ob.reshape(b, l, -1), oc, od.reshape(b, l, -1)], axis=-1)


def mix_context(h, lp):
    b, l, _ = h.shape
    cq, ckv, kr, qb, kb, vb, xc, qd, kd, vd = split_cols(h @ lp['w_in'], IN_SIZES)
    ckv_n = rms_norm(ckv, lp['g_kva'])
    qa = mla_queries(cq, lp, False)
    ka, va = mla_keys_values(ckv_n, kr, lp, False)
    oa = dense_attend(qa, ka, va)
    qb = rms_norm(qb.reshape(b, l, B_HEADS, B_HEAD_DIM), lp['qn_b'])
    kb = rms_norm(kb.reshape(b, l, B_KV_HEADS, B_HEAD_DIM), lp['kn_b'])
    vb = vb.reshape(b, l, B_KV_HEADS, B_HEAD_DIM)
    ob = dense_attend(qb, kb, vb, sink=lp['sink_b'])
    oc = fourier_mix(xc)
    qd = rms_norm(qd.reshape(b, l, D_HEADS, D_HEAD_DIM), lp['qn_d'])
    kd = rms_norm(kd.reshape(b, l, D_HEADS, D_HEAD_DIM), lp['kn_d'])
    vd = vd.reshape(b, l, D_HEADS, D_HEAD_DIM)
    od = dense_attend(qd, kd, vd)
    o = merge_heads(oa, ob, oc, od) @ lp['w_o']
    return o, (ckv_n, kr, kb, vb, kd, vd)


def mix_latent(h, lp, ctx):
    ckv_c, kr_c, kb_c, vb_c, kd_c, vd_c = ctx
    b, s, _ = h.shape
    cq, ckv, kr, qb, kb, vb, xc, qd, kd, vd = split_cols(h @ lp['w_in'], IN_SIZES)
    qa = mla_queries(cq, lp, True)
    ka, va = mla_keys_values(rms_norm(ckv, lp['g_kva']), kr, lp, True)
    ka_c, va_c = mla_keys_values(ckv_c, kr_c, lp, False)
    oa = dense_attend(qa, jnp.concatenate([ka_c, ka], axis=1), jnp.concatenate([va_c, va], axis=1))
    qb = axial_rope(rms_norm(qb.reshape(b, s, B_HEADS, B_HEAD_DIM), lp['qn_b']))
    kb = axial_rope(rms_norm(kb.reshape(b, s, B_KV_HEADS, B_HEAD_DIM), lp['kn_b']))
    ob = window_attend(qb, kb, vb.reshape(b, s, B_KV_HEADS, B_HEAD_DIM), kb_c, vb_c, lp['sink_b'])
    oc = fourier_mix(xc)
    qd = rms_norm(qd.reshape(b, s, D_HEADS, D_HEAD_DIM), lp['qn_d'])
    kd = rms_norm(kd.reshape(b, s, D_HEADS, D_HEAD_DIM), lp['kn_d'])
    od = neighbourhood_attend(qd, kd, vd.reshape(b, s, D_HEADS, D_HEAD_DIM), kd_c, vd_c, lp['rpb_d'])
    o = merge_heads(oa, ob, oc, od) @ lp['w_o']
    return o, ()


def trunk_layer(x, mod, lp, mixer, *mixer_args):
    sh1, sc1, g1, sh2, sc2, g2, sh3, sc3, g3 = jnp.split(mod[:, None, :], N_MOD, axis=-1)
    h = rms_norm(x, lp['g_ffn1']) * (1 + sc1) + sh1
    x = x + 0.5 * g1 * swiglu(h, lp['w_gate1'], lp['w_up1'], lp['w_down1'])
    h = rms_norm(x, lp['g_mix']) * (1 + sc2) + sh2
    o, aux = mixer(h, lp, *mixer_args)
    x = x + g2 * o
    h = rms_norm(x, lp['g_ffn2']) * (1 + sc3) + sh3
    x = x + 0.5 * g3 * swiglu(h, lp['w_gate2'], lp['w_up2'], lp['w_down2'])
    return x, aux


def setup_inputs(seed: int = 0) -> dict:
    key = jax.random.key(seed)
    keys = iter(jax.random.split(key, 48))

    def nrm(shape, scale):
        return jax.random.normal(next(keys), shape, F32) * scale

    def gain(shape):
        return 1.0 + 0.02 * jax.random.normal(next(keys), shape, F32)

    return {
        'x_prompt': nrm((BATCH, SEQ, D_MODEL), 1.0),
        'x_sample': nrm((DEC_BATCH, DEC_SEQ, D_MODEL), 1.0),
        'cache_mla_ckv': nrm((DEC_BATCH, DEPTH, PAST_LEN, A_KV_RANK), 1.0),
        'cache_mla_krope': nrm((DEC_BATCH, DEPTH, PAST_LEN, A_ROPE), 1.0),
        'cache_win_k': nrm((DEC_BATCH, DEPTH, PAST_LEN, B_KV_HEADS, B_HEAD_DIM), 1.0),
        'cache_win_v': nrm((DEC_BATCH, DEPTH, PAST_LEN, B_KV_HEADS, B_HEAD_DIM), 1.0),
        'cache_na_k': nrm((DEC_BATCH, DEPTH, PAST_LEN, D_HEADS, D_HEAD_DIM), 1.0),
        'cache_na_v': nrm((DEC_BATCH, DEPTH, PAST_LEN, D_HEADS, D_HEAD_DIM), 1.0),
        'c': nrm((DEC_BATCH, D_MODEL), 1.0),
        'c_ctx': nrm((D_MODEL,), 1.0),
        'w_ada': nrm((DEPTH, D_MODEL, N_MOD * D_MODEL), 0.5 * D_MODEL ** -0.5),
        'b_ada': nrm((DEPTH, N_MOD * D_MODEL), 0.02),
        'g_ffn1': gain((DEPTH, D_MODEL)),
        'w_gate1': nrm((DEPTH, D_MODEL, D_FF), D_MODEL ** -0.5),
        'w_up1': nrm((DEPTH, D_MODEL, D_FF), D_MODEL ** -0.5),
        'w_down1': nrm((DEPTH, D_FF, D_MODEL), D_FF ** -0.5),
        'g_mix': gain((DEPTH, D_MODEL)),
        'w_in': nrm((DEPTH, D_MODEL, IN_WIDTH), D_MODEL ** -0.5),
        'g_qa': gain((DEPTH, A_Q_RANK)),
        'w_uq': nrm((DEPTH, A_Q_RANK, A_HEADS * A_QK), A_Q_RANK ** -0.5),
        'g_kva': gain((DEPTH, A_KV_RANK)),
        'w_ukv': nrm((DEPTH, A_KV_RANK, A_HEADS * (A_NOPE + A_V)), A_KV_RANK ** -0.5),
        'qn_a': gain((DEPTH, A_QK)),
        'kn_a': gain((DEPTH, A_QK)),
        'qn_b': gain((DEPTH, B_HEAD_DIM)),
        'kn_b': gain((DEPTH, B_HEAD_DIM)),
        'sink_b': nrm((DEPTH, B_HEADS), 0.5),
        'qn_d': gain((DEPTH, D_HEAD_DIM)),
        'kn_d': gain((DEPTH, D_HEAD_DIM)),
        'rpb_d': nrm((DEPTH, D_HEADS, 2 * NA_ROWS - 1, 2 * NA_COLS - 1), 0.2),
        'w_o': nrm((DEPTH, MIX_WIDTH, D_MODEL), MIX_WIDTH ** -0.5),
        'g_ffn2': gain((DEPTH, D_MODEL)),
        'w_gate2': nrm((DEPTH, D_MODEL, D_FF), D_MODEL ** -0.5),
        'w_up2': nrm((DEPTH, D_MODEL, D_FF), D_MODEL ** -0.5),
        'w_down2': nrm((DEPTH, D_FF, D_MODEL), D_FF ** -0.5),
    }


def reference(x_prompt, x_sample, cache_mla_ckv, cache_mla_krope, cache_win_k, cache_win_v,
              cache_na_k, cache_na_v, c, c_ctx, w_ada, b_ada, g_ffn1, w_gate1, w_up1, w_down1,
              g_mix, w_in, g_qa, w_uq, g_kva, w_ukv, qn_a, kn_a, qn_b, kn_b, sink_b, qn_d, kn_d,
              rpb_d, w_o, g_ffn2, w_gate2, w_up2, w_down2):
    y_prompt, y_sample = x_prompt, x_sample
    new_ckv, new_krope, new_wk, new_wv, new_nk, new_nv = [], [], [], [], [], []
    for i in range(DEPTH):
        lp = dict(g_ffn1=g_ffn1[i], w_gate1=w_gate1[i], w_up1=w_up1[i], w_down1=w_down1[i],
                  g_mix=g_mix[i], w_in=w_in[i], g_qa=g_qa[i], w_uq=w_uq[i], g_kva=g_kva[i],
                  w_ukv=w_ukv[i], qn_a=qn_a[i], kn_a=kn_a[i], qn_b=qn_b[i], kn_b=kn_b[i],
                  sink_b=sink_b[i], qn_d=qn_d[i], kn_d=kn_d[i], rpb_d=rpb_d[i], w_o=w_o[i],
                  g_ffn2=g_ffn2[i], w_gate2=w_gate2[i], w_up2=w_up2[i], w_down2=w_down2[i])
        mod_ctx = jax.nn.silu(c_ctx[None, :]) @ w_ada[i] + b_ada[i]
        mod_lat = jax.nn.silu(c) @ w_ada[i] + b_ada[i]
        y_prompt, (ckv_n, kr, kb, vb, kd, vd) = trunk_layer(y_prompt, mod_ctx, lp, mix_context)
        new_ckv.append(ckv_n)
        new_krope.append(kr)
        new_wk.append(kb)
        new_wv.append(vb)
        new_nk.append(kd)
        new_nv.append(vd)
        ctx = (cache_mla_ckv[:, i], cache_mla_krope[:, i], cache_win_k[:, i], cache_win_v[:, i],
               cache_na_k[:, i], cache_na_v[:, i])
        y_sample, _ = trunk_layer(y_sample, mod_lat, lp, mix_latent, ctx)
    return (y_prompt, y_sample, jnp.stack(new_ckv, axis=1), jnp.stack(new_krope, axis=1),
            jnp.stack(new_wk, axis=1), jnp.stack(new_wv, axis=1),
            jnp.stack(new_nk, axis=1), jnp.stack(new_nv, axis=1))
```

```python
import numpy as np
import ml_dtypes
from contextlib import ExitStack
import concourse.bass as bass
import concourse.mybir as mybir
from concourse.bass_utils import run_bass_kernel_spmd

F32 = mybir.dt.float32
BF16 = mybir.dt.bfloat16
AF = mybir.ActivationFunctionType
ALU = mybir.AluOpType
bf16 = ml_dtypes.bfloat16

NL = 4
D = 1024
T = 1024
DFF = 2816
EPS = 1e-6
NEG = -30000.0
BIG = 32768.0
RING_ELEMS = 4096
NSLOT = 4
NTMP = 6
NPT = 12
NG = 13


class Prog:
    COMPUTE = ('pe', 'act', 'dve', 'pool')

    def __init__(self, nc, es, same_engine_sync=True):
        self.nc = nc
        self.es = es
        self.ops = []
        self.lastw = {}
        self.readers = {}
        self.chan_last = {}
        self.chan_cnt = {}
        self.same_engine_sync = same_engine_sync

    def add(self, eng, fn, reads=(), writes=(), chan=None, extra_deps=()):
        idx = len(self.ops)
        deps = set(extra_deps)
        for k in reads:
            w = self.lastw.get(k)
            if w is not None:
                deps.add(w)
        for k in writes:
            w = self.lastw.get(k)
            if w is not None:
                deps.add(w)
            for r in self.readers.get(k, ()):
                deps.add(r)
        for k in reads:
            self.readers.setdefault(k, []).append(idx)
        for k in writes:
            self.lastw[k] = idx
            self.readers[k] = []
        if chan is not None:
            p = self.chan_last.get(chan)
            if p is not None:
                deps.add(p)
            self.chan_last[chan] = idx
            self.chan_cnt[chan] = self.chan_cnt.get(chan, 0) + 1
        deps.discard(idx)
        self.ops.append(dict(eng=eng, fn=fn, deps=deps, chan=chan, signal=False, sigval=None,
                             chanval=(16 * self.chan_cnt[chan] if chan is not None else None)))
        return idx

    def emit(self):
        nc = self.nc
        ops = self.ops
        for op in ops:
            for d in op['deps']:
                if ops[d]['chan'] is None:
                    ops[d]['signal'] = True
        cnt = {}
        for op in ops:
            if op['chan'] is None and op['signal']:
                cnt[op['eng']] = cnt.get(op['eng'], 0) + 1
                op['sigval'] = cnt[op['eng']]
        self.sig_counts = dict(cnt)
        self.chan_counts = dict(self.chan_cnt)
        self.n_ops = len(ops)
        sems = {e: self.es.enter_context(nc.semaphore('s_' + e)) for e in self.COMPUTE + ('sp',)}
        csems = {c: self.es.enter_context(nc.semaphore('c_%d' % i)) for i, c in enumerate(self.chan_cnt)}
        by_eng = {}
        for i, op in enumerate(ops):
            by_eng.setdefault(op['eng'], []).append(i)

        def run_engine(ename, e):
            waited = {}
            for i in by_eng.get(ename, ()):
                op = ops[i]
                need = {}
                for d in op['deps']:
                    dop = ops[d]
                    if dop['chan'] is not None:
                        key = ('c', dop['chan'])
                        val = dop['chanval']
                    else:
                        if dop['eng'] == ename and (ename == 'pe' or not self.same_engine_sync):
                            continue
                        key = ('e', dop['eng'])
                        val = dop['sigval']
                    if val > need.get(key, 0):
                        need[key] = val
                for key, val in need.items():
                    if waited.get(key, 0) >= val:
                        continue
                    waited[key] = val
                    s = csems[key[1]] if key[0] == 'c' else sems[key[1]]
                    e.wait_ge(s, val)
                if op['fn'] is None:
                    continue
                ins = op['fn'](e)
                if op['chan'] is not None:
                    ins.then_inc(csems[op['chan']], 16)
                elif op['signal']:
                    ins.then_inc(sems[ename], 1)

        with nc.Block() as block:
            @block.tensor
            def _(e):
                run_engine('pe', e)

            @block.scalar
            def _(e):
                run_engine('act', e)

            @block.vector
            def _(e):
                run_engine('dve', e)

            @block.gpsimd
            def _(e):
                run_engine('pool', e)

            @block.sync
            def _(e):
                run_engine('sp', e)


INPUT_SPECS = [
    ('xT', [1024, 1024], F32), ('cond', [128, 8], F32), ('ctxflag', [128, 1], F32),
    ('w_ada', [NL, 1024, 9216], F32), ('b_adaT', [NL, 128, 72], F32), ('gT', [NL, 128, 3, 8], F32),
    ('w_gate1', [NL, 1024, DFF], F32), ('w_up1', [NL, 1024, DFF], F32), ('w_down1', [NL, DFF, 1024], F32),
    ('w_gate2', [NL, 1024, DFF], F32), ('w_up2', [NL, 1024, DFF], F32), ('w_down2', [NL, DFF, 1024], F32),
    ('w_in', [NL, 1024, 1952], F32), ('w_o', [NL, 1024, 1024], F32),
    ('w_uq', [NL, 256, 384], F32), ('w_ukv', [NL, 128, 512], F32), ('gvec', [NL, 128, NG], F32),
    ('ckvT_c', [NL, 128, 256], F32), ('krT_c', [NL, 32, 256], F32), ('winkT_c', [NL, 128, 256], F32),
    ('winv_c', [NL, 256, 128], F32), ('nakT_c', [NL, 256, 256], F32), ('nav_c', [NL, 256, 256], F32),
    ('identb', [128, 128], BF16), ('onesb', [128, 128], BF16), ('blk64b', [128, 128], BF16),
    ('permB', [128, 128], F32), ('permA', [128, 128], F32),
    ('ropeB', [128, 2, 1024], F32), ('ropeA', [128, 2, 1024], F32),
    ('cs64', [128, 256], BF16), ('dftC', [1024, 1024], BF16), ('dftS', [1024, 1024], BF16),
    ('EA', [8, 1280], BF16), ('FA', [8, 1024], BF16), ('maskB', [128, 6, 512], BF16),
    ('biasD', [NL, 4, 1024, 1024], F32),
]
OUTPUT_SPECS = [
    ('yT', [1024, 1024]), ('o_ckvT', [NL, 128, 1024]), ('o_krT', [NL, 32, 1024]), ('o_kbT', [NL, 128, 1024]),
    ('o_vb', [NL, 1024, 128]), ('o_kdT', [NL, 256, 1024]), ('o_vd', [NL, 1024, 256]),
]


class _Stop(Exception):
    pass


def build_program(nl=NL, stop=None):
    nc = bass.Bass("TRN2", target_bir_lowering=False)
    IN = {n: nc.dram_tensor(n, list(s), dt, kind="ExternalInput").ap() for n, s, dt in INPUT_SPECS}
    OUT = {n: nc.dram_tensor(n, list(s), F32, kind="ExternalOutput").ap() for n, s in OUTPUT_SPECS}
    es = ExitStack()
    with es:
        P = Prog(nc, es)

        def sb(name, shape, dt):
            return es.enter_context(nc.sbuf_tensor(name, list(shape), dt))

        xT = sb('xT_sb', [128, 8, 1024], F32)
        hT = sb('hT', [128, 8, 1024], BF16)
        ring = [sb('ring%d' % i, [128, RING_ELEMS], BF16) for i in range(NSLOT)]
        actb = [sb('actb%d' % i, [128, 512], BF16) for i in range(8)]
        rstdN = sb('rstdN', [128, 512], F32)
        tmp = [sb('tmp%d' % i, [128, 512], F32) for i in range(NTMP)]
        sqb = [sb('sqb%d' % i, [128, 512], BF16) for i in range(3)]
        ps = [es.enter_context(nc.psum_tensor('ps%d' % i, [128, 512], F32)) for i in range(8)]
        identb = sb('identb_sb', [128, 128], BF16)
        onesb = sb('onesb_sb', [128, 128], BF16)
        blk64b = sb('blk64b_sb', [128, 128], BF16)
        permB = sb('permB_sb', [128, 128], F32)
        permA = sb('permA_sb', [128, 128], F32)
        ropeB = sb('ropeB_sb', [128, 2, 1024], F32)
        ropeA = sb('ropeA_sb', [128, 2, 1024], F32)
        cs64 = sb('cs64_sb', [128, 256], BF16)
        EA = sb('EA_sb', [8, 1280], BF16)
        FA = sb('FA_sb', [8, 1024], BF16)
        maskB = sb('maskB_sb', [128, 6, 512], BF16)
        epsT = sb('epsT', [128, 1], F32)
        gvec = sb('gvec_sb', [128, NL, NG], F32)
        gT = sb('gT_sb', [128, NL, 3, 8], F32)
        badaT = sb('badaT_sb', [128, NL, 72], F32)
        condf = sb('condf', [128, 8], F32)
        condb = sb('condb', [128, 8], BF16)
        ctxf = sb('ctxf', [128, 1], F32)
        modT = [sb('modT%d' % i, [128, 72], F32) for i in range(2)]
        modA = [sb('modA%d' % i, [128, 3, 8], F32) for i in range(2)]
        modG = [sb('modG%d' % i, [128, 3, 8], F32) for i in range(2)]
        sinkexp = sb('sinkexp', [128, 4], F32)
        rec = sb('rec', [128, 4], F32)
        QB = sb('QB', [128, 2, 1024], BF16)
        QD = sb('QD', [128, 2, 1024], BF16)
        QA = sb('QA', [128, 4, 1024], BF16)
        KB = sb('KB', [128, 1280], BF16)
        KD = sb('KD', [128, 2, 1280], BF16)
        KA = sb('KA', [128, 4, 1280], BF16)
        VA = sb('VA', [128, 10, 4, 65], BF16)
        VB = sb('VB', [128, 10, 2, 65], BF16)
        VD = sb('VD', [128, 10, 4, 65], BF16)
        PT = sb('PT', [128, NPT, 512], BF16)
        ABt = PT[:, 0:8, :].rearrange("p l (c x) -> p l c x", c=2)
        XcT = PT[:, 8:12, :].rearrange("p (c t) x -> p c (t x)", c=2)
        mtok = sb('mtok', [128, 2, 4, 128], BF16)
        cqn = sb('cqn', [128, 2, 512], BF16)
        ckvn = sb('ckvn', [128, 1024], BF16)
        sqkr = sb('sqkr', [128, 512], BF16)
        krr = sb('krr', [128, 512], F32)
        wuq = sb('wuq', [128, 2, 384], BF16)
        wukv = sb('wukv', [128, 512], BF16)
        ckvc = sb('ckvc', [128, 256], BF16)
        krc = sb('krc', [128, 256], F32)

        st = dict(ps=0, tmp=0, sq=0, ring=0, pt=0, act=0, ld=0, out=0)
        out_ids = []

        def rot(name, n):
            i = st[name]
            st[name] = (i + 1) % n
            return i

        newps = lambda: rot('ps', 7)
        newtmp = lambda: rot('tmp', NTMP)
        newsq = lambda: rot('sq', 3)

        def MM(out, lhsT, rhs, start, stop, rd, wr):
            P.add('pe', lambda e: e.matmul(out, lhsT=lhsT, rhs=rhs, start=start, stop=stop), rd, wr)

        def ACT(out, in_, func, rd, wr, scale=None, bias=None):
            kw = {}
            if scale is not None:
                kw['scale'] = scale
            if bias is not None:
                kw['bias'] = bias
            P.add('act', lambda e: e.activation(out=out, in_=in_, func=func, **kw), rd, wr)

        def TT(out, in0, in1, op, rd, wr):
            P.add('dve', lambda e: e.tensor_tensor(out=out, in0=in0, in1=in1, op=op), rd, wr)

        def STT(out, in0, scalar, in1, op0, op1, rd, wr):
            P.add('dve', lambda e: e.scalar_tensor_tensor(out=out, in0=in0, scalar=scalar, in1=in1, op0=op0, op1=op1), rd, wr)

        def TS(out, in0, s1, op0, rd, wr):
            P.add('dve', lambda e: e.tensor_scalar(out=out, in0=in0, scalar1=s1, scalar2=None, op0=op0), rd, wr)

        def RECIP(out, in_, rd, wr):
            P.add('dve', lambda e: e.reciprocal(out=out, in_=in_), rd, wr)

        def DMA(q, out, in_, rd, wr, chan):
            return P.add(q, lambda e: e.dma_start(out=out, in_=in_), rd, wr, chan=chan)

        def LOAD(out, in_, keys, q='sp'):
            return DMA(q, out, in_, [], keys, 'ld%d' % rot('ld', 4))

        def STORE(out, in_, rd):
            out_ids.append(DMA('sp', out, in_, rd, [], 'st%d' % rot('out', 4)))

        def ring_get(src_ap, a, b):
            s = rot('ring', NSLOT)
            view = ring[s][:, 0:a * b].rearrange("p (a b) -> p a b", b=b)
            DMA('pool', view, src_ap, [], [('ring', s)], 'ring%d' % s)
            return view, ('ring', s)

        def CK(name):
            if stop == name:
                raise _Stop()

        def kcp(ap):
            return ap.rearrange("(kc p) n -> p kc n", p=128)

        C = 'consts'
        cl = []
        for name, tile in [('identb', identb), ('onesb', onesb), ('blk64b', blk64b), ('permB', permB), ('permA', permA),
                           ('ropeB', ropeB), ('ropeA', ropeA), ('cs64', cs64), ('EA', EA), ('FA', FA), ('maskB', maskB),
                           ('cond', condf), ('ctxflag', ctxf)]:
            cl.append(LOAD(tile[:], IN[name], []))
        cl.append(LOAD(gvec[:], IN['gvec'].rearrange("l p g -> p l g"), []))
        cl.append(LOAD(gT[:], IN['gT'].rearrange("l p k c -> p l k c"), []))
        cl.append(LOAD(badaT[:], IN['b_adaT'].rearrange("l p j -> p l j"), []))
        xkeys = [('x', c, t) for c in range(8) for t in range(2)]
        LOAD(xT[:], kcp(IN['xT']), xkeys)
        P.add('dve', lambda e: e.memset(epsT[:], EPS), [], [C], extra_deps=cl)
        P.add('dve', lambda e: e.memset(VA[:].rearrange("p a b c -> p (a b c)"), 1.0), [], ['Vinit'])
        P.add('dve', lambda e: e.memset(VB[:].rearrange("p a b c -> p (a b c)"), 1.0), [], ['Vinit'])
        P.add('dve', lambda e: e.memset(VD[:].rearrange("p a b c -> p (a b c)"), 1.0), [], ['Vinit'])
        for Vt, nh in ((VA, 4), (VB, 2), (VD, 4)):
            for kt in (8, 9):
                for h in range(nh):
                    P.add('dve', lambda e, Vt=Vt, kt=kt, h=h: e.tensor_copy(out=Vt[:, kt, h, 64:65], in_=ctxf[:, 0:1]), ['Vinit', C], ['Vinit'])
        P.add('dve', lambda e: e.memset(krc[:], 0.0), [], ['krc'])
        ACT(condb[:], condf[:], AF.Silu, [C], ['condb'])

        def rstd_from(ps_ap, pskey, inv_d, p0=0, p1=128, n=512):
            ti = newtmp()
            t = tmp[ti][p0:p1, 0:n]
            ACT(t, ps_ap, AF.Sqrt, [pskey, C], [('tmp', ti)], scale=inv_d, bias=epsT[p0:p1, 0:1])
            RECIP(t, t, [('tmp', ti)], [('tmp', ti)])
            return ti

        def mod_steps(l):
            par = l % 2
            for blk in range(18):
                wt, wk = ring_get(kcp(IN['w_ada'][l][:, blk * 512:(blk + 1) * 512]), 8, 512)
                for j in range(4):
                    col = blk * 4 + j
                    for kc in range(8):
                        MM(ps[7][:, col:col + 1], wt[:, kc, j * 128:(j + 1) * 128], condb[:, kc:kc + 1], kc == 0, kc == 7,
                           [wk, 'condb'], ['psM'])
                yield
            TT(modT[par][:, :], ps[7][:, 0:72], badaT[:, l, :], ALU.add, ['psM', C], [('mod', par)])
            for k in range(3):
                STT(modA[par][:, k, :], modT[par][:, (3 * k + 1) * 8:(3 * k + 2) * 8], 1.0, gT[:, l, k, :], ALU.add, ALU.mult,
                    [('mod', par), C], [('modA', par)])
            for k, f in ((0, 0.5), (1, 1.0), (2, 0.5)):
                TS(modG[par][:, k, :], modT[par][:, (3 * k + 2) * 8:(3 * k + 3) * 8], f, ALU.mult, [('mod', par)], [('modG', par)])
            yield

        def norm_mod(l, k):
            par = l % 2
            for t in range(2):
                hs = slice(t * 512, (t + 1) * 512)
                pi = newps()
                for c in range(8):
                    si = newsq()
                    ACT(sqb[si][:, :], xT[:, c, hs], AF.Square, [('x', c, t)], [('sq', si)])
                    MM(ps[pi][:, :], onesb[:, :], sqb[si][:, :], c == 0, c == 7, [('sq', si), C], [('ps', pi)])
                ACT(rstdN[:, :], ps[pi][:, :], AF.Sqrt, [('ps', pi), C], ['rstdN'], scale=1.0 / 1024, bias=epsT[:, 0:1])
                RECIP(rstdN[:, :], rstdN[:, :], ['rstdN'], ['rstdN'])
                for c in range(8):
                    ti = newtmp()
                    STT(tmp[ti][:, :], xT[:, c, hs], modA[par][:, k, c:c + 1], rstdN[:, :], ALU.mult, ALU.mult,
                        [('x', c, t), ('modA', par), 'rstdN'], [('tmp', ti)])
                    ACT(hT[:, c, hs], tmp[ti][:, :], AF.Identity, [('tmp', ti), ('mod', par)], [('h', c, t)],
                        bias=modT[par][:, 3 * k * 8 + c:3 * k * 8 + c + 1])

        def ffn(l, which, modgen):
            par = l % 2
            wg = IN['w_gate%d' % which][l]
            wu = IN['w_up%d' % which][l]
            wd = IN['w_down%d' % which][l]
            gk = 0 if which == 1 else 2
            for blk in range(6):
                nch = 4 if blk < 5 else 2
                c0 = blk * 512
                ncol = nch * 128
                gw, gkey = ring_get(kcp(wg[:, c0:c0 + ncol]), 8, ncol)
                uw, ukey = ring_get(kcp(wu[:, c0:c0 + ncol]), 8, ncol)
                dw, dkey = ring_get(wd[c0:c0 + ncol, :].rearrange("(j p) n -> p j n", p=128), nch, 1024)
                for t in range(2):
                    hs = slice(t * 512, (t + 1) * 512)
                    aslots = []
                    for j in range(nch):
                        ai = rot('act', 8)
                        aslots.append(ai)
                        pg = newps()
                        for kc in range(8):
                            MM(ps[pg][:, :], gw[:, kc, j * 128:(j + 1) * 128], hT[:, kc, hs], kc == 0, kc == 7,
                               [gkey, ('h', kc, t)], [('ps', pg)])
                        pu = newps()
                        for kc in range(8):
                            MM(ps[pu][:, :], uw[:, kc, j * 128:(j + 1) * 128], hT[:, kc, hs], kc == 0, kc == 7,
                               [ukey, ('h', kc, t)], [('ps', pu)])
                        ti = newtmp()
                        ACT(tmp[ti][:, :], ps[pg][:, :], AF.Silu, [('ps', pg)], [('tmp', ti)])
                        TT(actb[ai][:, :], tmp[ti][:, :], ps[pu][:, :], ALU.mult, [('tmp', ti), ('ps', pu)], [('act', ai)])
                    for dc in range(8):
                        po = newps()
                        for j in range(nch):
                            MM(ps[po][:, :], dw[:, j, dc * 128:(dc + 1) * 128], actb[aslots[j]][:, :], j == 0, j == nch - 1,
                               [dkey, ('act', aslots[j])], [('ps', po)])
                        STT(xT[:, dc, hs], ps[po][:, :], modG[par][:, gk, dc:dc + 1], xT[:, dc, hs], ALU.mult, ALU.add,
                            [('ps', po), ('modG', par), ('x', dc, t)], [('x', dc, t)])
                if modgen is not None:
                    next(modgen, None)

        def rope_norm(psraw, pskey, p0, p1, ones_lhsT, inv_d, gain, out16, outkeys, rope=None, store=None, n=512):
            si = newsq()
            ACT(sqb[si][p0:p1, 0:n], psraw, AF.Square, [pskey], [('sq', si)])
            pss = newps()
            MM(ps[pss][p0:p1, 0:n], ones_lhsT, sqb[si][p0:p1, 0:n], True, True, [('sq', si), C], [('ps', pss)])
            ri = rstd_from(ps[pss][p0:p1, 0:n], ('ps', pss), inv_d, p0, p1, n)
            r = tmp[ri][p0:p1, 0:n]
            if rope is None and store is None:
                STT(out16, psraw, gain, r, ALU.mult, ALU.mult, [pskey, ('tmp', ri), C], outkeys)
                return
            xi = newtmp()
            xn = tmp[xi][p0:p1, 0:n]
            STT(xn, psraw, gain, r, ALU.mult, ALU.mult, [pskey, ('tmp', ri), C], [('tmp', xi)])
            if rope is None:
                STORE(store, xn, [('tmp', xi)])
                ACT(out16, xn, AF.Copy, [('tmp', xi)], outkeys)
                return
            perm_lhsT, ropeC, ropeS = rope
            pp = newps()
            MM(ps[pp][p0:p1, 0:n], perm_lhsT, xn, True, True, [('tmp', xi), C], [('ps', pp)])
            t1 = newtmp()
            TT(tmp[t1][p0:p1, 0:n], xn, ropeC, ALU.mult, [('tmp', xi), C], [('tmp', t1)])
            t2 = newtmp()
            TT(tmp[t2][p0:p1, 0:n], ps[pp][p0:p1, 0:n], ropeS, ALU.mult, [('ps', pp), C], [('tmp', t2)])
            if store is None:
                TT(out16, tmp[t1][p0:p1, 0:n], tmp[t2][p0:p1, 0:n], ALU.add, [('tmp', t1), ('tmp', t2)], outkeys)
            else:
                TT(tmp[t1][p0:p1, 0:n], tmp[t1][p0:p1, 0:n], tmp[t2][p0:p1, 0:n], ALU.add, [('tmp', t1), ('tmp', t2)], [('tmp', t1)])
                STORE(store, tmp[t1][p0:p1, 0:n], [('tmp', t1)])
                ACT(out16, tmp[t1][p0:p1, 0:n], AF.Copy, [('tmp', t1)], outkeys)

        def mixer(l, modgen):
            par = l % 2

            def gv(col, p0=0, p1=128):
                return gvec[p0:p1, l, col:col + 1]

            def step():
                if modgen is not None:
                    next(modgen, None)

            DMA('pool', wuq[:], IN['w_uq'][l].rearrange("(c p) n -> p c n", p=128), [], ['wuq'], 'cx0')
            DMA('pool', wukv[:], IN['w_ukv'][l], [], ['wukv'], 'cx1')
            DMA('pool', ckvc[:], IN['ckvT_c'][l], [], ['ckvc'], 'cx2')
            DMA('sp', krc[64:96, :], IN['krT_c'][l], [], ['krc'], 'cx3')
            DMA('pool', KB[:, 1024:1280], IN['winkT_c'][l], [], [('KB', 'c')], 'cx4')
            DMA('pool', KD[:, :, 1024:1280], IN['nakT_c'][l].rearrange("(c p) k -> p c k", p=128), [], [('KD', 0, 'c'), ('KD', 1, 'c')], 'cx5')
            for i in range(2):
                DMA('pool', VB[:, 8 + i, :, 0:64], IN['winv_c'][l][i * 128:(i + 1) * 128, :].rearrange("p (h d) -> p h d", d=64),
                    [], [('VB', 8 + i)], 'cx6')
                DMA('pool', VD[:, 8 + i, :, 0:64], IN['nav_c'][l][i * 128:(i + 1) * 128, :].rearrange("p (h d) -> p h d", d=64),
                    [], [('VD', 8 + i)], 'cx7')
            ACT(sinkexp[:, :], gvec[:, l, 9:13], AF.Exp, [C], ['sinkexp'])
            wukv_v = wukv[:, 0:512].rearrange("p (h x) -> p h x", x=128)[:, :, 64:128]

            b0, b0k = ring_get(kcp(IN['w_in'][l][:, 0:416]), 8, 416)
            for t in range(2):
                hs = slice(t * 512, (t + 1) * 512)
                pcs = []
                for c in range(2):
                    pi = newps()
                    pcs.append(pi)
                    for kc in range(8):
                        MM(ps[pi][:, :], b0[:, kc, c * 128:(c + 1) * 128], hT[:, kc, hs], kc == 0, kc == 7, [b0k, ('h', kc, t)], [('ps', pi)])
                pss = newps()
                for c in range(2):
                    si = newsq()
                    ACT(sqb[si][:, :], ps[pcs[c]][:, :], AF.Square, [('ps', pcs[c])], [('sq', si)])
                    MM(ps[pss][:, :], onesb[:, :], sqb[si][:, :], c == 0, c == 1, [('sq', si), C], [('ps', pss)])
                ri = rstd_from(ps[pss][:, :], ('ps', pss), 1.0 / 256)
                for c in range(2):
                    STT(cqn[:, c, :], ps[pcs[c]][:, :], gv(c), tmp[ri][:, :], ALU.mult, ALU.mult,
                        [('ps', pcs[c]), ('tmp', ri), C], [('cqn', c)])
                pk = newps()
                for kc in range(8):
                    MM(ps[pk][:, :], b0[:, kc, 256:384], hT[:, kc, hs], kc == 0, kc == 7, [b0k, ('h', kc, t)], [('ps', pk)])
                rope_norm(ps[pk][:, :], ('ps', pk), 0, 128, onesb[:, :], 1.0 / 128, gv(2), ckvn[:, hs], [('ckvn', t)],
                          rope=None, store=OUT['o_ckvT'][l][:, hs])
                pr = newps()
                for kc in range(8):
                    MM(ps[pr][64:96, :], b0[:, kc, 384:416], hT[:, kc, hs], kc == 0, kc == 7, [b0k, ('h', kc, t)], [('ps', pr)])
                ti = newtmp()
                ACT(tmp[ti][64:96, :], ps[pr][64:96, :], AF.Copy, [('ps', pr)], [('tmp', ti)])
                STORE(OUT['o_krT'][l][:, hs], tmp[ti][64:96, :], [('tmp', ti)])
                ACT(sqkr[64:96, :], ps[pr][64:96, :], AF.Square, [('ps', pr)], ['sqkr'])
                tg = newtmp()
                TS(tmp[tg][64:96, :], ps[pr][64:96, :], gv(4, 64, 96), ALU.mult, [('ps', pr), C], [('tmp', tg)])
                pp = newps()
                MM(ps[pp][64:96, :], permA[64:96, 64:96], tmp[tg][64:96, :], True, True, [('tmp', tg), C], [('ps', pp)])
                t1 = newtmp()
                TT(tmp[t1][64:96, :], tmp[tg][64:96, :], ropeA[64:96, 0, hs], ALU.mult, [('tmp', tg), C], [('tmp', t1)])
                TT(krr[64:96, :], ps[pp][64:96, :], ropeA[64:96, 1, hs], ALU.mult, [('ps', pp), C], ['krr'])
                TT(krr[64:96, :], krr[64:96, :], tmp[t1][64:96, :], ALU.add, ['krr', ('tmp', t1)], ['krr'])
                for h in range(4):
                    pn = newps()
                    MM(ps[pn][0:64, :], wukv[:, h * 128:h * 128 + 64], ckvn[:, hs], True, True, ['wukv', ('ckvn', t)], [('ps', pn)])
                    si = newsq()
                    ACT(sqb[si][0:64, :], ps[pn][0:64, :], AF.Square, [('ps', pn)], [('sq', si)])
                    pss = newps()
                    MM(ps[pss][0:96, :], onesb[0:64, 0:96], sqb[si][0:64, :], True, False, [('sq', si), C], [('ps', pss)])
                    MM(ps[pss][0:96, :], onesb[64:96, 0:96], sqkr[64:96, :], False, True, ['sqkr', C], [('ps', pss)])
                    ri = rstd_from(ps[pss][0:96, :], ('ps', pss), 1.0 / 96, 0, 96)
                    STT(KA[0:64, h, hs], ps[pn][0:64, :], gv(4, 0, 64), tmp[ri][0:64, :], ALU.mult, ALU.mult,
                        [('ps', pn), ('tmp', ri), C], [('KA', h, t)])
                    TT(KA[64:96, h, hs], krr[64:96, :], tmp[ri][64:96, :], ALU.mult, ['krr', ('tmp', ri)], [('KA', h, t)])
                    pq = newps()
                    for c in range(2):
                        MM(ps[pq][0:96, :], wuq[:, c, h * 96:(h + 1) * 96], cqn[:, c, :], c == 0, c == 1, ['wuq', ('cqn', c)], [('ps', pq)])
                    rope_norm(ps[pq][0:96, :], ('ps', pq), 0, 96, onesb[0:96, 0:96], 1.0 / 96, gv(3, 0, 96), QA[0:96, h, hs],
                              [('QA', h, t)], rope=(permA[0:96, 0:96], ropeA[0:96, 0, hs], ropeA[0:96, 1, hs]))
                for tt in range(4):
                    kt = t * 4 + tt
                    pv = newps()
                    MM(ps[pv][:, 0:256], ckvn[:, kt * 128:(kt + 1) * 128], wukv_v, True, True, ['wukv', ('ckvn', t)], [('ps', pv)])
                    ACT(VA[:, kt, :, 0:64], ps[pv][:, 0:256].rearrange("p (h d) -> p h d", d=64), AF.Copy, [('ps', pv), 'Vinit'], [('VA', kt)])
            step()
            CK('projA')
            si = newsq()
            ACT(sqb[si][64:96, 0:256], krc[64:96, :], AF.Square, ['krc'], [('sq', si)])
            sq_krc = si
            tgc = newtmp()
            TS(tmp[tgc][64:96, 0:256], krc[64:96, :], gv(4, 64, 96), ALU.mult, ['krc', C], [('tmp', tgc)])
            for h in range(4):
                pn = newps()
                MM(ps[pn][0:64, 0:256], wukv[:, h * 128:h * 128 + 64], ckvc[:, :], True, True, ['wukv', 'ckvc'], [('ps', pn)])
                si = newsq()
                if si == sq_krc:
                    si = newsq()
                ACT(sqb[si][0:64, 0:256], ps[pn][0:64, 0:256], AF.Square, [('ps', pn)], [('sq', si)])
                pss = newps()
                MM(ps[pss][0:96, 0:256], onesb[0:64, 0:96], sqb[si][0:64, 0:256], True, False, [('sq', si), C], [('ps', pss)])
                MM(ps[pss][0:96, 0:256], onesb[64:96, 0:96], sqb[sq_krc][64:96, 0:256], False, True, [('sq', sq_krc), C], [('ps', pss)])
                ri = rstd_from(ps[pss][0:96, 0:256], ('ps', pss), 1.0 / 96, 0, 96, 256)
                if ri == tgc:
                    raise RuntimeError("tmp rotation clash")
                STT(KA[0:64, h, 1024:1280], ps[pn][0:64, 0:256], gv(4, 0, 64), tmp[ri][0:64, 0:256], ALU.mult, ALU.mult,
                    [('ps', pn), ('tmp', ri), C], [('KA', h, 'c')])
                TT(KA[64:96, h, 1024:1280], tmp[tgc][64:96, 0:256], tmp[ri][64:96, 0:256], ALU.mult, [('tmp', tgc), ('tmp', ri)], [('KA', h, 'c')])
            for i in range(2):
                pv = newps()
                MM(ps[pv][:, 0:256], ckvc[:, i * 128:(i + 1) * 128], wukv_v, True, True, ['wukv', 'ckvc'], [('ps', pv)])
                ACT(VA[:, 8 + i, :, 0:64], ps[pv][:, 0:256].rearrange("p (h d) -> p h d", d=64), AF.Copy, [('ps', pv), 'Vinit'], [('VA', 8 + i)])
            step()
            CK('ctxA')

            b1, b1k = ring_get(kcp(IN['w_in'][l][:, 416:928]), 8, 512)
            for t in range(2):
                hs = slice(t * 512, (t + 1) * 512)
                for ci, hpair in enumerate(((0, 2), (1, 3))):
                    pi = newps()
                    for hh, pb in zip(hpair, (0, 64)):
                        for kc in range(8):
                            MM(ps[pi][pb:pb + 64, :], b1[:, kc, hh * 64:(hh + 1) * 64], hT[:, kc, hs], kc == 0, kc == 7,
                               [b1k, ('h', kc, t)], [('ps', pi)])
                    rope_norm(ps[pi][:, :], ('ps', pi), 0, 128, blk64b[:, :], 1.0 / 64, gv(5), QB[:, ci, hs], [('QB', ci, t)],
                              rope=(permB[:, :], ropeB[:, 0, hs], ropeB[:, 1, hs]))
                    CK('qb')
                pi = newps()
                for kc in range(8):
                    MM(ps[pi][:, :], b1[:, kc, 256:384], hT[:, kc, hs], kc == 0, kc == 7, [b1k, ('h', kc, t)], [('ps', pi)])
                rope_norm(ps[pi][:, :], ('ps', pi), 0, 128, blk64b[:, :], 1.0 / 64, gv(6), KB[:, hs], [('KB', t)],
                          rope=(permB[:, :], ropeB[:, 0, hs], ropeB[:, 1, hs]), store=OUT['o_kbT'][l][:, hs])
                CK('kb')
                for tt in range(4):
                    kt = t * 4 + tt
                    pv = newps()
                    for kc in range(8):
                        MM(ps[pv][:, 0:128], hT[:, kc, kt * 128:(kt + 1) * 128], b1[:, kc, 384:512], kc == 0, kc == 7,
                           [b1k, ('h', kc, t)], [('ps', pv)])
                    ti = newtmp()
                    ACT(tmp[ti][:, 0:128], ps[pv][:, 0:128], AF.Copy, [('ps', pv)], [('tmp', ti)])
                    STORE(OUT['o_vb'][l][kt * 128:(kt + 1) * 128, :], tmp[ti][:, 0:128], [('tmp', ti)])
                    ACT(VB[:, kt, :, 0:64], ps[pv][:, 0:128].rearrange("p (h d) -> p h d", d=64), AF.Copy, [('ps', pv), 'Vinit'], [('VB', kt)])
            step()
            CK('projB')
            b2, b2k = ring_get(kcp(IN['w_in'][l][:, 928:1440]), 8, 512)
            for t in range(2):
                hs = slice(t * 512, (t + 1) * 512)
                for ch in range(2):
                    pi = newps()
                    for kc in range(8):
                        MM(ps[pi][:, :], b2[:, kc, ch * 128:(ch + 1) * 128], hT[:, kc, hs], kc == 0, kc == 7, [b2k, ('h', kc, t)], [('ps', pi)])
                    ACT(XcT[:, ch, hs], ps[pi][:, :], AF.Copy, [('ps', pi)], [('pt', 8 + 2 * ch + t)])
                for ch in range(2):
                    pi = newps()
                    for kc in range(8):
                        MM(ps[pi][:, :], b2[:, kc, 256 + ch * 128:256 + (ch + 1) * 128], hT[:, kc, hs], kc == 0, kc == 7,
                           [b2k, ('h', kc, t)], [('ps', pi)])
                    rope_norm(ps[pi][:, :], ('ps', pi), 0, 128, blk64b[:, :], 1.0 / 64, gv(7), QD[:, ch, hs], [('QD', ch, t)])
            step()
            b3, b3k = ring_get(kcp(IN['w_in'][l][:, 1440:1952]), 8, 512)
            for t in range(2):
                hs = slice(t * 512, (t + 1) * 512)
                for ch in range(2):
                    pi = newps()
                    for kc in range(8):
                        MM(ps[pi][:, :], b3[:, kc, ch * 128:(ch + 1) * 128], hT[:, kc, hs], kc == 0, kc == 7, [b3k, ('h', kc, t)], [('ps', pi)])
                    rope_norm(ps[pi][:, :], ('ps', pi), 0, 128, blk64b[:, :], 1.0 / 64, gv(8), KD[:, ch, hs], [('KD', ch, t)],
                              rope=None, store=OUT['o_kdT'][l][ch * 128:(ch + 1) * 128, hs])
                for tt in range(4):
                    kt = t * 4 + tt
                    pv = newps()
                    for kc in range(8):
                        MM(ps[pv][:, 0:256], hT[:, kc, kt * 128:(kt + 1) * 128], b3[:, kc, 256:512], kc == 0, kc == 7,
                           [b3k, ('h', kc, t)], [('ps', pv)])
                    ti = newtmp()
                    ACT(tmp[ti][:, 0:256], ps[pv][:, 0:256], AF.Copy, [('ps', pv)], [('tmp', ti)])
                    STORE(OUT['o_vd'][l][kt * 128:(kt + 1) * 128, :], tmp[ti][:, 0:256], [('tmp', ti)])
                    ACT(VD[:, kt, :, 0:64], ps[pv][:, 0:256].rearrange("p (h d) -> p h d", d=64), AF.Copy, [('ps', pv), 'Vinit'], [('VD', kt)])
            step()
            CK('projD')
            for lt in range(8):
                for ch in range(2):
                    pi = newps()
                    MM(ps[pi][:, 0:256], XcT[:, ch, lt * 128:(lt + 1) * 128], cs64[:, :], True, True, [('pt', 8 + 2 * ch + lt // 4), C], [('ps', pi)])
                    P.add('dve', lambda e, lt=lt, ch=ch, pi=pi: e.tensor_copy(out=ABt[:, lt, ch, :], in_=ps[pi][:, 0:256]),
                          [('ps', pi)], [('ab', lt, ch)])
            for t in range(2):
                hs = slice(t * 512, (t + 1) * 512)
                cb, cbk = ring_get(IN['dftC'][:, hs].rearrange("(lt p) n -> p lt n", p=128), 8, 512)
                sbk_, sbkk = ring_get(IN['dftS'][:, hs].rearrange("(lt p) n -> p lt n", p=128), 8, 512)
                for ch in range(2):
                    pi = newps()
                    for lt in range(8):
                        MM(ps[pi][:, :], ABt[:, lt, ch, 0:128], cb[:, lt, :], lt == 0, False, [('ab', lt, ch), ('pt', lt), cbk], [('ps', pi)])
                    for lt in range(8):
                        MM(ps[pi][:, :], ABt[:, lt, ch, 128:256], sbk_[:, lt, :], False, lt == 7, [('ab', lt, ch), ('pt', lt), sbkk], [('ps', pi)])
                    ACT(hT[:, 4 + ch, hs], ps[pi][:, :], AF.Copy, [('ps', pi)], [('h', 4 + ch, t)])
            step()
            CK('fourier')

            def attend(kind, h, t, bias_views=None):
                hs = slice(t * 512, (t + 1) * 512)
                if kind == 'B':
                    kts = [kt for kt in range(4 * t - 1, 4 * t + 5) if 0 <= kt <= 7] + [8, 9]
                else:
                    kts = list(range(10))
                slots = {}
                for kt in kts:
                    pi = newps()
                    ksl = slice(kt * 128, (kt + 1) * 128)
                    own = kt < 8
                    kk = (kt // 4) if own else 'c'
                    if kind == 'A':
                        MM(ps[pi][:, :], KA[0:96, h, ksl], QA[0:96, h, hs], True, not own, [('KA', h, kk), ('QA', h, t)], [('ps', pi)])
                        if own:
                            MM(ps[pi][:, :], EA[0:8, ksl], FA[0:8, hs], False, True, [C], [('ps', pi)])
                        scale = 96.0 ** -0.5
                    elif kind == 'B':
                        pb = (h // 2) * 64
                        ci = h % 2
                        MM(ps[pi][:, :], KB[pb:pb + 64, ksl], QB[pb:pb + 64, ci, hs], True, not own, [('KB', kk), ('QB', ci, t)], [('ps', pi)])
                        if own:
                            MM(ps[pi][:, :], identb[:, :], maskB[:, kt - 4 * t + 1, :], False, True, [C], [('ps', pi)])
                        scale = 0.125
                    else:
                        pb = (h % 2) * 64
                        ch = h // 2
                        MM(ps[pi][:, :], KD[pb:pb + 64, ch, ksl], QD[pb:pb + 64, ch, hs], True, not own, [('KD', ch, kk), ('QD', ch, t)], [('ps', pi)])
                        if own:
                            bv, bk = bias_views[kt // 4]
                            MM(ps[pi][:, :], identb[:, :], bv[:, kt % 4, hs], False, True, [C, bk], [('ps', pi)])
                        scale = 0.125
                    s = rot('pt', NPT)
                    slots[kt] = s
                    ACT(PT[:, s, :], ps[pi][:, :], AF.Exp, [('ps', pi)], [('pt', s)], scale=scale)
                V, vh, vname = {'A': (VA, h, 'VA'), 'B': (VB, h // 2, 'VB'), 'D': (VD, h, 'VD')}[kind]
                po = newps()
                for qi in range(4):
                    for i, kt in enumerate(kts):
                        MM(ps[po][:, qi * 65:(qi + 1) * 65], PT[:, slots[kt], qi * 128:(qi + 1) * 128], V[:, kt, vh, 0:65],
                           i == 0, i == len(kts) - 1, [('pt', slots[kt]), (vname, kt), 'Vinit'], [('ps', po)])
                for qi in range(4):
                    den = ps[po][:, qi * 65 + 64:qi * 65 + 65]
                    if kind == 'B':
                        TS(rec[:, qi:qi + 1], den, sinkexp[:, h:h + 1], ALU.add, [('ps', po), 'sinkexp'], [('rec', qi)])
                        RECIP(rec[:, qi:qi + 1], rec[:, qi:qi + 1], [('rec', qi)], [('rec', qi)])
                    else:
                        RECIP(rec[:, qi:qi + 1], den, [('ps', po)], [('rec', qi)])
                    hslot = h % 2
                    TS(mtok[:, t, qi, hslot * 64:(hslot + 1) * 64], ps[po][:, qi * 65:qi * 65 + 64], rec[:, qi:qi + 1], ALU.mult,
                       [('ps', po), ('rec', qi)], [('mtok', t, qi, hslot)])

            for kind, chunk0 in (('A', 0), ('B', 2), ('D', 6)):
                for pair in range(2):
                    for h in (2 * pair, 2 * pair + 1):
                        bvs = None
                        if kind == 'D':
                            bvs = [ring_get(IN['biasD'][l][h][g * 512:(g + 1) * 512, :].rearrange("(kt p) q -> p kt q", p=128), 4, 1024)
                                   for g in range(2)]
                        for t in range(2):
                            attend(kind, h, t, bvs)
                    for t in range(2):
                        hs = slice(t * 512, (t + 1) * 512)
                        pt_ = newps()
                        for qi in range(4):
                            MM(ps[pt_][:, qi * 128:(qi + 1) * 128], mtok[:, t, qi, :], identb[:, :], True, True,
                               [('mtok', t, qi, 0), ('mtok', t, qi, 1), C], [('ps', pt_)])
                        P.add('dve', lambda e, t=t, hs=hs, pt_=pt_, cc=chunk0 + pair: e.tensor_copy(out=hT[:, cc, hs], in_=ps[pt_][:, :]),
                              [('ps', pt_)], [('h', chunk0 + pair, t)])
                    step()
                CK('attn' + kind)
            for blk in range(2):
                wo, wok = ring_get(kcp(IN['w_o'][l][:, blk * 512:(blk + 1) * 512]), 8, 512)
                for dcc in range(4):
                    dc = blk * 4 + dcc
                    for t in range(2):
                        hs = slice(t * 512, (t + 1) * 512)
                        po = newps()
                        for c in range(8):
                            MM(ps[po][:, :], wo[:, c, dcc * 128:(dcc + 1) * 128], hT[:, c, hs], c == 0, c == 7, [wok, ('h', c, t)], [('ps', po)])
                        STT(xT[:, dc, hs], ps[po][:, :], modG[par][:, 1, dc:dc + 1], xT[:, dc, hs], ALU.mult, ALU.add,
                            [('ps', po), ('modG', par), ('x', dc, t)], [('x', dc, t)])
            step()

        try:
            CK('load')
            g0 = mod_steps(0)
            for _ in g0:
                pass
            CK('mod')
            for l in range(nl):
                mg = mod_steps(l + 1) if l + 1 < nl else None
                norm_mod(l, 0)
                CK('norm0')
                ffn(l, 1, mg)
                CK('ffn1')
                norm_mod(l, 1)
                mixer(l, mg)
                CK('mixer')
                norm_mod(l, 2)
                ffn(l, 2, mg)
                if mg is not None:
                    for _ in mg:
                        pass
        except _Stop:
            pass
        STORE(kcp(OUT['yT']), xT[:], xkeys)
        P.add('sp', None, extra_deps=out_ids)
        P.emit()
        nc._prog_stats = (P.n_ops, P.sig_counts, P.chan_counts)
    return nc


def _rope_tables(r, sample):
    Cm = np.ones((r, T), np.float32)
    Sm = np.zeros((r, T), np.float32)
    if not sample:
        return Cm, Sm
    half = r // 2
    t = np.arange(T)
    inv = (10000.0 ** (-np.arange(0, half, 2, dtype=np.float32) / np.float32(half))).astype(np.float32)
    q = half // 2
    for part, pos in ((0, t // 64), (1, t % 64)):
        ang = pos.astype(np.float32)[:, None] * inv[None, :]
        cos = np.cos(ang).astype(np.float32).T
        sin = np.sin(ang).astype(np.float32).T
        b = part * half
        Cm[b:b + q] = cos
        Cm[b + q:b + half] = cos
        Sm[b:b + q] = -sin
        Sm[b + q:b + half] = sin
    return Cm, Sm


def _perm(r):
    M = np.zeros((r, r), np.float32)
    half = r // 2
    q = half // 2
    for d in range(r):
        o = d % half
        p = d + q if o < q else d - q
        M[p, d] = 1.0
    return M


def _consts(sample):
    c = {}
    c['identb'] = np.eye(128, dtype=np.float32).astype(bf16)
    c['onesb'] = np.ones((128, 128), np.float32).astype(bf16)
    c['blk64b'] = np.kron(np.eye(2, dtype=np.float32), np.ones((64, 64), np.float32)).astype(bf16)
    pb = np.zeros((128, 128), np.float32)
    pb[0:64, 0:64] = _perm(64)
    pb[64:128, 64:128] = _perm(64)
    c['permB'] = pb
    pa = np.zeros((128, 128), np.float32)
    pa[64:96, 64:96] = _perm(32)
    c['permA'] = pa
    Cb, Sb = _rope_tables(64, sample)
    rb = np.zeros((128, 2, T), np.float32)
    rb[0:64, 0], rb[64:128, 0] = Cb, Cb
    rb[0:64, 1], rb[64:128, 1] = Sb, Sb
    c['ropeB'] = rb
    Ca, Sa = _rope_tables(32, sample)
    ra = np.zeros((128, 2, T), np.float32)
    ra[:, 0] = 1.0
    ra[64:96, 0] = Ca
    ra[64:96, 1] = Sa
    c['ropeA'] = ra
    k = np.arange(64)
    ang = 2 * np.pi * np.outer(k, k) / 64.0
    C64 = np.cos(ang) / 8.0
    S64 = np.sin(ang) / 8.0
    cs = np.zeros((128, 256), np.float64)
    cs[0:64, 0:64] = C64
    cs[64:128, 64:128] = C64
    cs[0:64, 128:192] = S64
    cs[64:128, 192:256] = S64
    c['cs64'] = cs.astype(np.float32).astype(bf16)
    L = 1024 if sample else 256
    kk = np.arange(L)
    angL = 2 * np.pi * (np.outer(kk, kk) % L) / float(L)
    CL = np.cos(angL) / np.sqrt(L)
    SL = -np.sin(angL) / np.sqrt(L)
    if sample:
        dC, dS = CL, SL
    else:
        dC = np.kron(np.eye(4), CL)
        dS = np.kron(np.eye(4), SL)
    c['dftC'] = dC.astype(np.float32).astype(bf16)
    c['dftS'] = dS.astype(np.float32).astype(bf16)
    ea = np.zeros((8, 1280), np.float32)
    fa = np.zeros((8, 1024), np.float32)
    if not sample:
        for s in range(4):
            ea[s, s * 256:(s + 1) * 256] = 1.0
            fa[s, s * 256:(s + 1) * 256] = BIG
        ea[4, 0:1024] = 1.0
        fa[4, :] = -BIG
    c['EA'] = ea.astype(bf16)
    c['FA'] = fa.astype(bf16)
    mb = np.full((128, 6, 512), NEG, np.float32)
    kl = np.arange(128)[:, None]
    ql = np.arange(128)[None, :]
    for ki in range(6):
        ktp = ki - 1
        for qi in range(4):
            off = ktp - qi
            blkm = np.full((128, 128), NEG, np.float32)
            if sample:
                if off == 0:
                    blkm[:] = 0.0
                elif off == -1:
                    blkm = np.where(ql <= kl, 0.0, NEG).astype(np.float32)
                elif off == 1:
                    blkm = np.where(kl <= ql, 0.0, NEG).astype(np.float32)
            else:
                if off == 0 or (off == 1 and qi % 2 == 0) or (off == -1 and qi % 2 == 1):
                    blkm[:] = 0.0
            mb[:, ki, qi * 128:(qi + 1) * 128] = blkm
    c['maskB'] = mb.astype(bf16)
    return c


def _na_index():
    rows = 16
    r = np.arange(rows)
    row_start = np.clip(r - 4, 0, rows - 8)
    col = np.arange(64)
    col_start = np.clip(col - 8, 0, 64 - 16)
    q = np.arange(1024)
    qr, qc = q // 64, q % 64
    kr, kc = qr, qc
    KR, QR = kr[:, None], qr[None, :]
    KC, QC = kc[:, None], qc[None, :]
    valid = (KR >= row_start[QR]) & (KR < row_start[QR] + 8) & (KC >= col_start[QC]) & (KC < col_start[QC] + 16)
    dr = np.clip(KR - QR + 7, 0, 14)
    dc = np.clip(KC - QC + 15, 0, 30)
    return valid, dr, dc


def _vecT(v):
    return np.ascontiguousarray(v.reshape(-1, 128).T)


_PROG = {}


def _prep(x_prompt, x_sample, cache_mla_ckv, cache_mla_krope, cache_win_k, cache_win_v, cache_na_k, cache_na_v,
           c, c_ctx, w_ada, b_ada, g_ffn1, w_gate1, w_up1, w_down1, g_mix, w_in, g_qa, w_uq, g_kva, w_ukv,
           qn_a, kn_a, qn_b, kn_b, sink_b, qn_d, kn_d, rpb_d, w_o, g_ffn2, w_gate2, w_up2, w_down2):
    f = lambda a: np.ascontiguousarray(np.asarray(a, dtype=np.float32))
    x_prompt, x_sample, c, c_ctx = f(x_prompt), f(x_sample), f(c), f(c_ctx)
    shared = {n: f(v) for n, v in dict(w_ada=w_ada, w_gate1=w_gate1, w_up1=w_up1, w_down1=w_down1, w_gate2=w_gate2,
                                       w_up2=w_up2, w_down2=w_down2, w_in=w_in, w_o=w_o, w_uq=w_uq, w_ukv=w_ukv).items()}
    b_ada = f(b_ada)
    shared['b_adaT'] = np.ascontiguousarray(b_ada.reshape(NL, 72, 128).transpose(0, 2, 1))
    gs = np.stack([f(g_ffn1), f(g_mix), f(g_ffn2)], axis=1)
    shared['gT'] = np.ascontiguousarray(gs.reshape(NL, 3, 8, 128).transpose(0, 3, 1, 2))
    gv = np.zeros((NL, 128, NG), np.float32)
    gv[:, :, 0:2] = f(g_qa).reshape(NL, 2, 128).transpose(0, 2, 1)
    gv[:, :, 2] = f(g_kva)
    gv[:, 0:96, 3] = f(qn_a)
    gv[:, 0:96, 4] = f(kn_a)
    gv[:, :, 5] = np.tile(f(qn_b), (1, 2))
    gv[:, :, 6] = np.tile(f(kn_b), (1, 2))
    gv[:, :, 7] = np.tile(f(qn_d), (1, 2))
    gv[:, :, 8] = np.tile(f(kn_d), (1, 2))
    gv[:, :, 9:13] = f(sink_b)[:, None, :]
    shared['gvec'] = gv
    consts = {True: _consts(True), False: _consts(False)}
    valid, dr, dc = _na_index()
    rpb = f(rpb_d)
    bias_s = np.where(valid[None, None], rpb[:, :, dr, dc], np.float32(NEG)).astype(np.float32)
    seq = np.arange(1024) // 256
    bias_p1 = np.where(seq[:, None] == seq[None, :], np.float32(0.0), np.float32(NEG)).astype(np.float32)
    bias_p = np.ascontiguousarray(np.broadcast_to(bias_p1, (NL, 4, 1024, 1024)))
    cm_ckv, cm_kr = f(cache_mla_ckv), f(cache_mla_krope)
    cw_k, cw_v, cn_k, cn_v = f(cache_win_k), f(cache_win_v), f(cache_na_k), f(cache_na_v)
    in_maps = []
    for core in range(8):
        sample = core >= 4
        m = dict(shared)
        m.update(consts[sample])
        if sample:
            b = core - 4
            xs = x_sample[b]
            cond = c[b]
            m['ckvT_c'] = np.ascontiguousarray(cm_ckv[b].transpose(0, 2, 1))
            m['krT_c'] = np.ascontiguousarray(cm_kr[b].transpose(0, 2, 1))
            m['winkT_c'] = np.ascontiguousarray(cw_k[b].reshape(NL, 256, 128).transpose(0, 2, 1))
            m['winv_c'] = np.ascontiguousarray(cw_v[b].reshape(NL, 256, 128))
            m['nakT_c'] = np.ascontiguousarray(cn_k[b].reshape(NL, 256, 256).transpose(0, 2, 1))
            m['nav_c'] = np.ascontiguousarray(cn_v[b].reshape(NL, 256, 256))
            m['biasD'] = bias_s
            m['ctxflag'] = np.ones((128, 1), np.float32)
        else:
            xs = x_prompt[4 * core:4 * core + 4].reshape(1024, 1024)
            cond = c_ctx
            m['ckvT_c'] = np.zeros((NL, 128, 256), np.float32)
            m['krT_c'] = np.zeros((NL, 32, 256), np.float32)
            m['winkT_c'] = np.zeros((NL, 128, 256), np.float32)
            m['winv_c'] = np.zeros((NL, 256, 128), np.float32)
            m['nakT_c'] = np.zeros((NL, 256, 256), np.float32)
            m['nav_c'] = np.zeros((NL, 256, 256), np.float32)
            m['biasD'] = bias_p
            m['ctxflag'] = np.zeros((128, 1), np.float32)
        m['xT'] = np.ascontiguousarray(xs.T)
        m['cond'] = _vecT(cond)
        in_maps.append(m)
    return in_maps


def _assemble(R):
    y_prompt = np.concatenate([R[i]['yT'].T.reshape(4, 256, 1024) for i in range(4)], axis=0)
    y_sample = np.stack([R[4 + b]['yT'].T for b in range(4)], axis=0)

    def featmaj(name, feat):
        outs = []
        for i in range(4):
            a = R[i][name]
            a = a.reshape(NL, feat, 4, 256).transpose(2, 0, 3, 1)
            outs.append(a)
        return np.ascontiguousarray(np.concatenate(outs, axis=0))

    def tokmaj(name, feat):
        outs = []
        for i in range(4):
            a = R[i][name].reshape(NL, 4, 256, feat).transpose(1, 0, 2, 3)
            outs.append(a)
        return np.ascontiguousarray(np.concatenate(outs, axis=0))

    new_ckv = featmaj('o_ckvT', 128)
    new_kr = featmaj('o_krT', 32)
    new_wk = featmaj('o_kbT', 128).reshape(16, NL, 256, 2, 64)
    new_wv = tokmaj('o_vb', 128).reshape(16, NL, 256, 2, 64)
    new_nk = featmaj('o_kdT', 256).reshape(16, NL, 256, 4, 64)
    new_nv = tokmaj('o_vd', 256).reshape(16, NL, 256, 4, 64)
    return (np.ascontiguousarray(y_prompt.astype(np.float32)), np.ascontiguousarray(y_sample.astype(np.float32)),
            new_ckv, new_kr, new_wk, new_wv, new_nk, new_nv)


def kernel(**inputs):
    in_maps = _prep(**inputs)
    if 'nc' not in _PROG:
        _PROG['nc'] = build_program(NL)
    res = run_bass_kernel_spmd(_PROG['nc'], in_maps, core_ids=list(range(8)))
    return _assemble(res.results)
```

```python
import numpy as np
import ml_dtypes
from contextlib import ExitStack
import concourse.bass as bass
import concourse.mybir as mybir
from concourse.bass_utils import run_bass_kernel_spmd

F32 = mybir.dt.float32
BF16 = mybir.dt.bfloat16
AF = mybir.ActivationFunctionType
ALU = mybir.AluOpType
bf16 = ml_dtypes.bfloat16

NL = 4
D = 1024
T = 1024
DFF = 2816
EPS = 1e-6
NEG = -30000.0
BIG = 32768.0
RING_ELEMS = 4096
NSLOT = 4
NTMP = 6
NPT = 22
NG = 13


class Prog:
    COMPUTE = ('pe', 'act', 'dve', 'pool')

    def __init__(self, nc, es, same_engine_sync=True):
        self.nc = nc
        self.es = es
        self.ops = []
        self.lastw = {}
        self.readers = {}
        self.chan_last = {}
        self.chan_cnt = {}
        self.same_engine_sync = same_engine_sync

    def add(self, eng, fn, reads=(), writes=(), chan=None, extra_deps=()):
        idx = len(self.ops)
        deps = set(extra_deps)
        for k in reads:
            w = self.lastw.get(k)
            if w is not None:
                deps.add(w)
        for k in writes:
            w = self.lastw.get(k)
            if w is not None:
                deps.add(w)
            last = {}
            for r in self.readers.get(k, ()):
                rop = self.ops[r]
                if rop['chan'] is not None:
                    deps.add(r)
                else:
                    last[rop['eng']] = r
            deps.update(last.values())
        for k in reads:
            self.readers.setdefault(k, []).append(idx)
        for k in writes:
            self.lastw[k] = idx
            self.readers[k] = []
        if chan is not None:
            p = self.chan_last.get(chan)
            if p is not None:
                deps.add(p)
            self.chan_last[chan] = idx
            self.chan_cnt[chan] = self.chan_cnt.get(chan, 0) + 1
        deps.discard(idx)
        self.ops.append(dict(eng=eng, fn=fn, deps=deps, chan=chan, signal=False, sigval=None,
                             chanval=(16 * self.chan_cnt[chan] if chan is not None else None)))
        return idx

    def emit(self):
        nc = self.nc
        ops = self.ops
        for op in ops:
            for d in op['deps']:
                dop = ops[d]
                if dop['chan'] is None:
                    if dop['eng'] == op['eng'] and (op['eng'] == 'pe' or not self.same_engine_sync):
                        continue
                    dop['signal'] = True
        cnt = {}
        for op in ops:
            if op['chan'] is None and op['signal']:
                cnt[op['eng']] = cnt.get(op['eng'], 0) + 1
                op['sigval'] = cnt[op['eng']]
        self.sig_counts = dict(cnt)
        self.chan_counts = dict(self.chan_cnt)
        self.n_ops = len(ops)
        sems = {e: self.es.enter_context(nc.semaphore('s_' + e)) for e in self.COMPUTE + ('sp',)}
        csems = {c: self.es.enter_context(nc.semaphore('c_%d' % i)) for i, c in enumerate(self.chan_cnt)}
        by_eng = {}
        for i, op in enumerate(ops):
            by_eng.setdefault(op['eng'], []).append(i)

        def run_engine(ename, e):
            waited = {}
            for i in by_eng.get(ename, ()):
                op = ops[i]
                need = {}
                for d in op['deps']:
                    dop = ops[d]
                    if dop['chan'] is not None:
                        key = ('c', dop['chan'])
                        val = dop['chanval']
                    else:
                        if dop['eng'] == ename and (ename == 'pe' or not self.same_engine_sync):
                            continue
                        key = ('e', dop['eng'])
                        val = dop['sigval']
                    if val > need.get(key, 0):
                        need[key] = val
                for key, val in need.items():
                    if waited.get(key, 0) >= val:
                        continue
                    waited[key] = val
                    s = csems[key[1]] if key[0] == 'c' else sems[key[1]]
                    e.wait_ge(s, val)
                if op['fn'] is None:
                    continue
                ins = op['fn'](e)
                if op['chan'] is not None:
                    ins.then_inc(csems[op['chan']], 16)
                elif op['signal']:
                    ins.then_inc(sems[ename], 1)

        with nc.Block() as block:
            @block.tensor
            def _(e):
                run_engine('pe', e)

            @block.scalar
            def _(e):
                run_engine('act', e)

            @block.vector
            def _(e):
                run_engine('dve', e)

            @block.gpsimd
            def _(e):
                run_engine('pool', e)

            @block.sync
            def _(e):
                run_engine('sp', e)


INPUT_SPECS = [
    ('xT', [1024, 1024], F32), ('cond', [128, 8], F32), ('ctxflag', [128, 1], F32),
    ('w_ada', [NL, 1024, 9216], F32), ('b_adaT', [NL, 128, 72], F32), ('gT', [NL, 128, 3, 8], F32),
    ('w_gate1', [NL, 1024, DFF], F32), ('w_up1', [NL, 1024, DFF], F32), ('w_down1', [NL, DFF, 1024], F32),
    ('w_gate2', [NL, 1024, DFF], F32), ('w_up2', [NL, 1024, DFF], F32), ('w_down2', [NL, DFF, 1024], F32),
    ('w_in', [NL, 1024, 1952], F32), ('w_o', [NL, 1024, 1024], F32),
    ('w_uq', [NL, 256, 384], F32), ('w_ukv', [NL, 128, 512], F32), ('gvec', [NL, 128, NG], F32),
    ('ckvT_c', [NL, 128, 256], F32), ('krT_c', [NL, 32, 256], F32), ('winkT_c', [NL, 128, 256], F32),
    ('winv_c', [NL, 256, 128], F32), ('nakT_c', [NL, 256, 256], F32), ('nav_c', [NL, 256, 256], F32),
    ('identb', [128, 128], BF16), ('onesb', [128, 128], BF16), ('blk64b', [128, 128], BF16),
    ('permB', [128, 128], F32), ('permA', [128, 128], F32),
    ('ropeB', [128, 2, 1024], F32), ('ropeA', [128, 2, 1024], F32),
    ('cs64', [128, 256], BF16), ('dftC', [1024, 1024], BF16), ('dftS', [1024, 1024], BF16),
    ('EFA', [8, 2304], BF16), ('maskB', [128, 6, 512], BF16),
    ('biasD', [NL, 4, 1024, 1024], F32),
]
OUTPUT_SPECS = [
    ('yT', [1024, 1024]), ('o_ckvT', [NL, 128, 1024]), ('o_krT', [NL, 32, 1024]), ('o_kbT', [NL, 128, 1024]),
    ('o_vb', [NL, 1024, 128]), ('o_kdT', [NL, 256, 1024]), ('o_vd', [NL, 1024, 256]),
]


class _Stop(Exception):
    pass


def build_program(nl=NL, stop=None):
    nc = bass.Bass("TRN2", target_bir_lowering=False)
    IN = {n: nc.dram_tensor(n, list(s), dt, kind="ExternalInput").ap() for n, s, dt in INPUT_SPECS}
    OUT = {n: nc.dram_tensor(n, list(s), F32, kind="ExternalOutput").ap() for n, s in OUTPUT_SPECS}
    es = ExitStack()
    with es:
        P = Prog(nc, es)

        def sb(name, shape, dt):
            return es.enter_context(nc.sbuf_tensor(name, list(shape), dt))

        xT = sb('xT_sb', [128, 8, 1024], F32)
        hT = sb('hT', [128, 8, 1024], BF16)
        ring = [sb('ring%d' % i, [128, RING_ELEMS], BF16) for i in range(NSLOT)]
        actb = [sb('actb%d' % i, [128, 512], BF16) for i in range(8)]
        rstdN = sb('rstdN', [128, 512], F32)
        tmp = [sb('tmp%d' % i, [128, 512], F32) for i in range(NTMP)]
        sqb = [sb('sqb%d' % i, [128, 512], BF16) for i in range(3)]
        ps = [es.enter_context(nc.psum_tensor('ps%d' % i, [128, 512], F32)) for i in range(8)]
        identb = sb('identb_sb', [128, 128], BF16)
        onesb = sb('onesb_sb', [128, 128], BF16)
        blk64b = sb('blk64b_sb', [128, 128], BF16)
        permB = sb('permB_sb', [128, 128], F32)
        permA = sb('permA_sb', [128, 128], F32)
        ropeB = sb('ropeB_sb', [128, 2, 1024], F32)
        ropeA = sb('ropeA_sb', [128, 2, 1024], F32)
        cs64 = sb('cs64_sb', [128, 256], BF16)
        epsT = sb('epsT', [128, 1], F32)
        gvec = sb('gvec_sb', [128, NL, NG], F32)
        gT = sb('gT_sb', [128, NL, 3, 8], F32)
        badaT = sb('badaT_sb', [128, NL, 72], F32)
        condf = sb('condf', [128, 8], F32)
        condb = sb('condb', [128, 8], BF16)
        ctxf = sb('ctxf', [128, 1], F32)
        modT = [sb('modT%d' % i, [128, 72], F32) for i in range(2)]
        modA = [sb('modA%d' % i, [128, 3, 8], F32) for i in range(2)]
        modG = [sb('modG%d' % i, [128, 3, 8], F32) for i in range(2)]
        sinkexp = sb('sinkexp', [128, 4], F32)
        rec = sb('rec', [128, 4], F32)
        QB = sb('QB', [128, 2, 1024], BF16)
        QD = sb('QD', [128, 2, 1024], BF16)
        QA = sb('QA', [128, 4, 1024], BF16)
        KB = sb('KB', [128, 1280], BF16)
        KD = sb('KD', [128, 2, 1280], BF16)
        KA = sb('KA', [128, 4, 1280], BF16)
        VA = sb('VA', [128, 10, 4, 65], BF16)
        VB = sb('VB', [128, 10, 2, 65], BF16)
        VD = sb('VD', [128, 10, 4, 65], BF16)
        PT = sb('PT', [128, NPT, 512], BF16)
        ABt = PT[:, 0:8, :].rearrange("p l (c x) -> p l c x", c=2)
        XcT = PT[:, 8:12, :].rearrange("p (c t) x -> p c (t x)", c=2)
        mtok = sb('mtok', [128, 2, 4, 128], BF16)
        cqn = sb('cqn', [128, 2, 512], BF16)
        ckvn = sb('ckvn', [128, 1024], BF16)
        sqkr = sb('sqkr', [128, 512], BF16)
        krr = sb('krr', [128, 512], F32)
        wuq = sb('wuq', [128, 2, 384], BF16)
        wukv = sb('wukv', [128, 512], BF16)
        ckvc = sb('ckvc', [128, 256], BF16)
        krc = sb('krc', [128, 256], F32)

        st = dict(ps=0, tmp=0, sq=0, ring=0, pt=0, act=0, ld=0, out=0)
        out_ids = []

        def rot(name, n):
            i = st[name]
            st[name] = (i + 1) % n
            return i

        newps = lambda: rot('ps', 7)
        newtmp = lambda: rot('tmp', NTMP)
        newsq = lambda: rot('sq', 3)

        def MM(out, lhsT, rhs, start, stop, rd, wr):
            P.add('pe', lambda e: e.matmul(out, lhsT=lhsT, rhs=rhs, start=start, stop=stop), rd, wr)

        def ACT(out, in_, func, rd, wr, scale=None, bias=None):
            kw = {}
            if scale is not None:
                kw['scale'] = scale
            if bias is not None:
                kw['bias'] = bias
            P.add('act', lambda e: e.activation(out=out, in_=in_, func=func, **kw), rd, wr)

        def TT(out, in0, in1, op, rd, wr):
            P.add('dve', lambda e: e.tensor_tensor(out=out, in0=in0, in1=in1, op=op), rd, wr)

        def STT(out, in0, scalar, in1, op0, op1, rd, wr):
            P.add('dve', lambda e: e.scalar_tensor_tensor(out=out, in0=in0, scalar=scalar, in1=in1, op0=op0, op1=op1), rd, wr)

        def TS(out, in0, s1, op0, rd, wr):
            P.add('dve', lambda e: e.tensor_scalar(out=out, in0=in0, scalar1=s1, scalar2=None, op0=op0), rd, wr)

        def RECIP(out, in_, rd, wr):
            P.add('dve', lambda e: e.reciprocal(out=out, in_=in_), rd, wr)

        def DMA(q, out, in_, rd, wr, chan):
            return P.add(q, lambda e: e.dma_start(out=out, in_=in_), rd, wr, chan=chan)

        def LOAD(out, in_, keys, q='sp'):
            return DMA(q, out, in_, [], keys, 'ld%d' % rot('ld', 4))

        def STORE(out, in_, rd):
            out_ids.append(DMA('sp', out, in_, rd, [], 'st%d' % rot('out', 4)))

        def ring_get(src_ap, a, b):
            s = rot('ring', NSLOT)
            view = ring[s][:, 0:a * b].rearrange("p (a b) -> p a b", b=b)
            DMA('pool', view, src_ap, [], [('ring', s)], 'ring%d' % s)
            return view, ('ring', s)

        def CK(name):
            if stop == name:
                raise _Stop()

        def kcp(ap):
            return ap.rearrange("(kc p) n -> p kc n", p=128)

        C = 'consts'
        cl = []
        for name, tile in [('identb', identb), ('onesb', onesb), ('blk64b', blk64b), ('permB', permB), ('permA', permA),
                           ('ropeB', ropeB), ('ropeA', ropeA), ('cs64', cs64),
                           ('cond', condf), ('ctxflag', ctxf)]:
            cl.append(LOAD(tile[:], IN[name], []))
        cl.append(LOAD(gvec[:], IN['gvec'].rearrange("l p g -> p l g"), []))
        cl.append(LOAD(gT[:], IN['gT'].rearrange("l p k c -> p l k c"), []))
        cl.append(LOAD(badaT[:], IN['b_adaT'].rearrange("l p j -> p l j"), []))
        xkeys = [('x', c, t) for c in range(8) for t in range(2)]
        LOAD(xT[:], kcp(IN['xT']), xkeys)
        P.add('dve', lambda e: e.memset(epsT[:], EPS), [], [C], extra_deps=cl)
        P.add('dve', lambda e: e.memset(VA[:].rearrange("p a b c -> p (a b c)"), 1.0), [], ['Vinit'])
        P.add('dve', lambda e: e.memset(VB[:].rearrange("p a b c -> p (a b c)"), 1.0), [], ['Vinit'])
        P.add('dve', lambda e: e.memset(VD[:].rearrange("p a b c -> p (a b c)"), 1.0), [], ['Vinit'])
        for Vt, nh in ((VA, 4), (VB, 2), (VD, 4)):
            for kt in (8, 9):
                for h in range(nh):
                    P.add('dve', lambda e, Vt=Vt, kt=kt, h=h: e.tensor_copy(out=Vt[:, kt, h, 64:65], in_=ctxf[:, 0:1]), ['Vinit', C], ['Vinit'])
        P.add('dve', lambda e: e.memset(krc[:], 0.0), [], ['krc'])
        ACT(condb[:], condf[:], AF.Silu, [C], ['condb'])

        def rstd_from(ps_ap, pskey, inv_d, p0=0, p1=128, n=512):
            ti = newtmp()
            t = tmp[ti][p0:p1, 0:n]
            ACT(t, ps_ap, AF.Sqrt, [pskey, C], [('tmp', ti)], scale=inv_d, bias=epsT[p0:p1, 0:1])
            RECIP(t, t, [('tmp', ti)], [('tmp', ti)])
            return ti

        def mod_steps(l):
            par = l % 2
            for blk in range(18):
                wt, wk = ring_get(kcp(IN['w_ada'][l][:, blk * 512:(blk + 1) * 512]), 8, 512)
                for j in range(4):
                    col = blk * 4 + j
                    for kc in range(8):
                        MM(ps[7][:, col:col + 1], wt[:, kc, j * 128:(j + 1) * 128], condb[:, kc:kc + 1], kc == 0, kc == 7,
                           [wk, 'condb'], ['psM'])
                yield
            TT(modT[par][:, :], ps[7][:, 0:72], badaT[:, l, :], ALU.add, ['psM', C], [('mod', par)])
            for k in range(3):
                STT(modA[par][:, k, :], modT[par][:, (3 * k + 1) * 8:(3 * k + 2) * 8], 1.0, gT[:, l, k, :], ALU.add, ALU.mult,
                    [('mod', par), C], [('modA', par)])
            for k, f in ((0, 0.5), (1, 1.0), (2, 0.5)):
                TS(modG[par][:, k, :], modT[par][:, (3 * k + 2) * 8:(3 * k + 3) * 8], f, ALU.mult, [('mod', par)], [('modG', par)])
            yield

        def norm_mod(l, k):
            par = l % 2
            for t in range(2):
                hs = slice(t * 512, (t + 1) * 512)
                pi = newps()
                for c in range(8):
                    si = newsq()
                    ACT(sqb[si][:, :], xT[:, c, hs], AF.Square, [('x', c, t)], [('sq', si)])
                    MM(ps[pi][:, :], onesb[:, :], sqb[si][:, :], c == 0, c == 7, [('sq', si), C], [('ps', pi)])
                ACT(rstdN[:, :], ps[pi][:, :], AF.Sqrt, [('ps', pi), C], ['rstdN'], scale=1.0 / 1024, bias=epsT[:, 0:1])
                RECIP(rstdN[:, :], rstdN[:, :], ['rstdN'], ['rstdN'])
                for c in range(8):
                    ti = newtmp()
                    STT(tmp[ti][:, :], xT[:, c, hs], modA[par][:, k, c:c + 1], rstdN[:, :], ALU.mult, ALU.mult,
                        [('x', c, t), ('modA', par), 'rstdN'], [('tmp', ti)])
                    ACT(hT[:, c, hs], tmp[ti][:, :], AF.Identity, [('tmp', ti), ('mod', par)], [('h', c, t)],
                        bias=modT[par][:, 3 * k * 8 + c:3 * k * 8 + c + 1])

        def ffn(l, which, modgen):
            par = l % 2
            wg = IN['w_gate%d' % which][l]
            wu = IN['w_up%d' % which][l]
            wd = IN['w_down%d' % which][l]
            gk = 0 if which == 1 else 2
            for blk in range(6):
                nch = 4 if blk < 5 else 2
                c0 = blk * 512
                ncol = nch * 128
                gw, gkey = ring_get(kcp(wg[:, c0:c0 + ncol]), 8, ncol)
                uw, ukey = ring_get(kcp(wu[:, c0:c0 + ncol]), 8, ncol)
                dw, dkey = ring_get(wd[c0:c0 + ncol, :].rearrange("(j p) n -> p j n", p=128), nch, 1024)
                for t in range(2):
                    hs = slice(t * 512, (t + 1) * 512)
                    aslots = []
                    for j in range(nch):
                        ai = rot('act', 8)
                        aslots.append(ai)
                        pg = newps()
                        for kc in range(8):
                            MM(ps[pg][:, :], gw[:, kc, j * 128:(j + 1) * 128], hT[:, kc, hs], kc == 0, kc == 7,
                               [gkey, ('h', kc, t)], [('ps', pg)])
                        pu = newps()
                        for kc in range(8):
                            MM(ps[pu][:, :], uw[:, kc, j * 128:(j + 1) * 128], hT[:, kc, hs], kc == 0, kc == 7,
                               [ukey, ('h', kc, t)], [('ps', pu)])
                        ti = newtmp()
                        ACT(tmp[ti][:, :], ps[pg][:, :], AF.Silu, [('ps', pg)], [('tmp', ti)])
                        TT(actb[ai][:, :], tmp[ti][:, :], ps[pu][:, :], ALU.mult, [('tmp', ti), ('ps', pu)], [('act', ai)])
                    for dc in range(8):
                        po = newps()
                        for j in range(nch):
                            MM(ps[po][:, :], dw[:, j, dc * 128:(dc + 1) * 128], actb[aslots[j]][:, :], j == 0, j == nch - 1,
                               [dkey, ('act', aslots[j])], [('ps', po)])
                        STT(xT[:, dc, hs], ps[po][:, :], modG[par][:, gk, dc:dc + 1], xT[:, dc, hs], ALU.mult, ALU.add,
                            [('ps', po), ('modG', par), ('x', dc, t)], [('x', dc, t)])
                if modgen is not None:
                    next(modgen, None)

        def rope_norm(psraw, pskey, p0, p1, ones_lhsT, inv_d, gain, out16, outkeys, rope=None, store=None, n=512):
            si = newsq()
            ACT(sqb[si][p0:p1, 0:n], psraw, AF.Square, [pskey], [('sq', si)])
            pss = newps()
            MM(ps[pss][p0:p1, 0:n], ones_lhsT, sqb[si][p0:p1, 0:n], True, True, [('sq', si), C], [('ps', pss)])
            ri = rstd_from(ps[pss][p0:p1, 0:n], ('ps', pss), inv_d, p0, p1, n)
            r = tmp[ri][p0:p1, 0:n]
            if rope is None and store is None:
                STT(out16, psraw, gain, r, ALU.mult, ALU.mult, [pskey, ('tmp', ri), C], outkeys)
                return
            xi = newtmp()
            xn = tmp[xi][p0:p1, 0:n]
            STT(xn, psraw, gain, r, ALU.mult, ALU.mult, [pskey, ('tmp', ri), C], [('tmp', xi)])
            if rope is None:
                STORE(store, xn, [('tmp', xi)])
                ACT(out16, xn, AF.Copy, [('tmp', xi)], outkeys)
                return
            perm_lhsT, ropeC, ropeS = rope
            pp = newps()
            MM(ps[pp][p0:p1, 0:n], perm_lhsT, xn, True, True, [('tmp', xi), C], [('ps', pp)])
            t1 = newtmp()
            TT(tmp[t1][p0:p1, 0:n], xn, ropeC, ALU.mult, [('tmp', xi), C], [('tmp', t1)])
            t2 = newtmp()
            TT(tmp[t2][p0:p1, 0:n], ps[pp][p0:p1, 0:n], ropeS, ALU.mult, [('ps', pp), C], [('tmp', t2)])
            if store is None:
                TT(out16, tmp[t1][p0:p1, 0:n], tmp[t2][p0:p1, 0:n], ALU.add, [('tmp', t1), ('tmp', t2)], outkeys)
            else:
                TT(tmp[t1][p0:p1, 0:n], tmp[t1][p0:p1, 0:n], tmp[t2][p0:p1, 0:n], ALU.add, [('tmp', t1), ('tmp', t2)], [('tmp', t1)])
                STORE(store, tmp[t1][p0:p1, 0:n], [('tmp', t1)])
                ACT(out16, tmp[t1][p0:p1, 0:n], AF.Copy, [('tmp', t1)], outkeys)

        def mixer(l, modgen):
            par = l % 2

            def gv(col, p0=0, p1=128):
                return gvec[p0:p1, l, col:col + 1]

            def step():
                if modgen is not None:
                    next(modgen, None)

            DMA('pool', wuq[:], IN['w_uq'][l].rearrange("(c p) n -> p c n", p=128), [], ['wuq'], 'cx0')
            DMA('pool', wukv[:], IN['w_ukv'][l], [], ['wukv'], 'cx1')
            DMA('pool', ckvc[:], IN['ckvT_c'][l], [], ['ckvc'], 'cx2')
            DMA('sp', krc[64:96, :], IN['krT_c'][l], [], ['krc'], 'cx3')
            DMA('pool', KB[:, 1024:1280], IN['winkT_c'][l], [], [('KB', 'c')], 'cx4')
            DMA('pool', KD[:, :, 1024:1280], IN['nakT_c'][l].rearrange("(c p) k -> p c k", p=128), [], [('KD', 0, 'c'), ('KD', 1, 'c')], 'cx5')
            for i in range(2):
                DMA('pool', VB[:, 8 + i, :, 0:64], IN['winv_c'][l][i * 128:(i + 1) * 128, :].rearrange("p (h d) -> p h d", d=64),
                    [], [('VB', 8 + i)], 'cx6')
                DMA('pool', VD[:, 8 + i, :, 0:64], IN['nav_c'][l][i * 128:(i + 1) * 128, :].rearrange("p (h d) -> p h d", d=64),
                    [], [('VD', 8 + i)], 'cx7')
            ACT(sinkexp[:, :], gvec[:, l, 9:13], AF.Exp, [C], ['sinkexp'])
            wukv_v = wukv[:, 0:512].rearrange("p (h x) -> p h x", x=128)[:, :, 64:128]

            b0, b0k = ring_get(kcp(IN['w_in'][l][:, 0:416]), 8, 416)
            for t in range(2):
                hs = slice(t * 512, (t + 1) * 512)
                pcs = []
                for c in range(2):
                    pi = newps()
                    pcs.append(pi)
                    for kc in range(8):
                        MM(ps[pi][:, :], b0[:, kc, c * 128:(c + 1) * 128], hT[:, kc, hs], kc == 0, kc == 7, [b0k, ('h', kc, t)], [('ps', pi)])
                pss = newps()
                for c in range(2):
                    si = newsq()
                    ACT(sqb[si][:, :], ps[pcs[c]][:, :], AF.Square, [('ps', pcs[c])], [('sq', si)])
                    MM(ps[pss][:, :], onesb[:, :], sqb[si][:, :], c == 0, c == 1, [('sq', si), C], [('ps', pss)])
                ri = rstd_from(ps[pss][:, :], ('ps', pss), 1.0 / 256)
                for c in range(2):
                    STT(cqn[:, c, :], ps[pcs[c]][:, :], gv(c), tmp[ri][:, :], ALU.mult, ALU.mult,
                        [('ps', pcs[c]), ('tmp', ri), C], [('cqn', c)])
                pk = newps()
                for kc in range(8):
                    MM(ps[pk][:, :], b0[:, kc, 256:384], hT[:, kc, hs], kc == 0, kc == 7, [b0k, ('h', kc, t)], [('ps', pk)])
                rope_norm(ps[pk][:, :], ('ps', pk), 0, 128, onesb[:, :], 1.0 / 128, gv(2), ckvn[:, hs], [('ckvn', t)],
                          rope=None, store=OUT['o_ckvT'][l][:, hs])
                pr = newps()
                for kc in range(8):
                    MM(ps[pr][64:96, :], b0[:, kc, 384:416], hT[:, kc, hs], kc == 0, kc == 7, [b0k, ('h', kc, t)], [('ps', pr)])
                ti = newtmp()
                ACT(tmp[ti][64:96, :], ps[pr][64:96, :], AF.Copy, [('ps', pr)], [('tmp', ti)])
                STORE(OUT['o_krT'][l][:, hs], tmp[ti][64:96, :], [('tmp', ti)])
                ACT(sqkr[64:96, :], ps[pr][64:96, :], AF.Square, [('ps', pr)], ['sqkr'])
                tg = newtmp()
                TS(tmp[tg][64:96, :], ps[pr][64:96, :], gv(4, 64, 96), ALU.mult, [('ps', pr), C], [('tmp', tg)])
                pp = newps()
                MM(ps[pp][64:96, :], permA[64:96, 64:96], tmp[tg][64:96, :], True, True, [('tmp', tg), C], [('ps', pp)])
                t1 = newtmp()
                TT(tmp[t1][64:96, :], tmp[tg][64:96, :], ropeA[64:96, 0, hs], ALU.mult, [('tmp', tg), C], [('tmp', t1)])
                TT(krr[64:96, :], ps[pp][64:96, :], ropeA[64:96, 1, hs], ALU.mult, [('ps', pp), C], ['krr'])
                TT(krr[64:96, :], krr[64:96, :], tmp[t1][64:96, :], ALU.add, ['krr', ('tmp', t1)], ['krr'])
                for h in range(4):
                    pn = newps()
                    MM(ps[pn][0:64, :], wukv[:, h * 128:h * 128 + 64], ckvn[:, hs], True, True, ['wukv', ('ckvn', t)], [('ps', pn)])
                    si = newsq()
                    ACT(sqb[si][0:64, :], ps[pn][0:64, :], AF.Square, [('ps', pn)], [('sq', si)])
                    pss = newps()
                    MM(ps[pss][0:96, :], onesb[0:64, 0:96], sqb[si][0:64, :], True, False, [('sq', si), C], [('ps', pss)])
                    MM(ps[pss][0:96, :], onesb[64:96, 0:96], sqkr[64:96, :], False, True, ['sqkr', C], [('ps', pss)])
                    ri = rstd_from(ps[pss][0:96, :], ('ps', pss), 1.0 / 96, 0, 96)
                    STT(KA[0:64, h, hs], ps[pn][0:64, :], gv(4, 0, 64), tmp[ri][0:64, :], ALU.mult, ALU.mult,
                        [('ps', pn), ('tmp', ri), C], [('KA', h, t)])
                    TT(KA[64:96, h, hs], krr[64:96, :], tmp[ri][64:96, :], ALU.mult, ['krr', ('tmp', ri)], [('KA', h, t)])
                    pq = newps()
                    for c in range(2):
                        MM(ps[pq][0:96, :], wuq[:, c, h * 96:(h + 1) * 96], cqn[:, c, :], c == 0, c == 1, ['wuq', ('cqn', c)], [('ps', pq)])
                    rope_norm(ps[pq][0:96, :], ('ps', pq), 0, 96, onesb[0:96, 0:96], 1.0 / 96, gv(3, 0, 96), QA[0:96, h, hs],
                              [('QA', h, t)], rope=(permA[0:96, 0:96], ropeA[0:96, 0, hs], ropeA[0:96, 1, hs]))
                for tt in range(4):
                    kt = t * 4 + tt
                    pv = newps()
                    MM(ps[pv][:, 0:256], ckvn[:, kt * 128:(kt + 1) * 128], wukv_v, True, True, ['wukv', ('ckvn', t)], [('ps', pv)])
                    ACT(VA[:, kt, :, 0:64], ps[pv][:, 0:256].rearrange("p (h d) -> p h d", d=64), AF.Copy, [('ps', pv), 'Vinit'], [('VA', kt)])
            step()
            CK('projA')
            si = newsq()
            ACT(sqb[si][64:96, 0:256], krc[64:96, :], AF.Square, ['krc'], [('sq', si)])
            sq_krc = si
            tgc = newtmp()
            TS(tmp[tgc][64:96, 0:256], krc[64:96, :], gv(4, 64, 96), ALU.mult, ['krc', C], [('tmp', tgc)])
            for h in range(4):
                pn = newps()
                MM(ps[pn][0:64, 0:256], wukv[:, h * 128:h * 128 + 64], ckvc[:, :], True, True, ['wukv', 'ckvc'], [('ps', pn)])
                si = newsq()
                if si == sq_krc:
                    si = newsq()
                ACT(sqb[si][0:64, 0:256], ps[pn][0:64, 0:256], AF.Square, [('ps', pn)], [('sq', si)])
                pss = newps()
                MM(ps[pss][0:96, 0:256], onesb[0:64, 0:96], sqb[si][0:64, 0:256], True, False, [('sq', si), C], [('ps', pss)])
                MM(ps[pss][0:96, 0:256], onesb[64:96, 0:96], sqb[sq_krc][64:96, 0:256], False, True, [('sq', sq_krc), C], [('ps', pss)])
                ri = rstd_from(ps[pss][0:96, 0:256], ('ps', pss), 1.0 / 96, 0, 96, 256)
                if ri == tgc:
                    raise RuntimeError("tmp rotation clash")
                STT(KA[0:64, h, 1024:1280], ps[pn][0:64, 0:256], gv(4, 0, 64), tmp[ri][0:64, 0:256], ALU.mult, ALU.mult,
                    [('ps', pn), ('tmp', ri), C], [('KA', h, 'c')])
                TT(KA[64:96, h, 1024:1280], tmp[tgc][64:96, 0:256], tmp[ri][64:96, 0:256], ALU.mult, [('tmp', tgc), ('tmp', ri)], [('KA', h, 'c')])
            for i in range(2):
                pv = newps()
                MM(ps[pv][:, 0:256], ckvc[:, i * 128:(i + 1) * 128], wukv_v, True, True, ['wukv', 'ckvc'], [('ps', pv)])
                ACT(VA[:, 8 + i, :, 0:64], ps[pv][:, 0:256].rearrange("p (h d) -> p h d", d=64), AF.Copy, [('ps', pv), 'Vinit'], [('VA', 8 + i)])
            step()
            CK('ctxA')

            b1, b1k = ring_get(kcp(IN['w_in'][l][:, 416:928]), 8, 512)
            for t in range(2):
                hs = slice(t * 512, (t + 1) * 512)
                for ci, hpair in enumerate(((0, 2), (1, 3))):
                    pi = newps()
                    for hh, pb in zip(hpair, (0, 64)):
                        for kc in range(8):
                            MM(ps[pi][pb:pb + 64, :], b1[:, kc, hh * 64:(hh + 1) * 64], hT[:, kc, hs], kc == 0, kc == 7,
                               [b1k, ('h', kc, t)], [('ps', pi)])
                    rope_norm(ps[pi][:, :], ('ps', pi), 0, 128, blk64b[:, :], 1.0 / 64, gv(5), QB[:, ci, hs], [('QB', ci, t)],
                              rope=(permB[:, :], ropeB[:, 0, hs], ropeB[:, 1, hs]))
                    CK('qb')
                pi = newps()
                for kc in range(8):
                    MM(ps[pi][:, :], b1[:, kc, 256:384], hT[:, kc, hs], kc == 0, kc == 7, [b1k, ('h', kc, t)], [('ps', pi)])
                rope_norm(ps[pi][:, :], ('ps', pi), 0, 128, blk64b[:, :], 1.0 / 64, gv(6), KB[:, hs], [('KB', t)],
                          rope=(permB[:, :], ropeB[:, 0, hs], ropeB[:, 1, hs]), store=OUT['o_kbT'][l][:, hs])
                CK('kb')
                for tt in range(4):
                    kt = t * 4 + tt
                    pv = newps()
                    for kc in range(8):
                        MM(ps[pv][:, 0:128], hT[:, kc, kt * 128:(kt + 1) * 128], b1[:, kc, 384:512], kc == 0, kc == 7,
                           [b1k, ('h', kc, t)], [('ps', pv)])
                    ti = newtmp()
                    ACT(tmp[ti][:, 0:128], ps[pv][:, 0:128], AF.Copy, [('ps', pv)], [('tmp', ti)])
                    STORE(OUT['o_vb'][l][kt * 128:(kt + 1) * 128, :], tmp[ti][:, 0:128], [('tmp', ti)])
                    ACT(VB[:, kt, :, 0:64], ps[pv][:, 0:128].rearrange("p (h d) -> p h d", d=64), AF.Copy, [('ps', pv), 'Vinit'], [('VB', kt)])
            step()
            CK('projB')
            b2, b2k = ring_get(kcp(IN['w_in'][l][:, 928:1440]), 8, 512)
            for t in range(2):
                hs = slice(t * 512, (t + 1) * 512)
                for ch in range(2):
                    pi = newps()
                    for kc in range(8):
                        MM(ps[pi][:, :], b2[:, kc, ch * 128:(ch + 1) * 128], hT[:, kc, hs], kc == 0, kc == 7, [b2k, ('h', kc, t)], [('ps', pi)])
                    ACT(XcT[:, ch, hs], ps[pi][:, :], AF.Copy, [('ps', pi)], [('pt', 8 + 2 * ch + t)])
                for ch in range(2):
                    pi = newps()
                    for kc in range(8):
                        MM(ps[pi][:, :], b2[:, kc, 256 + ch * 128:256 + (ch + 1) * 128], hT[:, kc, hs], kc == 0, kc == 7,
                           [b2k, ('h', kc, t)], [('ps', pi)])
                    rope_norm(ps[pi][:, :], ('ps', pi), 0, 128, blk64b[:, :], 1.0 / 64, gv(7), QD[:, ch, hs], [('QD', ch, t)])
            step()
            b3, b3k = ring_get(kcp(IN['w_in'][l][:, 1440:1952]), 8, 512)
            for t in range(2):
                hs = slice(t * 512, (t + 1) * 512)
                for ch in range(2):
                    pi = newps()
                    for kc in range(8):
                        MM(ps[pi][:, :], b3[:, kc, ch * 128:(ch + 1) * 128], hT[:, kc, hs], kc == 0, kc == 7, [b3k, ('h', kc, t)], [('ps', pi)])
                    rope_norm(ps[pi][:, :], ('ps', pi), 0, 128, blk64b[:, :], 1.0 / 64, gv(8), KD[:, ch, hs], [('KD', ch, t)],
                              rope=None, store=OUT['o_kdT'][l][ch * 128:(ch + 1) * 128, hs])
                for tt in range(4):
                    kt = t * 4 + tt
                    pv = newps()
                    for kc in range(8):
                        MM(ps[pv][:, 0:256], hT[:, kc, kt * 128:(kt + 1) * 128], b3[:, kc, 256:512], kc == 0, kc == 7,
                           [b3k, ('h', kc, t)], [('ps', pv)])
                    ti = newtmp()
                    ACT(tmp[ti][:, 0:256], ps[pv][:, 0:256], AF.Copy, [('ps', pv)], [('tmp', ti)])
                    STORE(OUT['o_vd'][l][kt * 128:(kt + 1) * 128, :], tmp[ti][:, 0:256], [('tmp', ti)])
                    ACT(VD[:, kt, :, 0:64], ps[pv][:, 0:256].rearrange("p (h d) -> p h d", d=64), AF.Copy, [('ps', pv), 'Vinit'], [('VD', kt)])
            step()
            CK('projD')
            for lt in range(8):
                for ch in range(2):
                    pi = newps()
                    MM(ps[pi][:, 0:256], XcT[:, ch, lt * 128:(lt + 1) * 128], cs64[:, :], True, True, [('pt', 8 + 2 * ch + lt // 4), C], [('ps', pi)])
                    P.add('dve', lambda e, lt=lt, ch=ch, pi=pi: e.tensor_copy(out=ABt[:, lt, ch, :], in_=ps[pi][:, 0:256]),
                          [('ps', pi)], [('ab', lt, ch)])
            for t in range(2):
                hs = slice(t * 512, (t + 1) * 512)
                cb, cbk = ring_get(IN['dftC'][:, hs].rearrange("(lt p) n -> p lt n", p=128), 8, 512)
                sbk_, sbkk = ring_get(IN['dftS'][:, hs].rearrange("(lt p) n -> p lt n", p=128), 8, 512)
                for ch in range(2):
                    pi = newps()
                    for lt in range(8):
                        MM(ps[pi][:, :], ABt[:, lt, ch, 0:128], cb[:, lt, :], lt == 0, False, [('ab', lt, ch), ('pt', lt), cbk], [('ps', pi)])
                    for lt in range(8):
                        MM(ps[pi][:, :], ABt[:, lt, ch, 128:256], sbk_[:, lt, :], False, lt == 7, [('ab', lt, ch), ('pt', lt), sbkk], [('ps', pi)])
                    ACT(hT[:, 4 + ch, hs], ps[pi][:, :], AF.Copy, [('ps', pi)], [('h', 4 + ch, t)])
            step()
            CK('fourier')

            shared = {}

            def attend_qk(kind, h, t):
                hs = slice(t * 512, (t + 1) * 512)
                if kind == 'B':
                    kts = [kt for kt in range(4 * t - 1, 4 * t + 5) if 0 <= kt <= 7] + [8, 9]
                    if 'maskB' not in shared:
                        shared['maskB'] = ring_get(IN['maskB'], 6, 512)
                elif kind == 'D':
                    kts = list(range(0, 6) if t == 0 else range(2, 8)) + [8, 9]
                    if ('biasD', h) not in shared:
                        shared[('biasD', h)] = [ring_get(IN['biasD'][l][h][g * 512:(g + 1) * 512, :].rearrange("(kt p) q -> p kt q", p=128), 4, 1024)
                                                for g in range(2)]
                else:
                    kts = list(range(10))
                    if 'EF' not in shared:
                        s_ = rot('ring', NSLOT)
                        DMA('pool', ring[s_][0:8, 0:2304], IN['EFA'], [], [('ring', s_)], 'ring%d' % s_)
                        shared['EF'] = (ring[s_], ('ring', s_))
                slots = {}
                for kt in kts:
                    pi = newps()
                    ksl = slice(kt * 128, (kt + 1) * 128)
                    own = kt < 8
                    kk = (kt // 4) if own else 'c'
                    if kind == 'A':
                        MM(ps[pi][:, :], KA[0:96, h, ksl], QA[0:96, h, hs], True, not own, [('KA', h, kk), ('QA', h, t)], [('ps', pi)])
                        if own:
                            ef, efk = shared['EF']
                            MM(ps[pi][:, :], ef[0:8, kt * 128:(kt + 1) * 128], ef[0:8, 1280 + t * 512:1280 + (t + 1) * 512], False, True, [efk], [('ps', pi)])
                        scale = 96.0 ** -0.5
                    elif kind == 'B':
                        pb = (h // 2) * 64
                        ci = h % 2
                        MM(ps[pi][:, :], KB[pb:pb + 64, ksl], QB[pb:pb + 64, ci, hs], True, not own, [('KB', kk), ('QB', ci, t)], [('ps', pi)])
                        if own:
                            mv, mk = shared['maskB']
                            MM(ps[pi][:, :], identb[:, :], mv[:, kt - 4 * t + 1, :], False, True, [C, mk], [('ps', pi)])
                        scale = 0.125
                    else:
                        pb = (h % 2) * 64
                        ch = h // 2
                        MM(ps[pi][:, :], KD[pb:pb + 64, ch, ksl], QD[pb:pb + 64, ch, hs], True, not own, [('KD', ch, kk), ('QD', ch, t)], [('ps', pi)])
                        if own:
                            bv, bk = shared[('biasD', h)][kt // 4]
                            MM(ps[pi][:, :], identb[:, :], bv[:, kt % 4, hs], False, True, [C, bk], [('ps', pi)])
                        scale = 0.125
                    s = rot('pt', NPT)
                    slots[kt] = s
                    ACT(PT[:, s, :], ps[pi][:, :], AF.Exp, [('ps', pi)], [('pt', s)], scale=scale)
                return (kind, h, t, kts, slots)

            def attend_pv(ctx):
                kind, h, t, kts, slots = ctx
                V, vh, vname = {'A': (VA, h, 'VA'), 'B': (VB, h // 2, 'VB'), 'D': (VD, h, 'VD')}[kind]
                po = newps()
                for qi in range(4):
                    for i, kt in enumerate(kts):
                        MM(ps[po][:, qi * 65:(qi + 1) * 65], PT[:, slots[kt], qi * 128:(qi + 1) * 128], V[:, kt, vh, 0:65],
                           i == 0, i == len(kts) - 1, [('pt', slots[kt]), (vname, kt), 'Vinit'], [('ps', po)])
                for qi in range(4):
                    den = ps[po][:, qi * 65 + 64:qi * 65 + 65]
                    if kind == 'B':
                        TS(rec[:, qi:qi + 1], den, sinkexp[:, h:h + 1], ALU.add, [('ps', po), 'sinkexp'], [('rec', qi)])
                        RECIP(rec[:, qi:qi + 1], rec[:, qi:qi + 1], [('rec', qi)], [('rec', qi)])
                    else:
                        RECIP(rec[:, qi:qi + 1], den, [('ps', po)], [('rec', qi)])
                    hslot = h % 2
                    TS(mtok[:, t, qi, hslot * 64:(hslot + 1) * 64], ps[po][:, qi * 65:qi * 65 + 64], rec[:, qi:qi + 1], ALU.mult,
                       [('ps', po), ('rec', qi)], [('mtok', t, qi, hslot)])

            def attend_tr(chunk):
                for t in range(2):
                    hs = slice(t * 512, (t + 1) * 512)
                    pt_ = newps()
                    for qi in range(4):
                        MM(ps[pt_][:, qi * 128:(qi + 1) * 128], mtok[:, t, qi, :], identb[:, :], True, True,
                           [('mtok', t, qi, 0), ('mtok', t, qi, 1), C], [('ps', pt_)])
                    P.add('dve', lambda e, hs=hs, pt_=pt_, cc=chunk: e.tensor_copy(out=hT[:, cc, hs], in_=ps[pt_][:, :]),
                          [('ps', pt_)], [('h', chunk, t)])

            units = []
            for kind, chunk0 in (('A', 0), ('B', 2), ('D', 6)):
                for pair in range(2):
                    for h in (2 * pair, 2 * pair + 1):
                        for t in range(2):
                            units.append((kind, h, t, chunk0 + pair, (h % 2 == 1 and t == 1)))
            prev = None
            for (kind, h, t, chunk, last) in units:
                ctx = attend_qk(kind, h, t)
                if prev is not None:
                    attend_pv(prev[0])
                    if prev[2]:
                        attend_tr(prev[1])
                        step()
                prev = (ctx, chunk, last)
            attend_pv(prev[0])
            attend_tr(prev[1])
            step()
            CK('attn')

            for blk in range(2):
                wo, wok = ring_get(kcp(IN['w_o'][l][:, blk * 512:(blk + 1) * 512]), 8, 512)
                for dcc in range(4):
                    dc = blk * 4 + dcc
                    for t in range(2):
                        hs = slice(t * 512, (t + 1) * 512)
                        po = newps()
                        for c in range(8):
                            MM(ps[po][:, :], wo[:, c, dcc * 128:(dcc + 1) * 128], hT[:, c, hs], c == 0, c == 7, [wok, ('h', c, t)], [('ps', po)])
                        STT(xT[:, dc, hs], ps[po][:, :], modG[par][:, 1, dc:dc + 1], xT[:, dc, hs], ALU.mult, ALU.add,
                            [('ps', po), ('modG', par), ('x', dc, t)], [('x', dc, t)])
            step()

        try:
            CK('load')
            g0 = mod_steps(0)
            for _ in g0:
                pass
            CK('mod')
            for l in range(nl):
                mg = mod_steps(l + 1) if l + 1 < nl else None
                norm_mod(l, 0)
                CK('norm0')
                ffn(l, 1, mg)
                CK('ffn1')
                norm_mod(l, 1)
                mixer(l, mg)
                CK('mixer')
                norm_mod(l, 2)
                ffn(l, 2, mg)
                if mg is not None:
                    for _ in mg:
                        pass
        except _Stop:
            pass
        STORE(kcp(OUT['yT']), xT[:], xkeys)
        P.add('sp', None, extra_deps=out_ids)
        P.emit()
        nc._prog_stats = (P.n_ops, P.sig_counts, P.chan_counts)
    return nc


def _rope_tables(r, sample):
    Cm = np.ones((r, T), np.float32)
    Sm = np.zeros((r, T), np.float32)
    if not sample:
        return Cm, Sm
    half = r // 2
    t = np.arange(T)
    inv = (10000.0 ** (-np.arange(0, half, 2, dtype=np.float32) / np.float32(half))).astype(np.float32)
    q = half // 2
    for part, pos in ((0, t // 64), (1, t % 64)):
        ang = pos.astype(np.float32)[:, None] * inv[None, :]
        cos = np.cos(ang).astype(np.float32).T
        sin = np.sin(ang).astype(np.float32).T
        b = part * half
        Cm[b:b + q] = cos
        Cm[b + q:b + half] = cos
        Sm[b:b + q] = -sin
        Sm[b + q:b + half] = sin
    return Cm, Sm


def _perm(r):
    M = np.zeros((r, r), np.float32)
    half = r // 2
    q = half // 2
    for d in range(r):
        o = d % half
        p = d + q if o < q else d - q
        M[p, d] = 1.0
    return M


def _consts(sample):
    c = {}
    c['identb'] = np.eye(128, dtype=np.float32).astype(bf16)
    c['onesb'] = np.ones((128, 128), np.float32).astype(bf16)
    c['blk64b'] = np.kron(np.eye(2, dtype=np.float32), np.ones((64, 64), np.float32)).astype(bf16)
    pb = np.zeros((128, 128), np.float32)
    pb[0:64, 0:64] = _perm(64)
    pb[64:128, 64:128] = _perm(64)
    c['permB'] = pb
    pa = np.zeros((128, 128), np.float32)
    pa[64:96, 64:96] = _perm(32)
    c['permA'] = pa
    Cb, Sb = _rope_tables(64, sample)
    rb = np.zeros((128, 2, T), np.float32)
    rb[0:64, 0], rb[64:128, 0] = Cb, Cb
    rb[0:64, 1], rb[64:128, 1] = Sb, Sb
    c['ropeB'] = rb
    Ca, Sa = _rope_tables(32, sample)
    ra = np.zeros((128, 2, T), np.float32)
    ra[:, 0] = 1.0
    ra[64:96, 0] = Ca
    ra[64:96, 1] = Sa
    c['ropeA'] = ra
    k = np.arange(64)
    ang = 2 * np.pi * np.outer(k, k) / 64.0
    C64 = np.cos(ang) / 8.0
    S64 = np.sin(ang) / 8.0
    cs = np.zeros((128, 256), np.float64)
    cs[0:64, 0:64] = C64
    cs[64:128, 64:128] = C64
    cs[0:64, 128:192] = S64
    cs[64:128, 192:256] = S64
    c['cs64'] = cs.astype(np.float32).astype(bf16)
    L = 1024 if sample else 256
    kk = np.arange(L)
    angL = 2 * np.pi * (np.outer(kk, kk) % L) / float(L)
    CL = np.cos(angL) / np.sqrt(L)
    SL = -np.sin(angL) / np.sqrt(L)
    if sample:
        dC, dS = CL, SL
    else:
        dC = np.kron(np.eye(4), CL)
        dS = np.kron(np.eye(4), SL)
    c['dftC'] = dC.astype(np.float32).astype(bf16)
    c['dftS'] = dS.astype(np.float32).astype(bf16)
    ea = np.zeros((8, 1280), np.float32)
    fa = np.zeros((8, 1024), np.float32)
    if not sample:
        for s in range(4):
            ea[s, s * 256:(s + 1) * 256] = 1.0
            fa[s, s * 256:(s + 1) * 256] = BIG
        ea[4, 0:1024] = 1.0
        fa[4, :] = -BIG
    c['EFA'] = np.concatenate([ea, fa], axis=1).astype(bf16)
    mb = np.full((128, 6, 512), NEG, np.float32)
    kl = np.arange(128)[:, None]
    ql = np.arange(128)[None, :]
    for ki in range(6):
        ktp = ki - 1
        for qi in range(4):
            off = ktp - qi
            blkm = np.full((128, 128), NEG, np.float32)
            if sample:
                if off == 0:
                    blkm[:] = 0.0
                elif off == -1:
                    blkm = np.where(ql <= kl, 0.0, NEG).astype(np.float32)
                elif off == 1:
                    blkm = np.where(kl <= ql, 0.0, NEG).astype(np.float32)
            else:
                if off == 0 or (off == 1 and qi % 2 == 0) or (off == -1 and qi % 2 == 1):
                    blkm[:] = 0.0
            mb[:, ki, qi * 128:(qi + 1) * 128] = blkm
    c['maskB'] = mb.astype(bf16)
    return c


def _na_index():
    rows = 16
    r = np.arange(rows)
    row_start = np.clip(r - 4, 0, rows - 8)
    col = np.arange(64)
    col_start = np.clip(col - 8, 0, 64 - 16)
    q = np.arange(1024)
    qr, qc = q // 64, q % 64
    kr, kc = qr, qc
    KR, QR = kr[:, None], qr[None, :]
    KC, QC = kc[:, None], qc[None, :]
    valid = (KR >= row_start[QR]) & (KR < row_start[QR] + 8) & (KC >= col_start[QC]) & (KC < col_start[QC] + 16)
    dr = np.clip(KR - QR + 7, 0, 14)
    dc = np.clip(KC - QC + 15, 0, 30)
    return valid, dr, dc


def _vecT(v):
    return np.ascontiguousarray(v.reshape(-1, 128).T)


_PROG = {}


def _prep(x_prompt, x_sample, cache_mla_ckv, cache_mla_krope, cache_win_k, cache_win_v, cache_na_k, cache_na_v,
           c, c_ctx, w_ada, b_ada, g_ffn1, w_gate1, w_up1, w_down1, g_mix, w_in, g_qa, w_uq, g_kva, w_ukv,
           qn_a, kn_a, qn_b, kn_b, sink_b, qn_d, kn_d, rpb_d, w_o, g_ffn2, w_gate2, w_up2, w_down2):
    f = lambda a: np.ascontiguousarray(np.asarray(a, dtype=np.float32))
    x_prompt, x_sample, c, c_ctx = f(x_prompt), f(x_sample), f(c), f(c_ctx)
    shared = {n: f(v) for n, v in dict(w_ada=w_ada, w_gate1=w_gate1, w_up1=w_up1, w_down1=w_down1, w_gate2=w_gate2,
                                       w_up2=w_up2, w_down2=w_down2, w_in=w_in, w_o=w_o, w_uq=w_uq, w_ukv=w_ukv).items()}
    b_ada = f(b_ada)
    shared['b_adaT'] = np.ascontiguousarray(b_ada.reshape(NL, 72, 128).transpose(0, 2, 1))
    gs = np.stack([f(g_ffn1), f(g_mix), f(g_ffn2)], axis=1)
    shared['gT'] = np.ascontiguousarray(gs.reshape(NL, 3, 8, 128).transpose(0, 3, 1, 2))
    gv = np.zeros((NL, 128, NG), np.float32)
    gv[:, :, 0:2] = f(g_qa).reshape(NL, 2, 128).transpose(0, 2, 1)
    gv[:, :, 2] = f(g_kva)
    gv[:, 0:96, 3] = f(qn_a)
    gv[:, 0:96, 4] = f(kn_a)
    gv[:, :, 5] = np.tile(f(qn_b), (1, 2))
    gv[:, :, 6] = np.tile(f(kn_b), (1, 2))
    gv[:, :, 7] = np.tile(f(qn_d), (1, 2))
    gv[:, :, 8] = np.tile(f(kn_d), (1, 2))
    gv[:, :, 9:13] = f(sink_b)[:, None, :]
    shared['gvec'] = gv
    consts = {True: _consts(True), False: _consts(False)}
    valid, dr, dc = _na_index()
    rpb = f(rpb_d)
    bias_s = np.where(valid[None, None], rpb[:, :, dr, dc], np.float32(NEG)).astype(np.float32)
    seq = np.arange(1024) // 256
    bias_p1 = np.where(seq[:, None] == seq[None, :], np.float32(0.0), np.float32(NEG)).astype(np.float32)
    bias_p = np.ascontiguousarray(np.broadcast_to(bias_p1, (NL, 4, 1024, 1024)))
    cm_ckv, cm_kr = f(cache_mla_ckv), f(cache_mla_krope)
    cw_k, cw_v, cn_k, cn_v = f(cache_win_k), f(cache_win_v), f(cache_na_k), f(cache_na_v)
    in_maps = []
    for core in range(8):
        sample = core >= 4
        m = dict(shared)
        m.update(consts[sample])
        if sample:
            b = core - 4
            xs = x_sample[b]
            cond = c[b]
            m['ckvT_c'] = np.ascontiguousarray(cm_ckv[b].transpose(0, 2, 1))
            m['krT_c'] = np.ascontiguousarray(cm_kr[b].transpose(0, 2, 1))
            m['winkT_c'] = np.ascontiguousarray(cw_k[b].reshape(NL, 256, 128).transpose(0, 2, 1))
            m['winv_c'] = np.ascontiguousarray(cw_v[b].reshape(NL, 256, 128))
            m['nakT_c'] = np.ascontiguousarray(cn_k[b].reshape(NL, 256, 256).transpose(0, 2, 1))
            m['nav_c'] = np.ascontiguousarray(cn_v[b].reshape(NL, 256, 256))
            m['biasD'] = bias_s
            m['ctxflag'] = np.ones((128, 1), np.float32)
        else:
            xs = x_prompt[4 * core:4 * core + 4].reshape(1024, 1024)
            cond = c_ctx
            m['ckvT_c'] = np.zeros((NL, 128, 256), np.float32)
            m['krT_c'] = np.zeros((NL, 32, 256), np.float32)
            m['winkT_c'] = np.zeros((NL, 128, 256), np.float32)
            m['winv_c'] = np.zeros((NL, 256, 128), np.float32)
            m['nakT_c'] = np.zeros((NL, 256, 256), np.float32)
            m['nav_c'] = np.zeros((NL, 256, 256), np.float32)
            m['biasD'] = bias_p
            m['ctxflag'] = np.zeros((128, 1), np.float32)
        m['xT'] = np.ascontiguousarray(xs.T)
        m['cond'] = _vecT(cond)
        in_maps.append(m)
    return in_maps


def _assemble(R):
    y_prompt = np.concatenate([R[i]['yT'].T.reshape(4, 256, 1024) for i in range(4)], axis=0)
    y_sample = np.stack([R[4 + b]['yT'].T for b in range(4)], axis=0)

    def featmaj(name, feat):
        outs = []
        for i in range(4):
            a = R[i][name]
            a = a.reshape(NL, feat, 4, 256).transpose(2, 0, 3, 1)
            outs.append(a)
        return np.ascontiguousarray(np.concatenate(outs, axis=0))

    def tokmaj(name, feat):
        outs = []
        for i in range(4):
            a = R[i][name].reshape(NL, 4, 256, feat).transpose(1, 0, 2, 3)
            outs.append(a)
        return np.ascontiguousarray(np.concatenate(outs, axis=0))

    new_ckv = featmaj('o_ckvT', 128)
    new_kr = featmaj('o_krT', 32)
    new_wk = featmaj('o_kbT', 128).reshape(16, NL, 256, 2, 64)
    new_wv = tokmaj('o_vb', 128).reshape(16, NL, 256, 2, 64)
    new_nk = featmaj('o_kdT', 256).reshape(16, NL, 256, 4, 64)
    new_nv = tokmaj('o_vd', 256).reshape(16, NL, 256, 4, 64)
    return (np.ascontiguousarray(y_prompt.astype(np.float32)), np.ascontiguousarray(y_sample.astype(np.float32)),
            new_ckv, new_kr, new_wk, new_wv, new_nk, new_nv)


def kernel(**inputs):
    in_maps = _prep(**inputs)
    if 'nc' not in _PROG:
        _PROG['nc'] = build_program(NL)
    res = run_bass_kernel_spmd(_PROG['nc'], in_maps, core_ids=list(range(8)))
    return _assemble(res.results)
```

```python
import numpy as np
import ml_dtypes
from contextlib import ExitStack
import concourse.bass as bass
import concourse.mybir as mybir
from concourse.bass_utils import run_bass_kernel_spmd

F32 = mybir.dt.float32
BF16 = mybir.dt.bfloat16
AF = mybir.ActivationFunctionType
ALU = mybir.AluOpType
bf16 = ml_dtypes.bfloat16

NL = 4
D = 1024
T = 1024
DFF = 2816
EPS = 1e-6
NEG = -30000.0
BIG = 32768.0
RING_ELEMS = 4096
NSLOT = 4
NTMP = 6
NPT = 22
NG = 13


class Prog:
    COMPUTE = ('pe', 'act', 'dve', 'pool')

    def __init__(self, nc, es, same_engine_sync=True):
        self.nc = nc
        self.es = es
        self.ops = []
        self.lastw = {}
        self.readers = {}
        self.chan_last = {}
        self.chan_cnt = {}
        self.same_engine_sync = same_engine_sync

    def add(self, eng, fn, reads=(), writes=(), chan=None, extra_deps=()):
        idx = len(self.ops)
        deps = set(extra_deps)
        if eng in ('act', 'dve'):
            pk = [k for k in reads if k == 'psM' or (isinstance(k, tuple) and k[0] == 'ps')]
            if pk:
                writes = list(writes) + pk
        for k in reads:
            w = self.lastw.get(k)
            if w is not None:
                deps.add(w)
        for k in writes:
            w = self.lastw.get(k)
            if w is not None:
                deps.add(w)
            last = {}
            for r in self.readers.get(k, ()):
                rop = self.ops[r]
                if rop['chan'] is not None:
                    deps.add(r)
                else:
                    last[rop['eng']] = r
            deps.update(last.values())
        for k in reads:
            self.readers.setdefault(k, []).append(idx)
        for k in writes:
            self.lastw[k] = idx
            self.readers[k] = []
        if chan is not None:
            p = self.chan_last.get(chan)
            if p is not None:
                deps.add(p)
            self.chan_last[chan] = idx
            self.chan_cnt[chan] = self.chan_cnt.get(chan, 0) + 1
        deps.discard(idx)
        self.ops.append(dict(eng=eng, fn=fn, deps=deps, chan=chan, signal=False, sigval=None,
                             chanval=(16 * self.chan_cnt[chan] if chan is not None else None)))
        return idx

    def emit(self):
        nc = self.nc
        ops = self.ops
        for op in ops:
            for d in op['deps']:
                dop = ops[d]
                if dop['chan'] is None:
                    if dop['eng'] == op['eng'] and (op['eng'] == 'pe' or not self.same_engine_sync):
                        continue
                    dop['signal'] = True
        cnt = {}
        for op in ops:
            if op['chan'] is None and op['signal']:
                cnt[op['eng']] = cnt.get(op['eng'], 0) + 1
                op['sigval'] = cnt[op['eng']]
        self.sig_counts = dict(cnt)
        self.chan_counts = dict(self.chan_cnt)
        self.n_ops = len(ops)
        sems = {e: self.es.enter_context(nc.semaphore('s_' + e)) for e in self.COMPUTE + ('sp',)}
        csems = {c: self.es.enter_context(nc.semaphore('c_%d' % i)) for i, c in enumerate(self.chan_cnt)}
        by_eng = {}
        for i, op in enumerate(ops):
            by_eng.setdefault(op['eng'], []).append(i)

        def run_engine(ename, e):
            waited = {}
            for i in by_eng.get(ename, ()):
                op = ops[i]
                need = {}
                for d in op['deps']:
                    dop = ops[d]
                    if dop['chan'] is not None:
                        key = ('c', dop['chan'])
                        val = dop['chanval']
                    else:
                        if dop['eng'] == ename and (ename == 'pe' or not self.same_engine_sync):
                            continue
                        key = ('e', dop['eng'])
                        val = dop['sigval']
                    if val > need.get(key, 0):
                        need[key] = val
                for key, val in need.items():
                    if waited.get(key, 0) >= val:
                        continue
                    waited[key] = val
                    s = csems[key[1]] if key[0] == 'c' else sems[key[1]]
                    e.wait_ge(s, val)
                if op['fn'] is None:
                    continue
                ins = op['fn'](e)
                if op['chan'] is not None:
                    ins.then_inc(csems[op['chan']], 16)
                elif op['signal']:
                    ins.then_inc(sems[ename], 1)

        with nc.Block() as block:
            @block.tensor
            def _(e):
                run_engine('pe', e)

            @block.scalar
            def _(e):
                run_engine('act', e)

            @block.vector
            def _(e):
                run_engine('dve', e)

            @block.gpsimd
            def _(e):
                run_engine('pool', e)

            @block.sync
            def _(e):
                run_engine('sp', e)


INPUT_SPECS = [
    ('xT', [1024, 1024], F32), ('cond', [128, 8], F32), ('ctxflag', [128, 1], F32),
    ('w_ada', [NL, 1024, 9216], F32), ('b_adaT', [NL, 128, 72], F32), ('gT', [NL, 128, 3, 8], F32),
    ('w_gate1', [NL, 1024, DFF], F32), ('w_up1', [NL, 1024, DFF], F32), ('w_down1', [NL, DFF, 1024], F32),
    ('w_gate2', [NL, 1024, DFF], F32), ('w_up2', [NL, 1024, DFF], F32), ('w_down2', [NL, DFF, 1024], F32),
    ('w_in', [NL, 1024, 1952], F32), ('w_o', [NL, 1024, 1024], F32),
    ('w_uq', [NL, 256, 384], F32), ('w_ukv', [NL, 128, 512], F32), ('gvec', [NL, 128, NG], F32),
    ('ckvT_c', [NL, 128, 256], F32), ('krT_c', [NL, 32, 256], F32), ('winkT_c', [NL, 128, 256], F32),
    ('winv_c', [NL, 256, 128], F32), ('nakT_c', [NL, 256, 256], F32), ('nav_c', [NL, 256, 256], F32),
    ('identb', [128, 128], BF16), ('onesb', [128, 128], BF16), ('blk64b', [128, 128], BF16),
    ('permB', [128, 128], F32), ('permA', [128, 128], F32),
    ('ropeB', [128, 2, 1024], F32), ('ropeA', [128, 2, 1024], F32),
    ('cs64', [128, 256], BF16), ('dftC', [1024, 1024], BF16), ('dftS', [1024, 1024], BF16),
    ('EFA', [8, 2304], BF16), ('maskB', [128, 6, 512], BF16),
    ('biasD', [NL, 4, 1024, 1024], F32),
]
OUTPUT_SPECS = [
    ('yT', [1024, 1024]), ('o_ckvT', [NL, 128, 1024]), ('o_krT', [NL, 32, 1024]), ('o_kbT', [NL, 128, 1024]),
    ('o_vb', [NL, 1024, 128]), ('o_kdT', [NL, 256, 1024]), ('o_vd', [NL, 1024, 256]),
]


class _Stop(Exception):
    pass


def build_program(nl=NL, stop=None):
    nc = bass.Bass("TRN2", target_bir_lowering=False)
    IN = {n: nc.dram_tensor(n, list(s), dt, kind="ExternalInput").ap() for n, s, dt in INPUT_SPECS}
    OUT = {n: nc.dram_tensor(n, list(s), F32, kind="ExternalOutput").ap() for n, s in OUTPUT_SPECS}
    es = ExitStack()
    with es:
        P = Prog(nc, es)

        def sb(name, shape, dt):
            return es.enter_context(nc.sbuf_tensor(name, list(shape), dt))

        xT = sb('xT_sb', [128, 8, 1024], F32)
        hT = sb('hT', [128, 8, 1024], BF16)
        ring = [sb('ring%d' % i, [128, RING_ELEMS], BF16) for i in range(NSLOT)]
        actb = [sb('actb%d' % i, [128, 512], BF16) for i in range(8)]
        rstdN = sb('rstdN', [128, 512], F32)
        tmp = [sb('tmp%d' % i, [128, 512], F32) for i in range(NTMP)]
        sqb = [sb('sqb%d' % i, [128, 512], BF16) for i in range(3)]
        ps = [es.enter_context(nc.psum_tensor('ps%d' % i, [128, 512], F32)) for i in range(8)]
        identb = sb('identb_sb', [128, 128], BF16)
        onesb = sb('onesb_sb', [128, 128], BF16)
        blk64b = sb('blk64b_sb', [128, 128], BF16)
        permB = sb('permB_sb', [128, 128], F32)
        permA = sb('permA_sb', [128, 128], F32)
        ropeB = sb('ropeB_sb', [128, 2, 1024], F32)
        ropeA = sb('ropeA_sb', [128, 2, 1024], F32)
        cs64 = sb('cs64_sb', [128, 256], BF16)
        epsT = sb('epsT', [128, 1], F32)
        gvec = sb('gvec_sb', [128, NL, NG], F32)
        gT = sb('gT_sb', [128, NL, 3, 8], F32)
        badaT = sb('badaT_sb', [128, NL, 72], F32)
        condf = sb('condf', [128, 8], F32)
        condb = sb('condb', [128, 8], BF16)
        ctxf = sb('ctxf', [128, 1], F32)
        modT = [sb('modT%d' % i, [128, 72], F32) for i in range(2)]
        modA = [sb('modA%d' % i, [128, 3, 8], F32) for i in range(2)]
        modG = [sb('modG%d' % i, [128, 3, 8], F32) for i in range(2)]
        sinkexp = sb('sinkexp', [128, 4], F32)
        rec = sb('rec', [128, 4], F32)
        QB = sb('QB', [128, 2, 1024], BF16)
        QD = sb('QD', [128, 2, 1024], BF16)
        QA = sb('QA', [128, 4, 1024], BF16)
        KB = sb('KB', [128, 1280], BF16)
        KD = sb('KD', [128, 2, 1280], BF16)
        KA = sb('KA', [128, 4, 1280], BF16)
        VA = sb('VA', [128, 10, 4, 65], BF16)
        VB = sb('VB', [128, 10, 2, 65], BF16)
        VD = sb('VD', [128, 10, 4, 65], BF16)
        PT = sb('PT', [128, NPT, 512], BF16)
        ABt = PT[:, 0:8, :].rearrange("p l (c x) -> p l c x", c=2)
        XcT = PT[:, 8:12, :].rearrange("p (c t) x -> p c (t x)", c=2)
        mtok = sb('mtok', [128, 2, 4, 128], BF16)
        cqn = sb('cqn', [128, 2, 512], BF16)
        ckvn = sb('ckvn', [128, 1024], BF16)
        sqkr = sb('sqkr', [128, 512], BF16)
        krr = sb('krr', [128, 512], F32)
        wuq = sb('wuq', [128, 2, 384], BF16)
        wukv = sb('wukv', [128, 512], BF16)
        ckvc = sb('ckvc', [128, 256], BF16)
        krc = sb('krc', [128, 256], F32)

        st = dict(ps=0, tmp=0, sq=0, ring=0, pt=0, act=0, ld=0, out=0)
        out_ids = []

        def rot(name, n):
            i = st[name]
            st[name] = (i + 1) % n
            return i

        newps = lambda: rot('ps', 7)
        newtmp = lambda: rot('tmp', NTMP)
        newsq = lambda: rot('sq', 3)

        def MM(out, lhsT, rhs, start, stop, rd, wr):
            P.add('pe', lambda e: e.matmul(out, lhsT=lhsT, rhs=rhs, start=start, stop=stop), rd, wr)

        def ACT(out, in_, func, rd, wr, scale=None, bias=None):
            kw = {}
            if scale is not None:
                kw['scale'] = scale
            if bias is not None:
                kw['bias'] = bias
            P.add('act', lambda e: e.activation(out=out, in_=in_, func=func, **kw), rd, wr)

        def TT(out, in0, in1, op, rd, wr):
            P.add('dve', lambda e: e.tensor_tensor(out=out, in0=in0, in1=in1, op=op), rd, wr)

        def STT(out, in0, scalar, in1, op0, op1, rd, wr):
            P.add('dve', lambda e: e.scalar_tensor_tensor(out=out, in0=in0, scalar=scalar, in1=in1, op0=op0, op1=op1), rd, wr)

        def TS(out, in0, s1, op0, rd, wr):
            P.add('dve', lambda e: e.tensor_scalar(out=out, in0=in0, scalar1=s1, scalar2=None, op0=op0), rd, wr)

        def RECIP(out, in_, rd, wr):
            P.add('dve', lambda e: e.reciprocal(out=out, in_=in_), rd, wr)

        def RECIPF(out, in_, rd, wr):
            P.add('dve', lambda e: e.reciprocal_approx_fast(out=out, in_=in_), rd, wr)

        def DMA(q, out, in_, rd, wr, chan):
            return P.add(q, lambda e: e.dma_start(out=out, in_=in_), rd, wr, chan=chan)

        def LOAD(out, in_, keys, q='sp'):
            return DMA(q, out, in_, [], keys, 'ld%d' % rot('ld', 4))

        def STORE(out, in_, rd):
            out_ids.append(DMA('sp', out, in_, rd, [], 'st%d' % rot('out', 4)))

        def ring_get(src_ap, a, b):
            s = rot('ring', NSLOT)
            view = ring[s][:, 0:a * b].rearrange("p (a b) -> p a b", b=b)
            DMA('pool', view, src_ap, [], [('ring', s)], 'ring%d' % s)
            return view, ('ring', s)

        cur_layer = [0]

        def CK(name):
            if stop == name or stop == '%s@%d' % (name, cur_layer[0]):
                raise _Stop()

        def kcp(ap):
            return ap.rearrange("(kc p) n -> p kc n", p=128)

        C = 'consts'
        cl = []
        for name, tile in [('identb', identb), ('onesb', onesb), ('blk64b', blk64b), ('permB', permB), ('permA', permA),
                           ('ropeB', ropeB), ('ropeA', ropeA), ('cs64', cs64),
                           ('cond', condf), ('ctxflag', ctxf)]:
            cl.append(LOAD(tile[:], IN[name], []))
        cl.append(LOAD(gvec[:], IN['gvec'].rearrange("l p g -> p l g"), []))
        cl.append(LOAD(gT[:], IN['gT'].rearrange("l p k c -> p l k c"), []))
        cl.append(LOAD(badaT[:], IN['b_adaT'].rearrange("l p j -> p l j"), []))
        xkeys = [('x', c, t) for c in range(8) for t in range(2)]
        LOAD(xT[:], kcp(IN['xT']), xkeys)
        P.add('dve', lambda e: e.memset(epsT[:], EPS), [], [C], extra_deps=cl)
        P.add('dve', lambda e: e.memset(VA[:].rearrange("p a b c -> p (a b c)"), 1.0), [], ['Vinit'])
        P.add('dve', lambda e: e.memset(VB[:].rearrange("p a b c -> p (a b c)"), 1.0), [], ['Vinit'])
        P.add('dve', lambda e: e.memset(VD[:].rearrange("p a b c -> p (a b c)"), 1.0), [], ['Vinit'])
        for Vt, nh in ((VA, 4), (VB, 2), (VD, 4)):
            for kt in (8, 9):
                for h in range(nh):
                    P.add('dve', lambda e, Vt=Vt, kt=kt, h=h: e.tensor_copy(out=Vt[:, kt, h, 64:65], in_=ctxf[:, 0:1]), ['Vinit', C], ['Vinit'])
        P.add('dve', lambda e: e.memset(krc[:], 0.0), [], ['krc'])
        ACT(condb[:], condf[:], AF.Silu, [C], ['condb'])

        def rstd_from(ps_ap, pskey, inv_d, p0=0, p1=128, n=512):
            ti = newtmp()
            t = tmp[ti][p0:p1, 0:n]
            ACT(t, ps_ap, AF.Ln, [pskey, C], [('tmp', ti)], scale=inv_d, bias=epsT[p0:p1, 0:1])
            ACT(t, t, AF.Exp, [('tmp', ti)], [('tmp', ti)], scale=-0.5)
            return ti

        def mod_steps(l):
            par = l % 2
            for blk in range(18):
                wt, wk = ring_get(kcp(IN['w_ada'][l][:, blk * 512:(blk + 1) * 512]), 8, 512)
                for j in range(4):
                    col = blk * 4 + j
                    for kc in range(8):
                        MM(ps[7][:, col:col + 1], wt[:, kc, j * 128:(j + 1) * 128], condb[:, kc:kc + 1], kc == 0, kc == 7,
                           [wk, 'condb'], ['psM'])
                yield
            TT(modT[par][:, :], ps[7][:, 0:72], badaT[:, l, :], ALU.add, ['psM', C], [('mod', par)])
            for k in range(3):
                STT(modA[par][:, k, :], modT[par][:, (3 * k + 1) * 8:(3 * k + 2) * 8], 1.0, gT[:, l, k, :], ALU.add, ALU.mult,
                    [('mod', par), C], [('modA', par)])
            for k, f in ((0, 0.5), (1, 1.0), (2, 0.5)):
                TS(modG[par][:, k, :], modT[par][:, (3 * k + 2) * 8:(3 * k + 3) * 8], f, ALU.mult, [('mod', par)], [('modG', par)])
            yield

        def norm_mod(l, k):
            par = l % 2
            for t in range(2):
                hs = slice(t * 512, (t + 1) * 512)
                pi = newps()
                for c in range(8):
                    si = newsq()
                    ACT(sqb[si][:, :], xT[:, c, hs], AF.Square, [('x', c, t)], [('sq', si)])
                    MM(ps[pi][:, :], onesb[:, :], sqb[si][:, :], c == 0, c == 7, [('sq', si), C], [('ps', pi)])
                ACT(rstdN[:, :], ps[pi][:, :], AF.Ln, [('ps', pi), C], ['rstdN'], scale=1.0 / 1024, bias=epsT[:, 0:1])
                ACT(rstdN[:, :], rstdN[:, :], AF.Exp, ['rstdN'], ['rstdN'], scale=-0.5)
                for c in range(8):
                    ti = newtmp()
                    STT(tmp[ti][:, :], xT[:, c, hs], modA[par][:, k, c:c + 1], rstdN[:, :], ALU.mult, ALU.mult,
                        [('x', c, t), ('modA', par), 'rstdN'], [('tmp', ti)])
                    ACT(hT[:, c, hs], tmp[ti][:, :], AF.Identity, [('tmp', ti), ('mod', par)], [('h', c, t)],
                        bias=modT[par][:, 3 * k * 8 + c:3 * k * 8 + c + 1])

        def ffn(l, which, modgen):
            par = l % 2
            wg = IN['w_gate%d' % which][l]
            wu = IN['w_up%d' % which][l]
            wd = IN['w_down%d' % which][l]
            gk = 0 if which == 1 else 2
            for blk in range(6):
                nch = 4 if blk < 5 else 2
                c0 = blk * 512
                ncol = nch * 128
                gw, gkey = ring_get(kcp(wg[:, c0:c0 + ncol]), 8, ncol)
                uw, ukey = ring_get(kcp(wu[:, c0:c0 + ncol]), 8, ncol)
                dw, dkey = ring_get(wd[c0:c0 + ncol, :].rearrange("(j p) n -> p j n", p=128), nch, 1024)
                for t in range(2):
                    hs = slice(t * 512, (t + 1) * 512)
                    aslots = []
                    for j in range(nch):
                        ai = rot('act', 8)
                        aslots.append(ai)
                        pg = newps()
                        for kc in range(8):
                            MM(ps[pg][:, :], gw[:, kc, j * 128:(j + 1) * 128], hT[:, kc, hs], kc == 0, kc == 7,
                               [gkey, ('h', kc, t)], [('ps', pg)])
                        pu = newps()
                        for kc in range(8):
                            MM(ps[pu][:, :], uw[:, kc, j * 128:(j + 1) * 128], hT[:, kc, hs], kc == 0, kc == 7,
                               [ukey, ('h', kc, t)], [('ps', pu)])
                        ti = newtmp()
                        ACT(tmp[ti][:, :], ps[pg][:, :], AF.Silu, [('ps', pg)], [('tmp', ti)])
                        TT(actb[ai][:, :], tmp[ti][:, :], ps[pu][:, :], ALU.mult, [('tmp', ti), ('ps', pu)], [('act', ai)])
                    for dc in range(8):
                        po = newps()
                        for j in range(nch):
                            MM(ps[po][:, :], dw[:, j, dc * 128:(dc + 1) * 128], actb[aslots[j]][:, :], j == 0, j == nch - 1,
                               [dkey, ('act', aslots[j])], [('ps', po)])
                        STT(xT[:, dc, hs], ps[po][:, :], modG[par][:, gk, dc:dc + 1], xT[:, dc, hs], ALU.mult, ALU.add,
                            [('ps', po), ('modG', par), ('x', dc, t)], [('x', dc, t)])
                if modgen is not None:
                    next(modgen, None)

        def rope_norm(psraw, pskey, p0, p1, ones_lhsT, inv_d, gain, out16, outkeys, rope=None, store=None, n=512):
            si = newsq()
            ACT(sqb[si][p0:p1, 0:n], psraw, AF.Square, [pskey], [('sq', si)])
            pss = newps()
            MM(ps[pss][p0:p1, 0:n], ones_lhsT, sqb[si][p0:p1, 0:n], True, True, [('sq', si), C], [('ps', pss)])
            ri = rstd_from(ps[pss][p0:p1, 0:n], ('ps', pss), inv_d, p0, p1, n)
            r = tmp[ri][p0:p1, 0:n]
            if rope is None and store is None:
                STT(out16, psraw, gain, r, ALU.mult, ALU.mult, [pskey, ('tmp', ri), C], outkeys)
                return
            xi = newtmp()
            xn = tmp[xi][p0:p1, 0:n]
            STT(xn, psraw, gain, r, ALU.mult, ALU.mult, [pskey, ('tmp', ri), C], [('tmp', xi)])
            if rope is None:
                STORE(store, xn, [('tmp', xi)])
                ACT(out16, xn, AF.Copy, [('tmp', xi)], outkeys)
                return
            perm_lhsT, ropeC, ropeS = rope
            pp = newps()
            MM(ps[pp][p0:p1, 0:n], perm_lhsT, xn, True, True, [('tmp', xi), C], [('ps', pp)])
            t1 = newtmp()
            TT(tmp[t1][p0:p1, 0:n], xn, ropeC, ALU.mult, [('tmp', xi), C], [('tmp', t1)])
            t2 = newtmp()
            TT(tmp[t2][p0:p1, 0:n], ps[pp][p0:p1, 0:n], ropeS, ALU.mult, [('ps', pp), C], [('tmp', t2)])
            if store is None:
                TT(out16, tmp[t1][p0:p1, 0:n], tmp[t2][p0:p1, 0:n], ALU.add, [('tmp', t1), ('tmp', t2)], outkeys)
            else:
                TT(tmp[t1][p0:p1, 0:n], tmp[t1][p0:p1, 0:n], tmp[t2][p0:p1, 0:n], ALU.add, [('tmp', t1), ('tmp', t2)], [('tmp', t1)])
                STORE(store, tmp[t1][p0:p1, 0:n], [('tmp', t1)])
                ACT(out16, tmp[t1][p0:p1, 0:n], AF.Copy, [('tmp', t1)], outkeys)

        def mixer(l, modgen):
            par = l % 2

            def gv(col, p0=0, p1=128):
                return gvec[p0:p1, l, col:col + 1]

            def step():
                if modgen is not None:
                    next(modgen, None)

            DMA('pool', wuq[:], IN['w_uq'][l].rearrange("(c p) n -> p c n", p=128), [], ['wuq'], 'cx0')
            DMA('pool', wukv[:], IN['w_ukv'][l], [], ['wukv'], 'cx1')
            DMA('pool', ckvc[:], IN['ckvT_c'][l], [], ['ckvc'], 'cx2')
            DMA('sp', krc[64:96, :], IN['krT_c'][l], [], ['krc'], 'cx3')
            DMA('pool', KB[:, 1024:1280], IN['winkT_c'][l], [], [('KB', 'c')], 'cx4')
            DMA('pool', KD[:, :, 1024:1280], IN['nakT_c'][l].rearrange("(c p) k -> p c k", p=128), [], [('KD', 0, 'c'), ('KD', 1, 'c')], 'cx5')
            for i in range(2):
                DMA('pool', VB[:, 8 + i, :, 0:64], IN['winv_c'][l][i * 128:(i + 1) * 128, :].rearrange("p (h d) -> p h d", d=64),
                    [], [('VB', 8 + i)], 'cx6')
                DMA('pool', VD[:, 8 + i, :, 0:64], IN['nav_c'][l][i * 128:(i + 1) * 128, :].rearrange("p (h d) -> p h d", d=64),
                    [], [('VD', 8 + i)], 'cx7')
            ACT(sinkexp[:, :], gvec[:, l, 9:13], AF.Exp, [C], ['sinkexp'])
            wukv_v = wukv[:, 0:512].rearrange("p (h x) -> p h x", x=128)[:, :, 64:128]

            b0, b0k = ring_get(kcp(IN['w_in'][l][:, 0:416]), 8, 416)
            for t in range(2):
                hs = slice(t * 512, (t + 1) * 512)
                pcs = []
                for c in range(2):
                    pi = newps()
                    pcs.append(pi)
                    for kc in range(8):
                        MM(ps[pi][:, :], b0[:, kc, c * 128:(c + 1) * 128], hT[:, kc, hs], kc == 0, kc == 7, [b0k, ('h', kc, t)], [('ps', pi)])
                pss = newps()
                for c in range(2):
                    si = newsq()
                    ACT(sqb[si][:, :], ps[pcs[c]][:, :], AF.Square, [('ps', pcs[c])], [('sq', si)])
                    MM(ps[pss][:, :], onesb[:, :], sqb[si][:, :], c == 0, c == 1, [('sq', si), C], [('ps', pss)])
                ri = rstd_from(ps[pss][:, :], ('ps', pss), 1.0 / 256)
                for c in range(2):
                    STT(cqn[:, c, :], ps[pcs[c]][:, :], gv(c), tmp[ri][:, :], ALU.mult, ALU.mult,
                        [('ps', pcs[c]), ('tmp', ri), C], [('cqn', c)])
                pk = newps()
                for kc in range(8):
                    MM(ps[pk][:, :], b0[:, kc, 256:384], hT[:, kc, hs], kc == 0, kc == 7, [b0k, ('h', kc, t)], [('ps', pk)])
                rope_norm(ps[pk][:, :], ('ps', pk), 0, 128, onesb[:, :], 1.0 / 128, gv(2), ckvn[:, hs], [('ckvn', t)],
                          rope=None, store=OUT['o_ckvT'][l][:, hs])
                pr = newps()
                for kc in range(8):
                    MM(ps[pr][64:96, :], b0[:, kc, 384:416], hT[:, kc, hs], kc == 0, kc == 7, [b0k, ('h', kc, t)], [('ps', pr)])
                ti = newtmp()
                ACT(tmp[ti][64:96, :], ps[pr][64:96, :], AF.Copy, [('ps', pr)], [('tmp', ti)])
                STORE(OUT['o_krT'][l][:, hs], tmp[ti][64:96, :], [('tmp', ti)])
                ACT(sqkr[64:96, :], ps[pr][64:96, :], AF.Square, [('ps', pr)], ['sqkr'])
                tg = newtmp()
                TS(tmp[tg][64:96, :], ps[pr][64:96, :], gv(4, 64, 96), ALU.mult, [('ps', pr), C], [('tmp', tg)])
                pp = newps()
                MM(ps[pp][64:96, :], permA[64:96, 64:96], tmp[tg][64:96, :], True, True, [('tmp', tg), C], [('ps', pp)])
                t1 = newtmp()
                TT(tmp[t1][64:96, :], tmp[tg][64:96, :], ropeA[64:96, 0, hs], ALU.mult, [('tmp', tg), C], [('tmp', t1)])
                TT(krr[64:96, :], ps[pp][64:96, :], ropeA[64:96, 1, hs], ALU.mult, [('ps', pp), C], ['krr'])
                TT(krr[64:96, :], krr[64:96, :], tmp[t1][64:96, :], ALU.add, ['krr', ('tmp', t1)], ['krr'])
                for h in range(4):
                    pn = newps()
                    MM(ps[pn][0:64, :], wukv[:, h * 128:h * 128 + 64], ckvn[:, hs], True, True, ['wukv', ('ckvn', t)], [('ps', pn)])
                    si = newsq()
                    ACT(sqb[si][0:64, :], ps[pn][0:64, :], AF.Square, [('ps', pn)], [('sq', si)])
                    pss = newps()
                    MM(ps[pss][0:96, :], onesb[0:64, 0:96], sqb[si][0:64, :], True, False, [('sq', si), C], [('ps', pss)])
                    MM(ps[pss][0:96, :], onesb[64:96, 0:96], sqkr[64:96, :], False, True, ['sqkr', C], [('ps', pss)])
                    ri = rstd_from(ps[pss][0:96, :], ('ps', pss), 1.0 / 96, 0, 96)
                    STT(KA[0:64, h, hs], ps[pn][0:64, :], gv(4, 0, 64), tmp[ri][0:64, :], ALU.mult, ALU.mult,
                        [('ps', pn), ('tmp', ri), C], [('KA', h, t)])
                    TT(KA[64:96, h, hs], krr[64:96, :], tmp[ri][64:96, :], ALU.mult, ['krr', ('tmp', ri)], [('KA', h, t)])
                    pq = newps()
                    for c in range(2):
                        MM(ps[pq][0:96, :], wuq[:, c, h * 96:(h + 1) * 96], cqn[:, c, :], c == 0, c == 1, ['wuq', ('cqn', c)], [('ps', pq)])
                    rope_norm(ps[pq][0:96, :], ('ps', pq), 0, 96, onesb[0:96, 0:96], 1.0 / 96, gv(3, 0, 96), QA[0:96, h, hs],
                              [('QA', h, t)], rope=(permA[0:96, 0:96], ropeA[0:96, 0, hs], ropeA[0:96, 1, hs]))
                for tt in range(4):
                    kt = t * 4 + tt
                    pv = newps()
                    MM(ps[pv][:, 0:256], ckvn[:, kt * 128:(kt + 1) * 128], wukv_v, True, True, ['wukv', ('ckvn', t)], [('ps', pv)])
                    ACT(VA[:, kt, :, 0:64], ps[pv][:, 0:256].rearrange("p (h d) -> p h d", d=64), AF.Copy, [('ps', pv), 'Vinit'], [('VA', kt)])
            step()
            CK('projA')
            si = newsq()
            ACT(sqb[si][64:96, 0:256], krc[64:96, :], AF.Square, ['krc'], [('sq', si)])
            sq_krc = si
            tgc = newtmp()
            TS(tmp[tgc][64:96, 0:256], krc[64:96, :], gv(4, 64, 96), ALU.mult, ['krc', C], [('tmp', tgc)])
            for h in range(4):
                pn = newps()
                MM(ps[pn][0:64, 0:256], wukv[:, h * 128:h * 128 + 64], ckvc[:, :], True, True, ['wukv', 'ckvc'], [('ps', pn)])
                si = newsq()
                if si == sq_krc:
                    si = newsq()
                ACT(sqb[si][0:64, 0:256], ps[pn][0:64, 0:256], AF.Square, [('ps', pn)], [('sq', si)])
                pss = newps()
                MM(ps[pss][0:96, 0:256], onesb[0:64, 0:96], sqb[si][0:64, 0:256], True, False, [('sq', si), C], [('ps', pss)])
                MM(ps[pss][0:96, 0:256], onesb[64:96, 0:96], sqb[sq_krc][64:96, 0:256], False, True, [('sq', sq_krc), C], [('ps', pss)])
                ri = rstd_from(ps[pss][0:96, 0:256], ('ps', pss), 1.0 / 96, 0, 96, 256)
                if ri == tgc:
                    raise RuntimeError("tmp rotation clash")
                STT(KA[0:64, h, 1024:1280], ps[pn][0:64, 0:256], gv(4, 0, 64), tmp[ri][0:64, 0:256], ALU.mult, ALU.mult,
                    [('ps', pn), ('tmp', ri), C], [('KA', h, 'c')])
                TT(KA[64:96, h, 1024:1280], tmp[tgc][64:96, 0:256], tmp[ri][64:96, 0:256], ALU.mult, [('tmp', tgc), ('tmp', ri)], [('KA', h, 'c')])
            for i in range(2):
                pv = newps()
                MM(ps[pv][:, 0:256], ckvc[:, i * 128:(i + 1) * 128], wukv_v, True, True, ['wukv', 'ckvc'], [('ps', pv)])
                ACT(VA[:, 8 + i, :, 0:64], ps[pv][:, 0:256].rearrange("p (h d) -> p h d", d=64), AF.Copy, [('ps', pv), 'Vinit'], [('VA', 8 + i)])
            step()
            CK('ctxA')

            b1, b1k = ring_get(kcp(IN['w_in'][l][:, 416:928]), 8, 512)
            for t in range(2):
                hs = slice(t * 512, (t + 1) * 512)
                for ci, hpair in enumerate(((0, 2), (1, 3))):
                    pi = newps()
                    for hh, pb in zip(hpair, (0, 64)):
                        for kc in range(8):
                            MM(ps[pi][pb:pb + 64, :], b1[:, kc, hh * 64:(hh + 1) * 64], hT[:, kc, hs], kc == 0, kc == 7,
                               [b1k, ('h', kc, t)], [('ps', pi)])
                    rope_norm(ps[pi][:, :], ('ps', pi), 0, 128, blk64b[:, :], 1.0 / 64, gv(5), QB[:, ci, hs], [('QB', ci, t)],
                              rope=(permB[:, :], ropeB[:, 0, hs], ropeB[:, 1, hs]))
                    CK('qb')
                pi = newps()
                for kc in range(8):
                    MM(ps[pi][:, :], b1[:, kc, 256:384], hT[:, kc, hs], kc == 0, kc == 7, [b1k, ('h', kc, t)], [('ps', pi)])
                rope_norm(ps[pi][:, :], ('ps', pi), 0, 128, blk64b[:, :], 1.0 / 64, gv(6), KB[:, hs], [('KB', t)],
                          rope=(permB[:, :], ropeB[:, 0, hs], ropeB[:, 1, hs]), store=OUT['o_kbT'][l][:, hs])
                CK('kb')
                for tt in range(4):
                    kt = t * 4 + tt
                    pv = newps()
                    for kc in range(8):
                        MM(ps[pv][:, 0:128], hT[:, kc, kt * 128:(kt + 1) * 128], b1[:, kc, 384:512], kc == 0, kc == 7,
                           [b1k, ('h', kc, t)], [('ps', pv)])
                    ti = newtmp()
                    ACT(tmp[ti][:, 0:128], ps[pv][:, 0:128], AF.Copy, [('ps', pv)], [('tmp', ti)])
                    STORE(OUT['o_vb'][l][kt * 128:(kt + 1) * 128, :], tmp[ti][:, 0:128], [('tmp', ti)])
                    ACT(VB[:, kt, :, 0:64], ps[pv][:, 0:128].rearrange("p (h d) -> p h d", d=64), AF.Copy, [('ps', pv), 'Vinit'], [('VB', kt)])
            step()
            CK('projB')
            b2, b2k = ring_get(kcp(IN['w_in'][l][:, 928:1440]), 8, 512)
            for t in range(2):
                hs = slice(t * 512, (t + 1) * 512)
                for ch in range(2):
                    pi = newps()
                    for kc in range(8):
                        MM(ps[pi][:, :], b2[:, kc, ch * 128:(ch + 1) * 128], hT[:, kc, hs], kc == 0, kc == 7, [b2k, ('h', kc, t)], [('ps', pi)])
                    ACT(XcT[:, ch, hs], ps[pi][:, :], AF.Copy, [('ps', pi)], [('pt', 8 + 2 * ch + t)])
                for ch in range(2):
                    pi = newps()
                    for kc in range(8):
                        MM(ps[pi][:, :], b2[:, kc, 256 + ch * 128:256 + (ch + 1) * 128], hT[:, kc, hs], kc == 0, kc == 7,
                           [b2k, ('h', kc, t)], [('ps', pi)])
                    rope_norm(ps[pi][:, :], ('ps', pi), 0, 128, blk64b[:, :], 1.0 / 64, gv(7), QD[:, ch, hs], [('QD', ch, t)])
            step()
            b3, b3k = ring_get(kcp(IN['w_in'][l][:, 1440:1952]), 8, 512)
            for t in range(2):
                hs = slice(t * 512, (t + 1) * 512)
                for ch in range(2):
                    pi = newps()
                    for kc in range(8):
                        MM(ps[pi][:, :], b3[:, kc, ch * 128:(ch + 1) * 128], hT[:, kc, hs], kc == 0, kc == 7, [b3k, ('h', kc, t)], [('ps', pi)])
                    rope_norm(ps[pi][:, :], ('ps', pi), 0, 128, blk64b[:, :], 1.0 / 64, gv(8), KD[:, ch, hs], [('KD', ch, t)],
                              rope=None, store=OUT['o_kdT'][l][ch * 128:(ch + 1) * 128, hs])
                for tt in range(4):
                    kt = t * 4 + tt
                    pv = newps()
                    for kc in range(8):
                        MM(ps[pv][:, 0:256], hT[:, kc, kt * 128:(kt + 1) * 128], b3[:, kc, 256:512], kc == 0, kc == 7,
                           [b3k, ('h', kc, t)], [('ps', pv)])
                    ti = newtmp()
                    ACT(tmp[ti][:, 0:256], ps[pv][:, 0:256], AF.Copy, [('ps', pv)], [('tmp', ti)])
                    STORE(OUT['o_vd'][l][kt * 128:(kt + 1) * 128, :], tmp[ti][:, 0:256], [('tmp', ti)])
                    ACT(VD[:, kt, :, 0:64], ps[pv][:, 0:256].rearrange("p (h d) -> p h d", d=64), AF.Copy, [('ps', pv), 'Vinit'], [('VD', kt)])
            step()
            CK('projD')
            for lt in range(8):
                for ch in range(2):
                    pi = newps()
                    MM(ps[pi][:, 0:256], XcT[:, ch, lt * 128:(lt + 1) * 128], cs64[:, :], True, True, [('pt', 8 + 2 * ch + lt // 4), C], [('ps', pi)])
                    P.add('dve', lambda e, lt=lt, ch=ch, pi=pi: e.tensor_copy(out=ABt[:, lt, ch, :], in_=ps[pi][:, 0:256]),
                          [('ps', pi)], [('ab', lt, ch)])
            for t in range(2):
                hs = slice(t * 512, (t + 1) * 512)
                cb, cbk = ring_get(IN['dftC'][:, hs].rearrange("(lt p) n -> p lt n", p=128), 8, 512)
                sbk_, sbkk = ring_get(IN['dftS'][:, hs].rearrange("(lt p) n -> p lt n", p=128), 8, 512)
                for ch in range(2):
                    pi = newps()
                    for lt in range(8):
                        MM(ps[pi][:, :], ABt[:, lt, ch, 0:128], cb[:, lt, :], lt == 0, False, [('ab', lt, ch), ('pt', lt), cbk], [('ps', pi)])
                    for lt in range(8):
                        MM(ps[pi][:, :], ABt[:, lt, ch, 128:256], sbk_[:, lt, :], False, lt == 7, [('ab', lt, ch), ('pt', lt), sbkk], [('ps', pi)])
                    ACT(hT[:, 4 + ch, hs], ps[pi][:, :], AF.Copy, [('ps', pi)], [('h', 4 + ch, t)])
            step()
            CK('fourier')

            shared = {}

            def attend_qk(kind, h, t):
                hs = slice(t * 512, (t + 1) * 512)
                if kind == 'B':
                    kts = [kt for kt in range(4 * t - 1, 4 * t + 5) if 0 <= kt <= 7] + [8, 9]
                    if 'maskB' not in shared:
                        shared['maskB'] = ring_get(IN['maskB'], 6, 512)
                elif kind == 'D':
                    kts = list(range(0, 6) if t == 0 else range(2, 8)) + [8, 9]
                    if ('biasD', h) not in shared:
                        shared[('biasD', h)] = [ring_get(IN['biasD'][l][h][g * 512:(g + 1) * 512, :].rearrange("(kt p) q -> p kt q", p=128), 4, 1024)
                                                for g in range(2)]
                else:
                    kts = list(range(10))
                    if 'EF' not in shared:
                        s_ = rot('ring', NSLOT)
                        DMA('pool', ring[s_][0:8, 0:2304], IN['EFA'], [], [('ring', s_)], 'ring%d' % s_)
                        shared['EF'] = (ring[s_], ('ring', s_))
                slots = {}
                for kt in kts:
                    pi = newps()
                    ksl = slice(kt * 128, (kt + 1) * 128)
                    own = kt < 8
                    kk = (kt // 4) if own else 'c'
                    if kind == 'A':
                        MM(ps[pi][:, :], KA[0:96, h, ksl], QA[0:96, h, hs], True, not own, [('KA', h, kk), ('QA', h, t)], [('ps', pi)])
                        if own:
                            ef, efk = shared['EF']
                            MM(ps[pi][:, :], ef[0:8, kt * 128:(kt + 1) * 128], ef[0:8, 1280 + t * 512:1280 + (t + 1) * 512], False, True, [efk], [('ps', pi)])
                        scale = 96.0 ** -0.5
                    elif kind == 'B':
                        pb = (h // 2) * 64
                        ci = h % 2
                        MM(ps[pi][:, :], KB[pb:pb + 64, ksl], QB[pb:pb + 64, ci, hs], True, not own, [('KB', kk), ('QB', ci, t)], [('ps', pi)])
                        if own:
                            mv, mk = shared['maskB']
                            MM(ps[pi][:, :], identb[:, :], mv[:, kt - 4 * t + 1, :], False, True, [C, mk], [('ps', pi)])
                        scale = 0.125
                    else:
                        pb = (h % 2) * 64
                        ch = h // 2
                        MM(ps[pi][:, :], KD[pb:pb + 64, ch, ksl], QD[pb:pb + 64, ch, hs], True, not own, [('KD', ch, kk), ('QD', ch, t)], [('ps', pi)])
                        if own:
                            bv, bk = shared[('biasD', h)][kt // 4]
                            MM(ps[pi][:, :], identb[:, :], bv[:, kt % 4, hs], False, True, [C, bk], [('ps', pi)])
                        scale = 0.125
                    s = rot('pt', NPT)
                    slots[kt] = s
                    ACT(PT[:, s, :], ps[pi][:, :], AF.Exp, [('ps', pi)], [('pt', s)], scale=scale)
                return (kind, h, t, kts, slots)

            def attend_pv(ctx):
                kind, h, t, kts, slots = ctx
                V, vh, vname = {'A': (VA, h, 'VA'), 'B': (VB, h // 2, 'VB'), 'D': (VD, h, 'VD')}[kind]
                po = newps()
                for qi in range(4):
                    for i, kt in enumerate(kts):
                        MM(ps[po][:, qi * 65:(qi + 1) * 65], PT[:, slots[kt], qi * 128:(qi + 1) * 128], V[:, kt, vh, 0:65],
                           i == 0, i == len(kts) - 1, [('pt', slots[kt]), (vname, kt), 'Vinit'], [('ps', po)])
                for qi in range(4):
                    den = ps[po][:, qi * 65 + 64:qi * 65 + 65]
                    if kind == 'B':
                        TS(rec[:, qi:qi + 1], den, sinkexp[:, h:h + 1], ALU.add, [('ps', po), 'sinkexp'], [('rec', qi)])
                        RECIP(rec[:, qi:qi + 1], rec[:, qi:qi + 1], [('rec', qi)], [('rec', qi)])
                    else:
                        RECIP(rec[:, qi:qi + 1], den, [('ps', po)], [('rec', qi)])
                    hslot = h % 2
                    TS(mtok[:, t, qi, hslot * 64:(hslot + 1) * 64], ps[po][:, qi * 65:qi * 65 + 64], rec[:, qi:qi + 1], ALU.mult,
                       [('ps', po), ('rec', qi)], [('mtok', t, qi, hslot)])

            def attend_tr(chunk):
                for t in range(2):
                    hs = slice(t * 512, (t + 1) * 512)
                    pt_ = newps()
                    for qi in range(4):
                        MM(ps[pt_][:, qi * 128:(qi + 1) * 128], mtok[:, t, qi, :], identb[:, :], True, True,
                           [('mtok', t, qi, 0), ('mtok', t, qi, 1), C], [('ps', pt_)])
                    P.add('dve', lambda e, hs=hs, pt_=pt_, cc=chunk: e.tensor_copy(out=hT[:, cc, hs], in_=ps[pt_][:, :]),
                          [('ps', pt_)], [('h', chunk, t)])

            units = []
            for kind, chunk0 in (('A', 0), ('B', 2), ('D', 6)):
                for pair in range(2):
                    for h in (2 * pair, 2 * pair + 1):
                        for t in range(2):
                            units.append((kind, h, t, chunk0 + pair, (h % 2 == 1 and t == 1)))
            prev = None
            for (kind, h, t, chunk, last) in units:
                ctx = attend_qk(kind, h, t)
                if prev is not None:
                    attend_pv(prev[0])
                    if prev[2]:
                        attend_tr(prev[1])
                        step()
                prev = (ctx, chunk, last)
            attend_pv(prev[0])
            attend_tr(prev[1])
            step()
            CK('attn')

            for blk in range(2):
                wo, wok = ring_get(kcp(IN['w_o'][l][:, blk * 512:(blk + 1) * 512]), 8, 512)
                for dcc in range(4):
                    dc = blk * 4 + dcc
                    for t in range(2):
                        hs = slice(t * 512, (t + 1) * 512)
                        po = newps()
                        for c in range(8):
                            MM(ps[po][:, :], wo[:, c, dcc * 128:(dcc + 1) * 128], hT[:, c, hs], c == 0, c == 7, [wok, ('h', c, t)], [('ps', po)])
                        STT(xT[:, dc, hs], ps[po][:, :], modG[par][:, 1, dc:dc + 1], xT[:, dc, hs], ALU.mult, ALU.add,
                            [('ps', po), ('modG', par), ('x', dc, t)], [('x', dc, t)])
            step()

        try:
            CK('load')
            g0 = mod_steps(0)
            for _ in g0:
                pass
            CK('mod')
            for l in range(nl):
                cur_layer[0] = l
                mg = mod_steps(l + 1) if l + 1 < nl else None
                norm_mod(l, 0)
                CK('norm0')
                ffn(l, 1, mg)
                CK('ffn1')
                norm_mod(l, 1)
                mixer(l, mg)
                CK('mixer')
                norm_mod(l, 2)
                ffn(l, 2, mg)
                if mg is not None:
                    for _ in mg:
                        pass
                CK('layer')
        except _Stop:
            pass
        STORE(kcp(OUT['yT']), xT[:], xkeys)
        P.add('sp', None, extra_deps=out_ids)
        P.emit()
        nc._prog_stats = (P.n_ops, P.sig_counts, P.chan_counts)
    return nc


def _rope_tables(r, sample):
    Cm = np.ones((r, T), np.float32)
    Sm = np.zeros((r, T), np.float32)
    if not sample:
        return Cm, Sm
    half = r // 2
    t = np.arange(T)
    inv = (10000.0 ** (-np.arange(0, half, 2, dtype=np.float32) / np.float32(half))).astype(np.float32)
    q = half // 2
    for part, pos in ((0, t // 64), (1, t % 64)):
        ang = pos.astype(np.float32)[:, None] * inv[None, :]
        cos = np.cos(ang).astype(np.float32).T
        sin = np.sin(ang).astype(np.float32).T
        b = part * half
        Cm[b:b + q] = cos
        Cm[b + q:b + half] = cos
        Sm[b:b + q] = -sin
        Sm[b + q:b + half] = sin
    return Cm, Sm


def _perm(r):
    M = np.zeros((r, r), np.float32)
    half = r // 2
    q = half // 2
    for d in range(r):
        o = d % half
        p = d + q if o < q else d - q
        M[p, d] = 1.0
    return M


def _consts(sample):
    c = {}
    c['identb'] = np.eye(128, dtype=np.float32).astype(bf16)
    c['onesb'] = np.ones((128, 128), np.float32).astype(bf16)
    c['blk64b'] = np.kron(np.eye(2, dtype=np.float32), np.ones((64, 64), np.float32)).astype(bf16)
    pb = np.zeros((128, 128), np.float32)
    pb[0:64, 0:64] = _perm(64)
    pb[64:128, 64:128] = _perm(64)
    c['permB'] = pb
    pa = np.zeros((128, 128), np.float32)
    pa[64:96, 64:96] = _perm(32)
    c['permA'] = pa
    Cb, Sb = _rope_tables(64, sample)
    rb = np.zeros((128, 2, T), np.float32)
    rb[0:64, 0], rb[64:128, 0] = Cb, Cb
    rb[0:64, 1], rb[64:128, 1] = Sb, Sb
    c['ropeB'] = rb
    Ca, Sa = _rope_tables(32, sample)
    ra = np.zeros((128, 2, T), np.float32)
    ra[:, 0] = 1.0
    ra[64:96, 0] = Ca
    ra[64:96, 1] = Sa
    c['ropeA'] = ra
    k = np.arange(64)
    ang = 2 * np.pi * np.outer(k, k) / 64.0
    C64 = np.cos(ang) / 8.0
    S64 = np.sin(ang) / 8.0
    cs = np.zeros((128, 256), np.float64)
    cs[0:64, 0:64] = C64
    cs[64:128, 64:128] = C64
    cs[0:64, 128:192] = S64
    cs[64:128, 192:256] = S64
    c['cs64'] = cs.astype(np.float32).astype(bf16)
    L = 1024 if sample else 256
    kk = np.arange(L)
    angL = 2 * np.pi * (np.outer(kk, kk) % L) / float(L)
    CL = np.cos(angL) / np.sqrt(L)
    SL = -np.sin(angL) / np.sqrt(L)
    if sample:
        dC, dS = CL, SL
    else:
        dC = np.kron(np.eye(4), CL)
        dS = np.kron(np.eye(4), SL)
    c['dftC'] = dC.astype(np.float32).astype(bf16)
    c['dftS'] = dS.astype(np.float32).astype(bf16)
    ea = np.zeros((8, 1280), np.float32)
    fa = np.zeros((8, 1024), np.float32)
    if not sample:
        for s in range(4):
            ea[s, s * 256:(s + 1) * 256] = 1.0
            fa[s, s * 256:(s + 1) * 256] = BIG
        ea[4, 0:1024] = 1.0
        fa[4, :] = -BIG
    c['EFA'] = np.concatenate([ea, fa], axis=1).astype(bf16)
    mb = np.full((128, 6, 512), NEG, np.float32)
    kl = np.arange(128)[:, None]
    ql = np.arange(128)[None, :]
    for ki in range(6):
        ktp = ki - 1
        for qi in range(4):
            off = ktp - qi
            blkm = np.full((128, 128), NEG, np.float32)
            if sample:
                if off == 0:
                    blkm[:] = 0.0
                elif off == -1:
                    blkm = np.where(ql <= kl, 0.0, NEG).astype(np.float32)
                elif off == 1:
                    blkm = np.where(kl <= ql, 0.0, NEG).astype(np.float32)
            else:
                if off == 0 or (off == 1 and qi % 2 == 0) or (off == -1 and qi % 2 == 1):
                    blkm[:] = 0.0
            mb[:, ki, qi * 128:(qi + 1) * 128] = blkm
    c['maskB'] = mb.astype(bf16)
    return c


def _na_index():
    rows = 16
    r = np.arange(rows)
    row_start = np.clip(r - 4, 0, rows - 8)
    col = np.arange(64)
    col_start = np.clip(col - 8, 0, 64 - 16)
    q = np.arange(1024)
    qr, qc = q // 64, q % 64
    kr, kc = qr, qc
    KR, QR = kr[:, None], qr[None, :]
    KC, QC = kc[:, None], qc[None, :]
    valid = (KR >= row_start[QR]) & (KR < row_start[QR] + 8) & (KC >= col_start[QC]) & (KC < col_start[QC] + 16)
    dr = np.clip(KR - QR + 7, 0, 14)
    dc = np.clip(KC - QC + 15, 0, 30)
    return valid, dr, dc


def _vecT(v):
    return np.ascontiguousarray(v.reshape(-1, 128).T)


_PROG = {}


def _prep(x_prompt, x_sample, cache_mla_ckv, cache_mla_krope, cache_win_k, cache_win_v, cache_na_k, cache_na_v,
           c, c_ctx, w_ada, b_ada, g_ffn1, w_gate1, w_up1, w_down1, g_mix, w_in, g_qa, w_uq, g_kva, w_ukv,
           qn_a, kn_a, qn_b, kn_b, sink_b, qn_d, kn_d, rpb_d, w_o, g_ffn2, w_gate2, w_up2, w_down2):
    f = lambda a: np.ascontiguousarray(np.asarray(a, dtype=np.float32))
    x_prompt, x_sample, c, c_ctx = f(x_prompt), f(x_sample), f(c), f(c_ctx)
    shared = {n: f(v) for n, v in dict(w_ada=w_ada, w_gate1=w_gate1, w_up1=w_up1, w_down1=w_down1, w_gate2=w_gate2,
                                       w_up2=w_up2, w_down2=w_down2, w_in=w_in, w_o=w_o, w_uq=w_uq, w_ukv=w_ukv).items()}
    b_ada = f(b_ada)
    shared['b_adaT'] = np.ascontiguousarray(b_ada.reshape(NL, 72, 128).transpose(0, 2, 1))
    gs = np.stack([f(g_ffn1), f(g_mix), f(g_ffn2)], axis=1)
    shared['gT'] = np.ascontiguousarray(gs.reshape(NL, 3, 8, 128).transpose(0, 3, 1, 2))
    gv = np.zeros((NL, 128, NG), np.float32)
    gv[:, :, 0:2] = f(g_qa).reshape(NL, 2, 128).transpose(0, 2, 1)
    gv[:, :, 2] = f(g_kva)
    gv[:, 0:96, 3] = f(qn_a)
    gv[:, 0:96, 4] = f(kn_a)
    gv[:, :, 5] = np.tile(f(qn_b), (1, 2))
    gv[:, :, 6] = np.tile(f(kn_b), (1, 2))
    gv[:, :, 7] = np.tile(f(qn_d), (1, 2))
    gv[:, :, 8] = np.tile(f(kn_d), (1, 2))
    gv[:, :, 9:13] = f(sink_b)[:, None, :]
    shared['gvec'] = gv
    consts = {True: _consts(True), False: _consts(False)}
    valid, dr, dc = _na_index()
    rpb = f(rpb_d)
    bias_s = np.where(valid[None, None], rpb[:, :, dr, dc], np.float32(NEG)).astype(np.float32)
    seq = np.arange(1024) // 256
    bias_p1 = np.where(seq[:, None] == seq[None, :], np.float32(0.0), np.float32(NEG)).astype(np.float32)
    bias_p = np.ascontiguousarray(np.broadcast_to(bias_p1, (NL, 4, 1024, 1024)))
    cm_ckv, cm_kr = f(cache_mla_ckv), f(cache_mla_krope)
    cw_k, cw_v, cn_k, cn_v = f(cache_win_k), f(cache_win_v), f(cache_na_k), f(cache_na_v)
    in_maps = []
    for core in range(8):
        sample = core >= 4
        m = dict(shared)
        m.update(consts[sample])
        if sample:
            b = core - 4
            xs = x_sample[b]
            cond = c[b]
            m['ckvT_c'] = np.ascontiguousarray(cm_ckv[b].transpose(0, 2, 1))
            m['krT_c'] = np.ascontiguousarray(cm_kr[b].transpose(0, 2, 1))
            m['winkT_c'] = np.ascontiguousarray(cw_k[b].reshape(NL, 256, 128).transpose(0, 2, 1))
            m['winv_c'] = np.ascontiguousarray(cw_v[b].reshape(NL, 256, 128))
            m['nakT_c'] = np.ascontiguousarray(cn_k[b].reshape(NL, 256, 256).transpose(0, 2, 1))
            m['nav_c'] = np.ascontiguousarray(cn_v[b].reshape(NL, 256, 256))
            m['biasD'] = bias_s
            m['ctxflag'] = np.ones((128, 1), np.float32)
        else:
            xs = x_prompt[4 * core:4 * core + 4].reshape(1024, 1024)
            cond = c_ctx
            m['ckvT_c'] = np.zeros((NL, 128, 256), np.float32)
            m['krT_c'] = np.zeros((NL, 32, 256), np.float32)
            m['winkT_c'] = np.zeros((NL, 128, 256), np.float32)
            m['winv_c'] = np.zeros((NL, 256, 128), np.float32)
            m['nakT_c'] = np.zeros((NL, 256, 256), np.float32)
            m['nav_c'] = np.zeros((NL, 256, 256), np.float32)
            m['biasD'] = bias_p
            m['ctxflag'] = np.zeros((128, 1), np.float32)
        m['xT'] = np.ascontiguousarray(xs.T)
        m['cond'] = _vecT(cond)
        in_maps.append(m)
    return in_maps


def _assemble(R):
    y_prompt = np.concatenate([R[i]['yT'].T.reshape(4, 256, 1024) for i in range(4)], axis=0)
    y_sample = np.stack([R[4 + b]['yT'].T for b in range(4)], axis=0)

    def featmaj(name, feat):
        outs = []
        for i in range(4):
            a = R[i][name]
            a = a.reshape(NL, feat, 4, 256).transpose(2, 0, 3, 1)
            outs.append(a)
        return np.ascontiguousarray(np.concatenate(outs, axis=0))

    def tokmaj(name, feat):
        outs = []
        for i in range(4):
            a = R[i][name].reshape(NL, 4, 256, feat).transpose(1, 0, 2, 3)
            outs.append(a)
        return np.ascontiguousarray(np.concatenate(outs, axis=0))

    new_ckv = featmaj('o_ckvT', 128)
    new_kr = featmaj('o_krT', 32)
    new_wk = featmaj('o_kbT', 128).reshape(16, NL, 256, 2, 64)
    new_wv = tokmaj('o_vb', 128).reshape(16, NL, 256, 2, 64)
    new_nk = featmaj('o_kdT', 256).reshape(16, NL, 256, 4, 64)
    new_nv = tokmaj('o_vd', 256).reshape(16, NL, 256, 4, 64)
    return (np.ascontiguousarray(y_prompt.astype(np.float32)), np.ascontiguousarray(y_sample.astype(np.float32)),
            new_ckv, new_kr, new_wk, new_wv, new_nk, new_nv)


def kernel(**inputs):
    in_maps = _prep(**inputs)
    if 'nc' not in _PROG:
        _PROG['nc'] = build_program(NL)
    res = run_bass_kernel_spmd(_PROG['nc'], in_maps, core_ids=list(range(8)))
    return _assemble(res.results)
```

```python
import numpy as np
import ml_dtypes
from contextlib import ExitStack
import concourse.bass as bass
import concourse.mybir as mybir
from concourse.bass_utils import run_bass_kernel_spmd

F32 = mybir.dt.float32
BF16 = mybir.dt.bfloat16
AF = mybir.ActivationFunctionType
ALU = mybir.AluOpType
bf16 = ml_dtypes.bfloat16

NL = 4
D = 1024
T = 1024
DFF = 2816
EPS = 1e-6
NEG = -30000.0
BIG = 1024.0
RING_ELEMS = 4096
NSLOT = 4
NTMP = 6
NPT = 16
NG = 13


class Prog:
    COMPUTE = ('pe', 'act', 'dve', 'pool')

    def __init__(self, nc, es, same_engine_sync=True):
        self.nc = nc
        self.es = es
        self.ops = []
        self.lastw = {}
        self.readers = {}
        self.chan_last = {}
        self.chan_cnt = {}
        self.same_engine_sync = same_engine_sync

    def add(self, eng, fn, reads=(), writes=(), chan=None, extra_deps=()):
        idx = len(self.ops)
        deps = set(extra_deps)
        if eng in ('act', 'dve'):
            pk = [k for k in reads if k == 'psM' or (isinstance(k, tuple) and k[0] == 'ps')]
            if pk:
                writes = list(writes) + pk
        for k in reads:
            w = self.lastw.get(k)
            if w is not None:
                deps.add(w)
        for k in writes:
            w = self.lastw.get(k)
            if w is not None:
                deps.add(w)
            last = {}
            for r in self.readers.get(k, ()):
                rop = self.ops[r]
                if rop['chan'] is not None:
                    deps.add(r)
                else:
                    last[rop['eng']] = r
            deps.update(last.values())
        for k in reads:
            self.readers.setdefault(k, []).append(idx)
        for k in writes:
            self.lastw[k] = idx
            self.readers[k] = []
        if chan is not None:
            p = self.chan_last.get(chan)
            if p is not None:
                deps.add(p)
            self.chan_last[chan] = idx
            self.chan_cnt[chan] = self.chan_cnt.get(chan, 0) + 1
        deps.discard(idx)
        self.ops.append(dict(eng=eng, fn=fn, deps=deps, chan=chan, signal=False, sigval=None,
                             chanval=(16 * self.chan_cnt[chan] if chan is not None else None)))
        return idx

    def emit(self):
        nc = self.nc
        ops = self.ops
        for op in ops:
            for d in op['deps']:
                dop = ops[d]
                if dop['chan'] is None:
                    if dop['eng'] == op['eng'] and (op['eng'] == 'pe' or not self.same_engine_sync):
                        continue
                    dop['signal'] = True
        cnt = {}
        for op in ops:
            if op['chan'] is None and op['signal']:
                cnt[op['eng']] = cnt.get(op['eng'], 0) + 1
                op['sigval'] = cnt[op['eng']]
        self.sig_counts = dict(cnt)
        self.chan_counts = dict(self.chan_cnt)
        self.n_ops = len(ops)
        sems = {e: self.es.enter_context(nc.semaphore('s_' + e)) for e in self.COMPUTE + ('sp',)}
        csems = {c: self.es.enter_context(nc.semaphore('c_%d' % i)) for i, c in enumerate(self.chan_cnt)}
        by_eng = {}
        for i, op in enumerate(ops):
            by_eng.setdefault(op['eng'], []).append(i)

        def run_engine(ename, e):
            waited = {}
            for i in by_eng.get(ename, ()):
                op = ops[i]
                need = {}
                for d in op['deps']:
                    dop = ops[d]
                    if dop['chan'] is not None:
                        key = ('c', dop['chan'])
                        val = dop['chanval']
                    else:
                        if dop['eng'] == ename and (ename == 'pe' or not self.same_engine_sync):
                            continue
                        key = ('e', dop['eng'])
                        val = dop['sigval']
                    if val > need.get(key, 0):
                        need[key] = val
                for key, val in need.items():
                    if waited.get(key, 0) >= val:
                        continue
                    waited[key] = val
                    s = csems[key[1]] if key[0] == 'c' else sems[key[1]]
                    e.wait_ge(s, val)
                if op['fn'] is None:
                    continue
                ins = op['fn'](e)
                if op['chan'] is not None:
                    ins.then_inc(csems[op['chan']], 16)
                elif op['signal']:
                    ins.then_inc(sems[ename], 1)

        with nc.Block() as block:
            @block.tensor
            def _(e):
                run_engine('pe', e)

            @block.scalar
            def _(e):
                run_engine('act', e)

            @block.vector
            def _(e):
                run_engine('dve', e)

            @block.gpsimd
            def _(e):
                run_engine('pool', e)

            @block.sync
            def _(e):
                run_engine('sp', e)


INPUT_SPECS = [
    ('xT', [1024, 1024], F32), ('cond', [128, 8], F32), ('ctxflag', [128, 1], F32),
    ('w_ada', [NL, 1024, 9216], F32), ('b_adaT', [NL, 128, 72], F32), ('gT', [NL, 128, 3, 8], F32),
    ('w_gate1', [NL, 1024, DFF], F32), ('w_up1', [NL, 1024, DFF], F32), ('w_down1', [NL, DFF, 1024], F32),
    ('w_gate2', [NL, 1024, DFF], F32), ('w_up2', [NL, 1024, DFF], F32), ('w_down2', [NL, DFF, 1024], F32),
    ('w_in', [NL, 1024, 1952], F32), ('w_o', [NL, 1024, 1024], F32),
    ('w_uq', [NL, 256, 384], F32), ('w_ukv', [NL, 128, 512], F32), ('gvec', [NL, 128, NG], F32),
    ('ckvT_c', [NL, 128, 256], F32), ('krT_c', [NL, 32, 256], F32), ('winkT_c', [NL, 128, 256], F32),
    ('winv_c', [NL, 256, 128], F32), ('nakT_c', [NL, 256, 256], F32), ('nav_c', [NL, 256, 256], F32),
    ('identb', [128, 128], BF16), ('onesb', [128, 128], BF16), ('blk64b', [128, 128], BF16),
    ('permB', [128, 128], F32), ('permA', [128, 128], F32),
    ('ropeB', [128, 2, 1024], F32), ('ropeA', [128, 2, 1024], F32),
    ('cs64', [128, 256], BF16), ('dftC', [1024, 1024], BF16), ('dftS', [1024, 1024], BF16),
    ('EA', [8, 1280], BF16), ('FA', [8, 1024], BF16), ('maskB', [128, 6, 512], BF16),
    ('biasD', [NL, 4, 1024, 1024], F32),
]
OUTPUT_SPECS = [
    ('yT', [1024, 1024]), ('o_ckvT', [NL, 128, 1024]), ('o_krT', [NL, 32, 1024]), ('o_kbT', [NL, 128, 1024]),
    ('o_vb', [NL, 1024, 128]), ('o_kdT', [NL, 256, 1024]), ('o_vd', [NL, 1024, 256]),
]


class _Stop(Exception):
    pass


def build_program(nl=NL, stop=None):
    nc = bass.Bass("TRN2", target_bir_lowering=False)
    IN = {n: nc.dram_tensor(n, list(s), dt, kind="ExternalInput").ap() for n, s, dt in INPUT_SPECS}
    OUT = {n: nc.dram_tensor(n, list(s), F32, kind="ExternalOutput").ap() for n, s in OUTPUT_SPECS}
    es = ExitStack()
    with es:
        P = Prog(nc, es)

        def sb(name, shape, dt):
            return es.enter_context(nc.sbuf_tensor(name, list(shape), dt))

        xT = sb('xT_sb', [128, 8, 1024], F32)
        hT = sb('hT', [128, 8, 1024], BF16)
        ring = [sb('ring%d' % i, [128, RING_ELEMS], BF16) for i in range(NSLOT)]
        actb = [sb('actb%d' % i, [128, 512], BF16) for i in range(8)]
        rstdN = sb('rstdN', [128, 512], F32)
        tmp = [sb('tmp%d' % i, [128, 512], F32) for i in range(NTMP)]
        sqb = [sb('sqb%d' % i, [128, 512], BF16) for i in range(3)]
        ps = [es.enter_context(nc.psum_tensor('ps%d' % i, [128, 512], F32)) for i in range(8)]
        identb = sb('identb_sb', [128, 128], BF16)
        onesb = sb('onesb_sb', [128, 128], BF16)
        blk64b = sb('blk64b_sb', [128, 128], BF16)
        permB = sb('permB_sb', [128, 128], F32)
        permA = sb('permA_sb', [128, 128], F32)
        ropeB = sb('ropeB_sb', [128, 2, 1024], F32)
        ropeA = sb('ropeA_sb', [128, 2, 1024], F32)
        cs64 = sb('cs64_sb', [128, 256], BF16)
        epsT = sb('epsT', [128, 1], F32)
        gvec = sb('gvec_sb', [128, NL, NG], F32)
        gT = sb('gT_sb', [128, NL, 3, 8], F32)
        badaT = sb('badaT_sb', [128, NL, 72], F32)
        condf = sb('condf', [128, 8], F32)
        condb = sb('condb', [128, 8], BF16)
        ctxf = sb('ctxf', [128, 1], F32)
        modT = [sb('modT%d' % i, [128, 72], F32) for i in range(2)]
        modA = [sb('modA%d' % i, [128, 3, 8], F32) for i in range(2)]
        modG = [sb('modG%d' % i, [128, 3, 8], F32) for i in range(2)]
        sinkexp = sb('sinkexp', [128, 4], F32)
        rec = sb('rec', [128, 4], F32)
        QB = sb('QB', [128, 4, 1024], BF16)
        QD = sb('QD', [128, 4, 1024], BF16)
        QA = sb('QA', [128, 4, 1024], BF16)
        KB = sb('KB', [128, 1280], BF16)
        KD = sb('KD', [128, 2, 1280], BF16)
        KA = sb('KA', [128, 4, 1280], BF16)
        VA = sb('VA', [128, 10, 4, 65], BF16)
        VB = sb('VB', [128, 10, 2, 65], BF16)
        VD = sb('VD', [128, 10, 4, 65], BF16)
        PT = sb('PT', [128, NPT, 512], BF16)
        ABt = PT[:, 0:8, :].rearrange("p l (c x) -> p l c x", c=2)
        XcT = PT[:, 8:12, :].rearrange("p (c t) x -> p c (t x)", c=2)
        mtok = sb('mtok', [128, 2, 4, 128], BF16)
        cqn = sb('cqn', [128, 2, 512], BF16)
        ckvn = sb('ckvn', [128, 1024], BF16)
        sqkr = sb('sqkr', [128, 512], BF16)
        krr = rstdN
        wuq = sb('wuq', [128, 2, 384], BF16)
        wukv = sb('wukv', [128, 512], BF16)
        ckvc = sb('ckvc', [128, 256], BF16)
        krc = sb('krc', [128, 256], F32)

        st = dict(ps=0, tmp=0, sq=0, ring=0, pt=0, act=0, ld=0, out=0)
        out_ids = []

        def rot(name, n):
            i = st[name]
            st[name] = (i + 1) % n
            return i

        newps = lambda: rot('ps', 7)
        newtmp = lambda: rot('tmp', NTMP)
        newsq = lambda: rot('sq', 3)

        def MM(out, lhsT, rhs, start, stop, rd, wr):
            P.add('pe', lambda e: e.matmul(out, lhsT=lhsT, rhs=rhs, start=start, stop=stop), rd, wr)

        def ACT(out, in_, func, rd, wr, scale=None, bias=None):
            kw = {}
            if scale is not None:
                kw['scale'] = scale
            if bias is not None:
                kw['bias'] = bias
            P.add('act', lambda e: e.activation(out=out, in_=in_, func=func, **kw), rd, wr)

        def TT(out, in0, in1, op, rd, wr):
            P.add('dve', lambda e: e.tensor_tensor(out=out, in0=in0, in1=in1, op=op), rd, wr)

        def STT(out, in0, scalar, in1, op0, op1, rd, wr):
            P.add('dve', lambda e: e.scalar_tensor_tensor(out=out, in0=in0, scalar=scalar, in1=in1, op0=op0, op1=op1), rd, wr)

        def TS(out, in0, s1, op0, rd, wr):
            P.add('dve', lambda e: e.tensor_scalar(out=out, in0=in0, scalar1=s1, scalar2=None, op0=op0), rd, wr)

        def RECIP(out, in_, rd, wr):
            P.add('dve', lambda e: e.reciprocal(out=out, in_=in_), rd, wr)

        def RECIPF(out, in_, rd, wr):
            P.add('dve', lambda e: e.reciprocal_approx_fast(out=out, in_=in_), rd, wr)

        def DMA(q, out, in_, rd, wr, chan):
            return P.add(q, lambda e: e.dma_start(out=out, in_=in_), rd, wr, chan=chan)

        def LOAD(out, in_, keys, q='sp'):
            return DMA(q, out, in_, [], keys, 'ld%d' % rot('ld', 4))

        def STORE(out, in_, rd):
            out_ids.append(DMA('sp', out, in_, rd, [], 'st%d' % rot('out', 4)))

        def ring_get(src_ap, a, b):
            s = rot('ring', NSLOT)
            view = ring[s][:, 0:a * b].rearrange("p (a b) -> p a b", b=b)
            DMA('pool', view, src_ap, [], [('ring', s)], 'ring%d' % s)
            return view, ('ring', s)

        cur_layer = [0]

        def CK(name):
            if stop == name or stop == '%s@%d' % (name, cur_layer[0]):
                raise _Stop()

        def kcp(ap):
            return ap.rearrange("(kc p) n -> p kc n", p=128)

        C = 'consts'
        cl = []
        for name, tile in [('identb', identb), ('onesb', onesb), ('blk64b', blk64b), ('permB', permB), ('permA', permA),
                           ('ropeB', ropeB), ('ropeA', ropeA), ('cs64', cs64),
                           ('cond', condf), ('ctxflag', ctxf)]:
            cl.append(LOAD(tile[:], IN[name], []))
        cl.append(LOAD(gvec[:], IN['gvec'].rearrange("l p g -> p l g"), []))
        cl.append(LOAD(gT[:], IN['gT'].rearrange("l p k c -> p l k c"), []))
        cl.append(LOAD(badaT[:], IN['b_adaT'].rearrange("l p j -> p l j"), []))
        xkeys = [('x', c, t) for c in range(8) for t in range(2)]
        LOAD(xT[:], kcp(IN['xT']), xkeys)
        P.add('dve', lambda e: e.memset(VA[:].rearrange("p a b c -> p (a b c)"), 1.0), [], ['Vinit'])
        P.add('dve', lambda e: e.memset(VB[:].rearrange("p a b c -> p (a b c)"), 1.0), [], ['Vinit'])
        P.add('dve', lambda e: e.memset(VD[:].rearrange("p a b c -> p (a b c)"), 1.0), [], ['Vinit'])
        P.add('dve', lambda e: e.memset(krc[:], 0.0), [], ['krc'])
        zi = []
        for tl in (QB, QD, QA, KA):
            zi.append(P.add('dve', lambda e, tl=tl: e.memset(tl[:].rearrange("p a b -> p (a b)"), 0.0), [], []))
        for h in range(4):
            cl.append(P.add('sp', lambda e, h=h: e.dma_start(out=KA[96:104, h, :], in_=IN['EA']), [], [], chan='ld%d' % rot('ld', 4), extra_deps=zi))
            cl.append(P.add('sp', lambda e, h=h: e.dma_start(out=QA[96:104, h, :], in_=IN['FA']), [], [], chan='ld%d' % rot('ld', 4), extra_deps=zi))
        P.add('dve', lambda e: e.memset(epsT[:], EPS), [], [C], extra_deps=cl + zi)
        ACT(condb[:], condf[:], AF.Silu, [C], ['condb'])
        for Vt, nh in ((VA, 4), (VB, 2), (VD, 4)):
            for kt in (8, 9):
                for h in range(nh):
                    ACT(Vt[:, kt, h, 64:65], ctxf[:, 0:1], AF.Copy, ['Vinit', C], ['Vinit'])

        def rstd_from(ps_ap, pskey, inv_d, p0=0, p1=128, n=512):
            ti = newtmp()
            t = tmp[ti][p0:p1, 0:n]
            ACT(t, ps_ap, AF.Ln, [pskey, C], [('tmp', ti)], scale=inv_d, bias=epsT[p0:p1, 0:1])
            ACT(t, t, AF.Exp, [('tmp', ti)], [('tmp', ti)], scale=-0.5)
            return ti

        def mod_steps(l):
            par = l % 2
            for blk in range(18):
                wt, wk = ring_get(kcp(IN['w_ada'][l][:, blk * 512:(blk + 1) * 512]), 8, 512)
                for j in range(4):
                    col = blk * 4 + j
                    for kc in range(8):
                        MM(ps[7][:, col:col + 1], wt[:, kc, j * 128:(j + 1) * 128], condb[:, kc:kc + 1], kc == 0, kc == 7,
                           [wk, 'condb'], ['psM'])
                yield
            TT(modT[par][:, :], ps[7][:, 0:72], badaT[:, l, :], ALU.add, ['psM', C], [('mod', par)])
            for k in range(3):
                STT(modA[par][:, k, :], modT[par][:, (3 * k + 1) * 8:(3 * k + 2) * 8], 1.0, gT[:, l, k, :], ALU.add, ALU.mult,
                    [('mod', par), C], [('modA', par)])
            for k, f in ((0, 0.5), (1, 1.0), (2, 0.5)):
                TS(modG[par][:, k, :], modT[par][:, (3 * k + 2) * 8:(3 * k + 3) * 8], f, ALU.mult, [('mod', par)], [('modG', par)])
            yield

        def norm_mod(l, k):
            par = l % 2
            for t in range(2):
                hs = slice(t * 512, (t + 1) * 512)
                pi = newps()
                for c in range(8):
                    si = newsq()
                    ACT(sqb[si][:, :], xT[:, c, hs], AF.Square, [('x', c, t)], [('sq', si)])
                    MM(ps[pi][:, :], onesb[:, :], sqb[si][:, :], c == 0, c == 7, [('sq', si), C], [('ps', pi)])
                ACT(rstdN[:, :], ps[pi][:, :], AF.Ln, [('ps', pi), C], ['rstdN'], scale=1.0 / 1024, bias=epsT[:, 0:1])
                ACT(rstdN[:, :], rstdN[:, :], AF.Exp, ['rstdN'], ['rstdN'], scale=-0.5)
                for c in range(8):
                    ti = newtmp()
                    STT(tmp[ti][:, :], xT[:, c, hs], modA[par][:, k, c:c + 1], rstdN[:, :], ALU.mult, ALU.mult,
                        [('x', c, t), ('modA', par), 'rstdN'], [('tmp', ti)])
                    ACT(hT[:, c, hs], tmp[ti][:, :], AF.Identity, [('tmp', ti), ('mod', par)], [('h', c, t)],
                        bias=modT[par][:, 3 * k * 8 + c:3 * k * 8 + c + 1])

        def ffn(l, which, modgen):
            par = l % 2
            wg = IN['w_gate%d' % which][l]
            wu = IN['w_up%d' % which][l]
            wd = IN['w_down%d' % which][l]
            gk = 0 if which == 1 else 2
            for blk in range(6):
                nch = 4 if blk < 5 else 2
                c0 = blk * 512
                ncol = nch * 128
                gw, gkey = ring_get(kcp(wg[:, c0:c0 + ncol]), 8, ncol)
                uw, ukey = ring_get(kcp(wu[:, c0:c0 + ncol]), 8, ncol)
                dw, dkey = ring_get(wd[c0:c0 + ncol, :].rearrange("(j p) n -> p j n", p=128), nch, 1024)
                for t in range(2):
                    hs = slice(t * 512, (t + 1) * 512)
                    aslots = []
                    for j in range(nch):
                        ai = rot('act', 8)
                        aslots.append(ai)
                        pg = newps()
                        for kc in range(8):
                            MM(ps[pg][:, :], gw[:, kc, j * 128:(j + 1) * 128], hT[:, kc, hs], kc == 0, kc == 7,
                               [gkey, ('h', kc, t)], [('ps', pg)])
                        pu = newps()
                        for kc in range(8):
                            MM(ps[pu][:, :], uw[:, kc, j * 128:(j + 1) * 128], hT[:, kc, hs], kc == 0, kc == 7,
                               [ukey, ('h', kc, t)], [('ps', pu)])
                        ti = newtmp()
                        ACT(tmp[ti][:, :], ps[pg][:, :], AF.Silu, [('ps', pg)], [('tmp', ti)])
                        TT(actb[ai][:, :], tmp[ti][:, :], ps[pu][:, :], ALU.mult, [('tmp', ti), ('ps', pu)], [('act', ai)])
                    for dc in range(8):
                        po = newps()
                        for j in range(nch):
                            MM(ps[po][:, :], dw[:, j, dc * 128:(dc + 1) * 128], actb[aslots[j]][:, :], j == 0, j == nch - 1,
                               [dkey, ('act', aslots[j])], [('ps', po)])
                        STT(xT[:, dc, hs], ps[po][:, :], modG[par][:, gk, dc:dc + 1], xT[:, dc, hs], ALU.mult, ALU.add,
                            [('ps', po), ('modG', par), ('x', dc, t)], [('x', dc, t)])
                if modgen is not None:
                    next(modgen, None)

        def rope_norm(psraw, pskey, p0, p1, ones_lhsT, inv_d, gain, out16, outkeys, rope=None, store=None, n=512, split=None, psf=None, gainf=None):
            si = newsq()
            ACT(sqb[si][p0:p1, 0:n], psraw, AF.Square, [pskey], [('sq', si)])
            pss = newps()
            MM(ps[pss][p0:p1, 0:n], ones_lhsT, sqb[si][p0:p1, 0:n], True, True, [('sq', si), C], [('ps', pss)])
            ri = rstd_from(ps[pss][p0:p1, 0:n], ('ps', pss), inv_d, p0, p1, n)
            r = tmp[ri][p0:p1, 0:n]
            if rope is None and store is None and split is not None:
                for (a_, b_, ap_, ks_) in split:
                    STT(ap_, psf(a_, b_), gainf(a_, b_), tmp[ri][a_:b_, 0:n], ALU.mult, ALU.mult, [pskey, ('tmp', ri), C], ks_)
                return
            if rope is None and store is None:
                STT(out16, psraw, gain, r, ALU.mult, ALU.mult, [pskey, ('tmp', ri), C], outkeys)
                return
            xi = newtmp()
            xn = tmp[xi][p0:p1, 0:n]
            STT(xn, psraw, gain, r, ALU.mult, ALU.mult, [pskey, ('tmp', ri), C], [('tmp', xi)])
            if rope is None:
                STORE(store, xn, [('tmp', xi)])
                ACT(out16, xn, AF.Copy, [('tmp', xi)], outkeys)
                return
            perm_lhsT, ropeC, ropeS = rope
            pp = newps()
            MM(ps[pp][p0:p1, 0:n], perm_lhsT, xn, True, True, [('tmp', xi), C], [('ps', pp)])
            t1 = newtmp()
            TT(tmp[t1][p0:p1, 0:n], xn, ropeC, ALU.mult, [('tmp', xi), C], [('tmp', t1)])
            t2 = newtmp()
            TT(tmp[t2][p0:p1, 0:n], ps[pp][p0:p1, 0:n], ropeS, ALU.mult, [('ps', pp), C], [('tmp', t2)])
            if split is not None:
                for (a_, b_, ap_, ks_) in split:
                    TT(ap_, tmp[t1][a_:b_, 0:n], tmp[t2][a_:b_, 0:n], ALU.add, [('tmp', t1), ('tmp', t2)], ks_)
            elif store is None:
                TT(out16, tmp[t1][p0:p1, 0:n], tmp[t2][p0:p1, 0:n], ALU.add, [('tmp', t1), ('tmp', t2)], outkeys)
            else:
                TT(tmp[t1][p0:p1, 0:n], tmp[t1][p0:p1, 0:n], tmp[t2][p0:p1, 0:n], ALU.add, [('tmp', t1), ('tmp', t2)], [('tmp', t1)])
                STORE(store, tmp[t1][p0:p1, 0:n], [('tmp', t1)])
                ACT(out16, tmp[t1][p0:p1, 0:n], AF.Copy, [('tmp', t1)], outkeys)

        def mixer(l, modgen):
            par = l % 2

            def gv(col, p0=0, p1=128):
                return gvec[p0:p1, l, col:col + 1]

            def step():
                if modgen is not None:
                    next(modgen, None)

            DMA('pool', wuq[:], IN['w_uq'][l].rearrange("(c p) n -> p c n", p=128), [], ['wuq'], 'cx0')
            DMA('pool', wukv[:], IN['w_ukv'][l], [], ['wukv'], 'cx1')
            DMA('pool', ckvc[:], IN['ckvT_c'][l], [], ['ckvc'], 'cx2')
            DMA('sp', krc[64:96, :], IN['krT_c'][l], [], ['krc'], 'cx3')
            DMA('pool', KB[:, 1024:1280], IN['winkT_c'][l], [], [('KB', 'c')], 'cx4')
            DMA('pool', KD[:, :, 1024:1280], IN['nakT_c'][l].rearrange("(c p) k -> p c k", p=128), [], [('KD', 0, 'c'), ('KD', 1, 'c')], 'cx5')
            for i in range(2):
                DMA('pool', VB[:, 8 + i, :, 0:64], IN['winv_c'][l][i * 128:(i + 1) * 128, :].rearrange("p (h d) -> p h d", d=64),
                    [], [('VB', 8 + i)], 'cx6')
                DMA('pool', VD[:, 8 + i, :, 0:64], IN['nav_c'][l][i * 128:(i + 1) * 128, :].rearrange("p (h d) -> p h d", d=64),
                    [], [('VD', 8 + i)], 'cx7')
            ACT(sinkexp[:, :], gvec[:, l, 9:13], AF.Exp, [C], ['sinkexp'])
            wukv_v = wukv[:, 0:512].rearrange("p (h x) -> p h x", x=128)[:, :, 64:128]

            b0, b0k = ring_get(kcp(IN['w_in'][l][:, 0:416]), 8, 416)
            for t in range(2):
                hs = slice(t * 512, (t + 1) * 512)
                pcs = []
                for c in range(2):
                    pi = newps()
                    pcs.append(pi)
                    for kc in range(8):
                        MM(ps[pi][:, :], b0[:, kc, c * 128:(c + 1) * 128], hT[:, kc, hs], kc == 0, kc == 7, [b0k, ('h', kc, t)], [('ps', pi)])
                pss = newps()
                for c in range(2):
                    si = newsq()
                    ACT(sqb[si][:, :], ps[pcs[c]][:, :], AF.Square, [('ps', pcs[c])], [('sq', si)])
                    MM(ps[pss][:, :], onesb[:, :], sqb[si][:, :], c == 0, c == 1, [('sq', si), C], [('ps', pss)])
                ri = rstd_from(ps[pss][:, :], ('ps', pss), 1.0 / 256)
                for c in range(2):
                    STT(cqn[:, c, :], ps[pcs[c]][:, :], gv(c), tmp[ri][:, :], ALU.mult, ALU.mult,
                        [('ps', pcs[c]), ('tmp', ri), C], [('cqn', c)])
                pk = newps()
                for kc in range(8):
                    MM(ps[pk][:, :], b0[:, kc, 256:384], hT[:, kc, hs], kc == 0, kc == 7, [b0k, ('h', kc, t)], [('ps', pk)])
                rope_norm(ps[pk][:, :], ('ps', pk), 0, 128, onesb[:, :], 1.0 / 128, gv(2), ckvn[:, hs], [('ckvn', t)],
                          rope=None, store=OUT['o_ckvT'][l][:, hs])
                pr = newps()
                for kc in range(8):
                    MM(ps[pr][64:96, :], b0[:, kc, 384:416], hT[:, kc, hs], kc == 0, kc == 7, [b0k, ('h', kc, t)], [('ps', pr)])
                ti = newtmp()
                ACT(tmp[ti][64:96, :], ps[pr][64:96, :], AF.Copy, [('ps', pr)], [('tmp', ti)])
                STORE(OUT['o_krT'][l][:, hs], tmp[ti][64:96, :], [('tmp', ti)])
                ACT(sqkr[64:96, :], ps[pr][64:96, :], AF.Square, [('ps', pr)], ['sqkr'])
                tg = newtmp()
                TS(tmp[tg][64:96, :], ps[pr][64:96, :], gv(4, 64, 96), ALU.mult, [('ps', pr), C], [('tmp', tg)])
                pp = newps()
                MM(ps[pp][64:96, :], permA[64:96, 64:96], tmp[tg][64:96, :], True, True, [('tmp', tg), C], [('ps', pp)])
                t1 = newtmp()
                TT(tmp[t1][64:96, :], tmp[tg][64:96, :], ropeA[64:96, 0, hs], ALU.mult, [('tmp', tg), C], [('tmp', t1)])
                TT(krr[64:96, :], ps[pp][64:96, :], ropeA[64:96, 1, hs], ALU.mult, [('ps', pp), C], ['rstdN'])
                TT(krr[64:96, :], krr[64:96, :], tmp[t1][64:96, :], ALU.add, ['rstdN', ('tmp', t1)], ['rstdN'])
                for h in range(4):
                    pn = newps()
                    MM(ps[pn][0:64, :], wukv[:, h * 128:h * 128 + 64], ckvn[:, hs], True, True, ['wukv', ('ckvn', t)], [('ps', pn)])
                    si = newsq()
                    ACT(sqb[si][0:64, :], ps[pn][0:64, :], AF.Square, [('ps', pn)], [('sq', si)])
                    pss = newps()
                    MM(ps[pss][0:96, :], onesb[0:64, 0:96], sqb[si][0:64, :], True, False, [('sq', si), C], [('ps', pss)])
                    MM(ps[pss][0:96, :], onesb[64:96, 0:96], sqkr[64:96, :], False, True, ['sqkr', C], [('ps', pss)])
                    ri = rstd_from(ps[pss][0:96, :], ('ps', pss), 1.0 / 96, 0, 96)
                    STT(KA[0:64, h, hs], ps[pn][0:64, :], gv(4, 0, 64), tmp[ri][0:64, :], ALU.mult, ALU.mult,
                        [('ps', pn), ('tmp', ri), C], [('KA', h, t)])
                    TT(KA[64:96, h, hs], krr[64:96, :], tmp[ri][64:96, :], ALU.mult, ['rstdN', ('tmp', ri)], [('KA', h, t)])
                    pq = newps()
                    for c in range(2):
                        MM(ps[pq][0:96, :], wuq[:, c, h * 96:(h + 1) * 96], cqn[:, c, :], c == 0, c == 1, ['wuq', ('cqn', c)], [('ps', pq)])
                    rope_norm(ps[pq][0:96, :], ('ps', pq), 0, 96, onesb[0:96, 0:96], 1.0 / 96, gv(3, 0, 96), QA[0:96, h, hs],
                              [('QA', h, t)], rope=(permA[0:96, 0:96], ropeA[0:96, 0, hs], ropeA[0:96, 1, hs]))
                for tt in range(4):
                    kt = t * 4 + tt
                    pv = newps()
                    MM(ps[pv][:, 0:256], ckvn[:, kt * 128:(kt + 1) * 128], wukv_v, True, True, ['wukv', ('ckvn', t)], [('ps', pv)])
                    ACT(VA[:, kt, :, 0:64], ps[pv][:, 0:256].rearrange("p (h d) -> p h d", d=64), AF.Copy, [('ps', pv), 'Vinit'], [('VA', kt)])
            step()
            CK('projA')
            si = newsq()
            ACT(sqb[si][64:96, 0:256], krc[64:96, :], AF.Square, ['krc'], [('sq', si)])
            sq_krc = si
            tgc = newtmp()
            TS(tmp[tgc][64:96, 0:256], krc[64:96, :], gv(4, 64, 96), ALU.mult, ['krc', C], [('tmp', tgc)])
            for h in range(4):
                pn = newps()
                MM(ps[pn][0:64, 0:256], wukv[:, h * 128:h * 128 + 64], ckvc[:, :], True, True, ['wukv', 'ckvc'], [('ps', pn)])
                si = newsq()
                if si == sq_krc:
                    si = newsq()
                ACT(sqb[si][0:64, 0:256], ps[pn][0:64, 0:256], AF.Square, [('ps', pn)], [('sq', si)])
                pss = newps()
                MM(ps[pss][0:96, 0:256], onesb[0:64, 0:96], sqb[si][0:64, 0:256], True, False, [('sq', si), C], [('ps', pss)])
                MM(ps[pss][0:96, 0:256], onesb[64:96, 0:96], sqb[sq_krc][64:96, 0:256], False, True, [('sq', sq_krc), C], [('ps', pss)])
                ri = rstd_from(ps[pss][0:96, 0:256], ('ps', pss), 1.0 / 96, 0, 96, 256)
                if ri == tgc:
                    raise RuntimeError("tmp rotation clash")
                STT(KA[0:64, h, 1024:1280], ps[pn][0:64, 0:256], gv(4, 0, 64), tmp[ri][0:64, 0:256], ALU.mult, ALU.mult,
                    [('ps', pn), ('tmp', ri), C], [('KA', h, 'c')])
                TT(KA[64:96, h, 1024:1280], tmp[tgc][64:96, 0:256], tmp[ri][64:96, 0:256], ALU.mult, [('tmp', tgc), ('tmp', ri)], [('KA', h, 'c')])
            for i in range(2):
                pv = newps()
                MM(ps[pv][:, 0:256], ckvc[:, i * 128:(i + 1) * 128], wukv_v, True, True, ['wukv', 'ckvc'], [('ps', pv)])
                ACT(VA[:, 8 + i, :, 0:64], ps[pv][:, 0:256].rearrange("p (h d) -> p h d", d=64), AF.Copy, [('ps', pv), 'Vinit'], [('VA', 8 + i)])
            step()
            CK('ctxA')

            b1, b1k = ring_get(kcp(IN['w_in'][l][:, 416:928]), 8, 512)
            for t in range(2):
                hs = slice(t * 512, (t + 1) * 512)
                for ci, hpair in enumerate(((0, 2), (1, 3))):
                    pi = newps()
                    for hh, pb in zip(hpair, (0, 64)):
                        for kc in range(8):
                            MM(ps[pi][pb:pb + 64, :], b1[:, kc, hh * 64:(hh + 1) * 64], hT[:, kc, hs], kc == 0, kc == 7,
                               [b1k, ('h', kc, t)], [('ps', pi)])
                    rope_norm(ps[pi][:, :], ('ps', pi), 0, 128, blk64b[:, :], 1.0 / 64, gv(5), None, None,
                              rope=(permB[:, :], ropeB[:, 0, hs], ropeB[:, 1, hs]),
                              split=[(0, 64, QB[0:64, hpair[0], hs], [('QB', hpair[0], t)]),
                                     (64, 128, QB[64:128, hpair[1], hs], [('QB', hpair[1], t)])])
                    CK('qb')
                pi = newps()
                for kc in range(8):
                    MM(ps[pi][:, :], b1[:, kc, 256:384], hT[:, kc, hs], kc == 0, kc == 7, [b1k, ('h', kc, t)], [('ps', pi)])
                rope_norm(ps[pi][:, :], ('ps', pi), 0, 128, blk64b[:, :], 1.0 / 64, gv(6), KB[:, hs], [('KB', t)],
                          rope=(permB[:, :], ropeB[:, 0, hs], ropeB[:, 1, hs]), store=OUT['o_kbT'][l][:, hs])
                CK('kb')
                for tt in range(4):
                    kt = t * 4 + tt
                    pv = newps()
                    for kc in range(8):
                        MM(ps[pv][:, 0:128], hT[:, kc, kt * 128:(kt + 1) * 128], b1[:, kc, 384:512], kc == 0, kc == 7,
                           [b1k, ('h', kc, t)], [('ps', pv)])
                    ti = newtmp()
                    ACT(tmp[ti][:, 0:128], ps[pv][:, 0:128], AF.Copy, [('ps', pv)], [('tmp', ti)])
                    STORE(OUT['o_vb'][l][kt * 128:(kt + 1) * 128, :], tmp[ti][:, 0:128], [('tmp', ti)])
                    ACT(VB[:, kt, :, 0:64], ps[pv][:, 0:128].rearrange("p (h d) -> p h d", d=64), AF.Copy, [('ps', pv), 'Vinit'], [('VB', kt)])
            step()
            CK('projB')
            b2, b2k = ring_get(kcp(IN['w_in'][l][:, 928:1440]), 8, 512)
            for t in range(2):
                hs = slice(t * 512, (t + 1) * 512)
                for ch in range(2):
                    pi = newps()
                    for kc in range(8):
                        MM(ps[pi][:, :], b2[:, kc, ch * 128:(ch + 1) * 128], hT[:, kc, hs], kc == 0, kc == 7, [b2k, ('h', kc, t)], [('ps', pi)])
                    ACT(XcT[:, ch, hs], ps[pi][:, :], AF.Copy, [('ps', pi)], [('pt', 8 + 2 * ch + t)])
                for ch in range(2):
                    pi = newps()
                    for kc in range(8):
                        MM(ps[pi][:, :], b2[:, kc, 256 + ch * 128:256 + (ch + 1) * 128], hT[:, kc, hs], kc == 0, kc == 7,
                           [b2k, ('h', kc, t)], [('ps', pi)])
                    rope_norm(ps[pi][:, :], ('ps', pi), 0, 128, blk64b[:, :], 1.0 / 64, gv(7), None, None,
                              split=[(0, 64, QD[0:64, 2 * ch, hs], [('QD', 2 * ch, t)]),
                                     (64, 128, QD[64:128, 2 * ch + 1, hs], [('QD', 2 * ch + 1, t)])],
                              psf=lambda a_, b_, pi=pi: ps[pi][a_:b_, :], gainf=lambda a_, b_: gv(7, a_, b_))
            step()
            b3, b3k = ring_get(kcp(IN['w_in'][l][:, 1440:1952]), 8, 512)
            for t in range(2):
                hs = slice(t * 512, (t + 1) * 512)
                for ch in range(2):
                    pi = newps()
                    for kc in range(8):
                        MM(ps[pi][:, :], b3[:, kc, ch * 128:(ch + 1) * 128], hT[:, kc, hs], kc == 0, kc == 7, [b3k, ('h', kc, t)], [('ps', pi)])
                    rope_norm(ps[pi][:, :], ('ps', pi), 0, 128, blk64b[:, :], 1.0 / 64, gv(8), KD[:, ch, hs], [('KD', ch, t)],
                              rope=None, store=OUT['o_kdT'][l][ch * 128:(ch + 1) * 128, hs])
                for tt in range(4):
                    kt = t * 4 + tt
                    pv = newps()
                    for kc in range(8):
                        MM(ps[pv][:, 0:256], hT[:, kc, kt * 128:(kt + 1) * 128], b3[:, kc, 256:512], kc == 0, kc == 7,
                           [b3k, ('h', kc, t)], [('ps', pv)])
                    ti = newtmp()
                    ACT(tmp[ti][:, 0:256], ps[pv][:, 0:256], AF.Copy, [('ps', pv)], [('tmp', ti)])
                    STORE(OUT['o_vd'][l][kt * 128:(kt + 1) * 128, :], tmp[ti][:, 0:256], [('tmp', ti)])
                    ACT(VD[:, kt, :, 0:64], ps[pv][:, 0:256].rearrange("p (h d) -> p h d", d=64), AF.Copy, [('ps', pv), 'Vinit'], [('VD', kt)])
            step()
            CK('projD')
            for lt in range(8):
                for ch in range(2):
                    pi = newps()
                    MM(ps[pi][:, 0:256], XcT[:, ch, lt * 128:(lt + 1) * 128], cs64[:, :], True, True, [('pt', 8 + 2 * ch + lt // 4), C], [('ps', pi)])
                    P.add('dve', lambda e, lt=lt, ch=ch, pi=pi: e.tensor_copy(out=ABt[:, lt, ch, :], in_=ps[pi][:, 0:256]),
                          [('ps', pi)], [('ab', lt, ch)])
            for t in range(2):
                hs = slice(t * 512, (t + 1) * 512)
                cb, cbk = ring_get(IN['dftC'][:, hs].rearrange("(lt p) n -> p lt n", p=128), 8, 512)
                sbk_, sbkk = ring_get(IN['dftS'][:, hs].rearrange("(lt p) n -> p lt n", p=128), 8, 512)
                for ch in range(2):
                    pi = newps()
                    for lt in range(8):
                        MM(ps[pi][:, :], ABt[:, lt, ch, 0:128], cb[:, lt, :], lt == 0, False, [('ab', lt, ch), ('pt', lt), cbk], [('ps', pi)])
                    for lt in range(8):
                        MM(ps[pi][:, :], ABt[:, lt, ch, 128:256], sbk_[:, lt, :], False, lt == 7, [('ab', lt, ch), ('pt', lt), sbkk], [('ps', pi)])
                    ACT(hT[:, 4 + ch, hs], ps[pi][:, :], AF.Copy, [('ps', pi)], [('h', 4 + ch, t)])
            step()
            CK('fourier')

            shared = {}

            def unit_kts(kind, t):
                if kind == 'B':
                    return [kt for kt in range(4 * t - 1, 4 * t + 5) if 0 <= kt <= 7] + [8, 9]
                if kind == 'D':
                    return list(range(0, 6) if t == 0 else range(2, 8)) + [8, 9]
                return list(range(10))

            def attend_qk(kind, h, t):
                hs = slice(t * 512, (t + 1) * 512)
                kts = unit_kts(kind, t)
                if kind == 'B':
                    if 'maskB' not in shared:
                        shared['maskB'] = ring_get(IN['maskB'], 6, 512)
                elif kind == 'D':
                    if ('biasD', h) not in shared:
                        shared[('biasD', h)] = [ring_get(IN['biasD'][l][h][g * 512:(g + 1) * 512, :].rearrange("(kt p) q -> p kt q", p=128), 4, 1024)
                                                for g in range(2)]
                slots = {}
                for kt in kts:
                    pi = newps()
                    ksl = slice(kt * 128, (kt + 1) * 128)
                    own = kt < 8
                    kk = (kt // 4) if own else 'c'
                    if kind == 'A':
                        MM(ps[pi][:, :], KA[:, h, ksl], QA[:, h, hs], True, True, [('KA', h, kk), ('QA', h, t), C], [('ps', pi)])
                        scale = 96.0 ** -0.5
                    elif kind == 'B':
                        MM(ps[pi][:, :], KB[:, ksl], QB[:, h, hs], True, not own, [('KB', kk), ('QB', h, t), C], [('ps', pi)])
                        if own:
                            mv, mk = shared['maskB']
                            MM(ps[pi][:, :], identb[:, :], mv[:, kt - 4 * t + 1, :], False, True, [C, mk], [('ps', pi)])
                        scale = 0.125
                    else:
                        ch = h // 2
                        MM(ps[pi][:, :], KD[:, ch, ksl], QD[:, h, hs], True, not own, [('KD', ch, kk), ('QD', h, t), C], [('ps', pi)])
                        if own:
                            bv, bk = shared[('biasD', h)][kt // 4]
                            MM(ps[pi][:, :], identb[:, :], bv[:, kt % 4, hs], False, True, [C, bk], [('ps', pi)])
                        scale = 0.125
                    s = rot('pt', NPT)
                    slots[kt] = s
                    ACT(PT[:, s, :], ps[pi][:, :], AF.Exp, [('ps', pi)], [('pt', s)], scale=scale)
                return (kind, h, t, kts, slots)

            def attend_pv(ctx):
                kind, h, t, kts, slots = ctx
                V, vh, vname = {'A': (VA, h, 'VA'), 'B': (VB, h // 2, 'VB'), 'D': (VD, h, 'VD')}[kind]
                po = newps()
                for qi in range(4):
                    for i, kt in enumerate(kts):
                        MM(ps[po][:, qi * 65:(qi + 1) * 65], PT[:, slots[kt], qi * 128:(qi + 1) * 128], V[:, kt, vh, 0:65],
                           i == 0, i == len(kts) - 1, [('pt', slots[kt]), (vname, kt), 'Vinit'], [('ps', po)])
                for qi in range(4):
                    den = ps[po][:, qi * 65 + 64:qi * 65 + 65]
                    if kind == 'B':
                        TS(rec[:, qi:qi + 1], den, sinkexp[:, h:h + 1], ALU.add, [('ps', po), 'sinkexp'], [('rec', qi)])
                        RECIP(rec[:, qi:qi + 1], rec[:, qi:qi + 1], [('rec', qi)], [('rec', qi)])
                    else:
                        RECIP(rec[:, qi:qi + 1], den, [('ps', po)], [('rec', qi)])
                    hslot = h % 2
                    TS(mtok[:, t, qi, hslot * 64:(hslot + 1) * 64], ps[po][:, qi * 65:qi * 65 + 64], rec[:, qi:qi + 1], ALU.mult,
                       [('ps', po), ('rec', qi)], [('mtok', t, qi, hslot)])

            def attend_tr(chunk):
                for t in range(2):
                    hs = slice(t * 512, (t + 1) * 512)
                    pt_ = newps()
                    for qi in range(4):
                        MM(ps[pt_][:, qi * 128:(qi + 1) * 128], mtok[:, t, qi, :], identb[:, :], True, True,
                           [('mtok', t, qi, 0), ('mtok', t, qi, 1), C], [('ps', pt_)])
                    P.add('dve', lambda e, hs=hs, pt_=pt_, cc=chunk: e.tensor_copy(out=hT[:, cc, hs], in_=ps[pt_][:, :]),
                          [('ps', pt_)], [('h', chunk, t)])

            units = []
            for kind, chunk0 in (('A', 0), ('B', 2), ('D', 6)):
                for pair in range(2):
                    for h in (2 * pair, 2 * pair + 1):
                        for t in range(2):
                            units.append((kind, h, t, chunk0 + pair, (h % 2 == 1 and t == 1)))
            prev = None

            def flush(pv_):
                attend_pv(pv_[0])
                if pv_[2]:
                    attend_tr(pv_[1])
                    step()

            for (kind, h, t, chunk, last) in units:
                if prev is not None and len(prev[0][3]) + len(unit_kts(kind, t)) > NPT:
                    flush(prev)
                    prev = None
                ctx = attend_qk(kind, h, t)
                if prev is not None:
                    flush(prev)
                prev = (ctx, chunk, last)
            flush(prev)
            CK('attn')

            for blk in range(2):
                wo, wok = ring_get(kcp(IN['w_o'][l][:, blk * 512:(blk + 1) * 512]), 8, 512)
                for dcc in range(4):
                    dc = blk * 4 + dcc
                    for t in range(2):
                        hs = slice(t * 512, (t + 1) * 512)
                        po = newps()
                        for c in range(8):
                            MM(ps[po][:, :], wo[:, c, dcc * 128:(dcc + 1) * 128], hT[:, c, hs], c == 0, c == 7, [wok, ('h', c, t)], [('ps', po)])
                        STT(xT[:, dc, hs], ps[po][:, :], modG[par][:, 1, dc:dc + 1], xT[:, dc, hs], ALU.mult, ALU.add,
                            [('ps', po), ('modG', par), ('x', dc, t)], [('x', dc, t)])
            step()

        try:
            CK('load')
            g0 = mod_steps(0)
            for _ in g0:
                pass
            CK('mod')
            for l in range(nl):
                cur_layer[0] = l
                mg = mod_steps(l + 1) if l + 1 < nl else None
                norm_mod(l, 0)
                CK('norm0')
                ffn(l, 1, mg)
                CK('ffn1')
                norm_mod(l, 1)
                mixer(l, mg)
                CK('mixer')
                norm_mod(l, 2)
                ffn(l, 2, mg)
                if mg is not None:
                    for _ in mg:
                        pass
                CK('layer')
        except _Stop:
            pass
        STORE(kcp(OUT['yT']), xT[:], xkeys)
        P.add('sp', None, extra_deps=out_ids)
        P.emit()
        nc._prog_stats = (P.n_ops, P.sig_counts, P.chan_counts)
    return nc


def _rope_tables(r, sample):
    Cm = np.ones((r, T), np.float32)
    Sm = np.zeros((r, T), np.float32)
    if not sample:
        return Cm, Sm
    half = r // 2
    t = np.arange(T)
    inv = (10000.0 ** (-np.arange(0, half, 2, dtype=np.float32) / np.float32(half))).astype(np.float32)
    q = half // 2
    for part, pos in ((0, t // 64), (1, t % 64)):
        ang = pos.astype(np.float32)[:, None] * inv[None, :]
        cos = np.cos(ang).astype(np.float32).T
        sin = np.sin(ang).astype(np.float32).T
        b = part * half
        Cm[b:b + q] = cos
        Cm[b + q:b + half] = cos
        Sm[b:b + q] = -sin
        Sm[b + q:b + half] = sin
    return Cm, Sm


def _perm(r):
    M = np.zeros((r, r), np.float32)
    half = r // 2
    q = half // 2
    for d in range(r):
        o = d % half
        p = d + q if o < q else d - q
        M[p, d] = 1.0
    return M


def _consts(sample):
    c = {}
    c['identb'] = np.eye(128, dtype=np.float32).astype(bf16)
    c['onesb'] = np.ones((128, 128), np.float32).astype(bf16)
    c['blk64b'] = np.kron(np.eye(2, dtype=np.float32), np.ones((64, 64), np.float32)).astype(bf16)
    pb = np.zeros((128, 128), np.float32)
    pb[0:64, 0:64] = _perm(64)
    pb[64:128, 64:128] = _perm(64)
    c['permB'] = pb
    pa = np.zeros((128, 128), np.float32)
    pa[64:96, 64:96] = _perm(32)
    c['permA'] = pa
    Cb, Sb = _rope_tables(64, sample)
    rb = np.zeros((128, 2, T), np.float32)
    rb[0:64, 0], rb[64:128, 0] = Cb, Cb
    rb[0:64, 1], rb[64:128, 1] = Sb, Sb
    c['ropeB'] = rb
    Ca, Sa = _rope_tables(32, sample)
    ra = np.zeros((128, 2, T), np.float32)
    ra[:, 0] = 1.0
    ra[64:96, 0] = Ca
    ra[64:96, 1] = Sa
    c['ropeA'] = ra
    k = np.arange(64)
    ang = 2 * np.pi * np.outer(k, k) / 64.0
    C64 = np.cos(ang) / 8.0
    S64 = np.sin(ang) / 8.0
    cs = np.zeros((128, 256), np.float64)
    cs[0:64, 0:64] = C64
    cs[64:128, 64:128] = C64
    cs[0:64, 128:192] = S64
    cs[64:128, 192:256] = S64
    c['cs64'] = cs.astype(np.float32).astype(bf16)
    L = 1024 if sample else 256
    kk = np.arange(L)
    angL = 2 * np.pi * (np.outer(kk, kk) % L) / float(L)
    CL = np.cos(angL) / np.sqrt(L)
    SL = -np.sin(angL) / np.sqrt(L)
    if sample:
        dC, dS = CL, SL
    else:
        dC = np.kron(np.eye(4), CL)
        dS = np.kron(np.eye(4), SL)
    c['dftC'] = dC.astype(np.float32).astype(bf16)
    c['dftS'] = dS.astype(np.float32).astype(bf16)
    ea = np.zeros((8, 1280), np.float32)
    fa = np.zeros((8, 1024), np.float32)
    if not sample:
        for s in range(4):
            ea[s, s * 256:(s + 1) * 256] = 1.0
            fa[s, s * 256:(s + 1) * 256] = BIG
        ea[4, 0:1024] = 1.0
        fa[4, :] = -BIG
    c['EA'] = ea.astype(bf16)
    c['FA'] = fa.astype(bf16)
    mb = np.full((128, 6, 512), NEG, np.float32)
    kl = np.arange(128)[:, None]
    ql = np.arange(128)[None, :]
    for ki in range(6):
        ktp = ki - 1
        for qi in range(4):
            off = ktp - qi
            blkm = np.full((128, 128), NEG, np.float32)
            if sample:
                if off == 0:
                    blkm[:] = 0.0
                elif off == -1:
                    blkm = np.where(ql <= kl, 0.0, NEG).astype(np.float32)
                elif off == 1:
                    blkm = np.where(kl <= ql, 0.0, NEG).astype(np.float32)
            else:
                if off == 0 or (off == 1 and qi % 2 == 0) or (off == -1 and qi % 2 == 1):
                    blkm[:] = 0.0
            mb[:, ki, qi * 128:(qi + 1) * 128] = blkm
    c['maskB'] = mb.astype(bf16)
    return c


def _na_index():
    rows = 16
    r = np.arange(rows)
    row_start = np.clip(r - 4, 0, rows - 8)
    col = np.arange(64)
    col_start = np.clip(col - 8, 0, 64 - 16)
    q = np.arange(1024)
    qr, qc = q // 64, q % 64
    kr, kc = qr, qc
    KR, QR = kr[:, None], qr[None, :]
    KC, QC = kc[:, None], qc[None, :]
    valid = (KR >= row_start[QR]) & (KR < row_start[QR] + 8) & (KC >= col_start[QC]) & (KC < col_start[QC] + 16)
    dr = np.clip(KR - QR + 7, 0, 14)
    dc = np.clip(KC - QC + 15, 0, 30)
    return valid, dr, dc


def _vecT(v):
    return np.ascontiguousarray(v.reshape(-1, 128).T)


_PROG = {}


def _prep(x_prompt, x_sample, cache_mla_ckv, cache_mla_krope, cache_win_k, cache_win_v, cache_na_k, cache_na_v,
           c, c_ctx, w_ada, b_ada, g_ffn1, w_gate1, w_up1, w_down1, g_mix, w_in, g_qa, w_uq, g_kva, w_ukv,
           qn_a, kn_a, qn_b, kn_b, sink_b, qn_d, kn_d, rpb_d, w_o, g_ffn2, w_gate2, w_up2, w_down2):
    f = lambda a: np.ascontiguousarray(np.asarray(a, dtype=np.float32))
    x_prompt, x_sample, c, c_ctx = f(x_prompt), f(x_sample), f(c), f(c_ctx)
    shared = {n: f(v) for n, v in dict(w_ada=w_ada, w_gate1=w_gate1, w_up1=w_up1, w_down1=w_down1, w_gate2=w_gate2,
                                       w_up2=w_up2, w_down2=w_down2, w_in=w_in, w_o=w_o, w_uq=w_uq, w_ukv=w_ukv).items()}
    b_ada = f(b_ada)
    shared['b_adaT'] = np.ascontiguousarray(b_ada.reshape(NL, 72, 128).transpose(0, 2, 1))
    gs = np.stack([f(g_ffn1), f(g_mix), f(g_ffn2)], axis=1)
    shared['gT'] = np.ascontiguousarray(gs.reshape(NL, 3, 8, 128).transpose(0, 3, 1, 2))
    gv = np.zeros((NL, 128, NG), np.float32)
    gv[:, :, 0:2] = f(g_qa).reshape(NL, 2, 128).transpose(0, 2, 1)
    gv[:, :, 2] = f(g_kva)
    gv[:, 0:96, 3] = f(qn_a)
    gv[:, 0:96, 4] = f(kn_a)
    gv[:, :, 5] = np.tile(f(qn_b), (1, 2))
    gv[:, :, 6] = np.tile(f(kn_b), (1, 2))
    gv[:, :, 7] = np.tile(f(qn_d), (1, 2))
    gv[:, :, 8] = np.tile(f(kn_d), (1, 2))
    gv[:, :, 9:13] = f(sink_b)[:, None, :]
    shared['gvec'] = gv
    consts = {True: _consts(True), False: _consts(False)}
    valid, dr, dc = _na_index()
    rpb = f(rpb_d)
    bias_s = np.where(valid[None, None], rpb[:, :, dr, dc], np.float32(NEG)).astype(np.float32)
    seq = np.arange(1024) // 256
    bias_p1 = np.where(seq[:, None] == seq[None, :], np.float32(0.0), np.float32(NEG)).astype(np.float32)
    bias_p = np.ascontiguousarray(np.broadcast_to(bias_p1, (NL, 4, 1024, 1024)))
    cm_ckv, cm_kr = f(cache_mla_ckv), f(cache_mla_krope)
    cw_k, cw_v, cn_k, cn_v = f(cache_win_k), f(cache_win_v), f(cache_na_k), f(cache_na_v)
    in_maps = []
    for core in range(8):
        sample = core >= 4
        m = dict(shared)
        m.update(consts[sample])
        if sample:
            b = core - 4
            xs = x_sample[b]
            cond = c[b]
            m['ckvT_c'] = np.ascontiguousarray(cm_ckv[b].transpose(0, 2, 1))
            m['krT_c'] = np.ascontiguousarray(cm_kr[b].transpose(0, 2, 1))
            m['winkT_c'] = np.ascontiguousarray(cw_k[b].reshape(NL, 256, 128).transpose(0, 2, 1))
            m['winv_c'] = np.ascontiguousarray(cw_v[b].reshape(NL, 256, 128))
            m['nakT_c'] = np.ascontiguousarray(cn_k[b].reshape(NL, 256, 256).transpose(0, 2, 1))
            m['nav_c'] = np.ascontiguousarray(cn_v[b].reshape(NL, 256, 256))
            m['biasD'] = bias_s
            m['ctxflag'] = np.ones((128, 1), np.float32)
        else:
            xs = x_prompt[4 * core:4 * core + 4].reshape(1024, 1024)
            cond = c_ctx
            m['ckvT_c'] = np.zeros((NL, 128, 256), np.float32)
            m['krT_c'] = np.zeros((NL, 32, 256), np.float32)
            m['winkT_c'] = np.zeros((NL, 128, 256), np.float32)
            m['winv_c'] = np.zeros((NL, 256, 128), np.float32)
            m['nakT_c'] = np.zeros((NL, 256, 256), np.float32)
            m['nav_c'] = np.zeros((NL, 256, 256), np.float32)
            m['biasD'] = bias_p
            m['ctxflag'] = np.zeros((128, 1), np.float32)
        m['xT'] = np.ascontiguousarray(xs.T)
        m['cond'] = _vecT(cond)
        in_maps.append(m)
    return in_maps


def _assemble(R):
    y_prompt = np.concatenate([R[i]['yT'].T.reshape(4, 256, 1024) for i in range(4)], axis=0)
    y_sample = np.stack([R[4 + b]['yT'].T for b in range(4)], axis=0)

    def featmaj(name, feat):
        outs = []
        for i in range(4):
            a = R[i][name]
            a = a.reshape(NL, feat, 4, 256).transpose(2, 0, 3, 1)
            outs.append(a)
        return np.ascontiguousarray(np.concatenate(outs, axis=0))

    def tokmaj(name, feat):
        outs = []
        for i in range(4):
            a = R[i][name].reshape(NL, 4, 256, feat).transpose(1, 0, 2, 3)
            outs.append(a)
        return np.ascontiguousarray(np.concatenate(outs, axis=0))

    new_ckv = featmaj('o_ckvT', 128)
    new_kr = featmaj('o_krT', 32)
    new_wk = featmaj('o_kbT', 128).reshape(16, NL, 256, 2, 64)
    new_wv = tokmaj('o_vb', 128).reshape(16, NL, 256, 2, 64)
    new_nk = featmaj('o_kdT', 256).reshape(16, NL, 256, 4, 64)
    new_nv = tokmaj('o_vd', 256).reshape(16, NL, 256, 4, 64)
    return (np.ascontiguousarray(y_prompt.astype(np.float32)), np.ascontiguousarray(y_sample.astype(np.float32)),
            new_ckv, new_kr, new_wk, new_wv, new_nk, new_nv)


def kernel(**inputs):
    in_maps = _prep(**inputs)
    if 'nc' not in _PROG:
        _PROG['nc'] = build_program(NL)
    res = run_bass_kernel_spmd(_PROG['nc'], in_maps, core_ids=list(range(8)))
    return _assemble(res.results)
```

```python
import numpy as np
import ml_dtypes
from contextlib import ExitStack
import concourse.bass as bass
import concourse.mybir as mybir
from concourse.bass_utils import run_bass_kernel_spmd

F32 = mybir.dt.float32
BF16 = mybir.dt.bfloat16
AF = mybir.ActivationFunctionType
ALU = mybir.AluOpType
bf16 = ml_dtypes.bfloat16

NL = 4
D = 1024
T = 1024
DFF = 2816
EPS = 1e-6
NEG = -30000.0
BIG = 1024.0
RING_ELEMS = 4096
NSLOT = 4
NTMP = 6
NPT = 16
NG = 17


class Prog:
    COMPUTE = ('pe', 'act', 'dve', 'pool')

    def __init__(self, nc, es, same_engine_sync=True):
        self.nc = nc
        self.es = es
        self.ops = []
        self.lastw = {}
        self.readers = {}
        self.chan_last = {}
        self.chan_cnt = {}
        self.same_engine_sync = same_engine_sync

    def add(self, eng, fn, reads=(), writes=(), chan=None, extra_deps=()):
        idx = len(self.ops)
        deps = set(extra_deps)
        if eng in ('act', 'dve'):
            pk = [k for k in reads if k == 'psM' or (isinstance(k, tuple) and k[0] == 'ps')]
            if pk:
                writes = list(writes) + pk
        for k in reads:
            w = self.lastw.get(k)
            if w is not None:
                deps.add(w)
        for k in writes:
            w = self.lastw.get(k)
            if w is not None:
                deps.add(w)
            last = {}
            for r in self.readers.get(k, ()):
                rop = self.ops[r]
                if rop['chan'] is not None:
                    deps.add(r)
                else:
                    last[rop['eng']] = r
            deps.update(last.values())
        for k in reads:
            self.readers.setdefault(k, []).append(idx)
        for k in writes:
            self.lastw[k] = idx
            self.readers[k] = []
        if chan is not None:
            p = self.chan_last.get(chan)
            if p is not None:
                deps.add(p)
            self.chan_last[chan] = idx
            self.chan_cnt[chan] = self.chan_cnt.get(chan, 0) + 1
        deps.discard(idx)
        self.ops.append(dict(eng=eng, fn=fn, deps=deps, chan=chan, signal=False, sigval=None,
                             chanval=(16 * self.chan_cnt[chan] if chan is not None else None)))
        return idx

    def emit(self):
        nc = self.nc
        ops = self.ops
        for op in ops:
            for d in op['deps']:
                dop = ops[d]
                if dop['chan'] is None:
                    if dop['eng'] == op['eng'] and (op['eng'] == 'pe' or not self.same_engine_sync):
                        continue
                    dop['signal'] = True
        cnt = {}
        for op in ops:
            if op['chan'] is None and op['signal']:
                cnt[op['eng']] = cnt.get(op['eng'], 0) + 1
                op['sigval'] = cnt[op['eng']]
        self.sig_counts = dict(cnt)
        self.chan_counts = dict(self.chan_cnt)
        self.n_ops = len(ops)
        sems = {e: self.es.enter_context(nc.semaphore('s_' + e)) for e in self.COMPUTE + ('sp',)}
        csems = {c: self.es.enter_context(nc.semaphore('c_%d' % i)) for i, c in enumerate(self.chan_cnt)}
        by_eng = {}
        for i, op in enumerate(ops):
            by_eng.setdefault(op['eng'], []).append(i)

        def run_engine(ename, e):
            waited = {}
            for i in by_eng.get(ename, ()):
                op = ops[i]
                need = {}
                for d in op['deps']:
                    dop = ops[d]
                    if dop['chan'] is not None:
                        key = ('c', dop['chan'])
                        val = dop['chanval']
                    else:
                        if dop['eng'] == ename and (ename == 'pe' or not self.same_engine_sync):
                            continue
                        key = ('e', dop['eng'])
                        val = dop['sigval']
                    if val > need.get(key, 0):
                        need[key] = val
                for key, val in need.items():
                    if waited.get(key, 0) >= val:
                        continue
                    waited[key] = val
                    s = csems[key[1]] if key[0] == 'c' else sems[key[1]]
                    e.wait_ge(s, val)
                if op['fn'] is None:
                    continue
                ins = op['fn'](e)
                if op['chan'] is not None:
                    ins.then_inc(csems[op['chan']], 16)
                elif op['signal']:
                    ins.then_inc(sems[ename], 1)

        with nc.Block() as block:
            @block.tensor
            def _(e):
                run_engine('pe', e)

            @block.scalar
            def _(e):
                run_engine('act', e)

            @block.vector
            def _(e):
                run_engine('dve', e)

            @block.gpsimd
            def _(e):
                run_engine('pool', e)

            @block.sync
            def _(e):
                run_engine('sp', e)


INPUT_SPECS = [
    ('xT', [1024, 1024], F32), ('cond', [128, 8], F32), ('ctxflag', [128, 1], F32),
    ('w_ada', [NL, 1024, 9216], F32), ('b_adaT', [NL, 128, 72], F32), ('gT', [NL, 128, 3, 8], F32),
    ('w_gate1', [NL, 1024, DFF], F32), ('w_up1', [NL, 1024, DFF], F32), ('w_down1', [NL, DFF, 1024], F32),
    ('w_gate2', [NL, 1024, DFF], F32), ('w_up2', [NL, 1024, DFF], F32), ('w_down2', [NL, DFF, 1024], F32),
    ('w_in', [NL, 1024, 1952], F32), ('w_o', [NL, 1024, 1024], F32),
    ('w_uq', [NL, 256, 384], F32), ('w_uqP', [NL, 256, 384], F32), ('w_inPa', [NL, 1024, 32], F32), ('w_inPb', [NL, 1024, 384], F32),
    ('w_ukv', [NL, 128, 512], F32), ('gvec', [NL, 128, NG], F32),
    ('ckvT_c', [NL, 128, 256], F32), ('krT_c', [NL, 32, 256], F32), ('winkT_c', [NL, 128, 256], F32),
    ('winv_c', [NL, 256, 128], F32), ('nakT_c', [NL, 256, 256], F32), ('nav_c', [NL, 256, 256], F32),
    ('identb', [128, 128], BF16), ('onesb', [128, 128], BF16), ('blk64b', [128, 128], BF16),
    ('ropeB', [128, 2, 1024], F32), ('ropeA', [128, 2, 1024], F32),
    ('cs64', [128, 256], BF16), ('dftC', [1024, 1024], BF16), ('dftS', [1024, 1024], BF16),
    ('EA', [8, 1280], BF16), ('FA', [8, 1024], BF16), ('maskB', [128, 6, 512], BF16),
    ('biasD', [NL, 4, 1024, 1024], F32),
]
OUTPUT_SPECS = [
    ('yT', [1024, 1024]), ('o_ckvT', [NL, 128, 1024]), ('o_krT', [NL, 32, 1024]), ('o_kbT', [NL, 128, 1024]),
    ('o_vb', [NL, 1024, 128]), ('o_kdT', [NL, 256, 1024]), ('o_vd', [NL, 1024, 256]),
]


class _Stop(Exception):
    pass


def build_program(nl=NL, stop=None):
    nc = bass.Bass("TRN2", target_bir_lowering=False)
    IN = {n: nc.dram_tensor(n, list(s), dt, kind="ExternalInput").ap() for n, s, dt in INPUT_SPECS}
    OUT = {n: nc.dram_tensor(n, list(s), F32, kind="ExternalOutput").ap() for n, s in OUTPUT_SPECS}
    es = ExitStack()
    with es:
        P = Prog(nc, es)

        def sb(name, shape, dt):
            return es.enter_context(nc.sbuf_tensor(name, list(shape), dt))

        xT = sb('xT_sb', [128, 8, 1024], F32)
        hT = sb('hT', [128, 8, 1024], BF16)
        ring = [sb('ring%d' % i, [128, RING_ELEMS], BF16) for i in range(NSLOT)]
        actb = [sb('actb%d' % i, [128, 512], BF16) for i in range(8)]
        rstdN = sb('rstdN', [128, 512], F32)
        tmp = [sb('tmp%d' % i, [128, 512], F32) for i in range(NTMP)]
        sqb = [sb('sqb%d' % i, [128, 512], BF16) for i in range(3)]
        ps = [es.enter_context(nc.psum_tensor('ps%d' % i, [128, 512], F32)) for i in range(8)]
        identb = sb('identb_sb', [128, 128], BF16)
        onesb = sb('onesb_sb', [128, 128], BF16)
        blk64b = sb('blk64b_sb', [128, 128], BF16)
        ropeB = sb('ropeB_sb', [128, 2, 1024], F32)
        ropeA = sb('ropeA_sb', [128, 2, 1024], F32)
        cs64 = sb('cs64_sb', [128, 256], BF16)
        epsT = sb('epsT', [128, 1], F32)
        gvec = sb('gvec_sb', [128, NL, NG], F32)
        gT = sb('gT_sb', [128, NL, 3, 8], F32)
        badaT = sb('badaT_sb', [128, NL, 72], F32)
        condf = sb('condf', [128, 8], F32)
        condb = sb('condb', [128, 8], BF16)
        ctxf = sb('ctxf', [128, 1], F32)
        modT = [sb('modT%d' % i, [128, 72], F32) for i in range(2)]
        modA = [sb('modA%d' % i, [128, 3, 8], F32) for i in range(2)]
        modG = [sb('modG%d' % i, [128, 3, 8], F32) for i in range(2)]
        sinkexp = sb('sinkexp', [128, 4], F32)
        rec = sb('rec', [128, 4], F32)
        QB = sb('QB', [128, 4, 1024], BF16)
        QD = sb('QD', [128, 4, 1024], BF16)
        QA = sb('QA', [128, 4, 1024], BF16)
        KB = sb('KB', [128, 1280], BF16)
        KD = sb('KD', [128, 2, 1280], BF16)
        KA = sb('KA', [128, 4, 1280], BF16)
        VA = sb('VA', [128, 10, 4, 65], BF16)
        VB = sb('VB', [128, 10, 2, 65], BF16)
        VD = sb('VD', [128, 10, 4, 65], BF16)
        PT = sb('PT', [128, NPT, 512], BF16)
        ABt = PT[:, 0:8, :].rearrange("p l (c x) -> p l c x", c=2)
        XcT = PT[:, 8:12, :].rearrange("p (c t) x -> p c (t x)", c=2)
        mtok = sb('mtok', [128, 2, 4, 128], BF16)
        cqn = sb('cqn', [128, 2, 512], BF16)
        ckvn = sb('ckvn', [128, 1024], BF16)
        sqkr = sb('sqkr', [128, 512], BF16)
        krr = rstdN
        wuq = sb('wuq', [128, 2, 384], BF16)
        wuqP = sb('wuqP', [128, 2, 384], BF16)
        wukv = sb('wukv', [128, 512], BF16)
        ckvc = sb('ckvc', [128, 256], BF16)

        st = dict(ps=0, tmp=0, sq=0, ring=0, pt=0, act=0, ld=0, out=0)
        out_ids = []

        def rot(name, n):
            i = st[name]
            st[name] = (i + 1) % n
            return i

        newps = lambda: rot('ps', 7)
        newtmp = lambda: rot('tmp', NTMP)
        newsq = lambda: rot('sq', 3)

        def MM(out, lhsT, rhs, start, stop, rd, wr):
            P.add('pe', lambda e: e.matmul(out, lhsT=lhsT, rhs=rhs, start=start, stop=stop), rd, wr)

        def ACT(out, in_, func, rd, wr, scale=None, bias=None):
            kw = {}
            if scale is not None:
                kw['scale'] = scale
            if bias is not None:
                kw['bias'] = bias
            P.add('act', lambda e: e.activation(out=out, in_=in_, func=func, **kw), rd, wr)

        def TT(out, in0, in1, op, rd, wr):
            P.add('dve', lambda e: e.tensor_tensor(out=out, in0=in0, in1=in1, op=op), rd, wr)

        def STT(out, in0, scalar, in1, op0, op1, rd, wr):
            P.add('dve', lambda e: e.scalar_tensor_tensor(out=out, in0=in0, scalar=scalar, in1=in1, op0=op0, op1=op1), rd, wr)

        def TS(out, in0, s1, op0, rd, wr):
            P.add('dve', lambda e: e.tensor_scalar(out=out, in0=in0, scalar1=s1, scalar2=None, op0=op0), rd, wr)

        def RECIP(out, in_, rd, wr):
            P.add('dve', lambda e: e.reciprocal(out=out, in_=in_), rd, wr)

        def RECIPF(out, in_, rd, wr):
            P.add('dve', lambda e: e.reciprocal_approx_fast(out=out, in_=in_), rd, wr)

        def DMA(q, out, in_, rd, wr, chan):
            return P.add(q, lambda e: e.dma_start(out=out, in_=in_), rd, wr, chan=chan)

        def LOAD(out, in_, keys, q='sp'):
            return DMA(q, out, in_, [], keys, 'ld%d' % rot('ld', 4))

        def STORE(out, in_, rd):
            out_ids.append(DMA('sp', out, in_, rd, [], 'st%d' % rot('out', 4)))

        def ring_get(src_ap, a, b):
            s = rot('ring', NSLOT)
            view = ring[s][:, 0:a * b].rearrange("p (a b) -> p a b", b=b)
            DMA('pool', view, src_ap, [], [('ring', s)], 'ring%d' % s)
            return view, ('ring', s)

        cur_layer = [0]

        def CK(name):
            if stop == name or stop == '%s@%d' % (name, cur_layer[0]):
                raise _Stop()

        def kcp(ap):
            return ap.rearrange("(kc p) n -> p kc n", p=128)

        C = 'consts'
        cl = []
        for name, tile in [('identb', identb), ('onesb', onesb), ('blk64b', blk64b),
                           ('ropeB', ropeB), ('ropeA', ropeA), ('cs64', cs64),
                           ('cond', condf), ('ctxflag', ctxf)]:
            cl.append(LOAD(tile[:], IN[name], []))
        cl.append(LOAD(gvec[:], IN['gvec'].rearrange("l p g -> p l g"), []))
        cl.append(LOAD(gT[:], IN['gT'].rearrange("l p k c -> p l k c"), []))
        cl.append(LOAD(badaT[:], IN['b_adaT'].rearrange("l p j -> p l j"), []))
        xkeys = [('x', c, t) for c in range(8) for t in range(2)]
        LOAD(xT[:], kcp(IN['xT']), xkeys)
        P.add('dve', lambda e: e.memset(VA[:].rearrange("p a b c -> p (a b c)"), 1.0), [], ['Vinit'])
        P.add('dve', lambda e: e.memset(VB[:].rearrange("p a b c -> p (a b c)"), 1.0), [], ['Vinit'])
        P.add('dve', lambda e: e.memset(VD[:].rearrange("p a b c -> p (a b c)"), 1.0), [], ['Vinit'])
        zi = []
        for tl in (QB, QD, QA, KA):
            zi.append(P.add('dve', lambda e, tl=tl: e.memset(tl[:].rearrange("p a b -> p (a b)"), 0.0), [], []))
        for h in range(4):
            cl.append(P.add('sp', lambda e, h=h: e.dma_start(out=KA[96:104, h, :], in_=IN['EA']), [], [], chan='ld%d' % rot('ld', 4), extra_deps=zi))
            cl.append(P.add('sp', lambda e, h=h: e.dma_start(out=QA[96:104, h, :], in_=IN['FA']), [], [], chan='ld%d' % rot('ld', 4), extra_deps=zi))
        P.add('dve', lambda e: e.memset(epsT[:], EPS), [], [C], extra_deps=cl + zi)
        ACT(condb[:], condf[:], AF.Silu, [C], ['condb'])
        for Vt, nh in ((VA, 4), (VB, 2), (VD, 4)):
            for kt in (8, 9):
                for h in range(nh):
                    ACT(Vt[:, kt, h, 64:65], ctxf[:, 0:1], AF.Copy, ['Vinit', C], ['Vinit'])

        def rstd_from(ps_ap, pskey, inv_d, p0=0, p1=128, n=512):
            ti = newtmp()
            t = tmp[ti][p0:p1, 0:n]
            ACT(t, ps_ap, AF.Ln, [pskey, C], [('tmp', ti)], scale=inv_d, bias=epsT[p0:p1, 0:1])
            ACT(t, t, AF.Exp, [('tmp', ti)], [('tmp', ti)], scale=-0.5)
            return ti

        def mod_steps(l):
            par = l % 2
            for blk in range(18):
                wt, wk = ring_get(kcp(IN['w_ada'][l][:, blk * 512:(blk + 1) * 512]), 8, 512)
                for j in range(4):
                    col = blk * 4 + j
                    for kc in range(8):
                        MM(ps[7][:, col:col + 1], wt[:, kc, j * 128:(j + 1) * 128], condb[:, kc:kc + 1], kc == 0, kc == 7,
                           [wk, 'condb'], ['psM'])
                yield
            TT(modT[par][:, :], ps[7][:, 0:72], badaT[:, l, :], ALU.add, ['psM', C], [('mod', par)])
            for k in range(3):
                STT(modA[par][:, k, :], modT[par][:, (3 * k + 1) * 8:(3 * k + 2) * 8], 1.0, gT[:, l, k, :], ALU.add, ALU.mult,
                    [('mod', par), C], [('modA', par)])
            for k, f in ((0, 0.5), (1, 1.0), (2, 0.5)):
                TS(modG[par][:, k, :], modT[par][:, (3 * k + 2) * 8:(3 * k + 3) * 8], f, ALU.mult, [('mod', par)], [('modG', par)])
            yield

        def norm_mod(l, k):
            par = l % 2
            for t in range(2):
                hs = slice(t * 512, (t + 1) * 512)
                pi = newps()
                for c in range(8):
                    si = newsq()
                    ACT(sqb[si][:, :], xT[:, c, hs], AF.Square, [('x', c, t)], [('sq', si)])
                    MM(ps[pi][:, :], onesb[:, :], sqb[si][:, :], c == 0, c == 7, [('sq', si), C], [('ps', pi)])
                ACT(rstdN[:, :], ps[pi][:, :], AF.Ln, [('ps', pi), C], ['rstdN'], scale=1.0 / 1024, bias=epsT[:, 0:1])
                ACT(rstdN[:, :], rstdN[:, :], AF.Exp, ['rstdN'], ['rstdN'], scale=-0.5)
                for c in range(8):
                    ti = newtmp()
                    STT(tmp[ti][:, :], xT[:, c, hs], modA[par][:, k, c:c + 1], rstdN[:, :], ALU.mult, ALU.mult,
                        [('x', c, t), ('modA', par), 'rstdN'], [('tmp', ti)])
                    ACT(hT[:, c, hs], tmp[ti][:, :], AF.Identity, [('tmp', ti), ('mod', par)], [('h', c, t)],
                        bias=modT[par][:, 3 * k * 8 + c:3 * k * 8 + c + 1])

        def ffn(l, which, modgen):
            par = l % 2
            wg = IN['w_gate%d' % which][l]
            wu = IN['w_up%d' % which][l]
            wd = IN['w_down%d' % which][l]
            gk = 0 if which == 1 else 2
            for blk in range(6):
                nch = 4 if blk < 5 else 2
                c0 = blk * 512
                ncol = nch * 128
                gw, gkey = ring_get(kcp(wg[:, c0:c0 + ncol]), 8, ncol)
                uw, ukey = ring_get(kcp(wu[:, c0:c0 + ncol]), 8, ncol)
                dw, dkey = ring_get(wd[c0:c0 + ncol, :].rearrange("(j p) n -> p j n", p=128), nch, 1024)
                for t in range(2):
                    hs = slice(t * 512, (t + 1) * 512)
                    aslots = []
                    for j in range(nch):
                        ai = rot('act', 8)
                        aslots.append(ai)
                        pg = newps()
                        for kc in range(8):
                            MM(ps[pg][:, :], gw[:, kc, j * 128:(j + 1) * 128], hT[:, kc, hs], kc == 0, kc == 7,
                               [gkey, ('h', kc, t)], [('ps', pg)])
                        pu = newps()
                        for kc in range(8):
                            MM(ps[pu][:, :], uw[:, kc, j * 128:(j + 1) * 128], hT[:, kc, hs], kc == 0, kc == 7,
                               [ukey, ('h', kc, t)], [('ps', pu)])
                        ti = newtmp()
                        ACT(tmp[ti][:, :], ps[pg][:, :], AF.Silu, [('ps', pg)], [('tmp', ti)])
                        TT(actb[ai][:, :], tmp[ti][:, :], ps[pu][:, :], ALU.mult, [('tmp', ti), ('ps', pu)], [('act', ai)])
                    for dc in range(8):
                        po = newps()
                        for j in range(nch):
                            MM(ps[po][:, :], dw[:, j, dc * 128:(dc + 1) * 128], actb[aslots[j]][:, :], j == 0, j == nch - 1,
                               [dkey, ('act', aslots[j])], [('ps', po)])
                        STT(xT[:, dc, hs], ps[po][:, :], modG[par][:, gk, dc:dc + 1], xT[:, dc, hs], ALU.mult, ALU.add,
                            [('ps', po), ('modG', par), ('x', dc, t)], [('x', dc, t)])
                if modgen is not None:
                    next(modgen, None)

        def rope_norm(psraw, pskey, p0, p1, ones_lhsT, inv_d, gain, out16, outkeys, rope=None, store=None, n=512, split=None, psf=None, gainf=None):
            si = newsq()
            ACT(sqb[si][p0:p1, 0:n], psraw, AF.Square, [pskey], [('sq', si)])
            pss = newps()
            MM(ps[pss][p0:p1, 0:n], ones_lhsT, sqb[si][p0:p1, 0:n], True, True, [('sq', si), C], [('ps', pss)])
            ri = rstd_from(ps[pss][p0:p1, 0:n], ('ps', pss), inv_d, p0, p1, n)
            r = tmp[ri][p0:p1, 0:n]
            if rope is None and store is None and split is not None:
                for (a_, b_, ap_, ks_) in split:
                    STT(ap_, psf(a_, b_), gainf(a_, b_), tmp[ri][a_:b_, 0:n], ALU.mult, ALU.mult, [pskey, ('tmp', ri), C], ks_)
                return
            if rope is None and store is None:
                STT(out16, psraw, gain, r, ALU.mult, ALU.mult, [pskey, ('tmp', ri), C], outkeys)
                return
            if rope is None:
                xi = newtmp()
                xn = tmp[xi][p0:p1, 0:n]
                STT(xn, psraw, gain, r, ALU.mult, ALU.mult, [pskey, ('tmp', ri), C], [('tmp', xi)])
                STORE(store, xn, [('tmp', xi)])
                ACT(out16, xn, AF.Copy, [('tmp', xi)], outkeys)
                return
            psraw_p, pskey_p, gain_p, ropeC, ropeS = rope
            t1 = newtmp()
            STT(tmp[t1][p0:p1, 0:n], psraw, gain, ropeC, ALU.mult, ALU.mult, [pskey, C], [('tmp', t1)])
            t2 = newtmp()
            STT(tmp[t2][p0:p1, 0:n], psraw_p, gain_p, ropeS, ALU.mult, ALU.mult, [pskey_p, C], [('tmp', t2)])
            TT(tmp[t1][p0:p1, 0:n], tmp[t1][p0:p1, 0:n], tmp[t2][p0:p1, 0:n], ALU.add, [('tmp', t1), ('tmp', t2)], [('tmp', t1)])
            if split is not None:
                for (a_, b_, ap_, ks_) in split:
                    TT(ap_, tmp[t1][a_:b_, 0:n], tmp[ri][a_:b_, 0:n], ALU.mult, [('tmp', t1), ('tmp', ri)], ks_)
            elif store is None:
                TT(out16, tmp[t1][p0:p1, 0:n], r, ALU.mult, [('tmp', t1), ('tmp', ri)], outkeys)
            else:
                TT(tmp[t1][p0:p1, 0:n], tmp[t1][p0:p1, 0:n], r, ALU.mult, [('tmp', t1), ('tmp', ri)], [('tmp', t1)])
                STORE(store, tmp[t1][p0:p1, 0:n], [('tmp', t1)])
                ACT(out16, tmp[t1][p0:p1, 0:n], AF.Copy, [('tmp', t1)], outkeys)

        def mixer(l, modgen):
            par = l % 2

            def gv(col, p0=0, p1=128):
                return gvec[p0:p1, l, col:col + 1]

            def step():
                if modgen is not None:
                    next(modgen, None)

            DMA('pool', wuq[:], IN['w_uq'][l].rearrange("(c p) n -> p c n", p=128), [], ['wuq'], 'cx0')
            DMA('pool', wuqP[:], IN['w_uqP'][l].rearrange("(c p) n -> p c n", p=128), [], ['wuqP'], 'cx0')
            DMA('pool', wukv[:], IN['w_ukv'][l], [], ['wukv'], 'cx1')
            DMA('pool', ckvc[:], IN['ckvT_c'][l], [], ['ckvc'], 'cx2')
            DMA('pool', KB[:, 1024:1280], IN['winkT_c'][l], [], [('KB', 'c')], 'cx4')
            DMA('pool', KD[:, :, 1024:1280], IN['nakT_c'][l].rearrange("(c p) k -> p c k", p=128), [], [('KD', 0, 'c'), ('KD', 1, 'c')], 'cx5')
            for i in range(2):
                DMA('pool', VB[:, 8 + i, :, 0:64], IN['winv_c'][l][i * 128:(i + 1) * 128, :].rearrange("p (h d) -> p h d", d=64),
                    [], [('VB', 8 + i)], 'cx6')
                DMA('pool', VD[:, 8 + i, :, 0:64], IN['nav_c'][l][i * 128:(i + 1) * 128, :].rearrange("p (h d) -> p h d", d=64),
                    [], [('VD', 8 + i)], 'cx7')
            ACT(sinkexp[:, :], gvec[:, l, 9:13], AF.Exp, [C], ['sinkexp'])
            wukv_v = wukv[:, 0:512].rearrange("p (h x) -> p h x", x=128)[:, :, 64:128]

            b0, b0k = ring_get(kcp(IN['w_in'][l][:, 0:416]), 8, 416)
            bpa, bpak = ring_get(kcp(IN['w_inPa'][l]), 8, 32)
            for t in range(2):
                hs = slice(t * 512, (t + 1) * 512)
                pcs = []
                for c in range(2):
                    pi = newps()
                    pcs.append(pi)
                    for kc in range(8):
                        MM(ps[pi][:, :], b0[:, kc, c * 128:(c + 1) * 128], hT[:, kc, hs], kc == 0, kc == 7, [b0k, ('h', kc, t)], [('ps', pi)])
                pss = newps()
                for c in range(2):
                    si = newsq()
                    ACT(sqb[si][:, :], ps[pcs[c]][:, :], AF.Square, [('ps', pcs[c])], [('sq', si)])
                    MM(ps[pss][:, :], onesb[:, :], sqb[si][:, :], c == 0, c == 1, [('sq', si), C], [('ps', pss)])
                ri = rstd_from(ps[pss][:, :], ('ps', pss), 1.0 / 256)
                for c in range(2):
                    STT(cqn[:, c, :], ps[pcs[c]][:, :], gv(c), tmp[ri][:, :], ALU.mult, ALU.mult,
                        [('ps', pcs[c]), ('tmp', ri), C], [('cqn', c)])
                pk = newps()
                for kc in range(8):
                    MM(ps[pk][:, :], b0[:, kc, 256:384], hT[:, kc, hs], kc == 0, kc == 7, [b0k, ('h', kc, t)], [('ps', pk)])
                rope_norm(ps[pk][:, :], ('ps', pk), 0, 128, onesb[:, :], 1.0 / 128, gv(2), ckvn[:, hs], [('ckvn', t)],
                          rope=None, store=OUT['o_ckvT'][l][:, hs])
                pr = newps()
                for kc in range(8):
                    MM(ps[pr][64:96, :], b0[:, kc, 384:416], hT[:, kc, hs], kc == 0, kc == 7, [b0k, ('h', kc, t)], [('ps', pr)])
                ti = newtmp()
                ACT(tmp[ti][64:96, :], ps[pr][64:96, :], AF.Copy, [('ps', pr)], [('tmp', ti)])
                STORE(OUT['o_krT'][l][:, hs], tmp[ti][64:96, :], [('tmp', ti)])
                ACT(sqkr[64:96, :], ps[pr][64:96, :], AF.Square, [('ps', pr)], ['sqkr'])
                prp = newps()
                for kc in range(8):
                    MM(ps[prp][64:96, :], bpa[:, kc, 0:32], hT[:, kc, hs], kc == 0, kc == 7, [bpak, ('h', kc, t)], [('ps', prp)])
                t1 = newtmp()
                STT(tmp[t1][64:96, :], ps[prp][64:96, :], gv(14, 64, 96), ropeA[64:96, 1, hs], ALU.mult, ALU.mult, [('ps', prp), C], [('tmp', t1)])
                STT(krr[64:96, :], ps[pr][64:96, :], gv(4, 64, 96), ropeA[64:96, 0, hs], ALU.mult, ALU.mult, [('ps', pr), C], ['rstdN'])
                TT(krr[64:96, :], krr[64:96, :], tmp[t1][64:96, :], ALU.add, ['rstdN', ('tmp', t1)], ['rstdN'])
                for h in range(4):
                    pn = newps()
                    MM(ps[pn][0:64, :], wukv[:, h * 128:h * 128 + 64], ckvn[:, hs], True, True, ['wukv', ('ckvn', t)], [('ps', pn)])
                    si = newsq()
                    ACT(sqb[si][0:64, :], ps[pn][0:64, :], AF.Square, [('ps', pn)], [('sq', si)])
                    pss = newps()
                    MM(ps[pss][0:96, :], onesb[0:64, 0:96], sqb[si][0:64, :], True, False, [('sq', si), C], [('ps', pss)])
                    MM(ps[pss][0:96, :], onesb[64:96, 0:96], sqkr[64:96, :], False, True, ['sqkr', C], [('ps', pss)])
                    ri = rstd_from(ps[pss][0:96, :], ('ps', pss), 1.0 / 96, 0, 96)
                    STT(KA[0:64, h, hs], ps[pn][0:64, :], gv(4, 0, 64), tmp[ri][0:64, :], ALU.mult, ALU.mult,
                        [('ps', pn), ('tmp', ri), C], [('KA', h, t)])
                    TT(KA[64:96, h, hs], krr[64:96, :], tmp[ri][64:96, :], ALU.mult, ['rstdN', ('tmp', ri)], [('KA', h, t)])
                    pq = newps()
                    for c in range(2):
                        MM(ps[pq][0:96, :], wuq[:, c, h * 96:(h + 1) * 96], cqn[:, c, :], c == 0, c == 1, ['wuq', ('cqn', c)], [('ps', pq)])
                    pqp = newps()
                    for c in range(2):
                        MM(ps[pqp][0:96, :], wuqP[:, c, h * 96:(h + 1) * 96], cqn[:, c, :], c == 0, c == 1, ['wuqP', ('cqn', c)], [('ps', pqp)])
                    rope_norm(ps[pq][0:96, :], ('ps', pq), 0, 96, onesb[0:96, 0:96], 1.0 / 96, gv(3, 0, 96), QA[0:96, h, hs],
                              [('QA', h, t)], rope=(ps[pqp][0:96, :], ('ps', pqp), gv(13, 0, 96), ropeA[0:96, 0, hs], ropeA[0:96, 1, hs]))
                for tt in range(4):
                    kt = t * 4 + tt
                    pv = newps()
                    MM(ps[pv][:, 0:256], ckvn[:, kt * 128:(kt + 1) * 128], wukv_v, True, True, ['wukv', ('ckvn', t)], [('ps', pv)])
                    ACT(VA[:, kt, :, 0:64], ps[pv][:, 0:256].rearrange("p (h d) -> p h d", d=64), AF.Copy, [('ps', pv), 'Vinit'], [('VA', kt)])
            step()
            CK('projA')
            tkc = newtmp()
            krc_ap = tmp[tkc][64:96, 0:256]
            DMA('sp', krc_ap, IN['krT_c'][l], [], [('tmp', tkc)], 'cx3')
            si = newsq()
            ACT(sqb[si][64:96, 0:256], krc_ap, AF.Square, [('tmp', tkc)], [('sq', si)])
            sq_krc = si
            tgc = newtmp()
            TS(tmp[tgc][64:96, 0:256], krc_ap, gv(4, 64, 96), ALU.mult, [('tmp', tkc), C], [('tmp', tgc)])
            for h in range(4):
                pn = newps()
                MM(ps[pn][0:64, 0:256], wukv[:, h * 128:h * 128 + 64], ckvc[:, :], True, True, ['wukv', 'ckvc'], [('ps', pn)])
                si = newsq()
                if si == sq_krc:
                    si = newsq()
                ACT(sqb[si][0:64, 0:256], ps[pn][0:64, 0:256], AF.Square, [('ps', pn)], [('sq', si)])
                pss = newps()
                MM(ps[pss][0:96, 0:256], onesb[0:64, 0:96], sqb[si][0:64, 0:256], True, False, [('sq', si), C], [('ps', pss)])
                MM(ps[pss][0:96, 0:256], onesb[64:96, 0:96], sqb[sq_krc][64:96, 0:256], False, True, [('sq', sq_krc), C], [('ps', pss)])
                ri = rstd_from(ps[pss][0:96, 0:256], ('ps', pss), 1.0 / 96, 0, 96, 256)
                if ri == tgc:
                    raise RuntimeError("tmp rotation clash")
                STT(KA[0:64, h, 1024:1280], ps[pn][0:64, 0:256], gv(4, 0, 64), tmp[ri][0:64, 0:256], ALU.mult, ALU.mult,
                    [('ps', pn), ('tmp', ri), C], [('KA', h, 'c')])
                TT(KA[64:96, h, 1024:1280], tmp[tgc][64:96, 0:256], tmp[ri][64:96, 0:256], ALU.mult, [('tmp', tgc), ('tmp', ri)], [('KA', h, 'c')])
            for i in range(2):
                pv = newps()
                MM(ps[pv][:, 0:256], ckvc[:, i * 128:(i + 1) * 128], wukv_v, True, True, ['wukv', 'ckvc'], [('ps', pv)])
                ACT(VA[:, 8 + i, :, 0:64], ps[pv][:, 0:256].rearrange("p (h d) -> p h d", d=64), AF.Copy, [('ps', pv), 'Vinit'], [('VA', 8 + i)])
            step()
            CK('ctxA')

            b1, b1k = ring_get(kcp(IN['w_in'][l][:, 416:928]), 8, 512)
            b1p, b1pk = ring_get(kcp(IN['w_inPb'][l]), 8, 384)
            for t in range(2):
                hs = slice(t * 512, (t + 1) * 512)
                for ci, hpair in enumerate(((0, 2), (1, 3))):
                    pi = newps()
                    for hh, pb in zip(hpair, (0, 64)):
                        for kc in range(8):
                            MM(ps[pi][pb:pb + 64, :], b1[:, kc, hh * 64:(hh + 1) * 64], hT[:, kc, hs], kc == 0, kc == 7,
                               [b1k, ('h', kc, t)], [('ps', pi)])
                    pip = newps()
                    for hh, pb in zip(hpair, (0, 64)):
                        for kc in range(8):
                            MM(ps[pip][pb:pb + 64, :], b1p[:, kc, hh * 64:(hh + 1) * 64], hT[:, kc, hs], kc == 0, kc == 7,
                               [b1pk, ('h', kc, t)], [('ps', pip)])
                    rope_norm(ps[pi][:, :], ('ps', pi), 0, 128, blk64b[:, :], 1.0 / 64, gv(5), None, None,
                              rope=(ps[pip][:, :], ('ps', pip), gv(15), ropeB[:, 0, hs], ropeB[:, 1, hs]),
                              split=[(0, 64, QB[0:64, hpair[0], hs], [('QB', hpair[0], t)]),
                                     (64, 128, QB[64:128, hpair[1], hs], [('QB', hpair[1], t)])])
                    CK('qb')
                pi = newps()
                for kc in range(8):
                    MM(ps[pi][:, :], b1[:, kc, 256:384], hT[:, kc, hs], kc == 0, kc == 7, [b1k, ('h', kc, t)], [('ps', pi)])
                pip = newps()
                for kc in range(8):
                    MM(ps[pip][:, :], b1p[:, kc, 256:384], hT[:, kc, hs], kc == 0, kc == 7, [b1pk, ('h', kc, t)], [('ps', pip)])
                rope_norm(ps[pi][:, :], ('ps', pi), 0, 128, blk64b[:, :], 1.0 / 64, gv(6), KB[:, hs], [('KB', t)],
                          rope=(ps[pip][:, :], ('ps', pip), gv(16), ropeB[:, 0, hs], ropeB[:, 1, hs]), store=OUT['o_kbT'][l][:, hs])
                CK('kb')
                for tt in range(4):
                    kt = t * 4 + tt
                    pv = newps()
                    for kc in range(8):
                        MM(ps[pv][:, 0:128], hT[:, kc, kt * 128:(kt + 1) * 128], b1[:, kc, 384:512], kc == 0, kc == 7,
                           [b1k, ('h', kc, t)], [('ps', pv)])
                    ti = newtmp()
                    ACT(tmp[ti][:, 0:128], ps[pv][:, 0:128], AF.Copy, [('ps', pv)], [('tmp', ti)])
                    STORE(OUT['o_vb'][l][kt * 128:(kt + 1) * 128, :], tmp[ti][:, 0:128], [('tmp', ti)])
                    ACT(VB[:, kt, :, 0:64], ps[pv][:, 0:128].rearrange("p (h d) -> p h d", d=64), AF.Copy, [('ps', pv), 'Vinit'], [('VB', kt)])
            step()
            CK('projB')
            b2, b2k = ring_get(kcp(IN['w_in'][l][:, 928:1440]), 8, 512)
            for t in range(2):
                hs = slice(t * 512, (t + 1) * 512)
                for ch in range(2):
                    pi = newps()
                    for kc in range(8):
                        MM(ps[pi][:, :], b2[:, kc, ch * 128:(ch + 1) * 128], hT[:, kc, hs], kc == 0, kc == 7, [b2k, ('h', kc, t)], [('ps', pi)])
                    ACT(XcT[:, ch, hs], ps[pi][:, :], AF.Copy, [('ps', pi)], [('pt', 8 + 2 * ch + t)])
                for ch in range(2):
                    pi = newps()
                    for kc in range(8):
                        MM(ps[pi][:, :], b2[:, kc, 256 + ch * 128:256 + (ch + 1) * 128], hT[:, kc, hs], kc == 0, kc == 7,
                           [b2k, ('h', kc, t)], [('ps', pi)])
                    rope_norm(ps[pi][:, :], ('ps', pi), 0, 128, blk64b[:, :], 1.0 / 64, gv(7), None, None,
                              split=[(0, 64, QD[0:64, 2 * ch, hs], [('QD', 2 * ch, t)]),
                                     (64, 128, QD[64:128, 2 * ch + 1, hs], [('QD', 2 * ch + 1, t)])],
                              psf=lambda a_, b_, pi=pi: ps[pi][a_:b_, :], gainf=lambda a_, b_: gv(7, a_, b_))
            step()
            b3, b3k = ring_get(kcp(IN['w_in'][l][:, 1440:1952]), 8, 512)
            for t in range(2):
                hs = slice(t * 512, (t + 1) * 512)
                for ch in range(2):
                    pi = newps()
                    for kc in range(8):
                        MM(ps[pi][:, :], b3[:, kc, ch * 128:(ch + 1) * 128], hT[:, kc, hs], kc == 0, kc == 7, [b3k, ('h', kc, t)], [('ps', pi)])
                    rope_norm(ps[pi][:, :], ('ps', pi), 0, 128, blk64b[:, :], 1.0 / 64, gv(8), KD[:, ch, hs], [('KD', ch, t)],
                              rope=None, store=OUT['o_kdT'][l][ch * 128:(ch + 1) * 128, hs])
                for tt in range(4):
                    kt = t * 4 + tt
                    pv = newps()
                    for kc in range(8):
                        MM(ps[pv][:, 0:256], hT[:, kc, kt * 128:(kt + 1) * 128], b3[:, kc, 256:512], kc == 0, kc == 7,
                           [b3k, ('h', kc, t)], [('ps', pv)])
                    ti = newtmp()
                    ACT(tmp[ti][:, 0:256], ps[pv][:, 0:256], AF.Copy, [('ps', pv)], [('tmp', ti)])
                    STORE(OUT['o_vd'][l][kt * 128:(kt + 1) * 128, :], tmp[ti][:, 0:256], [('tmp', ti)])
                    ACT(VD[:, kt, :, 0:64], ps[pv][:, 0:256].rearrange("p (h d) -> p h d", d=64), AF.Copy, [('ps', pv), 'Vinit'], [('VD', kt)])
            step()
            CK('projD')
            for lt in range(8):
                for ch in range(2):
                    pi = newps()
                    MM(ps[pi][:, 0:256], XcT[:, ch, lt * 128:(lt + 1) * 128], cs64[:, :], True, True, [('pt', 8 + 2 * ch + lt // 4), C], [('ps', pi)])
                    P.add('dve', lambda e, lt=lt, ch=ch, pi=pi: e.tensor_copy(out=ABt[:, lt, ch, :], in_=ps[pi][:, 0:256]),
                          [('ps', pi)], [('ab', lt, ch)])
            for t in range(2):
                hs = slice(t * 512, (t + 1) * 512)
                cb, cbk = ring_get(IN['dftC'][:, hs].rearrange("(lt p) n -> p lt n", p=128), 8, 512)
                sbk_, sbkk = ring_get(IN['dftS'][:, hs].rearrange("(lt p) n -> p lt n", p=128), 8, 512)
                for ch in range(2):
                    pi = newps()
                    for lt in range(8):
                        MM(ps[pi][:, :], ABt[:, lt, ch, 0:128], cb[:, lt, :], lt == 0, False, [('ab', lt, ch), ('pt', lt), cbk], [('ps', pi)])
                    for lt in range(8):
                        MM(ps[pi][:, :], ABt[:, lt, ch, 128:256], sbk_[:, lt, :], False, lt == 7, [('ab', lt, ch), ('pt', lt), sbkk], [('ps', pi)])
                    ACT(hT[:, 4 + ch, hs], ps[pi][:, :], AF.Copy, [('ps', pi)], [('h', 4 + ch, t)])
            step()
            CK('fourier')

            shared = {}

            def unit_kts(kind, t):
                if kind == 'B':
                    return [kt for kt in range(4 * t - 1, 4 * t + 5) if 0 <= kt <= 7] + [8, 9]
                if kind == 'D':
                    return list(range(0, 6) if t == 0 else range(2, 8)) + [8, 9]
                return list(range(10))

            def attend_qk(kind, h, t):
                hs = slice(t * 512, (t + 1) * 512)
                kts = unit_kts(kind, t)
                if kind == 'B':
                    if 'maskB' not in shared:
                        shared['maskB'] = ring_get(IN['maskB'], 6, 512)
                elif kind == 'D':
                    if ('biasD', h) not in shared:
                        shared[('biasD', h)] = [ring_get(IN['biasD'][l][h][g * 512:(g + 1) * 512, :].rearrange("(kt p) q -> p kt q", p=128), 4, 1024)
                                                for g in range(2)]
                slots = {}
                for kt in kts:
                    pi = newps()
                    ksl = slice(kt * 128, (kt + 1) * 128)
                    own = kt < 8
                    kk = (kt // 4) if own else 'c'
                    if kind == 'A':
                        MM(ps[pi][:, :], KA[:, h, ksl], QA[:, h, hs], True, True, [('KA', h, kk), ('QA', h, t), C], [('ps', pi)])
                        scale = 96.0 ** -0.5
                    elif kind == 'B':
                        MM(ps[pi][:, :], KB[:, ksl], QB[:, h, hs], True, not own, [('KB', kk), ('QB', h, t), C], [('ps', pi)])
                        if own:
                            mv, mk = shared['maskB']
                            MM(ps[pi][:, :], identb[:, :], mv[:, kt - 4 * t + 1, :], False, True, [C, mk], [('ps', pi)])
                        scale = 0.125
                    else:
                        ch = h // 2
                        MM(ps[pi][:, :], KD[:, ch, ksl], QD[:, h, hs], True, not own, [('KD', ch, kk), ('QD', h, t), C], [('ps', pi)])
                        if own:
                            bv, bk = shared[('biasD', h)][kt // 4]
                            MM(ps[pi][:, :], identb[:, :], bv[:, kt % 4, hs], False, True, [C, bk], [('ps', pi)])
                        scale = 0.125
                    s = rot('pt', NPT)
                    slots[kt] = s
                    ACT(PT[:, s, :], ps[pi][:, :], AF.Exp, [('ps', pi)], [('pt', s)], scale=scale)
                return (kind, h, t, kts, slots)

            def attend_pv(ctx):
                kind, h, t, kts, slots = ctx
                V, vh, vname = {'A': (VA, h, 'VA'), 'B': (VB, h // 2, 'VB'), 'D': (VD, h, 'VD')}[kind]
                po = newps()
                for qi in range(4):
                    for i, kt in enumerate(kts):
                        MM(ps[po][:, qi * 65:(qi + 1) * 65], PT[:, slots[kt], qi * 128:(qi + 1) * 128], V[:, kt, vh, 0:65],
                           i == 0, i == len(kts) - 1, [('pt', slots[kt]), (vname, kt), 'Vinit'], [('ps', po)])
                for qi in range(4):
                    den = ps[po][:, qi * 65 + 64:qi * 65 + 65]
                    if kind == 'B':
                        TS(rec[:, qi:qi + 1], den, sinkexp[:, h:h + 1], ALU.add, [('ps', po), 'sinkexp'], [('rec', qi)])
                        RECIP(rec[:, qi:qi + 1], rec[:, qi:qi + 1], [('rec', qi)], [('rec', qi)])
                    else:
                        RECIP(rec[:, qi:qi + 1], den, [('ps', po)], [('rec', qi)])
                    hslot = h % 2
                    TS(mtok[:, t, qi, hslot * 64:(hslot + 1) * 64], ps[po][:, qi * 65:qi * 65 + 64], rec[:, qi:qi + 1], ALU.mult,
                       [('ps', po), ('rec', qi)], [('mtok', t, qi, hslot)])

            def attend_tr(chunk):
                for t in range(2):
                    hs = slice(t * 512, (t + 1) * 512)
                    pt_ = newps()
                    for qi in range(4):
                        MM(ps[pt_][:, qi * 128:(qi + 1) * 128], mtok[:, t, qi, :], identb[:, :], True, True,
                           [('mtok', t, qi, 0), ('mtok', t, qi, 1), C], [('ps', pt_)])
                    P.add('dve', lambda e, hs=hs, pt_=pt_, cc=chunk: e.tensor_copy(out=hT[:, cc, hs], in_=ps[pt_][:, :]),
                          [('ps', pt_)], [('h', chunk, t)])

            units = []
            for kind, chunk0 in (('A', 0), ('B', 2), ('D', 6)):
                for pair in range(2):
                    for h in (2 * pair, 2 * pair + 1):
                        for t in range(2):
                            units.append((kind, h, t, chunk0 + pair, (h % 2 == 1 and t == 1)))
            prev = None

            def flush(pv_):
                attend_pv(pv_[0])
                if pv_[2]:
                    attend_tr(pv_[1])
                    step()

            for (kind, h, t, chunk, last) in units:
                if prev is not None and len(prev[0][3]) + len(unit_kts(kind, t)) > NPT:
                    flush(prev)
                    prev = None
                ctx = attend_qk(kind, h, t)
                if prev is not None:
                    flush(prev)
                prev = (ctx, chunk, last)
            flush(prev)
            CK('attn')

            for blk in range(2):
                wo, wok = ring_get(kcp(IN['w_o'][l][:, blk * 512:(blk + 1) * 512]), 8, 512)
                for dcc in range(4):
                    dc = blk * 4 + dcc
                    for t in range(2):
                        hs = slice(t * 512, (t + 1) * 512)
                        po = newps()
                        for c in range(8):
                            MM(ps[po][:, :], wo[:, c, dcc * 128:(dcc + 1) * 128], hT[:, c, hs], c == 0, c == 7, [wok, ('h', c, t)], [('ps', po)])
                        STT(xT[:, dc, hs], ps[po][:, :], modG[par][:, 1, dc:dc + 1], xT[:, dc, hs], ALU.mult, ALU.add,
                            [('ps', po), ('modG', par), ('x', dc, t)], [('x', dc, t)])
            step()

        try:
            CK('load')
            g0 = mod_steps(0)
            for _ in g0:
                pass
            CK('mod')
            for l in range(nl):
                cur_layer[0] = l
                mg = mod_steps(l + 1) if l + 1 < nl else None
                norm_mod(l, 0)
                CK('norm0')
                ffn(l, 1, mg)
                CK('ffn1')
                norm_mod(l, 1)
                mixer(l, mg)
                CK('mixer')
                norm_mod(l, 2)
                ffn(l, 2, mg)
                if mg is not None:
                    for _ in mg:
                        pass
                CK('layer')
        except _Stop:
            pass
        STORE(kcp(OUT['yT']), xT[:], xkeys)
        P.add('sp', None, extra_deps=out_ids)
        P.emit()
        nc._prog_stats = (P.n_ops, P.sig_counts, P.chan_counts)
    return nc


def _rope_tables(r, sample):
    Cm = np.ones((r, T), np.float32)
    Sm = np.zeros((r, T), np.float32)
    if not sample:
        return Cm, Sm
    half = r // 2
    t = np.arange(T)
    inv = (10000.0 ** (-np.arange(0, half, 2, dtype=np.float32) / np.float32(half))).astype(np.float32)
    q = half // 2
    for part, pos in ((0, t // 64), (1, t % 64)):
        ang = pos.astype(np.float32)[:, None] * inv[None, :]
        cos = np.cos(ang).astype(np.float32).T
        sin = np.sin(ang).astype(np.float32).T
        b = part * half
        Cm[b:b + q] = cos
        Cm[b + q:b + half] = cos
        Sm[b:b + q] = -sin
        Sm[b + q:b + half] = sin
    return Cm, Sm


def _partner(r):
    half = r // 2
    q = half // 2
    return np.array([d + q if (d % half) < q else d - q for d in range(r)])


def _consts(sample):
    c = {}
    c['identb'] = np.eye(128, dtype=np.float32).astype(bf16)
    c['onesb'] = np.ones((128, 128), np.float32).astype(bf16)
    c['blk64b'] = np.kron(np.eye(2, dtype=np.float32), np.ones((64, 64), np.float32)).astype(bf16)
    Cb, Sb = _rope_tables(64, sample)
    rb = np.zeros((128, 2, T), np.float32)
    rb[0:64, 0], rb[64:128, 0] = Cb, Cb
    rb[0:64, 1], rb[64:128, 1] = Sb, Sb
    c['ropeB'] = rb
    Ca, Sa = _rope_tables(32, sample)
    ra = np.zeros((128, 2, T), np.float32)
    ra[:, 0] = 1.0
    ra[64:96, 0] = Ca
    ra[64:96, 1] = Sa
    c['ropeA'] = ra
    k = np.arange(64)
    ang = 2 * np.pi * np.outer(k, k) / 64.0
    C64 = np.cos(ang) / 8.0
    S64 = np.sin(ang) / 8.0
    cs = np.zeros((128, 256), np.float64)
    cs[0:64, 0:64] = C64
    cs[64:128, 64:128] = C64
    cs[0:64, 128:192] = S64
    cs[64:128, 192:256] = S64
    c['cs64'] = cs.astype(np.float32).astype(bf16)
    L = 1024 if sample else 256
    kk = np.arange(L)
    angL = 2 * np.pi * (np.outer(kk, kk) % L) / float(L)
    CL = np.cos(angL) / np.sqrt(L)
    SL = -np.sin(angL) / np.sqrt(L)
    if sample:
        dC, dS = CL, SL
    else:
        dC = np.kron(np.eye(4), CL)
        dS = np.kron(np.eye(4), SL)
    c['dftC'] = dC.astype(np.float32).astype(bf16)
    c['dftS'] = dS.astype(np.float32).astype(bf16)
    ea = np.zeros((8, 1280), np.float32)
    fa = np.zeros((8, 1024), np.float32)
    if not sample:
        for s in range(4):
            ea[s, s * 256:(s + 1) * 256] = 1.0
            fa[s, s * 256:(s + 1) * 256] = BIG
        ea[4, 0:1024] = 1.0
        fa[4, :] = -BIG
    c['EA'] = ea.astype(bf16)
    c['FA'] = fa.astype(bf16)
    mb = np.full((128, 6, 512), NEG, np.float32)
    kl = np.arange(128)[:, None]
    ql = np.arange(128)[None, :]
    for ki in range(6):
        ktp = ki - 1
        for qi in range(4):
            off = ktp - qi
            blkm = np.full((128, 128), NEG, np.float32)
            if sample:
                if off == 0:
                    blkm[:] = 0.0
                elif off == -1:
                    blkm = np.where(ql <= kl, 0.0, NEG).astype(np.float32)
                elif off == 1:
                    blkm = np.where(kl <= ql, 0.0, NEG).astype(np.float32)
            else:
                if off == 0 or (off == 1 and qi % 2 == 0) or (off == -1 and qi % 2 == 1):
                    blkm[:] = 0.0
            mb[:, ki, qi * 128:(qi + 1) * 128] = blkm
    c['maskB'] = mb.astype(bf16)
    return c


def _na_index():
    rows = 16
    r = np.arange(rows)
    row_start = np.clip(r - 4, 0, rows - 8)
    col = np.arange(64)
    col_start = np.clip(col - 8, 0, 64 - 16)
    q = np.arange(1024)
    qr, qc = q // 64, q % 64
    kr, kc = qr, qc
    KR, QR = kr[:, None], qr[None, :]
    KC, QC = kc[:, None], qc[None, :]
    valid = (KR >= row_start[QR]) & (KR < row_start[QR] + 8) & (KC >= col_start[QC]) & (KC < col_start[QC] + 16)
    dr = np.clip(KR - QR + 7, 0, 14)
    dc = np.clip(KC - QC + 15, 0, 30)
    return valid, dr, dc


def _vecT(v):
    return np.ascontiguousarray(v.reshape(-1, 128).T)


_PROG = {}


def _prep(x_prompt, x_sample, cache_mla_ckv, cache_mla_krope, cache_win_k, cache_win_v, cache_na_k, cache_na_v,
           c, c_ctx, w_ada, b_ada, g_ffn1, w_gate1, w_up1, w_down1, g_mix, w_in, g_qa, w_uq, g_kva, w_ukv,
           qn_a, kn_a, qn_b, kn_b, sink_b, qn_d, kn_d, rpb_d, w_o, g_ffn2, w_gate2, w_up2, w_down2):
    f = lambda a: np.ascontiguousarray(np.asarray(a, dtype=np.float32))
    x_prompt, x_sample, c, c_ctx = f(x_prompt), f(x_sample), f(c), f(c_ctx)
    shared = {n: f(v) for n, v in dict(w_ada=w_ada, w_gate1=w_gate1, w_up1=w_up1, w_down1=w_down1, w_gate2=w_gate2,
                                       w_up2=w_up2, w_down2=w_down2, w_in=w_in, w_o=w_o, w_uq=w_uq, w_ukv=w_ukv).items()}
    b_ada = f(b_ada)
    shared['b_adaT'] = np.ascontiguousarray(b_ada.reshape(NL, 72, 128).transpose(0, 2, 1))
    gs = np.stack([f(g_ffn1), f(g_mix), f(g_ffn2)], axis=1)
    shared['gT'] = np.ascontiguousarray(gs.reshape(NL, 3, 8, 128).transpose(0, 3, 1, 2))
    gv = np.zeros((NL, 128, NG), np.float32)
    gv[:, :, 0:2] = f(g_qa).reshape(NL, 2, 128).transpose(0, 2, 1)
    gv[:, :, 2] = f(g_kva)
    gv[:, 0:96, 3] = f(qn_a)
    gv[:, 0:96, 4] = f(kn_a)
    gv[:, :, 5] = np.tile(f(qn_b), (1, 2))
    gv[:, :, 6] = np.tile(f(kn_b), (1, 2))
    gv[:, :, 7] = np.tile(f(qn_d), (1, 2))
    gv[:, :, 8] = np.tile(f(kn_d), (1, 2))
    gv[:, :, 9:13] = f(sink_b)[:, None, :]
    p64, p32 = _partner(64), _partner(32)
    ia = np.concatenate([np.arange(64), 64 + p32])
    gv[:, 0:96, 13] = f(qn_a)[:, ia]
    gv[:, 0:96, 14] = f(kn_a)[:, ia]
    gv[:, :, 15] = np.tile(f(qn_b)[:, p64], (1, 2))
    gv[:, :, 16] = np.tile(f(kn_b)[:, p64], (1, 2))
    w_in_f, w_uq_f = shared['w_in'], shared['w_uq']
    shared['w_inPa'] = np.ascontiguousarray(w_in_f[:, :, 384 + p32])
    qb_cols = np.concatenate([416 + h * 64 + p64 for h in range(4)])
    kb_cols = np.concatenate([672 + h * 64 + p64 for h in range(2)])
    shared['w_inPb'] = np.ascontiguousarray(w_in_f[:, :, np.concatenate([qb_cols, kb_cols])])
    shared['w_uqP'] = np.ascontiguousarray(w_uq_f[:, :, np.concatenate([h * 96 + ia for h in range(4)])])
    shared['gvec'] = gv
    consts = {True: _consts(True), False: _consts(False)}
    valid, dr, dc = _na_index()
    rpb = f(rpb_d)
    bias_s = np.where(valid[None, None], rpb[:, :, dr, dc], np.float32(NEG)).astype(np.float32)
    seq = np.arange(1024) // 256
    bias_p1 = np.where(seq[:, None] == seq[None, :], np.float32(0.0), np.float32(NEG)).astype(np.float32)
    bias_p = np.ascontiguousarray(np.broadcast_to(bias_p1, (NL, 4, 1024, 1024)))
    cm_ckv, cm_kr = f(cache_mla_ckv), f(cache_mla_krope)
    cw_k, cw_v, cn_k, cn_v = f(cache_win_k), f(cache_win_v), f(cache_na_k), f(cache_na_v)
    in_maps = []
    for core in range(8):
        sample = core >= 4
        m = dict(shared)
        m.update(consts[sample])
        if sample:
            b = core - 4
            xs = x_sample[b]
            cond = c[b]
            m['ckvT_c'] = np.ascontiguousarray(cm_ckv[b].transpose(0, 2, 1))
            m['krT_c'] = np.ascontiguousarray(cm_kr[b].transpose(0, 2, 1))
            m['winkT_c'] = np.ascontiguousarray(cw_k[b].reshape(NL, 256, 128).transpose(0, 2, 1))
            m['winv_c'] = np.ascontiguousarray(cw_v[b].reshape(NL, 256, 128))
            m['nakT_c'] = np.ascontiguousarray(cn_k[b].reshape(NL, 256, 256).transpose(0, 2, 1))
            m['nav_c'] = np.ascontiguousarray(cn_v[b].reshape(NL, 256, 256))
            m['biasD'] = bias_s
            m['ctxflag'] = np.ones((128, 1), np.float32)
        else:
            xs = x_prompt[4 * core:4 * core + 4].reshape(1024, 1024)
            cond = c_ctx
            m['ckvT_c'] = np.zeros((NL, 128, 256), np.float32)
            m['krT_c'] = np.zeros((NL, 32, 256), np.float32)
            m['winkT_c'] = np.zeros((NL, 128, 256), np.float32)
            m['winv_c'] = np.zeros((NL, 256, 128), np.float32)
            m['nakT_c'] = np.zeros((NL, 256, 256), np.float32)
            m['nav_c'] = np.zeros((NL, 256, 256), np.float32)
            m['biasD'] = bias_p
            m['ctxflag'] = np.zeros((128, 1), np.float32)
        m['xT'] = np.ascontiguousarray(xs.T)
        m['cond'] = _vecT(cond)
        in_maps.append(m)
    return in_maps


def _assemble(R):
    y_prompt = np.concatenate([R[i]['yT'].T.reshape(4, 256, 1024) for i in range(4)], axis=0)
    y_sample = np.stack([R[4 + b]['yT'].T for b in range(4)], axis=0)

    def featmaj(name, feat):
        outs = []
        for i in range(4):
            a = R[i][name]
            a = a.reshape(NL, feat, 4, 256).transpose(2, 0, 3, 1)
            outs.append(a)
        return np.ascontiguousarray(np.concatenate(outs, axis=0))

    def tokmaj(name, feat):
        outs = []
        for i in range(4):
            a = R[i][name].reshape(NL, 4, 256, feat).transpose(1, 0, 2, 3)
            outs.append(a)
        return np.ascontiguousarray(np.concatenate(outs, axis=0))

    new_ckv = featmaj('o_ckvT', 128)
    new_kr = featmaj('o_krT', 32)
    new_wk = featmaj('o_kbT', 128).reshape(16, NL, 256, 2, 64)
    new_wv = tokmaj('o_vb', 128).reshape(16, NL, 256, 2, 64)
    new_nk = featmaj('o_kdT', 256).reshape(16, NL, 256, 4, 64)
    new_nv = tokmaj('o_vd', 256).reshape(16, NL, 256, 4, 64)
    return (np.ascontiguousarray(y_prompt.astype(np.float32)), np.ascontiguousarray(y_sample.astype(np.float32)),
            new_ckv, new_kr, new_wk, new_wv, new_nk, new_nv)


def kernel(**inputs):
    in_maps = _prep(**inputs)
    if 'nc' not in _PROG:
        _PROG['nc'] = build_program(NL)
    res = run_bass_kernel_spmd(_PROG['nc'], in_maps, core_ids=list(range(8)))
    return _assemble(res.results)
```

```python
import numpy as np
import ml_dtypes
from contextlib import ExitStack
import concourse.bass as bass
import concourse.mybir as mybir
from concourse.bass_utils import run_bass_kernel_spmd

F32 = mybir.dt.float32
BF16 = mybir.dt.bfloat16
AF = mybir.ActivationFunctionType
ALU = mybir.AluOpType
bf16 = ml_dtypes.bfloat16

NL = 4
D = 1024
T = 1024
DFF = 2816
EPS = 1e-6
NEG = -30000.0
BIG = 1024.0
RING_ELEMS = 4096
NSLOT = 4
NTMP = 6
NPT = 20
NG = 17


class Prog:
    COMPUTE = ('pe', 'act', 'dve', 'pool')

    def __init__(self, nc, es, same_engine_sync=True):
        self.nc = nc
        self.es = es
        self.ops = []
        self.lastw = {}
        self.readers = {}
        self.chan_last = {}
        self.chan_cnt = {}
        self.same_engine_sync = same_engine_sync

    def add(self, eng, fn, reads=(), writes=(), chan=None, extra_deps=()):
        idx = len(self.ops)
        deps = set(extra_deps)
        if eng in ('act', 'dve'):
            pk = [k for k in reads if k == 'psM' or (isinstance(k, tuple) and k[0] == 'ps')]
            if pk:
                writes = list(writes) + pk
        for k in reads:
            w = self.lastw.get(k)
            if w is not None:
                deps.add(w)
        for k in writes:
            w = self.lastw.get(k)
            if w is not None:
                deps.add(w)
            last = {}
            for r in self.readers.get(k, ()):
                rop = self.ops[r]
                if rop['chan'] is not None:
                    deps.add(r)
                else:
                    last[rop['eng']] = r
            deps.update(last.values())
        for k in reads:
            self.readers.setdefault(k, []).append(idx)
        for k in writes:
            self.lastw[k] = idx
            self.readers[k] = []
        if chan is not None:
            p = self.chan_last.get(chan)
            if p is not None:
                deps.add(p)
            self.chan_last[chan] = idx
            self.chan_cnt[chan] = self.chan_cnt.get(chan, 0) + 1
        deps.discard(idx)
        self.ops.append(dict(eng=eng, fn=fn, deps=deps, chan=chan, signal=False, sigval=None,
                             chanval=(16 * self.chan_cnt[chan] if chan is not None else None)))
        return idx

    def emit(self):
        nc = self.nc
        ops = self.ops
        for op in ops:
            for d in op['deps']:
                dop = ops[d]
                if dop['chan'] is None:
                    if dop['eng'] == op['eng'] and (op['eng'] == 'pe' or not self.same_engine_sync):
                        continue
                    dop['signal'] = True
        cnt = {}
        for op in ops:
            if op['chan'] is None and op['signal']:
                cnt[op['eng']] = cnt.get(op['eng'], 0) + 1
                op['sigval'] = cnt[op['eng']]
        self.sig_counts = dict(cnt)
        self.chan_counts = dict(self.chan_cnt)
        self.n_ops = len(ops)
        sems = {e: self.es.enter_context(nc.semaphore('s_' + e)) for e in self.COMPUTE + ('sp',)}
        csems = {c: self.es.enter_context(nc.semaphore('c_%d' % i)) for i, c in enumerate(self.chan_cnt)}
        by_eng = {}
        for i, op in enumerate(ops):
            by_eng.setdefault(op['eng'], []).append(i)

        def run_engine(ename, e):
            waited = {}
            for i in by_eng.get(ename, ()):
                op = ops[i]
                need = {}
                for d in op['deps']:
                    dop = ops[d]
                    if dop['chan'] is not None:
                        key = ('c', dop['chan'])
                        val = dop['chanval']
                    else:
                        if dop['eng'] == ename and (ename == 'pe' or not self.same_engine_sync):
                            continue
                        key = ('e', dop['eng'])
                        val = dop['sigval']
                    if val > need.get(key, 0):
                        need[key] = val
                for key, val in need.items():
                    if waited.get(key, 0) >= val:
                        continue
                    waited[key] = val
                    s = csems[key[1]] if key[0] == 'c' else sems[key[1]]
                    e.wait_ge(s, val)
                if op['fn'] is None:
                    continue
                ins = op['fn'](e)
                if op['chan'] is not None:
                    ins.then_inc(csems[op['chan']], 16)
                elif op['signal']:
                    ins.then_inc(sems[ename], 1)

        with nc.Block() as block:
            @block.tensor
            def _(e):
                run_engine('pe', e)

            @block.scalar
            def _(e):
                run_engine('act', e)

            @block.vector
            def _(e):
                run_engine('dve', e)

            @block.gpsimd
            def _(e):
                run_engine('pool', e)

            @block.sync
            def _(e):
                run_engine('sp', e)


INPUT_SPECS = [
    ('xT', [1024, 1024], F32), ('cond', [128, 8], F32), ('ctxflag', [128, 1], F32),
    ('w_ada', [NL, 1024, 9216], F32), ('b_adaT', [NL, 128, 72], F32), ('gT', [NL, 128, 3, 8], F32),
    ('w_gate1', [NL, 1024, DFF], F32), ('w_up1', [NL, 1024, DFF], F32), ('w_down1', [NL, DFF, 1024], F32),
    ('w_gate2', [NL, 1024, DFF], F32), ('w_up2', [NL, 1024, DFF], F32), ('w_down2', [NL, DFF, 1024], F32),
    ('w_in', [NL, 1024, 1952], F32), ('w_o', [NL, 1024, 1024], F32),
    ('w_uq', [NL, 256, 384], F32), ('w_uqP', [NL, 256, 384], F32), ('w_inPa', [NL, 1024, 32], F32), ('w_inPb', [NL, 1024, 384], F32),
    ('w_ukv', [NL, 128, 512], F32), ('gvec', [NL, 128, NG], F32),
    ('ckvT_c', [NL, 128, 256], F32), ('krT_c', [NL, 32, 256], F32), ('winkT_c', [NL, 128, 256], F32),
    ('winv_c', [NL, 256, 128], F32), ('nakT_c', [NL, 256, 256], F32), ('nav_c', [NL, 256, 256], F32),
    ('identb', [128, 128], BF16), ('onesb', [128, 128], BF16), ('blk64b', [128, 128], BF16),
    ('ropeB', [128, 2, 1024], F32), ('ropeA', [128, 2, 1024], BF16),
    ('cs64', [128, 256], BF16), ('dftC', [1024, 1024], BF16), ('dftS', [1024, 1024], BF16),
    ('EA', [8, 1280], BF16), ('FA', [8, 1024], BF16), ('maskB', [128, 6, 512], BF16),
    ('biasD', [NL, 4, 1024, 1024], F32),
]
OUTPUT_SPECS = [
    ('yT', [1024, 1024]), ('o_ckvT', [NL, 128, 1024]), ('o_krT', [NL, 32, 1024]), ('o_kbT', [NL, 128, 1024]),
    ('o_vb', [NL, 1024, 128]), ('o_kdT', [NL, 256, 1024]), ('o_vd', [NL, 1024, 256]),
]


class _Stop(Exception):
    pass


def build_program(nl=NL, stop=None):
    nc = bass.Bass("TRN2", target_bir_lowering=False)
    IN = {n: nc.dram_tensor(n, list(s), dt, kind="ExternalInput").ap() for n, s, dt in INPUT_SPECS}
    OUT = {n: nc.dram_tensor(n, list(s), F32, kind="ExternalOutput").ap() for n, s in OUTPUT_SPECS}
    es = ExitStack()
    with es:
        P = Prog(nc, es)

        def sb(name, shape, dt):
            return es.enter_context(nc.sbuf_tensor(name, list(shape), dt))

        xT = sb('xT_sb', [128, 8, 1024], F32)
        hT = sb('hT', [128, 8, 1024], BF16)
        ring = [sb('ring%d' % i, [128, RING_ELEMS], BF16) for i in range(NSLOT)]
        actb = [sb('actb%d' % i, [128, 512], BF16) for i in range(8)]
        rstdN = sb('rstdN', [128, 512], F32)
        tmp = [sb('tmp%d' % i, [128, 512], F32) for i in range(NTMP)]
        sqb = [sb('sqb%d' % i, [128, 512], BF16) for i in range(3)]
        ps = [es.enter_context(nc.psum_tensor('ps%d' % i, [128, 512], F32)) for i in range(8)]
        identb = sb('identb_sb', [128, 128], BF16)
        onesb = sb('onesb_sb', [128, 128], BF16)
        blk64b = sb('blk64b_sb', [128, 128], BF16)
        ropeB = sb('ropeB_sb', [128, 2, 1024], F32)
        ropeA = sb('ropeA_sb', [128, 2, 1024], BF16)
        cs64 = sb('cs64_sb', [128, 256], BF16)
        epsT = sb('epsT', [128, 1], F32)
        gvec = sb('gvec_sb', [128, NL, NG], F32)
        gT = sb('gT_sb', [128, NL, 3, 8], F32)
        badaT = sb('badaT_sb', [128, NL, 72], F32)
        condf = sb('condf', [128, 8], F32)
        condb = sb('condb', [128, 8], BF16)
        ctxf = sb('ctxf', [128, 1], F32)
        modT = [sb('modT%d' % i, [128, 72], F32) for i in range(2)]
        modA = [sb('modA%d' % i, [128, 3, 8], F32) for i in range(2)]
        modG = [sb('modG%d' % i, [128, 3, 8], F32) for i in range(2)]
        sinkexp = sb('sinkexp', [128, 4], F32)
        rec = sb('rec', [128, 4], F32)
        QB = sb('QB', [128, 4, 1024], BF16)
        QD = sb('QD', [128, 4, 1024], BF16)
        QA = sb('QA', [128, 4, 1024], BF16)
        KB = sb('KB', [128, 1280], BF16)
        KD = sb('KD', [128, 2, 1280], BF16)
        KA = sb('KA', [128, 4, 1280], BF16)
        VA = sb('VA', [128, 10, 4, 65], BF16)
        VB = sb('VB', [128, 10, 2, 65], BF16)
        VD = sb('VD', [128, 10, 4, 65], BF16)
        PT = sb('PT', [128, NPT, 512], BF16)
        ABt = PT[:, 0:8, :].rearrange("p l (c x) -> p l c x", c=2)
        XcT = PT[:, 8:12, :].rearrange("p (c t) x -> p c (t x)", c=2)
        mtok = sb('mtok', [128, 2, 4, 128], BF16)
        cqn = sb('cqn', [128, 2, 512], BF16)
        ckvn = sb('ckvn', [128, 1024], BF16)
        sqkr = sb('sqkr', [128, 512], BF16)
        krr = rstdN
        wuq = sb('wuq', [128, 2, 384], BF16)
        wuqP = sb('wuqP', [128, 2, 384], BF16)
        wukv = sb('wukv', [128, 512], BF16)
        ckvc = sb('ckvc', [128, 256], BF16)

        st = dict(ps=0, tmp=0, sq=0, ring=0, pt=0, act=0, ld=0, out=0)
        out_ids = []

        def rot(name, n):
            i = st[name]
            st[name] = (i + 1) % n
            return i

        newps = lambda: rot('ps', 7)
        newtmp = lambda: rot('tmp', NTMP)
        newsq = lambda: rot('sq', 3)

        def MM(out, lhsT, rhs, start, stop, rd, wr):
            P.add('pe', lambda e: e.matmul(out, lhsT=lhsT, rhs=rhs, start=start, stop=stop), rd, wr)

        def ACT(out, in_, func, rd, wr, scale=None, bias=None):
            kw = {}
            if scale is not None:
                kw['scale'] = scale
            if bias is not None:
                kw['bias'] = bias
            P.add('act', lambda e: e.activation(out=out, in_=in_, func=func, **kw), rd, wr)

        def TT(out, in0, in1, op, rd, wr):
            P.add('dve', lambda e: e.tensor_tensor(out=out, in0=in0, in1=in1, op=op), rd, wr)

        def STT(out, in0, scalar, in1, op0, op1, rd, wr):
            P.add('dve', lambda e: e.scalar_tensor_tensor(out=out, in0=in0, scalar=scalar, in1=in1, op0=op0, op1=op1), rd, wr)

        def TS(out, in0, s1, op0, rd, wr):
            P.add('dve', lambda e: e.tensor_scalar(out=out, in0=in0, scalar1=s1, scalar2=None, op0=op0), rd, wr)

        def RECIP(out, in_, rd, wr):
            P.add('dve', lambda e: e.reciprocal(out=out, in_=in_), rd, wr)

        def RECIPF(out, in_, rd, wr):
            P.add('dve', lambda e: e.reciprocal_approx_fast(out=out, in_=in_), rd, wr)

        def DMA(q, out, in_, rd, wr, chan):
            return P.add(q, lambda e: e.dma_start(out=out, in_=in_), rd, wr, chan=chan)

        def LOAD(out, in_, keys, q='sp'):
            return DMA(q, out, in_, [], keys, 'ld%d' % rot('ld', 4))

        def STORE(out, in_, rd):
            out_ids.append(DMA('sp', out, in_, rd, [], 'st%d' % rot('out', 4)))

        def ring_get(src_ap, a, b):
            s = rot('ring', NSLOT)
            view = ring[s][:, 0:a * b].rearrange("p (a b) -> p a b", b=b)
            DMA('pool', view, src_ap, [], [('ring', s)], 'ring%d' % s)
            return view, ('ring', s)

        cur_layer = [0]

        def CK(name):
            if stop == name or stop == '%s@%d' % (name, cur_layer[0]):
                raise _Stop()

        def kcp(ap):
            return ap.rearrange("(kc p) n -> p kc n", p=128)

        C = 'consts'
        cl = []
        for name, tile in [('identb', identb), ('onesb', onesb), ('blk64b', blk64b),
                           ('ropeB', ropeB), ('ropeA', ropeA), ('cs64', cs64),
                           ('cond', condf), ('ctxflag', ctxf)]:
            cl.append(LOAD(tile[:], IN[name], []))
        cl.append(LOAD(gvec[:], IN['gvec'].rearrange("l p g -> p l g"), []))
        cl.append(LOAD(gT[:], IN['gT'].rearrange("l p k c -> p l k c"), []))
        cl.append(LOAD(badaT[:], IN['b_adaT'].rearrange("l p j -> p l j"), []))
        xkeys = [('x', c, t) for c in range(8) for t in range(2)]
        LOAD(xT[:], kcp(IN['xT']), xkeys)
        P.add('dve', lambda e: e.memset(VA[:].rearrange("p a b c -> p (a b c)"), 1.0), [], ['Vinit'])
        P.add('dve', lambda e: e.memset(VB[:].rearrange("p a b c -> p (a b c)"), 1.0), [], ['Vinit'])
        P.add('dve', lambda e: e.memset(VD[:].rearrange("p a b c -> p (a b c)"), 1.0), [], ['Vinit'])
        zi = []
        for tl in (QB, QD, QA, KA):
            zi.append(P.add('dve', lambda e, tl=tl: e.memset(tl[:].rearrange("p a b -> p (a b)"), 0.0), [], []))
        for h in range(4):
            cl.append(P.add('sp', lambda e, h=h: e.dma_start(out=KA[96:104, h, :], in_=IN['EA']), [], [], chan='ld%d' % rot('ld', 4), extra_deps=zi))
            cl.append(P.add('sp', lambda e, h=h: e.dma_start(out=QA[96:104, h, :], in_=IN['FA']), [], [], chan='ld%d' % rot('ld', 4), extra_deps=zi))
        P.add('dve', lambda e: e.memset(epsT[:], EPS), [], [C], extra_deps=cl + zi)
        ACT(condb[:], condf[:], AF.Silu, [C], ['condb'])
        for Vt, nh in ((VA, 4), (VB, 2), (VD, 4)):
            for kt in (8, 9):
                for h in range(nh):
                    ACT(Vt[:, kt, h, 64:65], ctxf[:, 0:1], AF.Copy, ['Vinit', C], ['Vinit'])

        def rstd_from(ps_ap, pskey, inv_d, p0=0, p1=128, n=512):
            ti = newtmp()
            t = tmp[ti][p0:p1, 0:n]
            ACT(t, ps_ap, AF.Ln, [pskey, C], [('tmp', ti)], scale=inv_d, bias=epsT[p0:p1, 0:1])
            ACT(t, t, AF.Exp, [('tmp', ti)], [('tmp', ti)], scale=-0.5)
            return ti

        def mod_steps(l):
            par = l % 2
            for k in range(3):
                for b6 in range(6):
                    blk = k * 6 + b6
                    wt, wk = ring_get(kcp(IN['w_ada'][l][:, blk * 512:(blk + 1) * 512]), 8, 512)
                    for j in range(4):
                        col = blk * 4 + j
                        for kc in range(8):
                            MM(ps[7][:, col:col + 1], wt[:, kc, j * 128:(j + 1) * 128], condb[:, kc:kc + 1], kc == 0, kc == 7,
                               [wk, 'condb'], ['psM'])
                    yield
                c0 = 24 * k
                TT(modT[par][:, c0:c0 + 24], ps[7][:, c0:c0 + 24], badaT[:, l, c0:c0 + 24], ALU.add, ['psM', C], [('mod', par, k)])
                STT(modA[par][:, k, :], modT[par][:, c0 + 8:c0 + 16], 1.0, gT[:, l, k, :], ALU.add, ALU.mult,
                    [('mod', par, k), C], [('modA', par, k)])
                TS(modG[par][:, k, :], modT[par][:, c0 + 16:c0 + 24], (1.0 if k == 1 else 0.5), ALU.mult, [('mod', par, k)], [('modG', par, k)])
            yield

        def norm_mod(l, k):
            par = l % 2
            for t in range(2):
                hs = slice(t * 512, (t + 1) * 512)
                pi = newps()
                for c in range(8):
                    si = newsq()
                    ACT(sqb[si][:, :], xT[:, c, hs], AF.Square, [('x', c, t)], [('sq', si)])
                    MM(ps[pi][:, :], onesb[:, :], sqb[si][:, :], c == 0, c == 7, [('sq', si), C], [('ps', pi)])
                ACT(rstdN[:, :], ps[pi][:, :], AF.Ln, [('ps', pi), C], ['rstdN'], scale=1.0 / 1024, bias=epsT[:, 0:1])
                ACT(rstdN[:, :], rstdN[:, :], AF.Exp, ['rstdN'], ['rstdN'], scale=-0.5)
                for c in range(8):
                    ti = newtmp()
                    STT(tmp[ti][:, :], xT[:, c, hs], modA[par][:, k, c:c + 1], rstdN[:, :], ALU.mult, ALU.mult,
                        [('x', c, t), ('modA', par, k), 'rstdN'], [('tmp', ti)])
                    ACT(hT[:, c, hs], tmp[ti][:, :], AF.Identity, [('tmp', ti), ('mod', par, k)], [('h', c, t)],
                        bias=modT[par][:, 3 * k * 8 + c:3 * k * 8 + c + 1])

        def ffn(l, which, modgen):
            par = l % 2
            wg = IN['w_gate%d' % which][l]
            wu = IN['w_up%d' % which][l]
            wd = IN['w_down%d' % which][l]
            gk = 0 if which == 1 else 2
            for blk in range(6):
                nch = 4 if blk < 5 else 2
                c0 = blk * 512
                ncol = nch * 128
                gw, gkey = ring_get(kcp(wg[:, c0:c0 + ncol]), 8, ncol)
                uw, ukey = ring_get(kcp(wu[:, c0:c0 + ncol]), 8, ncol)
                dw, dkey = ring_get(wd[c0:c0 + ncol, :].rearrange("(j p) n -> p j n", p=128), nch, 1024)
                for t in range(2):
                    hs = slice(t * 512, (t + 1) * 512)
                    aslots = []
                    for j in range(nch):
                        ai = rot('act', 8)
                        aslots.append(ai)
                        pg = newps()
                        for kc in range(8):
                            MM(ps[pg][:, :], gw[:, kc, j * 128:(j + 1) * 128], hT[:, kc, hs], kc == 0, kc == 7,
                               [gkey, ('h', kc, t)], [('ps', pg)])
                        pu = newps()
                        for kc in range(8):
                            MM(ps[pu][:, :], uw[:, kc, j * 128:(j + 1) * 128], hT[:, kc, hs], kc == 0, kc == 7,
                               [ukey, ('h', kc, t)], [('ps', pu)])
                        ti = newtmp()
                        ACT(tmp[ti][:, :], ps[pg][:, :], AF.Silu, [('ps', pg)], [('tmp', ti)])
                        TT(actb[ai][:, :], tmp[ti][:, :], ps[pu][:, :], ALU.mult, [('tmp', ti), ('ps', pu)], [('act', ai)])
                    for dc in range(8):
                        po = newps()
                        for j in range(nch):
                            MM(ps[po][:, :], dw[:, j, dc * 128:(dc + 1) * 128], actb[aslots[j]][:, :], j == 0, j == nch - 1,
                               [dkey, ('act', aslots[j])], [('ps', po)])
                        STT(xT[:, dc, hs], ps[po][:, :], modG[par][:, gk, dc:dc + 1], xT[:, dc, hs], ALU.mult, ALU.add,
                            [('ps', po), ('modG', par, gk), ('x', dc, t)], [('x', dc, t)])
                if modgen is not None:
                    modgen()
                    modgen()

        def rope_norm(psraw, pskey, p0, p1, ones_lhsT, inv_d, gain, out16, outkeys, rope=None, store=None, n=512, split=None, psf=None, gainf=None):
            si = newsq()
            ACT(sqb[si][p0:p1, 0:n], psraw, AF.Square, [pskey], [('sq', si)])
            pss = newps()
            MM(ps[pss][p0:p1, 0:n], ones_lhsT, sqb[si][p0:p1, 0:n], True, True, [('sq', si), C], [('ps', pss)])
            ri = rstd_from(ps[pss][p0:p1, 0:n], ('ps', pss), inv_d, p0, p1, n)
            r = tmp[ri][p0:p1, 0:n]
            if rope is None and store is None and split is not None:
                for (a_, b_, ap_, ks_) in split:
                    STT(ap_, psf(a_, b_), gainf(a_, b_), tmp[ri][a_:b_, 0:n], ALU.mult, ALU.mult, [pskey, ('tmp', ri), C], ks_)
                return
            if rope is None and store is None:
                STT(out16, psraw, gain, r, ALU.mult, ALU.mult, [pskey, ('tmp', ri), C], outkeys)
                return
            if rope is None:
                xi = newtmp()
                xn = tmp[xi][p0:p1, 0:n]
                STT(xn, psraw, gain, r, ALU.mult, ALU.mult, [pskey, ('tmp', ri), C], [('tmp', xi)])
                STORE(store, xn, [('tmp', xi)])
                ACT(out16, xn, AF.Copy, [('tmp', xi)], outkeys)
                return
            psraw_p, pskey_p, gain_p, ropeC, ropeS = rope
            t1 = newtmp()
            STT(tmp[t1][p0:p1, 0:n], psraw, gain, ropeC, ALU.mult, ALU.mult, [pskey, C], [('tmp', t1)])
            t2 = newtmp()
            STT(tmp[t2][p0:p1, 0:n], psraw_p, gain_p, ropeS, ALU.mult, ALU.mult, [pskey_p, C], [('tmp', t2)])
            TT(tmp[t1][p0:p1, 0:n], tmp[t1][p0:p1, 0:n], tmp[t2][p0:p1, 0:n], ALU.add, [('tmp', t1), ('tmp', t2)], [('tmp', t1)])
            if split is not None:
                for (a_, b_, ap_, ks_) in split:
                    TT(ap_, tmp[t1][a_:b_, 0:n], tmp[ri][a_:b_, 0:n], ALU.mult, [('tmp', t1), ('tmp', ri)], ks_)
            elif store is None:
                TT(out16, tmp[t1][p0:p1, 0:n], r, ALU.mult, [('tmp', t1), ('tmp', ri)], outkeys)
            else:
                TT(tmp[t1][p0:p1, 0:n], tmp[t1][p0:p1, 0:n], r, ALU.mult, [('tmp', t1), ('tmp', ri)], [('tmp', t1)])
                STORE(store, tmp[t1][p0:p1, 0:n], [('tmp', t1)])
                ACT(out16, tmp[t1][p0:p1, 0:n], AF.Copy, [('tmp', t1)], outkeys)

        def mixer(l, modgen):
            par = l % 2

            def gv(col, p0=0, p1=128):
                return gvec[p0:p1, l, col:col + 1]

            def step():
                if modgen is not None:
                    modgen()

            DMA('pool', wuq[:], IN['w_uq'][l].rearrange("(c p) n -> p c n", p=128), [], ['wuq'], 'cx0')
            DMA('pool', wuqP[:], IN['w_uqP'][l].rearrange("(c p) n -> p c n", p=128), [], ['wuqP'], 'cx0')
            DMA('pool', wukv[:], IN['w_ukv'][l], [], ['wukv'], 'cx1')
            DMA('pool', ckvc[:], IN['ckvT_c'][l], [], ['ckvc'], 'cx2')
            DMA('pool', KB[:, 1024:1280], IN['winkT_c'][l], [], [('KB', 'c')], 'cx4')
            DMA('pool', KD[:, :, 1024:1280], IN['nakT_c'][l].rearrange("(c p) k -> p c k", p=128), [], [('KD', 0, 'c'), ('KD', 1, 'c')], 'cx5')
            for i in range(2):
                DMA('pool', VB[:, 8 + i, :, 0:64], IN['winv_c'][l][i * 128:(i + 1) * 128, :].rearrange("p (h d) -> p h d", d=64),
                    [], [('VB', 8 + i)], 'cx6')
                DMA('pool', VD[:, 8 + i, :, 0:64], IN['nav_c'][l][i * 128:(i + 1) * 128, :].rearrange("p (h d) -> p h d", d=64),
                    [], [('VD', 8 + i)], 'cx7')
            ACT(sinkexp[:, :], gvec[:, l, 9:13], AF.Exp, [C], ['sinkexp'])
            wukv_v = wukv[:, 0:512].rearrange("p (h x) -> p h x", x=128)[:, :, 64:128]

            b0, b0k = ring_get(kcp(IN['w_in'][l][:, 0:416]), 8, 416)
            bpa, bpak = ring_get(kcp(IN['w_inPa'][l]), 8, 32)
            for t in range(2):
                hs = slice(t * 512, (t + 1) * 512)
                pcs = []
                for c in range(2):
                    pi = newps()
                    pcs.append(pi)
                    for kc in range(8):
                        MM(ps[pi][:, :], b0[:, kc, c * 128:(c + 1) * 128], hT[:, kc, hs], kc == 0, kc == 7, [b0k, ('h', kc, t)], [('ps', pi)])
                pss = newps()
                for c in range(2):
                    si = newsq()
                    ACT(sqb[si][:, :], ps[pcs[c]][:, :], AF.Square, [('ps', pcs[c])], [('sq', si)])
                    MM(ps[pss][:, :], onesb[:, :], sqb[si][:, :], c == 0, c == 1, [('sq', si), C], [('ps', pss)])
                ri = rstd_from(ps[pss][:, :], ('ps', pss), 1.0 / 256)
                for c in range(2):
                    STT(cqn[:, c, :], ps[pcs[c]][:, :], gv(c), tmp[ri][:, :], ALU.mult, ALU.mult,
                        [('ps', pcs[c]), ('tmp', ri), C], [('cqn', c)])
                pk = newps()
                for kc in range(8):
                    MM(ps[pk][:, :], b0[:, kc, 256:384], hT[:, kc, hs], kc == 0, kc == 7, [b0k, ('h', kc, t)], [('ps', pk)])
                rope_norm(ps[pk][:, :], ('ps', pk), 0, 128, onesb[:, :], 1.0 / 128, gv(2), ckvn[:, hs], [('ckvn', t)],
                          rope=None, store=OUT['o_ckvT'][l][:, hs])
                pr = newps()
                for kc in range(8):
                    MM(ps[pr][64:96, :], b0[:, kc, 384:416], hT[:, kc, hs], kc == 0, kc == 7, [b0k, ('h', kc, t)], [('ps', pr)])
                ti = newtmp()
                ACT(tmp[ti][64:96, :], ps[pr][64:96, :], AF.Copy, [('ps', pr)], [('tmp', ti)])
                STORE(OUT['o_krT'][l][:, hs], tmp[ti][64:96, :], [('tmp', ti)])
                ACT(sqkr[64:96, :], ps[pr][64:96, :], AF.Square, [('ps', pr)], ['sqkr'])
                prp = newps()
                for kc in range(8):
                    MM(ps[prp][64:96, :], bpa[:, kc, 0:32], hT[:, kc, hs], kc == 0, kc == 7, [bpak, ('h', kc, t)], [('ps', prp)])
                t1 = newtmp()
                STT(tmp[t1][64:96, :], ps[prp][64:96, :], gv(14, 64, 96), ropeA[64:96, 1, hs], ALU.mult, ALU.mult, [('ps', prp), C], [('tmp', t1)])
                STT(krr[64:96, :], ps[pr][64:96, :], gv(4, 64, 96), ropeA[64:96, 0, hs], ALU.mult, ALU.mult, [('ps', pr), C], ['rstdN'])
                TT(krr[64:96, :], krr[64:96, :], tmp[t1][64:96, :], ALU.add, ['rstdN', ('tmp', t1)], ['rstdN'])
                for h in range(4):
                    pn = newps()
                    MM(ps[pn][0:64, :], wukv[:, h * 128:h * 128 + 64], ckvn[:, hs], True, True, ['wukv', ('ckvn', t)], [('ps', pn)])
                    si = newsq()
                    ACT(sqb[si][0:64, :], ps[pn][0:64, :], AF.Square, [('ps', pn)], [('sq', si)])
                    pss = newps()
                    MM(ps[pss][0:96, :], onesb[0:64, 0:96], sqb[si][0:64, :], True, False, [('sq', si), C], [('ps', pss)])
                    MM(ps[pss][0:96, :], onesb[64:96, 0:96], sqkr[64:96, :], False, True, ['sqkr', C], [('ps', pss)])
                    ri = rstd_from(ps[pss][0:96, :], ('ps', pss), 1.0 / 96, 0, 96)
                    STT(KA[0:64, h, hs], ps[pn][0:64, :], gv(4, 0, 64), tmp[ri][0:64, :], ALU.mult, ALU.mult,
                        [('ps', pn), ('tmp', ri), C], [('KA', h, t)])
                    TT(KA[64:96, h, hs], krr[64:96, :], tmp[ri][64:96, :], ALU.mult, ['rstdN', ('tmp', ri)], [('KA', h, t)])
                    pq = newps()
                    for c in range(2):
                        MM(ps[pq][0:96, :], wuq[:, c, h * 96:(h + 1) * 96], cqn[:, c, :], c == 0, c == 1, ['wuq', ('cqn', c)], [('ps', pq)])
                    pqp = newps()
                    for c in range(2):
                        MM(ps[pqp][0:96, :], wuqP[:, c, h * 96:(h + 1) * 96], cqn[:, c, :], c == 0, c == 1, ['wuqP', ('cqn', c)], [('ps', pqp)])
                    rope_norm(ps[pq][0:96, :], ('ps', pq), 0, 96, onesb[0:96, 0:96], 1.0 / 96, gv(3, 0, 96), QA[0:96, h, hs],
                              [('QA', h, t)], rope=(ps[pqp][0:96, :], ('ps', pqp), gv(13, 0, 96), ropeA[0:96, 0, hs], ropeA[0:96, 1, hs]))
                for tt in range(4):
                    kt = t * 4 + tt
                    pv = newps()
                    MM(ps[pv][:, 0:256], ckvn[:, kt * 128:(kt + 1) * 128], wukv_v, True, True, ['wukv', ('ckvn', t)], [('ps', pv)])
                    ACT(VA[:, kt, :, 0:64], ps[pv][:, 0:256].rearrange("p (h d) -> p h d", d=64), AF.Copy, [('ps', pv), 'Vinit'], [('VA', kt)])
            step()
            CK('projA')
            tkc = newtmp()
            krc_ap = tmp[tkc][64:96, 0:256]
            DMA('sp', krc_ap, IN['krT_c'][l], [], [('tmp', tkc)], 'cx3')
            si = newsq()
            ACT(sqb[si][64:96, 0:256], krc_ap, AF.Square, [('tmp', tkc)], [('sq', si)])
            sq_krc = si
            tgc = newtmp()
            TS(tmp[tgc][64:96, 0:256], krc_ap, gv(4, 64, 96), ALU.mult, [('tmp', tkc), C], [('tmp', tgc)])
            for h in range(4):
                pn = newps()
                MM(ps[pn][0:64, 0:256], wukv[:, h * 128:h * 128 + 64], ckvc[:, :], True, True, ['wukv', 'ckvc'], [('ps', pn)])
                si = newsq()
                if si == sq_krc:
                    si = newsq()
                ACT(sqb[si][0:64, 0:256], ps[pn][0:64, 0:256], AF.Square, [('ps', pn)], [('sq', si)])
                pss = newps()
                MM(ps[pss][0:96, 0:256], onesb[0:64, 0:96], sqb[si][0:64, 0:256], True, False, [('sq', si), C], [('ps', pss)])
                MM(ps[pss][0:96, 0:256], onesb[64:96, 0:96], sqb[sq_krc][64:96, 0:256], False, True, [('sq', sq_krc), C], [('ps', pss)])
                ri = rstd_from(ps[pss][0:96, 0:256], ('ps', pss), 1.0 / 96, 0, 96, 256)
                if ri == tgc:
                    raise RuntimeError("tmp rotation clash")
                STT(KA[0:64, h, 1024:1280], ps[pn][0:64, 0:256], gv(4, 0, 64), tmp[ri][0:64, 0:256], ALU.mult, ALU.mult,
                    [('ps', pn), ('tmp', ri), C], [('KA', h, 'c')])
                TT(KA[64:96, h, 1024:1280], tmp[tgc][64:96, 0:256], tmp[ri][64:96, 0:256], ALU.mult, [('tmp', tgc), ('tmp', ri)], [('KA', h, 'c')])
            for i in range(2):
                pv = newps()
                MM(ps[pv][:, 0:256], ckvc[:, i * 128:(i + 1) * 128], wukv_v, True, True, ['wukv', 'ckvc'], [('ps', pv)])
                ACT(VA[:, 8 + i, :, 0:64], ps[pv][:, 0:256].rearrange("p (h d) -> p h d", d=64), AF.Copy, [('ps', pv), 'Vinit'], [('VA', 8 + i)])
            step()
            CK('ctxA')

            b1, b1k = ring_get(kcp(IN['w_in'][l][:, 416:928]), 8, 512)
            b1p, b1pk = ring_get(kcp(IN['w_inPb'][l]), 8, 384)
            for t in range(2):
                hs = slice(t * 512, (t + 1) * 512)
                for ci, hpair in enumerate(((0, 2), (1, 3))):
                    pi = newps()
                    for hh, pb in zip(hpair, (0, 64)):
                        for kc in range(8):
                            MM(ps[pi][pb:pb + 64, :], b1[:, kc, hh * 64:(hh + 1) * 64], hT[:, kc, hs], kc == 0, kc == 7,
                               [b1k, ('h', kc, t)], [('ps', pi)])
                    pip = newps()
                    for hh, pb in zip(hpair, (0, 64)):
                        for kc in range(8):
                            MM(ps[pip][pb:pb + 64, :], b1p[:, kc, hh * 64:(hh + 1) * 64], hT[:, kc, hs], kc == 0, kc == 7,
                               [b1pk, ('h', kc, t)], [('ps', pip)])
                    rope_norm(ps[pi][:, :], ('ps', pi), 0, 128, blk64b[:, :], 1.0 / 64, gv(5), None, None,
                              rope=(ps[pip][:, :], ('ps', pip), gv(15), ropeB[:, 0, hs], ropeB[:, 1, hs]),
                              split=[(0, 64, QB[0:64, hpair[0], hs], [('QB', hpair[0], t)]),
                                     (64, 128, QB[64:128, hpair[1], hs], [('QB', hpair[1], t)])])
                    CK('qb')
                pi = newps()
                for kc in range(8):
                    MM(ps[pi][:, :], b1[:, kc, 256:384], hT[:, kc, hs], kc == 0, kc == 7, [b1k, ('h', kc, t)], [('ps', pi)])
                pip = newps()
                for kc in range(8):
                    MM(ps[pip][:, :], b1p[:, kc, 256:384], hT[:, kc, hs], kc == 0, kc == 7, [b1pk, ('h', kc, t)], [('ps', pip)])
                rope_norm(ps[pi][:, :], ('ps', pi), 0, 128, blk64b[:, :], 1.0 / 64, gv(6), KB[:, hs], [('KB', t)],
                          rope=(ps[pip][:, :], ('ps', pip), gv(16), ropeB[:, 0, hs], ropeB[:, 1, hs]), store=OUT['o_kbT'][l][:, hs])
                CK('kb')
                for tt in range(4):
                    kt = t * 4 + tt
                    pv = newps()
                    for kc in range(8):
                        MM(ps[pv][:, 0:128], hT[:, kc, kt * 128:(kt + 1) * 128], b1[:, kc, 384:512], kc == 0, kc == 7,
                           [b1k, ('h', kc, t)], [('ps', pv)])
                    ti = newtmp()
                    ACT(tmp[ti][:, 0:128], ps[pv][:, 0:128], AF.Copy, [('ps', pv)], [('tmp', ti)])
                    STORE(OUT['o_vb'][l][kt * 128:(kt + 1) * 128, :], tmp[ti][:, 0:128], [('tmp', ti)])
                    ACT(VB[:, kt, :, 0:64], ps[pv][:, 0:128].rearrange("p (h d) -> p h d", d=64), AF.Copy, [('ps', pv), 'Vinit'], [('VB', kt)])
            step()
            CK('projB')
            b2, b2k = ring_get(kcp(IN['w_in'][l][:, 928:1440]), 8, 512)
            for t in range(2):
                hs = slice(t * 512, (t + 1) * 512)
                for ch in range(2):
                    pi = newps()
                    for kc in range(8):
                        MM(ps[pi][:, :], b2[:, kc, ch * 128:(ch + 1) * 128], hT[:, kc, hs], kc == 0, kc == 7, [b2k, ('h', kc, t)], [('ps', pi)])
                    ACT(XcT[:, ch, hs], ps[pi][:, :], AF.Copy, [('ps', pi)], [('pt', 8 + 2 * ch + t)])
                for ch in range(2):
                    pi = newps()
                    for kc in range(8):
                        MM(ps[pi][:, :], b2[:, kc, 256 + ch * 128:256 + (ch + 1) * 128], hT[:, kc, hs], kc == 0, kc == 7,
                           [b2k, ('h', kc, t)], [('ps', pi)])
                    rope_norm(ps[pi][:, :], ('ps', pi), 0, 128, blk64b[:, :], 1.0 / 64, gv(7), None, None,
                              split=[(0, 64, QD[0:64, 2 * ch, hs], [('QD', 2 * ch, t)]),
                                     (64, 128, QD[64:128, 2 * ch + 1, hs], [('QD', 2 * ch + 1, t)])],
                              psf=lambda a_, b_, pi=pi: ps[pi][a_:b_, :], gainf=lambda a_, b_: gv(7, a_, b_))
            step()
            b3, b3k = ring_get(kcp(IN['w_in'][l][:, 1440:1952]), 8, 512)
            for t in range(2):
                hs = slice(t * 512, (t + 1) * 512)
                for ch in range(2):
                    pi = newps()
                    for kc in range(8):
                        MM(ps[pi][:, :], b3[:, kc, ch * 128:(ch + 1) * 128], hT[:, kc, hs], kc == 0, kc == 7, [b3k, ('h', kc, t)], [('ps', pi)])
                    rope_norm(ps[pi][:, :], ('ps', pi), 0, 128, blk64b[:, :], 1.0 / 64, gv(8), KD[:, ch, hs], [('KD', ch, t)],
                              rope=None, store=OUT['o_kdT'][l][ch * 128:(ch + 1) * 128, hs])
                for tt in range(4):
                    kt = t * 4 + tt
                    pv = newps()
                    for kc in range(8):
                        MM(ps[pv][:, 0:256], hT[:, kc, kt * 128:(kt + 1) * 128], b3[:, kc, 256:512], kc == 0, kc == 7,
                           [b3k, ('h', kc, t)], [('ps', pv)])
                    ti = newtmp()
                    ACT(tmp[ti][:, 0:256], ps[pv][:, 0:256], AF.Copy, [('ps', pv)], [('tmp', ti)])
                    STORE(OUT['o_vd'][l][kt * 128:(kt + 1) * 128, :], tmp[ti][:, 0:256], [('tmp', ti)])
                    ACT(VD[:, kt, :, 0:64], ps[pv][:, 0:256].rearrange("p (h d) -> p h d", d=64), AF.Copy, [('ps', pv), 'Vinit'], [('VD', kt)])
            step()
            CK('projD')
            for lt in range(8):
                for ch in range(2):
                    pi = newps()
                    MM(ps[pi][:, 0:256], XcT[:, ch, lt * 128:(lt + 1) * 128], cs64[:, :], True, True, [('pt', 8 + 2 * ch + lt // 4), C], [('ps', pi)])
                    P.add('dve', lambda e, lt=lt, ch=ch, pi=pi: e.tensor_copy(out=ABt[:, lt, ch, :], in_=ps[pi][:, 0:256]),
                          [('ps', pi)], [('ab', lt, ch)])
            for t in range(2):
                hs = slice(t * 512, (t + 1) * 512)
                cb, cbk = ring_get(IN['dftC'][:, hs].rearrange("(lt p) n -> p lt n", p=128), 8, 512)
                sbk_, sbkk = ring_get(IN['dftS'][:, hs].rearrange("(lt p) n -> p lt n", p=128), 8, 512)
                for ch in range(2):
                    pi = newps()
                    for lt in range(8):
                        MM(ps[pi][:, :], ABt[:, lt, ch, 0:128], cb[:, lt, :], lt == 0, False, [('ab', lt, ch), ('pt', lt), cbk], [('ps', pi)])
                    for lt in range(8):
                        MM(ps[pi][:, :], ABt[:, lt, ch, 128:256], sbk_[:, lt, :], False, lt == 7, [('ab', lt, ch), ('pt', lt), sbkk], [('ps', pi)])
                    ACT(hT[:, 4 + ch, hs], ps[pi][:, :], AF.Copy, [('ps', pi)], [('h', 4 + ch, t)])
            step()
            CK('fourier')

            shared = {}

            def unit_kts(kind, t):
                if kind == 'B':
                    return [kt for kt in range(4 * t - 1, 4 * t + 5) if 0 <= kt <= 7] + [8, 9]
                if kind == 'D':
                    return list(range(0, 6) if t == 0 else range(2, 8)) + [8, 9]
                return list(range(10))

            def attend_qk(kind, h, t):
                hs = slice(t * 512, (t + 1) * 512)
                kts = unit_kts(kind, t)
                if kind == 'B':
                    if 'maskB' not in shared:
                        shared['maskB'] = ring_get(IN['maskB'], 6, 512)
                elif kind == 'D':
                    if ('biasD', h) not in shared:
                        shared[('biasD', h)] = [ring_get(IN['biasD'][l][h][g * 512:(g + 1) * 512, :].rearrange("(kt p) q -> p kt q", p=128), 4, 1024)
                                                for g in range(2)]
                slots = {}
                for kt in kts:
                    pi = newps()
                    ksl = slice(kt * 128, (kt + 1) * 128)
                    own = kt < 8
                    kk = (kt // 4) if own else 'c'
                    if kind == 'A':
                        MM(ps[pi][:, :], KA[:, h, ksl], QA[:, h, hs], True, True, [('KA', h, kk), ('QA', h, t), C], [('ps', pi)])
                        scale = 96.0 ** -0.5
                    elif kind == 'B':
                        MM(ps[pi][:, :], KB[:, ksl], QB[:, h, hs], True, not own, [('KB', kk), ('QB', h, t), C], [('ps', pi)])
                        if own:
                            mv, mk = shared['maskB']
                            MM(ps[pi][:, :], identb[:, :], mv[:, kt - 4 * t + 1, :], False, True, [C, mk], [('ps', pi)])
                        scale = 0.125
                    else:
                        ch = h // 2
                        MM(ps[pi][:, :], KD[:, ch, ksl], QD[:, h, hs], True, not own, [('KD', ch, kk), ('QD', h, t), C], [('ps', pi)])
                        if own:
                            bv, bk = shared[('biasD', h)][kt // 4]
                            MM(ps[pi][:, :], identb[:, :], bv[:, kt % 4, hs], False, True, [C, bk], [('ps', pi)])
                        scale = 0.125
                    s = rot('pt', NPT)
                    slots[kt] = s
                    ACT(PT[:, s, :], ps[pi][:, :], AF.Exp, [('ps', pi)], [('pt', s)], scale=scale)
                return (kind, h, t, kts, slots)

            def attend_pv(ctx):
                kind, h, t, kts, slots = ctx
                V, vh, vname = {'A': (VA, h, 'VA'), 'B': (VB, h // 2, 'VB'), 'D': (VD, h, 'VD')}[kind]
                po = newps()
                for qi in range(4):
                    for i, kt in enumerate(kts):
                        MM(ps[po][:, qi * 65:(qi + 1) * 65], PT[:, slots[kt], qi * 128:(qi + 1) * 128], V[:, kt, vh, 0:65],
                           i == 0, i == len(kts) - 1, [('pt', slots[kt]), (vname, kt), 'Vinit'], [('ps', po)])
                for qi in range(4):
                    den = ps[po][:, qi * 65 + 64:qi * 65 + 65]
                    if kind == 'B':
                        TS(rec[:, qi:qi + 1], den, sinkexp[:, h:h + 1], ALU.add, [('ps', po), 'sinkexp'], [('rec', qi)])
                        RECIP(rec[:, qi:qi + 1], rec[:, qi:qi + 1], [('rec', qi)], [('rec', qi)])
                    else:
                        RECIP(rec[:, qi:qi + 1], den, [('ps', po)], [('rec', qi)])
                    hslot = h % 2
                    TS(mtok[:, t, qi, hslot * 64:(hslot + 1) * 64], ps[po][:, qi * 65:qi * 65 + 64], rec[:, qi:qi + 1], ALU.mult,
                       [('ps', po), ('rec', qi)], [('mtok', t, qi, hslot)])

            def attend_tr(chunk):
                for t in range(2):
                    hs = slice(t * 512, (t + 1) * 512)
                    pt_ = newps()
                    for qi in range(4):
                        MM(ps[pt_][:, qi * 128:(qi + 1) * 128], mtok[:, t, qi, :], identb[:, :], True, True,
                           [('mtok', t, qi, 0), ('mtok', t, qi, 1), C], [('ps', pt_)])
                    P.add('dve', lambda e, hs=hs, pt_=pt_, cc=chunk: e.tensor_copy(out=hT[:, cc, hs], in_=ps[pt_][:, :]),
                          [('ps', pt_)], [('h', chunk, t)])

            units = []
            for kind, chunk0 in (('A', 0), ('B', 2), ('D', 6)):
                for pair in range(2):
                    for h in (2 * pair, 2 * pair + 1):
                        for t in range(2):
                            units.append((kind, h, t, chunk0 + pair, (h % 2 == 1 and t == 1)))
            prev = None

            def flush(pv_):
                attend_pv(pv_[0])
                if pv_[2]:
                    attend_tr(pv_[1])
                    step()

            for (kind, h, t, chunk, last) in units:
                if prev is not None and len(prev[0][3]) + len(unit_kts(kind, t)) > NPT:
                    flush(prev)
                    prev = None
                ctx = attend_qk(kind, h, t)
                if prev is not None:
                    flush(prev)
                prev = (ctx, chunk, last)
            flush(prev)
            CK('attn')

            for blk in range(2):
                wo, wok = ring_get(kcp(IN['w_o'][l][:, blk * 512:(blk + 1) * 512]), 8, 512)
                for dcc in range(4):
                    dc = blk * 4 + dcc
                    for t in range(2):
                        hs = slice(t * 512, (t + 1) * 512)
                        po = newps()
                        for c in range(8):
                            MM(ps[po][:, :], wo[:, c, dcc * 128:(dcc + 1) * 128], hT[:, c, hs], c == 0, c == 7, [wok, ('h', c, t)], [('ps', po)])
                        STT(xT[:, dc, hs], ps[po][:, :], modG[par][:, 1, dc:dc + 1], xT[:, dc, hs], ALU.mult, ALU.add,
                            [('ps', po), ('modG', par, 1), ('x', dc, t)], [('x', dc, t)])
            step()

        try:
            CK('load')
            gens = [mod_steps(0)]

            def mg():
                while gens:
                    try:
                        next(gens[0])
                        return
                    except StopIteration:
                        gens.pop(0)

            def drain(n_keep):
                while len(gens) > n_keep:
                    mg()

            for _ in range(7):
                mg()
            CK('mod')
            for l in range(nl):
                cur_layer[0] = l
                if l + 1 < nl:
                    gens.append(mod_steps(l + 1))
                norm_mod(l, 0)
                CK('norm0')
                ffn(l, 1, mg)
                CK('ffn1')
                if l == 0:
                    drain(1 if nl > 1 else 0)
                norm_mod(l, 1)
                mixer(l, mg)
                CK('mixer')
                norm_mod(l, 2)
                ffn(l, 2, mg)
                drain(0)
                CK('layer')
        except _Stop:
            pass
        STORE(kcp(OUT['yT']), xT[:], xkeys)
        P.add('sp', None, extra_deps=out_ids)
        P.emit()
        nc._prog_stats = (P.n_ops, P.sig_counts, P.chan_counts)
    return nc


def _rope_tables(r, sample):
    Cm = np.ones((r, T), np.float32)
    Sm = np.zeros((r, T), np.float32)
    if not sample:
        return Cm, Sm
    half = r // 2
    t = np.arange(T)
    inv = (10000.0 ** (-np.arange(0, half, 2, dtype=np.float32) / np.float32(half))).astype(np.float32)
    q = half // 2
    for part, pos in ((0, t // 64), (1, t % 64)):
        ang = pos.astype(np.float32)[:, None] * inv[None, :]
        cos = np.cos(ang).astype(np.float32).T
        sin = np.sin(ang).astype(np.float32).T
        b = part * half
        Cm[b:b + q] = cos
        Cm[b + q:b + half] = cos
        Sm[b:b + q] = -sin
        Sm[b + q:b + half] = sin
    return Cm, Sm


def _partner(r):
    half = r // 2
    q = half // 2
    return np.array([d + q if (d % half) < q else d - q for d in range(r)])


def _consts(sample):
    c = {}
    c['identb'] = np.eye(128, dtype=np.float32).astype(bf16)
    c['onesb'] = np.ones((128, 128), np.float32).astype(bf16)
    c['blk64b'] = np.kron(np.eye(2, dtype=np.float32), np.ones((64, 64), np.float32)).astype(bf16)
    Cb, Sb = _rope_tables(64, sample)
    rb = np.zeros((128, 2, T), np.float32)
    rb[0:64, 0], rb[64:128, 0] = Cb, Cb
    rb[0:64, 1], rb[64:128, 1] = Sb, Sb
    c['ropeB'] = rb
    Ca, Sa = _rope_tables(32, sample)
    ra = np.zeros((128, 2, T), np.float32)
    ra[:, 0] = 1.0
    ra[64:96, 0] = Ca
    ra[64:96, 1] = Sa
    c['ropeA'] = ra.astype(bf16)
    k = np.arange(64)
    ang = 2 * np.pi * np.outer(k, k) / 64.0
    C64 = np.cos(ang) / 8.0
    S64 = np.sin(ang) / 8.0
    cs = np.zeros((128, 256), np.float64)
    cs[0:64, 0:64] = C64
    cs[64:128, 64:128] = C64
    cs[0:64, 128:192] = S64
    cs[64:128, 192:256] = S64
    c['cs64'] = cs.astype(np.float32).astype(bf16)
    L = 1024 if sample else 256
    kk = np.arange(L)
    angL = 2 * np.pi * (np.outer(kk, kk) % L) / float(L)
    CL = np.cos(angL) / np.sqrt(L)
    SL = -np.sin(angL) / np.sqrt(L)
    if sample:
        dC, dS = CL, SL
    else:
        dC = np.kron(np.eye(4), CL)
        dS = np.kron(np.eye(4), SL)
    c['dftC'] = dC.astype(np.float32).astype(bf16)
    c['dftS'] = dS.astype(np.float32).astype(bf16)
    ea = np.zeros((8, 1280), np.float32)
    fa = np.zeros((8, 1024), np.float32)
    if not sample:
        for s in range(4):
            ea[s, s * 256:(s + 1) * 256] = 1.0
            fa[s, s * 256:(s + 1) * 256] = BIG
        ea[4, 0:1024] = 1.0
        fa[4, :] = -BIG
    c['EA'] = ea.astype(bf16)
    c['FA'] = fa.astype(bf16)
    mb = np.full((128, 6, 512), NEG, np.float32)
    kl = np.arange(128)[:, None]
    ql = np.arange(128)[None, :]
    for ki in range(6):
        ktp = ki - 1
        for qi in range(4):
            off = ktp - qi
            blkm = np.full((128, 128), NEG, np.float32)
            if sample:
                if off == 0:
                    blkm[:] = 0.0
                elif off == -1:
                    blkm = np.where(ql <= kl, 0.0, NEG).astype(np.float32)
                elif off == 1:
                    blkm = np.where(kl <= ql, 0.0, NEG).astype(np.float32)
            else:
                if off == 0 or (off == 1 and qi % 2 == 0) or (off == -1 and qi % 2 == 1):
                    blkm[:] = 0.0
            mb[:, ki, qi * 128:(qi + 1) * 128] = blkm
    c['maskB'] = mb.astype(bf16)
    return c


def _na_index():
    rows = 16
    r = np.arange(rows)
    row_start = np.clip(r - 4, 0, rows - 8)
    col = np.arange(64)
    col_start = np.clip(col - 8, 0, 64 - 16)
    q = np.arange(1024)
    qr, qc = q // 64, q % 64
    kr, kc = qr, qc
    KR, QR = kr[:, None], qr[None, :]
    KC, QC = kc[:, None], qc[None, :]
    valid = (KR >= row_start[QR]) & (KR < row_start[QR] + 8) & (KC >= col_start[QC]) & (KC < col_start[QC] + 16)
    dr = np.clip(KR - QR + 7, 0, 14)
    dc = np.clip(KC - QC + 15, 0, 30)
    return valid, dr, dc


def _vecT(v):
    return np.ascontiguousarray(v.reshape(-1, 128).T)


_PROG = {}


def _prep(x_prompt, x_sample, cache_mla_ckv, cache_mla_krope, cache_win_k, cache_win_v, cache_na_k, cache_na_v,
           c, c_ctx, w_ada, b_ada, g_ffn1, w_gate1, w_up1, w_down1, g_mix, w_in, g_qa, w_uq, g_kva, w_ukv,
           qn_a, kn_a, qn_b, kn_b, sink_b, qn_d, kn_d, rpb_d, w_o, g_ffn2, w_gate2, w_up2, w_down2):
    f = lambda a: np.ascontiguousarray(np.asarray(a, dtype=np.float32))
    x_prompt, x_sample, c, c_ctx = f(x_prompt), f(x_sample), f(c), f(c_ctx)
    shared = {n: f(v) for n, v in dict(w_ada=w_ada, w_gate1=w_gate1, w_up1=w_up1, w_down1=w_down1, w_gate2=w_gate2,
                                       w_up2=w_up2, w_down2=w_down2, w_in=w_in, w_o=w_o, w_uq=w_uq, w_ukv=w_ukv).items()}
    b_ada = f(b_ada)
    shared['b_adaT'] = np.ascontiguousarray(b_ada.reshape(NL, 72, 128).transpose(0, 2, 1))
    gs = np.stack([f(g_ffn1), f(g_mix), f(g_ffn2)], axis=1)
    shared['gT'] = np.ascontiguousarray(gs.reshape(NL, 3, 8, 128).transpose(0, 3, 1, 2))
    gv = np.zeros((NL, 128, NG), np.float32)
    gv[:, :, 0:2] = f(g_qa).reshape(NL, 2, 128).transpose(0, 2, 1)
    gv[:, :, 2] = f(g_kva)
    gv[:, 0:96, 3] = f(qn_a)
    gv[:, 0:96, 4] = f(kn_a)
    gv[:, :, 5] = np.tile(f(qn_b), (1, 2))
    gv[:, :, 6] = np.tile(f(kn_b), (1, 2))
    gv[:, :, 7] = np.tile(f(qn_d), (1, 2))
    gv[:, :, 8] = np.tile(f(kn_d), (1, 2))
    gv[:, :, 9:13] = f(sink_b)[:, None, :]
    p64, p32 = _partner(64), _partner(32)
    ia = np.concatenate([np.arange(64), 64 + p32])
    gv[:, 0:96, 13] = f(qn_a)[:, ia]
    gv[:, 0:96, 14] = f(kn_a)[:, ia]
    gv[:, :, 15] = np.tile(f(qn_b)[:, p64], (1, 2))
    gv[:, :, 16] = np.tile(f(kn_b)[:, p64], (1, 2))
    w_in_f, w_uq_f = shared['w_in'], shared['w_uq']
    shared['w_inPa'] = np.ascontiguousarray(w_in_f[:, :, 384 + p32])
    qb_cols = np.concatenate([416 + h * 64 + p64 for h in range(4)])
    kb_cols = np.concatenate([672 + h * 64 + p64 for h in range(2)])
    shared['w_inPb'] = np.ascontiguousarray(w_in_f[:, :, np.concatenate([qb_cols, kb_cols])])
    shared['w_uqP'] = np.ascontiguousarray(w_uq_f[:, :, np.concatenate([h * 96 + ia for h in range(4)])])
    shared['gvec'] = gv
    consts = {True: _consts(True), False: _consts(False)}
    valid, dr, dc = _na_index()
    rpb = f(rpb_d)
    bias_s = np.where(valid[None, None], rpb[:, :, dr, dc], np.float32(NEG)).astype(np.float32)
    seq = np.arange(1024) // 256
    bias_p1 = np.where(seq[:, None] == seq[None, :], np.float32(0.0), np.float32(NEG)).astype(np.float32)
    bias_p = np.ascontiguousarray(np.broadcast_to(bias_p1, (NL, 4, 1024, 1024)))
    cm_ckv, cm_kr = f(cache_mla_ckv), f(cache_mla_krope)
    cw_k, cw_v, cn_k, cn_v = f(cache_win_k), f(cache_win_v), f(cache_na_k), f(cache_na_v)
    in_maps = []
    for core in range(8):
        sample = core >= 4
        m = dict(shared)
        m.update(consts[sample])
        if sample:
            b = core - 4
            xs = x_sample[b]
            cond = c[b]
            m['ckvT_c'] = np.ascontiguousarray(cm_ckv[b].transpose(0, 2, 1))
            m['krT_c'] = np.ascontiguousarray(cm_kr[b].transpose(0, 2, 1))
            m['winkT_c'] = np.ascontiguousarray(cw_k[b].reshape(NL, 256, 128).transpose(0, 2, 1))
            m['winv_c'] = np.ascontiguousarray(cw_v[b].reshape(NL, 256, 128))
            m['nakT_c'] = np.ascontiguousarray(cn_k[b].reshape(NL, 256, 256).transpose(0, 2, 1))
            m['nav_c'] = np.ascontiguousarray(cn_v[b].reshape(NL, 256, 256))
            m['biasD'] = bias_s
            m['ctxflag'] = np.ones((128, 1), np.float32)
        else:
            xs = x_prompt[4 * core:4 * core + 4].reshape(1024, 1024)
            cond = c_ctx
            m['ckvT_c'] = np.zeros((NL, 128, 256), np.float32)
            m['krT_c'] = np.zeros((NL, 32, 256), np.float32)
            m['winkT_c'] = np.zeros((NL, 128, 256), np.float32)
            m['winv_c'] = np.zeros((NL, 256, 128), np.float32)
            m['nakT_c'] = np.zeros((NL, 256, 256), np.float32)
            m['nav_c'] = np.zeros((NL, 256, 256), np.float32)
            m['biasD'] = bias_p
            m['ctxflag'] = np.zeros((128, 1), np.float32)
        m['xT'] = np.ascontiguousarray(xs.T)
        m['cond'] = _vecT(cond)
        in_maps.append(m)
    return in_maps


def _assemble(R):
    y_prompt = np.concatenate([R[i]['yT'].T.reshape(4, 256, 1024) for i in range(4)], axis=0)
    y_sample = np.stack([R[4 + b]['yT'].T for b in range(4)], axis=0)

    def featmaj(name, feat):
        outs = []
        for i in range(4):
            a = R[i][name]
            a = a.reshape(NL, feat, 4, 256).transpose(2, 0, 3, 1)
            outs.append(a)
        return np.ascontiguousarray(np.concatenate(outs, axis=0))

    def tokmaj(name, feat):
        outs = []
        for i in range(4):
            a = R[i][name].reshape(NL, 4, 256, feat).transpose(1, 0, 2, 3)
            outs.append(a)
        return np.ascontiguousarray(np.concatenate(outs, axis=0))

    new_ckv = featmaj('o_ckvT', 128)
    new_kr = featmaj('o_krT', 32)
    new_wk = featmaj('o_kbT', 128).reshape(16, NL, 256, 2, 64)
    new_wv = tokmaj('o_vb', 128).reshape(16, NL, 256, 2, 64)
    new_nk = featmaj('o_kdT', 256).reshape(16, NL, 256, 4, 64)
    new_nv = tokmaj('o_vd', 256).reshape(16, NL, 256, 4, 64)
    return (np.ascontiguousarray(y_prompt.astype(np.float32)), np.ascontiguousarray(y_sample.astype(np.float32)),
            new_ckv, new_kr, new_wk, new_wv, new_nk, new_nv)


def kernel(**inputs):
    in_maps = _prep(**inputs)
    if 'nc' not in _PROG:
        _PROG['nc'] = build_program(NL)
    res = run_bass_kernel_spmd(_PROG['nc'], in_maps, core_ids=list(range(8)))
    return _assemble(res.results)
```

```python
import numpy as np
import ml_dtypes
from contextlib import ExitStack
import concourse.bass as bass
import concourse.mybir as mybir
from concourse.bass_utils import run_bass_kernel_spmd

F32 = mybir.dt.float32
BF16 = mybir.dt.bfloat16
AF = mybir.ActivationFunctionType
ALU = mybir.AluOpType
bf16 = ml_dtypes.bfloat16

NL = 4
D = 1024
T = 1024
DFF = 2816
EPS = 1e-6
NEG = -30000.0
BIG = 1024.0
RING_ELEMS = 4096
NSLOT = 4
NTMP = 6
NPT = 20
NG = 17


class Prog:
    COMPUTE = ('pe', 'act', 'dve', 'pool')

    def __init__(self, nc, es, same_engine_sync=True):
        self.nc = nc
        self.es = es
        self.ops = []
        self.lastw = {}
        self.readers = {}
        self.chan_last = {}
        self.chan_cnt = {}
        self.same_engine_sync = same_engine_sync

    def add(self, eng, fn, reads=(), writes=(), chan=None, extra_deps=()):
        idx = len(self.ops)
        deps = set(extra_deps)
        if eng in ('act', 'dve'):
            pk = [k for k in reads if k == 'psM' or (isinstance(k, tuple) and k[0] == 'ps')]
            if pk:
                writes = list(writes) + pk
        for k in reads:
            w = self.lastw.get(k)
            if w is not None:
                deps.add(w)
        for k in writes:
            w = self.lastw.get(k)
            if w is not None:
                deps.add(w)
            last = {}
            for r in self.readers.get(k, ()):
                rop = self.ops[r]
                if rop['chan'] is not None:
                    deps.add(r)
                else:
                    last[rop['eng']] = r
            deps.update(last.values())
        for k in reads:
            self.readers.setdefault(k, []).append(idx)
        for k in writes:
            self.lastw[k] = idx
            self.readers[k] = []
        if chan is not None:
            p = self.chan_last.get(chan)
            if p is not None:
                deps.add(p)
            self.chan_last[chan] = idx
            self.chan_cnt[chan] = self.chan_cnt.get(chan, 0) + 1
        deps.discard(idx)
        self.ops.append(dict(eng=eng, fn=fn, deps=deps, chan=chan, signal=False, sigval=None,
                             chanval=(16 * self.chan_cnt[chan] if chan is not None else None)))
        return idx

    def emit(self):
        nc = self.nc
        ops = self.ops
        for op in ops:
            for d in op['deps']:
                dop = ops[d]
                if dop['chan'] is None:
                    if dop['eng'] == op['eng'] and (op['eng'] == 'pe' or not self.same_engine_sync):
                        continue
                    dop['signal'] = True
        cnt = {}
        for op in ops:
            if op['chan'] is None and op['signal']:
                cnt[op['eng']] = cnt.get(op['eng'], 0) + 1
                op['sigval'] = cnt[op['eng']]
        self.sig_counts = dict(cnt)
        self.chan_counts = dict(self.chan_cnt)
        self.n_ops = len(ops)
        sems = {e: self.es.enter_context(nc.semaphore('s_' + e)) for e in self.COMPUTE + ('sp',)}
        csems = {c: self.es.enter_context(nc.semaphore('c_%d' % i)) for i, c in enumerate(self.chan_cnt)}
        by_eng = {}
        for i, op in enumerate(ops):
            by_eng.setdefault(op['eng'], []).append(i)

        def run_engine(ename, e):
            waited = {}
            for i in by_eng.get(ename, ()):
                op = ops[i]
                need = {}
                for d in op['deps']:
                    dop = ops[d]
                    if dop['chan'] is not None:
                        key = ('c', dop['chan'])
                        val = dop['chanval']
                    else:
                        if dop['eng'] == ename and (ename == 'pe' or not self.same_engine_sync):
                            continue
                        key = ('e', dop['eng'])
                        val = dop['sigval']
                    if val > need.get(key, 0):
                        need[key] = val
                for key, val in need.items():
                    if waited.get(key, 0) >= val:
                        continue
                    waited[key] = val
                    s = csems[key[1]] if key[0] == 'c' else sems[key[1]]
                    e.wait_ge(s, val)
                if op['fn'] is None:
                    continue
                ins = op['fn'](e)
                if op['chan'] is not None:
                    ins.then_inc(csems[op['chan']], 16)
                elif op['signal']:
                    ins.then_inc(sems[ename], 1)

        with nc.Block() as block:
            @block.tensor
            def _(e):
                run_engine('pe', e)

            @block.scalar
            def _(e):
                run_engine('act', e)

            @block.vector
            def _(e):
                run_engine('dve', e)

            @block.gpsimd
            def _(e):
                run_engine('pool', e)

            @block.sync
            def _(e):
                run_engine('sp', e)


INPUT_SPECS = [
    ('xT', [1024, 1024], F32), ('cond', [128, 8], F32), ('ctxflag', [128, 1], F32),
    ('w_ada', [NL, 1024, 9216], F32), ('b_adaT', [NL, 128, 72], F32), ('gT', [NL, 128, 3, 8], F32),
    ('w_gate1', [NL, 1024, DFF], F32), ('w_up1', [NL, 1024, DFF], F32), ('w_down1', [NL, DFF, 1024], F32),
    ('w_gate2', [NL, 1024, DFF], F32), ('w_up2', [NL, 1024, DFF], F32), ('w_down2', [NL, DFF, 1024], F32),
    ('w_in', [NL, 1024, 1952], F32), ('w_o', [NL, 1024, 1024], F32),
    ('w_uq', [NL, 256, 384], F32), ('w_uqP', [NL, 256, 384], F32), ('w_inPa', [NL, 1024, 32], F32), ('w_inPb', [NL, 1024, 384], F32),
    ('w_ukv', [NL, 128, 512], F32), ('gvec', [NL, 128, NG], F32),
    ('ckvT_c', [NL, 128, 256], F32), ('krT_c', [NL, 32, 256], F32), ('winkT_c', [NL, 128, 256], F32),
    ('winv_c', [NL, 256, 128], F32), ('nakT_c', [NL, 256, 256], F32), ('nav_c', [NL, 256, 256], F32),
    ('identb', [128, 128], BF16), ('onesb', [128, 128], BF16), ('blk64b', [128, 128], BF16),
    ('ropeB', [128, 2, 1024], F32), ('ropeA', [128, 2, 1024], BF16),
    ('cs64', [128, 256], BF16), ('dftC', [1024, 1024], BF16), ('dftS', [1024, 1024], BF16),
    ('EA', [8, 1280], BF16), ('FA', [8, 1024], BF16), ('maskB', [128, 6, 512], BF16),
    ('biasD', [NL, 4, 1024, 1024], F32),
]
OUTPUT_SPECS = [
    ('yT', [1024, 1024]), ('o_ckvT', [NL, 128, 1024]), ('o_krT', [NL, 32, 1024]), ('o_kbT', [NL, 128, 1024]),
    ('o_vb', [NL, 1024, 128]), ('o_kdT', [NL, 256, 1024]), ('o_vd', [NL, 1024, 256]),
]


class _Stop(Exception):
    pass


def build_program(nl=NL, stop=None):
    nc = bass.Bass("TRN2", target_bir_lowering=False)
    IN = {n: nc.dram_tensor(n, list(s), dt, kind="ExternalInput").ap() for n, s, dt in INPUT_SPECS}
    OUT = {n: nc.dram_tensor(n, list(s), F32, kind="ExternalOutput").ap() for n, s in OUTPUT_SPECS}
    es = ExitStack()
    with es:
        P = Prog(nc, es)

        def sb(name, shape, dt):
            return es.enter_context(nc.sbuf_tensor(name, list(shape), dt))

        xT = sb('xT_sb', [128, 8, 1024], F32)
        hT = sb('hT', [128, 8, 1024], BF16)
        ring = [sb('ring%d' % i, [128, RING_ELEMS], BF16) for i in range(NSLOT)]
        actb = [sb('actb%d' % i, [128, 512], BF16) for i in range(8)]
        rstdN = sb('rstdN', [128, 512], F32)
        tmp = [sb('tmp%d' % i, [128, 512], F32) for i in range(NTMP)]
        sqb = [sb('sqb%d' % i, [128, 512], BF16) for i in range(3)]
        ps = [es.enter_context(nc.psum_tensor('ps%d' % i, [128, 512], F32)) for i in range(8)]
        identb = sb('identb_sb', [128, 128], BF16)
        onesb = sb('onesb_sb', [128, 128], BF16)
        blk64b = sb('blk64b_sb', [128, 128], BF16)
        ropeB = sb('ropeB_sb', [128, 2, 1024], F32)
        ropeA = sb('ropeA_sb', [128, 2, 1024], BF16)
        cs64 = sb('cs64_sb', [128, 256], BF16)
        epsT = sb('epsT', [128, 1], F32)
        gvec = sb('gvec_sb', [128, NL, NG], F32)
        gT = sb('gT_sb', [128, NL, 3, 8], F32)
        badaT = sb('badaT_sb', [128, NL, 72], F32)
        condf = sb('condf', [128, 8], F32)
        condb = sb('condb', [128, 8], BF16)
        ctxf = sb('ctxf', [128, 1], F32)
        modT = [sb('modT%d' % i, [128, 72], F32) for i in range(2)]
        modA = [sb('modA%d' % i, [128, 3, 8], F32) for i in range(2)]
        modG = [sb('modG%d' % i, [128, 3, 8], F32) for i in range(2)]
        sinkexp = sb('sinkexp', [128, 4], F32)
        rec = sb('rec', [128, 4], F32)
        QB = sb('QB', [128, 4, 1024], BF16)
        QD = sb('QD', [128, 4, 1024], BF16)
        QA = sb('QA', [128, 4, 1024], BF16)
        KB = sb('KB', [128, 1280], BF16)
        KD = sb('KD', [128, 2, 1280], BF16)
        KA = sb('KA', [128, 4, 1280], BF16)
        VA = sb('VA', [128, 10, 4, 65], BF16)
        VB = sb('VB', [128, 10, 2, 65], BF16)
        VD = sb('VD', [128, 10, 4, 65], BF16)
        PT = sb('PT', [128, NPT, 512], BF16)
        ABt = PT[:, 0:8, :].rearrange("p l (c x) -> p l c x", c=2)
        XcT = PT[:, 8:12, :].rearrange("p (c t) x -> p c (t x)", c=2)
        mtok = sb('mtok', [128, 2, 4, 128], BF16)
        cqn = sb('cqn', [128, 2, 512], BF16)
        ckvn = sb('ckvn', [128, 1024], BF16)
        sqkr = sb('sqkr', [128, 512], BF16)
        krr = rstdN
        wuq = sb('wuq', [128, 2, 384], BF16)
        wuqP = sb('wuqP', [128, 2, 384], BF16)
        wukv = sb('wukv', [128, 512], BF16)
        ckvc = sb('ckvc', [128, 256], BF16)

        st = dict(ps=0, tmp=0, sq=0, ring=0, pt=0, act=0, ld=0, out=0)
        out_ids = []

        def rot(name, n):
            i = st[name]
            st[name] = (i + 1) % n
            return i

        newps = lambda: rot('ps', 7)
        newtmp = lambda: rot('tmp', NTMP)
        newsq = lambda: rot('sq', 3)

        def MM(out, lhsT, rhs, start, stop, rd, wr):
            P.add('pe', lambda e: e.matmul(out, lhsT=lhsT, rhs=rhs, start=start, stop=stop), rd, wr)

        def ACT(out, in_, func, rd, wr, scale=None, bias=None):
            kw = {}
            if scale is not None:
                kw['scale'] = scale
            if bias is not None:
                kw['bias'] = bias
            P.add('act', lambda e: e.activation(out=out, in_=in_, func=func, **kw), rd, wr)

        def TT(out, in0, in1, op, rd, wr):
            P.add('dve', lambda e: e.tensor_tensor(out=out, in0=in0, in1=in1, op=op), rd, wr)

        def STT(out, in0, scalar, in1, op0, op1, rd, wr):
            P.add('dve', lambda e: e.scalar_tensor_tensor(out=out, in0=in0, scalar=scalar, in1=in1, op0=op0, op1=op1), rd, wr)

        def TS(out, in0, s1, op0, rd, wr):
            P.add('dve', lambda e: e.tensor_scalar(out=out, in0=in0, scalar1=s1, scalar2=None, op0=op0), rd, wr)

        def RECIP(out, in_, rd, wr):
            P.add('dve', lambda e: e.reciprocal(out=out, in_=in_), rd, wr)

        def RECIPF(out, in_, rd, wr):
            P.add('dve', lambda e: e.reciprocal_approx_fast(out=out, in_=in_), rd, wr)

        def DMA(q, out, in_, rd, wr, chan):
            return P.add(q, lambda e: e.dma_start(out=out, in_=in_), rd, wr, chan=chan)

        def LOAD(out, in_, keys, q='sp'):
            return DMA(q, out, in_, [], keys, 'ld%d' % rot('ld', 4))

        def STORE(out, in_, rd):
            out_ids.append(DMA('sp', out, in_, rd, [], 'st%d' % rot('out', 4)))

        def ring_get(src_ap, a, b):
            s = rot('ring', NSLOT)
            view = ring[s][:, 0:a * b].rearrange("p (a b) -> p a b", b=b)
            DMA('pool', view, src_ap, [], [('ring', s)], 'ring%d' % s)
            return view, ('ring', s)

        cur_layer = [0]

        def CK(name):
            if stop == name or stop == '%s@%d' % (name, cur_layer[0]):
                raise _Stop()

        def kcp(ap):
            return ap.rearrange("(kc p) n -> p kc n", p=128)

        C = 'consts'
        cl = []
        for name, tile in [('identb', identb), ('onesb', onesb), ('blk64b', blk64b),
                           ('ropeB', ropeB), ('ropeA', ropeA), ('cs64', cs64),
                           ('cond', condf), ('ctxflag', ctxf)]:
            cl.append(LOAD(tile[:], IN[name], []))
        cl.append(LOAD(gvec[:], IN['gvec'].rearrange("l p g -> p l g"), []))
        cl.append(LOAD(gT[:], IN['gT'].rearrange("l p k c -> p l k c"), []))
        cl.append(LOAD(badaT[:], IN['b_adaT'].rearrange("l p j -> p l j"), []))
        xkeys = [('x', c, t) for c in range(8) for t in range(2)]
        LOAD(xT[:], kcp(IN['xT']), xkeys)
        P.add('dve', lambda e: e.memset(VA[:].rearrange("p a b c -> p (a b c)"), 1.0), [], ['Vinit'])
        P.add('dve', lambda e: e.memset(VB[:].rearrange("p a b c -> p (a b c)"), 1.0), [], ['Vinit'])
        P.add('dve', lambda e: e.memset(VD[:].rearrange("p a b c -> p (a b c)"), 1.0), [], ['Vinit'])
        zi = []
        for tl in (QB, QD, QA, KA):
            zi.append(P.add('dve', lambda e, tl=tl: e.memset(tl[:].rearrange("p a b -> p (a b)"), 0.0), [], []))
        for h in range(4):
            cl.append(P.add('sp', lambda e, h=h: e.dma_start(out=KA[96:104, h, :], in_=IN['EA']), [], [], chan='ld%d' % rot('ld', 4), extra_deps=zi))
            cl.append(P.add('sp', lambda e, h=h: e.dma_start(out=QA[96:104, h, :], in_=IN['FA']), [], [], chan='ld%d' % rot('ld', 4), extra_deps=zi))
        P.add('dve', lambda e: e.memset(epsT[:], EPS), [], [C], extra_deps=cl + zi)
        ACT(condb[:], condf[:], AF.Silu, [C], ['condb'])
        for Vt, nh in ((VA, 4), (VB, 2), (VD, 4)):
            for kt in (8, 9):
                for h in range(nh):
                    ACT(Vt[:, kt, h, 64:65], ctxf[:, 0:1], AF.Copy, ['Vinit', C], ['Vinit'])

        def rstd_from(ps_ap, pskey, inv_d, p0=0, p1=128, n=512):
            ti = newtmp()
            t = tmp[ti][p0:p1, 0:n]
            ACT(t, ps_ap, AF.Ln, [pskey, C], [('tmp', ti)], scale=inv_d, bias=epsT[p0:p1, 0:1])
            ACT(t, t, AF.Exp, [('tmp', ti)], [('tmp', ti)], scale=-0.5)
            return ti

        def mod_steps(l):
            par = l % 2
            for k in range(3):
                for b6 in range(6):
                    blk = k * 6 + b6
                    wt, wk = ring_get(kcp(IN['w_ada'][l][:, blk * 512:(blk + 1) * 512]), 8, 512)
                    for j in range(4):
                        col = blk * 4 + j
                        for kc in range(8):
                            MM(ps[7][:, col:col + 1], wt[:, kc, j * 128:(j + 1) * 128], condb[:, kc:kc + 1], kc == 0, kc == 7,
                               [wk, 'condb'], ['psM'])
                    yield
                c0 = 24 * k
                TT(modT[par][:, c0:c0 + 24], ps[7][:, c0:c0 + 24], badaT[:, l, c0:c0 + 24], ALU.add, ['psM', C], [('mod', par, k)])
                STT(modA[par][:, k, :], modT[par][:, c0 + 8:c0 + 16], 1.0, gT[:, l, k, :], ALU.add, ALU.mult,
                    [('mod', par, k), C], [('modA', par, k)])
                TS(modG[par][:, k, :], modT[par][:, c0 + 16:c0 + 24], (1.0 if k == 1 else 0.5), ALU.mult, [('mod', par, k)], [('modG', par, k)])
            yield

        def norm_mod(l, k):
            par = l % 2
            for t in range(2):
                hs = slice(t * 512, (t + 1) * 512)
                pi = newps()
                for c in range(8):
                    si = newsq()
                    ACT(sqb[si][:, :], xT[:, c, hs], AF.Square, [('x', c, t)], [('sq', si)])
                    MM(ps[pi][:, :], onesb[:, :], sqb[si][:, :], c == 0, c == 7, [('sq', si), C], [('ps', pi)])
                ACT(rstdN[:, :], ps[pi][:, :], AF.Ln, [('ps', pi), C], ['rstdN'], scale=1.0 / 1024, bias=epsT[:, 0:1])
                ACT(rstdN[:, :], rstdN[:, :], AF.Exp, ['rstdN'], ['rstdN'], scale=-0.5)
                for c in range(8):
                    ti = newtmp()
                    STT(tmp[ti][:, :], xT[:, c, hs], modA[par][:, k, c:c + 1], rstdN[:, :], ALU.mult, ALU.mult,
                        [('x', c, t), ('modA', par, k), 'rstdN'], [('tmp', ti)])
                    ACT(hT[:, c, hs], tmp[ti][:, :], AF.Identity, [('tmp', ti), ('mod', par, k)], [('h', c, t)],
                        bias=modT[par][:, 3 * k * 8 + c:3 * k * 8 + c + 1])

        def ffn(l, which, modgen):
            par = l % 2
            wg = IN['w_gate%d' % which][l]
            wu = IN['w_up%d' % which][l]
            wd = IN['w_down%d' % which][l]
            gk = 0 if which == 1 else 2
            for blk in range(6):
                nch = 4 if blk < 5 else 2
                c0 = blk * 512
                ncol = nch * 128
                gw, gkey = ring_get(kcp(wg[:, c0:c0 + ncol]), 8, ncol)
                uw, ukey = ring_get(kcp(wu[:, c0:c0 + ncol]), 8, ncol)
                dw, dkey = ring_get(wd[c0:c0 + ncol, :].rearrange("(j p) n -> p j n", p=128), nch, 1024)
                for t in range(2):
                    hs = slice(t * 512, (t + 1) * 512)
                    aslots = []
                    for j in range(nch):
                        ai = rot('act', 8)
                        aslots.append(ai)
                        pg = newps()
                        for kc in range(8):
                            MM(ps[pg][:, :], gw[:, kc, j * 128:(j + 1) * 128], hT[:, kc, hs], kc == 0, kc == 7,
                               [gkey, ('h', kc, t)], [('ps', pg)])
                        pu = newps()
                        for kc in range(8):
                            MM(ps[pu][:, :], uw[:, kc, j * 128:(j + 1) * 128], hT[:, kc, hs], kc == 0, kc == 7,
                               [ukey, ('h', kc, t)], [('ps', pu)])
                        ti = newtmp()
                        ACT(tmp[ti][:, :], ps[pg][:, :], AF.Silu, [('ps', pg)], [('tmp', ti)])
                        TT(actb[ai][:, :], tmp[ti][:, :], ps[pu][:, :], ALU.mult, [('tmp', ti), ('ps', pu)], [('act', ai)])
                    for dc in range(8):
                        po = newps()
                        for j in range(nch):
                            MM(ps[po][:, :], dw[:, j, dc * 128:(dc + 1) * 128], actb[aslots[j]][:, :], j == 0, j == nch - 1,
                               [dkey, ('act', aslots[j])], [('ps', po)])
                        STT(xT[:, dc, hs], ps[po][:, :], modG[par][:, gk, dc:dc + 1], xT[:, dc, hs], ALU.mult, ALU.add,
                            [('ps', po), ('modG', par, gk), ('x', dc, t)], [('x', dc, t)])
                if modgen is not None:
                    modgen()
                    if l == 0:
                        modgen()

        def rope_norm(psraw, pskey, p0, p1, ones_lhsT, inv_d, gain, out16, outkeys, rope=None, store=None, n=512, split=None, psf=None, gainf=None):
            si = newsq()
            ACT(sqb[si][p0:p1, 0:n], psraw, AF.Square, [pskey], [('sq', si)])
            pss = newps()
            MM(ps[pss][p0:p1, 0:n], ones_lhsT, sqb[si][p0:p1, 0:n], True, True, [('sq', si), C], [('ps', pss)])
            ri = rstd_from(ps[pss][p0:p1, 0:n], ('ps', pss), inv_d, p0, p1, n)
            r = tmp[ri][p0:p1, 0:n]
            if rope is None and store is None and split is not None:
                for (a_, b_, ap_, ks_) in split:
                    STT(ap_, psf(a_, b_), gainf(a_, b_), tmp[ri][a_:b_, 0:n], ALU.mult, ALU.mult, [pskey, ('tmp', ri), C], ks_)
                return
            if rope is None and store is None:
                STT(out16, psraw, gain, r, ALU.mult, ALU.mult, [pskey, ('tmp', ri), C], outkeys)
                return
            if rope is None:
                xi = newtmp()
                xn = tmp[xi][p0:p1, 0:n]
                STT(xn, psraw, gain, r, ALU.mult, ALU.mult, [pskey, ('tmp', ri), C], [('tmp', xi)])
                STORE(store, xn, [('tmp', xi)])
                ACT(out16, xn, AF.Copy, [('tmp', xi)], outkeys)
                return
            psraw_p, pskey_p, gain_p, ropeC, ropeS = rope
            t1 = newtmp()
            STT(tmp[t1][p0:p1, 0:n], psraw, gain, ropeC, ALU.mult, ALU.mult, [pskey, C], [('tmp', t1)])
            t2 = newtmp()
            STT(tmp[t2][p0:p1, 0:n], psraw_p, gain_p, ropeS, ALU.mult, ALU.mult, [pskey_p, C], [('tmp', t2)])
            TT(tmp[t1][p0:p1, 0:n], tmp[t1][p0:p1, 0:n], tmp[t2][p0:p1, 0:n], ALU.add, [('tmp', t1), ('tmp', t2)], [('tmp', t1)])
            if split is not None:
                for (a_, b_, ap_, ks_) in split:
                    TT(ap_, tmp[t1][a_:b_, 0:n], tmp[ri][a_:b_, 0:n], ALU.mult, [('tmp', t1), ('tmp', ri)], ks_)
            elif store is None:
                TT(out16, tmp[t1][p0:p1, 0:n], r, ALU.mult, [('tmp', t1), ('tmp', ri)], outkeys)
            else:
                TT(tmp[t1][p0:p1, 0:n], tmp[t1][p0:p1, 0:n], r, ALU.mult, [('tmp', t1), ('tmp', ri)], [('tmp', t1)])
                STORE(store, tmp[t1][p0:p1, 0:n], [('tmp', t1)])
                ACT(out16, tmp[t1][p0:p1, 0:n], AF.Copy, [('tmp', t1)], outkeys)

        def mixer(l, modgen):
            par = l % 2

            def gv(col, p0=0, p1=128):
                return gvec[p0:p1, l, col:col + 1]

            def step():
                if modgen is not None:
                    modgen()

            DMA('pool', wuq[:], IN['w_uq'][l].rearrange("(c p) n -> p c n", p=128), [], ['wuq'], 'cx0')
            DMA('pool', wuqP[:], IN['w_uqP'][l].rearrange("(c p) n -> p c n", p=128), [], ['wuqP'], 'cx0')
            DMA('pool', wukv[:], IN['w_ukv'][l], [], ['wukv'], 'cx1')
            DMA('pool', ckvc[:], IN['ckvT_c'][l], [], ['ckvc'], 'cx2')
            DMA('pool', KB[:, 1024:1280], IN['winkT_c'][l], [], [('KB', 'c')], 'cx4')
            DMA('pool', KD[:, :, 1024:1280], IN['nakT_c'][l].rearrange("(c p) k -> p c k", p=128), [], [('KD', 0, 'c'), ('KD', 1, 'c')], 'cx5')
            for i in range(2):
                DMA('pool', VB[:, 8 + i, :, 0:64], IN['winv_c'][l][i * 128:(i + 1) * 128, :].rearrange("p (h d) -> p h d", d=64),
                    [], [('VB', 8 + i)], 'cx6')
                DMA('pool', VD[:, 8 + i, :, 0:64], IN['nav_c'][l][i * 128:(i + 1) * 128, :].rearrange("p (h d) -> p h d", d=64),
                    [], [('VD', 8 + i)], 'cx7')
            ACT(sinkexp[:, :], gvec[:, l, 9:13], AF.Exp, [C], ['sinkexp'])
            wukv_v = wukv[:, 0:512].rearrange("p (h x) -> p h x", x=128)[:, :, 64:128]

            b0, b0k = ring_get(kcp(IN['w_in'][l][:, 0:416]), 8, 416)
            bpa, bpak = ring_get(kcp(IN['w_inPa'][l]), 8, 32)
            for t in range(2):
                hs = slice(t * 512, (t + 1) * 512)
                pcs = []
                for c in range(2):
                    pi = newps()
                    pcs.append(pi)
                    for kc in range(8):
                        MM(ps[pi][:, :], b0[:, kc, c * 128:(c + 1) * 128], hT[:, kc, hs], kc == 0, kc == 7, [b0k, ('h', kc, t)], [('ps', pi)])
                pss = newps()
                for c in range(2):
                    si = newsq()
                    ACT(sqb[si][:, :], ps[pcs[c]][:, :], AF.Square, [('ps', pcs[c])], [('sq', si)])
                    MM(ps[pss][:, :], onesb[:, :], sqb[si][:, :], c == 0, c == 1, [('sq', si), C], [('ps', pss)])
                ri = rstd_from(ps[pss][:, :], ('ps', pss), 1.0 / 256)
                for c in range(2):
                    STT(cqn[:, c, :], ps[pcs[c]][:, :], gv(c), tmp[ri][:, :], ALU.mult, ALU.mult,
                        [('ps', pcs[c]), ('tmp', ri), C], [('cqn', c)])
                pk = newps()
                for kc in range(8):
                    MM(ps[pk][:, :], b0[:, kc, 256:384], hT[:, kc, hs], kc == 0, kc == 7, [b0k, ('h', kc, t)], [('ps', pk)])
                rope_norm(ps[pk][:, :], ('ps', pk), 0, 128, onesb[:, :], 1.0 / 128, gv(2), ckvn[:, hs], [('ckvn', t)],
                          rope=None, store=OUT['o_ckvT'][l][:, hs])
                pr = newps()
                for kc in range(8):
                    MM(ps[pr][64:96, :], b0[:, kc, 384:416], hT[:, kc, hs], kc == 0, kc == 7, [b0k, ('h', kc, t)], [('ps', pr)])
                ti = newtmp()
                ACT(tmp[ti][64:96, :], ps[pr][64:96, :], AF.Copy, [('ps', pr)], [('tmp', ti)])
                STORE(OUT['o_krT'][l][:, hs], tmp[ti][64:96, :], [('tmp', ti)])
                ACT(sqkr[64:96, :], ps[pr][64:96, :], AF.Square, [('ps', pr)], ['sqkr'])
                prp = newps()
                for kc in range(8):
                    MM(ps[prp][64:96, :], bpa[:, kc, 0:32], hT[:, kc, hs], kc == 0, kc == 7, [bpak, ('h', kc, t)], [('ps', prp)])
                t1 = newtmp()
                STT(tmp[t1][64:96, :], ps[prp][64:96, :], gv(14, 64, 96), ropeA[64:96, 1, hs], ALU.mult, ALU.mult, [('ps', prp), C], [('tmp', t1)])
                STT(krr[64:96, :], ps[pr][64:96, :], gv(4, 64, 96), ropeA[64:96, 0, hs], ALU.mult, ALU.mult, [('ps', pr), C], ['rstdN'])
                TT(krr[64:96, :], krr[64:96, :], tmp[t1][64:96, :], ALU.add, ['rstdN', ('tmp', t1)], ['rstdN'])
                for h in range(4):
                    pn = newps()
                    MM(ps[pn][0:64, :], wukv[:, h * 128:h * 128 + 64], ckvn[:, hs], True, True, ['wukv', ('ckvn', t)], [('ps', pn)])
                    si = newsq()
                    ACT(sqb[si][0:64, :], ps[pn][0:64, :], AF.Square, [('ps', pn)], [('sq', si)])
                    pss = newps()
                    MM(ps[pss][0:96, :], onesb[0:64, 0:96], sqb[si][0:64, :], True, False, [('sq', si), C], [('ps', pss)])
                    MM(ps[pss][0:96, :], onesb[64:96, 0:96], sqkr[64:96, :], False, True, ['sqkr', C], [('ps', pss)])
                    ri = rstd_from(ps[pss][0:96, :], ('ps', pss), 1.0 / 96, 0, 96)
                    STT(KA[0:64, h, hs], ps[pn][0:64, :], gv(4, 0, 64), tmp[ri][0:64, :], ALU.mult, ALU.mult,
                        [('ps', pn), ('tmp', ri), C], [('KA', h, t)])
                    TT(KA[64:96, h, hs], krr[64:96, :], tmp[ri][64:96, :], ALU.mult, ['rstdN', ('tmp', ri)], [('KA', h, t)])
                    pq = newps()
                    for c in range(2):
                        MM(ps[pq][0:96, :], wuq[:, c, h * 96:(h + 1) * 96], cqn[:, c, :], c == 0, c == 1, ['wuq', ('cqn', c)], [('ps', pq)])
                    pqp = newps()
                    for c in range(2):
                        MM(ps[pqp][0:96, :], wuqP[:, c, h * 96:(h + 1) * 96], cqn[:, c, :], c == 0, c == 1, ['wuqP', ('cqn', c)], [('ps', pqp)])
                    rope_norm(ps[pq][0:96, :], ('ps', pq), 0, 96, onesb[0:96, 0:96], 1.0 / 96, gv(3, 0, 96), QA[0:96, h, hs],
                              [('QA', h, t)], rope=(ps[pqp][0:96, :], ('ps', pqp), gv(13, 0, 96), ropeA[0:96, 0, hs], ropeA[0:96, 1, hs]))
                for tt in range(4):
                    kt = t * 4 + tt
                    pv = newps()
                    MM(ps[pv][:, 0:256], ckvn[:, kt * 128:(kt + 1) * 128], wukv_v, True, True, ['wukv', ('ckvn', t)], [('ps', pv)])
                    ACT(VA[:, kt, :, 0:64], ps[pv][:, 0:256].rearrange("p (h d) -> p h d", d=64), AF.Copy, [('ps', pv), 'Vinit'], [('VA', kt)])
            step()
            CK('projA')
            tkc = newtmp()
            krc_ap = tmp[tkc][64:96, 0:256]
            DMA('sp', krc_ap, IN['krT_c'][l], [], [('tmp', tkc)], 'cx3')
            si = newsq()
            ACT(sqb[si][64:96, 0:256], krc_ap, AF.Square, [('tmp', tkc)], [('sq', si)])
            sq_krc = si
            tgc = newtmp()
            TS(tmp[tgc][64:96, 0:256], krc_ap, gv(4, 64, 96), ALU.mult, [('tmp', tkc), C], [('tmp', tgc)])
            for h in range(4):
                pn = newps()
                MM(ps[pn][0:64, 0:256], wukv[:, h * 128:h * 128 + 64], ckvc[:, :], True, True, ['wukv', 'ckvc'], [('ps', pn)])
                si = newsq()
                if si == sq_krc:
                    si = newsq()
                ACT(sqb[si][0:64, 0:256], ps[pn][0:64, 0:256], AF.Square, [('ps', pn)], [('sq', si)])
                pss = newps()
                MM(ps[pss][0:96, 0:256], onesb[0:64, 0:96], sqb[si][0:64, 0:256], True, False, [('sq', si), C], [('ps', pss)])
                MM(ps[pss][0:96, 0:256], onesb[64:96, 0:96], sqb[sq_krc][64:96, 0:256], False, True, [('sq', sq_krc), C], [('ps', pss)])
                ri = rstd_from(ps[pss][0:96, 0:256], ('ps', pss), 1.0 / 96, 0, 96, 256)
                if ri == tgc:
                    raise RuntimeError("tmp rotation clash")
                STT(KA[0:64, h, 1024:1280], ps[pn][0:64, 0:256], gv(4, 0, 64), tmp[ri][0:64, 0:256], ALU.mult, ALU.mult,
                    [('ps', pn), ('tmp', ri), C], [('KA', h, 'c')])
                TT(KA[64:96, h, 1024:1280], tmp[tgc][64:96, 0:256], tmp[ri][64:96, 0:256], ALU.mult, [('tmp', tgc), ('tmp', ri)], [('KA', h, 'c')])
            for i in range(2):
                pv = newps()
                MM(ps[pv][:, 0:256], ckvc[:, i * 128:(i + 1) * 128], wukv_v, True, True, ['wukv', 'ckvc'], [('ps', pv)])
                ACT(VA[:, 8 + i, :, 0:64], ps[pv][:, 0:256].rearrange("p (h d) -> p h d", d=64), AF.Copy, [('ps', pv), 'Vinit'], [('VA', 8 + i)])
            step()
            CK('ctxA')

            b1, b1k = ring_get(kcp(IN['w_in'][l][:, 416:928]), 8, 512)
            b1p, b1pk = ring_get(kcp(IN['w_inPb'][l]), 8, 384)
            for t in range(2):
                hs = slice(t * 512, (t + 1) * 512)
                for ci, hpair in enumerate(((0, 2), (1, 3))):
                    pi = newps()
                    for hh, pb in zip(hpair, (0, 64)):
                        for kc in range(8):
                            MM(ps[pi][pb:pb + 64, :], b1[:, kc, hh * 64:(hh + 1) * 64], hT[:, kc, hs], kc == 0, kc == 7,
                               [b1k, ('h', kc, t)], [('ps', pi)])
                    pip = newps()
                    for hh, pb in zip(hpair, (0, 64)):
                        for kc in range(8):
                            MM(ps[pip][pb:pb + 64, :], b1p[:, kc, hh * 64:(hh + 1) * 64], hT[:, kc, hs], kc == 0, kc == 7,
                               [b1pk, ('h', kc, t)], [('ps', pip)])
                    rope_norm(ps[pi][:, :], ('ps', pi), 0, 128, blk64b[:, :], 1.0 / 64, gv(5), None, None,
                              rope=(ps[pip][:, :], ('ps', pip), gv(15), ropeB[:, 0, hs], ropeB[:, 1, hs]),
                              split=[(0, 64, QB[0:64, hpair[0], hs], [('QB', hpair[0], t)]),
                                     (64, 128, QB[64:128, hpair[1], hs], [('QB', hpair[1], t)])])
                    CK('qb')
                pi = newps()
                for kc in range(8):
                    MM(ps[pi][:, :], b1[:, kc, 256:384], hT[:, kc, hs], kc == 0, kc == 7, [b1k, ('h', kc, t)], [('ps', pi)])
                pip = newps()
                for kc in range(8):
                    MM(ps[pip][:, :], b1p[:, kc, 256:384], hT[:, kc, hs], kc == 0, kc == 7, [b1pk, ('h', kc, t)], [('ps', pip)])
                rope_norm(ps[pi][:, :], ('ps', pi), 0, 128, blk64b[:, :], 1.0 / 64, gv(6), KB[:, hs], [('KB', t)],
                          rope=(ps[pip][:, :], ('ps', pip), gv(16), ropeB[:, 0, hs], ropeB[:, 1, hs]), store=OUT['o_kbT'][l][:, hs])
                CK('kb')
                for tt in range(4):
                    kt = t * 4 + tt
                    pv = newps()
                    for kc in range(8):
                        MM(ps[pv][:, 0:128], hT[:, kc, kt * 128:(kt + 1) * 128], b1[:, kc, 384:512], kc == 0, kc == 7,
                           [b1k, ('h', kc, t)], [('ps', pv)])
                    ti = newtmp()
                    ACT(tmp[ti][:, 0:128], ps[pv][:, 0:128], AF.Copy, [('ps', pv)], [('tmp', ti)])
                    STORE(OUT['o_vb'][l][kt * 128:(kt + 1) * 128, :], tmp[ti][:, 0:128], [('tmp', ti)])
                    ACT(VB[:, kt, :, 0:64], ps[pv][:, 0:128].rearrange("p (h d) -> p h d", d=64), AF.Copy, [('ps', pv), 'Vinit'], [('VB', kt)])
            step()
            CK('projB')
            b2, b2k = ring_get(kcp(IN['w_in'][l][:, 928:1440]), 8, 512)
            for t in range(2):
                hs = slice(t * 512, (t + 1) * 512)
                for ch in range(2):
                    pi = newps()
                    for kc in range(8):
                        MM(ps[pi][:, :], b2[:, kc, ch * 128:(ch + 1) * 128], hT[:, kc, hs], kc == 0, kc == 7, [b2k, ('h', kc, t)], [('ps', pi)])
                    ACT(XcT[:, ch, hs], ps[pi][:, :], AF.Copy, [('ps', pi)], [('pt', 8 + 2 * ch + t)])
                for ch in range(2):
                    pi = newps()
                    for kc in range(8):
                        MM(ps[pi][:, :], b2[:, kc, 256 + ch * 128:256 + (ch + 1) * 128], hT[:, kc, hs], kc == 0, kc == 7,
                           [b2k, ('h', kc, t)], [('ps', pi)])
                    rope_norm(ps[pi][:, :], ('ps', pi), 0, 128, blk64b[:, :], 1.0 / 64, gv(7), None, None,
                              split=[(0, 64, QD[0:64, 2 * ch, hs], [('QD', 2 * ch, t)]),
                                     (64, 128, QD[64:128, 2 * ch + 1, hs], [('QD', 2 * ch + 1, t)])],
                              psf=lambda a_, b_, pi=pi: ps[pi][a_:b_, :], gainf=lambda a_, b_: gv(7, a_, b_))
            step()
            b3, b3k = ring_get(kcp(IN['w_in'][l][:, 1440:1952]), 8, 512)
            for t in range(2):
                hs = slice(t * 512, (t + 1) * 512)
                for ch in range(2):
                    pi = newps()
                    for kc in range(8):
                        MM(ps[pi][:, :], b3[:, kc, ch * 128:(ch + 1) * 128], hT[:, kc, hs], kc == 0, kc == 7, [b3k, ('h', kc, t)], [('ps', pi)])
                    rope_norm(ps[pi][:, :], ('ps', pi), 0, 128, blk64b[:, :], 1.0 / 64, gv(8), KD[:, ch, hs], [('KD', ch, t)],
                              rope=None, store=OUT['o_kdT'][l][ch * 128:(ch + 1) * 128, hs])
                for tt in range(4):
                    kt = t * 4 + tt
                    pv = newps()
                    for kc in range(8):
                        MM(ps[pv][:, 0:256], hT[:, kc, kt * 128:(kt + 1) * 128], b3[:, kc, 256:512], kc == 0, kc == 7,
                           [b3k, ('h', kc, t)], [('ps', pv)])
                    ti = newtmp()
                    ACT(tmp[ti][:, 0:256], ps[pv][:, 0:256], AF.Copy, [('ps', pv)], [('tmp', ti)])
                    STORE(OUT['o_vd'][l][kt * 128:(kt + 1) * 128, :], tmp[ti][:, 0:256], [('tmp', ti)])
                    ACT(VD[:, kt, :, 0:64], ps[pv][:, 0:256].rearrange("p (h d) -> p h d", d=64), AF.Copy, [('ps', pv), 'Vinit'], [('VD', kt)])
            step()
            CK('projD')
            for lt in range(8):
                for ch in range(2):
                    pi = newps()
                    MM(ps[pi][:, 0:256], XcT[:, ch, lt * 128:(lt + 1) * 128], cs64[:, :], True, True, [('pt', 8 + 2 * ch + lt // 4), C], [('ps', pi)])
                    P.add('dve', lambda e, lt=lt, ch=ch, pi=pi: e.tensor_copy(out=ABt[:, lt, ch, :], in_=ps[pi][:, 0:256]),
                          [('ps', pi)], [('ab', lt, ch)])
            for t in range(2):
                hs = slice(t * 512, (t + 1) * 512)
                cb, cbk = ring_get(IN['dftC'][:, hs].rearrange("(lt p) n -> p lt n", p=128), 8, 512)
                sbk_, sbkk = ring_get(IN['dftS'][:, hs].rearrange("(lt p) n -> p lt n", p=128), 8, 512)
                for ch in range(2):
                    pi = newps()
                    for lt in range(8):
                        MM(ps[pi][:, :], ABt[:, lt, ch, 0:128], cb[:, lt, :], lt == 0, False, [('ab', lt, ch), ('pt', lt), cbk], [('ps', pi)])
                    for lt in range(8):
                        MM(ps[pi][:, :], ABt[:, lt, ch, 128:256], sbk_[:, lt, :], False, lt == 7, [('ab', lt, ch), ('pt', lt), sbkk], [('ps', pi)])
                    ACT(hT[:, 4 + ch, hs], ps[pi][:, :], AF.Copy, [('ps', pi)], [('h', 4 + ch, t)])
            step()
            CK('fourier')

            shared = {}

            def unit_kts(kind, t):
                if kind == 'B':
                    return [kt for kt in range(4 * t - 1, 4 * t + 5) if 0 <= kt <= 7] + [8, 9]
                if kind == 'D':
                    return list(range(0, 6) if t == 0 else range(2, 8)) + [8, 9]
                return list(range(10))

            def prefetch(kind, h):
                if kind == 'B':
                    if 'maskB' not in shared:
                        shared['maskB'] = ring_get(IN['maskB'], 6, 512)
                elif kind == 'D':
                    if ('biasD', h) not in shared:
                        shared[('biasD', h)] = [ring_get(IN['biasD'][l][h][g * 512:(g + 1) * 512, :].rearrange("(kt p) q -> p kt q", p=128), 4, 1024)
                                                for g in range(2)]

            def attend_qk(kind, h, t):
                hs = slice(t * 512, (t + 1) * 512)
                kts = unit_kts(kind, t)
                prefetch(kind, h)
                slots = {}
                for kt in kts:
                    pi = newps()
                    ksl = slice(kt * 128, (kt + 1) * 128)
                    own = kt < 8
                    kk = (kt // 4) if own else 'c'
                    if kind == 'A':
                        MM(ps[pi][:, :], KA[:, h, ksl], QA[:, h, hs], True, True, [('KA', h, kk), ('QA', h, t), C], [('ps', pi)])
                        scale = 96.0 ** -0.5
                    elif kind == 'B':
                        MM(ps[pi][:, :], KB[:, ksl], QB[:, h, hs], True, not own, [('KB', kk), ('QB', h, t), C], [('ps', pi)])
                        if own:
                            mv, mk = shared['maskB']
                            MM(ps[pi][:, :], identb[:, :], mv[:, kt - 4 * t + 1, :], False, True, [C, mk], [('ps', pi)])
                        scale = 0.125
                    else:
                        ch = h // 2
                        MM(ps[pi][:, :], KD[:, ch, ksl], QD[:, h, hs], True, not own, [('KD', ch, kk), ('QD', h, t), C], [('ps', pi)])
                        if own:
                            bv, bk = shared[('biasD', h)][kt // 4]
                            MM(ps[pi][:, :], identb[:, :], bv[:, kt % 4, hs], False, True, [C, bk], [('ps', pi)])
                        scale = 0.125
                    s = rot('pt', NPT)
                    slots[kt] = s
                    ACT(PT[:, s, :], ps[pi][:, :], AF.Exp, [('ps', pi)], [('pt', s)], scale=scale)
                return (kind, h, t, kts, slots)

            def attend_pv(ctx):
                kind, h, t, kts, slots = ctx
                V, vh, vname = {'A': (VA, h, 'VA'), 'B': (VB, h // 2, 'VB'), 'D': (VD, h, 'VD')}[kind]
                po = newps()
                for qi in range(4):
                    for i, kt in enumerate(kts):
                        MM(ps[po][:, qi * 65:(qi + 1) * 65], PT[:, slots[kt], qi * 128:(qi + 1) * 128], V[:, kt, vh, 0:65],
                           i == 0, i == len(kts) - 1, [('pt', slots[kt]), (vname, kt), 'Vinit'], [('ps', po)])
                den = ps[po][:, 0:260].rearrange("p (q x) -> p q x", x=65)[:, :, 64:65]
                rec3 = rec[:, 0:4].rearrange("p (q x) -> p q x", x=1)
                if kind == 'B':
                    TS(rec3, den, sinkexp[:, h:h + 1], ALU.add, [('ps', po), 'sinkexp'], ['rec'])
                    RECIP(rec[:, 0:4], rec[:, 0:4], ['rec'], ['rec'])
                else:
                    RECIP(rec3, den, [('ps', po)], ['rec'])
                for qi in range(4):
                    hslot = h % 2
                    TS(mtok[:, t, qi, hslot * 64:(hslot + 1) * 64], ps[po][:, qi * 65:qi * 65 + 64], rec[:, qi:qi + 1], ALU.mult,
                       [('ps', po), 'rec'], [('mtok', t, qi, hslot)])

            def attend_tr(chunk):
                for t in range(2):
                    hs = slice(t * 512, (t + 1) * 512)
                    pt_ = newps()
                    for qi in range(4):
                        MM(ps[pt_][:, qi * 128:(qi + 1) * 128], mtok[:, t, qi, :], identb[:, :], True, True,
                           [('mtok', t, qi, 0), ('mtok', t, qi, 1), C], [('ps', pt_)])
                    P.add('dve', lambda e, hs=hs, pt_=pt_, cc=chunk: e.tensor_copy(out=hT[:, cc, hs], in_=ps[pt_][:, :]),
                          [('ps', pt_)], [('h', chunk, t)])

            units = []
            for kind, chunk0 in (('A', 0), ('B', 2), ('D', 6)):
                for pair in range(2):
                    for h in (2 * pair, 2 * pair + 1):
                        for t in range(2):
                            units.append((kind, h, t, chunk0 + pair, (h % 2 == 1 and t == 1)))
            prev = None

            pending = []

            def flush(pv_):
                for ch_ in pending:
                    attend_tr(ch_)
                    step()
                del pending[:]
                attend_pv(pv_[0])
                if pv_[2]:
                    pending.append(pv_[1])

            for ui, (kind, h, t, chunk, last) in enumerate(units):
                if prev is not None and len(prev[0][3]) + len(unit_kts(kind, t)) > NPT:
                    flush(prev)
                    prev = None
                ctx = attend_qk(kind, h, t)
                if ui + 1 < len(units):
                    prefetch(units[ui + 1][0], units[ui + 1][1])
                if prev is not None:
                    flush(prev)
                prev = (ctx, chunk, last)
            flush(prev)
            for ch_ in pending:
                attend_tr(ch_)
                step()
            CK('attn')

            for blk in range(2):
                wo, wok = ring_get(kcp(IN['w_o'][l][:, blk * 512:(blk + 1) * 512]), 8, 512)
                for dcc in range(4):
                    dc = blk * 4 + dcc
                    for t in range(2):
                        hs = slice(t * 512, (t + 1) * 512)
                        po = newps()
                        for c in range(8):
                            MM(ps[po][:, :], wo[:, c, dcc * 128:(dcc + 1) * 128], hT[:, c, hs], c == 0, c == 7, [wok, ('h', c, t)], [('ps', po)])
                        STT(xT[:, dc, hs], ps[po][:, :], modG[par][:, 1, dc:dc + 1], xT[:, dc, hs], ALU.mult, ALU.add,
                            [('ps', po), ('modG', par, 1), ('x', dc, t)], [('x', dc, t)])
            step()

        try:
            CK('load')
            gens = [mod_steps(0)]

            def mg():
                while gens:
                    try:
                        next(gens[0])
                        return
                    except StopIteration:
                        gens.pop(0)

            def drain(n_keep):
                while len(gens) > n_keep:
                    mg()

            for _ in range(7):
                mg()
            CK('mod')
            for l in range(nl):
                cur_layer[0] = l
                if l + 1 < nl:
                    gens.append(mod_steps(l + 1))
                norm_mod(l, 0)
                CK('norm0')
                ffn(l, 1, mg)
                CK('ffn1')
                if l == 0:
                    drain(1 if nl > 1 else 0)
                norm_mod(l, 1)
                mixer(l, mg)
                CK('mixer')
                norm_mod(l, 2)
                ffn(l, 2, mg)
                drain(0)
                CK('layer')
        except _Stop:
            pass
        STORE(kcp(OUT['yT']), xT[:], xkeys)
        P.add('sp', None, extra_deps=out_ids)
        P.emit()
        nc._prog_stats = (P.n_ops, P.sig_counts, P.chan_counts)
    return nc


def _rope_tables(r, sample):
    Cm = np.ones((r, T), np.float32)
    Sm = np.zeros((r, T), np.float32)
    if not sample:
        return Cm, Sm
    half = r // 2
    t = np.arange(T)
    inv = (10000.0 ** (-np.arange(0, half, 2, dtype=np.float32) / np.float32(half))).astype(np.float32)
    q = half // 2
    for part, pos in ((0, t // 64), (1, t % 64)):
        ang = pos.astype(np.float32)[:, None] * inv[None, :]
        cos = np.cos(ang).astype(np.float32).T
        sin = np.sin(ang).astype(np.float32).T
        b = part * half
        Cm[b:b + q] = cos
        Cm[b + q:b + half] = cos
        Sm[b:b + q] = -sin
        Sm[b + q:b + half] = sin
    return Cm, Sm


def _partner(r):
    half = r // 2
    q = half // 2
    return np.array([d + q if (d % half) < q else d - q for d in range(r)])


def _consts(sample):
    c = {}
    c['identb'] = np.eye(128, dtype=np.float32).astype(bf16)
    c['onesb'] = np.ones((128, 128), np.float32).astype(bf16)
    c['blk64b'] = np.kron(np.eye(2, dtype=np.float32), np.ones((64, 64), np.float32)).astype(bf16)
    Cb, Sb = _rope_tables(64, sample)
    rb = np.zeros((128, 2, T), np.float32)
    rb[0:64, 0], rb[64:128, 0] = Cb, Cb
    rb[0:64, 1], rb[64:128, 1] = Sb, Sb
    c['ropeB'] = rb
    Ca, Sa = _rope_tables(32, sample)
    ra = np.zeros((128, 2, T), np.float32)
    ra[:, 0] = 1.0
    ra[64:96, 0] = Ca
    ra[64:96, 1] = Sa
    c['ropeA'] = ra.astype(bf16)
    k = np.arange(64)
    ang = 2 * np.pi * np.outer(k, k) / 64.0
    C64 = np.cos(ang) / 8.0
    S64 = np.sin(ang) / 8.0
    cs = np.zeros((128, 256), np.float64)
    cs[0:64, 0:64] = C64
    cs[64:128, 64:128] = C64
    cs[0:64, 128:192] = S64
    cs[64:128, 192:256] = S64
    c['cs64'] = cs.astype(np.float32).astype(bf16)
    L = 1024 if sample else 256
    kk = np.arange(L)
    angL = 2 * np.pi * (np.outer(kk, kk) % L) / float(L)
    CL = np.cos(angL) / np.sqrt(L)
    SL = -np.sin(angL) / np.sqrt(L)
    if sample:
        dC, dS = CL, SL
    else:
        dC = np.kron(np.eye(4), CL)
        dS = np.kron(np.eye(4), SL)
    c['dftC'] = dC.astype(np.float32).astype(bf16)
    c['dftS'] = dS.astype(np.float32).astype(bf16)
    ea = np.zeros((8, 1280), np.float32)
    fa = np.zeros((8, 1024), np.float32)
    if not sample:
        for s in range(4):
            ea[s, s * 256:(s + 1) * 256] = 1.0
            fa[s, s * 256:(s + 1) * 256] = BIG
        ea[4, 0:1024] = 1.0
        fa[4, :] = -BIG
    c['EA'] = ea.astype(bf16)
    c['FA'] = fa.astype(bf16)
    mb = np.full((128, 6, 512), NEG, np.float32)
    kl = np.arange(128)[:, None]
    ql = np.arange(128)[None, :]
    for ki in range(6):
        ktp = ki - 1
        for qi in range(4):
            off = ktp - qi
            blkm = np.full((128, 128), NEG, np.float32)
            if sample:
                if off == 0:
                    blkm[:] = 0.0
                elif off == -1:
                    blkm = np.where(ql <= kl, 0.0, NEG).astype(np.float32)
                elif off == 1:
                    blkm = np.where(kl <= ql, 0.0, NEG).astype(np.float32)
            else:
                if off == 0 or (off == 1 and qi % 2 == 0) or (off == -1 and qi % 2 == 1):
                    blkm[:] = 0.0
            mb[:, ki, qi * 128:(qi + 1) * 128] = blkm
    c['maskB'] = mb.astype(bf16)
    return c


def _na_index():
    rows = 16
    r = np.arange(rows)
    row_start = np.clip(r - 4, 0, rows - 8)
    col = np.arange(64)
    col_start = np.clip(col - 8, 0, 64 - 16)
    q = np.arange(1024)
    qr, qc = q // 64, q % 64
    kr, kc = qr, qc
    KR, QR = kr[:, None], qr[None, :]
    KC, QC = kc[:, None], qc[None, :]
    valid = (KR >= row_start[QR]) & (KR < row_start[QR] + 8) & (KC >= col_start[QC]) & (KC < col_start[QC] + 16)
    dr = np.clip(KR - QR + 7, 0, 14)
    dc = np.clip(KC - QC + 15, 0, 30)
    return valid, dr, dc


def _vecT(v):
    return np.ascontiguousarray(v.reshape(-1, 128).T)


_PROG = {}


def _prep(x_prompt, x_sample, cache_mla_ckv, cache_mla_krope, cache_win_k, cache_win_v, cache_na_k, cache_na_v,
           c, c_ctx, w_ada, b_ada, g_ffn1, w_gate1, w_up1, w_down1, g_mix, w_in, g_qa, w_uq, g_kva, w_ukv,
           qn_a, kn_a, qn_b, kn_b, sink_b, qn_d, kn_d, rpb_d, w_o, g_ffn2, w_gate2, w_up2, w_down2):
    f = lambda a: np.ascontiguousarray(np.asarray(a, dtype=np.float32))
    x_prompt, x_sample, c, c_ctx = f(x_prompt), f(x_sample), f(c), f(c_ctx)
    shared = {n: f(v) for n, v in dict(w_ada=w_ada, w_gate1=w_gate1, w_up1=w_up1, w_down1=w_down1, w_gate2=w_gate2,
                                       w_up2=w_up2, w_down2=w_down2, w_in=w_in, w_o=w_o, w_uq=w_uq, w_ukv=w_ukv).items()}
    b_ada = f(b_ada)
    shared['b_adaT'] = np.ascontiguousarray(b_ada.reshape(NL, 72, 128).transpose(0, 2, 1))
    gs = np.stack([f(g_ffn1), f(g_mix), f(g_ffn2)], axis=1)
    shared['gT'] = np.ascontiguousarray(gs.reshape(NL, 3, 8, 128).transpose(0, 3, 1, 2))
    gv = np.zeros((NL, 128, NG), np.float32)
    gv[:, :, 0:2] = f(g_qa).reshape(NL, 2, 128).transpose(0, 2, 1)
    gv[:, :, 2] = f(g_kva)
    gv[:, 0:96, 3] = f(qn_a)
    gv[:, 0:96, 4] = f(kn_a)
    gv[:, :, 5] = np.tile(f(qn_b), (1, 2))
    gv[:, :, 6] = np.tile(f(kn_b), (1, 2))
    gv[:, :, 7] = np.tile(f(qn_d), (1, 2))
    gv[:, :, 8] = np.tile(f(kn_d), (1, 2))
    gv[:, :, 9:13] = f(sink_b)[:, None, :]
    p64, p32 = _partner(64), _partner(32)
    ia = np.concatenate([np.arange(64), 64 + p32])
    gv[:, 0:96, 13] = f(qn_a)[:, ia]
    gv[:, 0:96, 14] = f(kn_a)[:, ia]
    gv[:, :, 15] = np.tile(f(qn_b)[:, p64], (1, 2))
    gv[:, :, 16] = np.tile(f(kn_b)[:, p64], (1, 2))
    w_in_f, w_uq_f = shared['w_in'], shared['w_uq']
    shared['w_inPa'] = np.ascontiguousarray(w_in_f[:, :, 384 + p32])
    qb_cols = np.concatenate([416 + h * 64 + p64 for h in range(4)])
    kb_cols = np.concatenate([672 + h * 64 + p64 for h in range(2)])
    shared['w_inPb'] = np.ascontiguousarray(w_in_f[:, :, np.concatenate([qb_cols, kb_cols])])
    shared['w_uqP'] = np.ascontiguousarray(w_uq_f[:, :, np.concatenate([h * 96 + ia for h in range(4)])])
    shared['gvec'] = gv
    consts = {True: _consts(True), False: _consts(False)}
    valid, dr, dc = _na_index()
    rpb = f(rpb_d)
    bias_s = np.where(valid[None, None], rpb[:, :, dr, dc], np.float32(NEG)).astype(np.float32)
    seq = np.arange(1024) // 256
    bias_p1 = np.where(seq[:, None] == seq[None, :], np.float32(0.0), np.float32(NEG)).astype(np.float32)
    bias_p = np.ascontiguousarray(np.broadcast_to(bias_p1, (NL, 4, 1024, 1024)))
    cm_ckv, cm_kr = f(cache_mla_ckv), f(cache_mla_krope)
    cw_k, cw_v, cn_k, cn_v = f(cache_win_k), f(cache_win_v), f(cache_na_k), f(cache_na_v)
    in_maps = []
    for core in range(8):
        sample = core >= 4
        m = dict(shared)
        m.update(consts[sample])
        if sample:
            b = core - 4
            xs = x_sample[b]
            cond = c[b]
            m['ckvT_c'] = np.ascontiguousarray(cm_ckv[b].transpose(0, 2, 1))
            m['krT_c'] = np.ascontiguousarray(cm_kr[b].transpose(0, 2, 1))
            m['winkT_c'] = np.ascontiguousarray(cw_k[b].reshape(NL, 256, 128).transpose(0, 2, 1))
            m['winv_c'] = np.ascontiguousarray(cw_v[b].reshape(NL, 256, 128))
            m['nakT_c'] = np.ascontiguousarray(cn_k[b].reshape(NL, 256, 256).transpose(0, 2, 1))
            m['nav_c'] = np.ascontiguousarray(cn_v[b].reshape(NL, 256, 256))
            m['biasD'] = bias_s
            m['ctxflag'] = np.ones((128, 1), np.float32)
        else:
            xs = x_prompt[4 * core:4 * core + 4].reshape(1024, 1024)
            cond = c_ctx
            m['ckvT_c'] = np.zeros((NL, 128, 256), np.float32)
            m['krT_c'] = np.zeros((NL, 32, 256), np.float32)
            m['winkT_c'] = np.zeros((NL, 128, 256), np.float32)
            m['winv_c'] = np.zeros((NL, 256, 128), np.float32)
            m['nakT_c'] = np.zeros((NL, 256, 256), np.float32)
            m['nav_c'] = np.zeros((NL, 256, 256), np.float32)
            m['biasD'] = bias_p
            m['ctxflag'] = np.zeros((128, 1), np.float32)
        m['xT'] = np.ascontiguousarray(xs.T)
        m['cond'] = _vecT(cond)
        in_maps.append(m)
    return in_maps


def _assemble(R):
    y_prompt = np.concatenate([R[i]['yT'].T.reshape(4, 256, 1024) for i in range(4)], axis=0)
    y_sample = np.stack([R[4 + b]['yT'].T for b in range(4)], axis=0)

    def featmaj(name, feat):
        outs = []
        for i in range(4):
            a = R[i][name]
            a = a.reshape(NL, feat, 4, 256).transpose(2, 0, 3, 1)
            outs.append(a)
        return np.ascontiguousarray(np.concatenate(outs, axis=0))

    def tokmaj(name, feat):
        outs = []
        for i in range(4):
            a = R[i][name].reshape(NL, 4, 256, feat).transpose(1, 0, 2, 3)
            outs.append(a)
        return np.ascontiguousarray(np.concatenate(outs, axis=0))

    new_ckv = featmaj('o_ckvT', 128)
    new_kr = featmaj('o_krT', 32)
    new_wk = featmaj('o_kbT', 128).reshape(16, NL, 256, 2, 64)
    new_wv = tokmaj('o_vb', 128).reshape(16, NL, 256, 2, 64)
    new_nk = featmaj('o_kdT', 256).reshape(16, NL, 256, 4, 64)
    new_nv = tokmaj('o_vd', 256).reshape(16, NL, 256, 4, 64)
    return (np.ascontiguousarray(y_prompt.astype(np.float32)), np.ascontiguousarray(y_sample.astype(np.float32)),
            new_ckv, new_kr, new_wk, new_wv, new_nk, new_nv)


def kernel(**inputs):
    in_maps = _prep(**inputs)
    if 'nc' not in _PROG:
        _PROG['nc'] = build_program(NL)
    res = run_bass_kernel_spmd(_PROG['nc'], in_maps, core_ids=list(range(8)))
    return _assemble(res.results)
```

```python
import numpy as np
import ml_dtypes
from contextlib import ExitStack
import concourse.bass as bass
import concourse.mybir as mybir
from concourse.bass_utils import run_bass_kernel_spmd

F32 = mybir.dt.float32
BF16 = mybir.dt.bfloat16
AF = mybir.ActivationFunctionType
ALU = mybir.AluOpType
bf16 = ml_dtypes.bfloat16

NL = 4
D = 1024
T = 1024
DFF = 2816
EPS = 1e-6
NEG = -30000.0
BIG = 1024.0
RING_ELEMS = 4096
NSLOT = 4
NTMP = 6
NPT = 20
NG = 17


class Prog:
    COMPUTE = ('pe', 'act', 'dve', 'pool')

    def __init__(self, nc, es, same_engine_sync=True):
        self.nc = nc
        self.es = es
        self.ops = []
        self.lastw = {}
        self.readers = {}
        self.chan_last = {}
        self.chan_cnt = {}
        self.same_engine_sync = same_engine_sync

    def add(self, eng, fn, reads=(), writes=(), chan=None, extra_deps=()):
        idx = len(self.ops)
        deps = set(extra_deps)
        if eng in ('act', 'dve'):
            pk = [k for k in reads if k == 'psM' or (isinstance(k, tuple) and k[0] == 'ps')]
            if pk:
                writes = list(writes) + pk
        for k in reads:
            w = self.lastw.get(k)
            if w is not None:
                deps.add(w)
        for k in writes:
            w = self.lastw.get(k)
            if w is not None:
                deps.add(w)
            last = {}
            for r in self.readers.get(k, ()):
                rop = self.ops[r]
                if rop['chan'] is not None:
                    deps.add(r)
                else:
                    last[rop['eng']] = r
            deps.update(last.values())
        for k in reads:
            self.readers.setdefault(k, []).append(idx)
        for k in writes:
            self.lastw[k] = idx
            self.readers[k] = []
        if chan is not None:
            p = self.chan_last.get(chan)
            if p is not None:
                deps.add(p)
            self.chan_last[chan] = idx
            self.chan_cnt[chan] = self.chan_cnt.get(chan, 0) + 1
        deps.discard(idx)
        self.ops.append(dict(eng=eng, fn=fn, deps=deps, chan=chan, signal=False, sigval=None,
                             chanval=(16 * self.chan_cnt[chan] if chan is not None else None)))
        return idx

    def emit(self):
        nc = self.nc
        ops = self.ops
        for op in ops:
            for d in op['deps']:
                dop = ops[d]
                if dop['chan'] is None:
                    if dop['eng'] == op['eng'] and (op['eng'] == 'pe' or not self.same_engine_sync):
                        continue
                    dop['signal'] = True
        cnt = {}
        for op in ops:
            if op['chan'] is None and op['signal']:
                cnt[op['eng']] = cnt.get(op['eng'], 0) + 1
                op['sigval'] = cnt[op['eng']]
        self.sig_counts = dict(cnt)
        self.chan_counts = dict(self.chan_cnt)
        self.n_ops = len(ops)
        sems = {e: self.es.enter_context(nc.semaphore('s_' + e)) for e in self.COMPUTE + ('sp',)}
        csems = {c: self.es.enter_context(nc.semaphore('c_%d' % i)) for i, c in enumerate(self.chan_cnt)}
        by_eng = {}
        for i, op in enumerate(ops):
            by_eng.setdefault(op['eng'], []).append(i)

        def run_engine(ename, e):
            waited = {}
            for i in by_eng.get(ename, ()):
                op = ops[i]
                need = {}
                for d in op['deps']:
                    dop = ops[d]
                    if dop['chan'] is not None:
                        key = ('c', dop['chan'])
                        val = dop['chanval']
                    else:
                        if dop['eng'] == ename and (ename == 'pe' or not self.same_engine_sync):
                            continue
                        key = ('e', dop['eng'])
                        val = dop['sigval']
                    if val > need.get(key, 0):
                        need[key] = val
                for key, val in need.items():
                    if waited.get(key, 0) >= val:
                        continue
                    waited[key] = val
                    s = csems[key[1]] if key[0] == 'c' else sems[key[1]]
                    e.wait_ge(s, val)
                if op['fn'] is None:
                    continue
                ins = op['fn'](e)
                if op['chan'] is not None:
                    ins.then_inc(csems[op['chan']], 16)
                elif op['signal']:
                    ins.then_inc(sems[ename], 1)

        with nc.Block() as block:
            @block.tensor
            def _(e):
                run_engine('pe', e)

            @block.scalar
            def _(e):
                run_engine('act', e)

            @block.vector
            def _(e):
                run_engine('dve', e)

            @block.gpsimd
            def _(e):
                run_engine('pool', e)

            @block.sync
            def _(e):
                run_engine('sp', e)


INPUT_SPECS = [
    ('xT', [1024, 1024], F32), ('cond', [128, 8], F32), ('ctxflag', [128, 1], F32),
    ('w_ada', [NL, 1024, 9216], F32), ('b_adaT', [NL, 128, 72], F32), ('gT', [NL, 128, 3, 8], F32),
    ('w_gate1', [NL, 1024, DFF], F32), ('w_up1', [NL, 1024, DFF], F32), ('w_down1', [NL, DFF, 1024], F32),
    ('w_gate2', [NL, 1024, DFF], F32), ('w_up2', [NL, 1024, DFF], F32), ('w_down2', [NL, DFF, 1024], F32),
    ('w_in', [NL, 1024, 1952], F32), ('w_o', [NL, 1024, 1024], F32),
    ('w_uq', [NL, 256, 384], F32), ('w_uqP', [NL, 256, 384], F32), ('w_inPa', [NL, 1024, 32], F32), ('w_inPb', [NL, 1024, 384], F32),
    ('w_ukv', [NL, 128, 512], F32), ('gvec', [NL, 128, NG], F32),
    ('ckvT_c', [NL, 128, 256], F32), ('krT_c', [NL, 32, 256], F32), ('winkT_c', [NL, 128, 256], F32),
    ('winv_c', [NL, 256, 128], F32), ('nakT_c', [NL, 256, 256], F32), ('nav_c', [NL, 256, 256], F32),
    ('identb', [128, 128], BF16), ('onesb', [128, 128], BF16), ('blk64b', [128, 128], BF16),
    ('ropeB', [128, 2, 1024], F32), ('ropeA', [128, 2, 1024], BF16),
    ('cs64', [128, 256], BF16), ('dftC', [1024, 1024], BF16), ('dftS', [1024, 1024], BF16),
    ('EA', [8, 1280], BF16), ('FA', [8, 1024], BF16), ('maskB', [128, 6, 512], BF16),
    ('biasD', [NL, 4, 1024, 1024], F32),
]
OUTPUT_SPECS = [
    ('yT', [1024, 1024]), ('o_ckvT', [NL, 128, 1024]), ('o_krT', [NL, 32, 1024]), ('o_kbT', [NL, 128, 1024]),
    ('o_vb', [NL, 1024, 128]), ('o_kdT', [NL, 256, 1024]), ('o_vd', [NL, 1024, 256]),
]


class _Stop(Exception):
    pass


def build_program(nl=NL, stop=None):
    nc = bass.Bass("TRN2", target_bir_lowering=False)
    IN = {n: nc.dram_tensor(n, list(s), dt, kind="ExternalInput").ap() for n, s, dt in INPUT_SPECS}
    OUT = {n: nc.dram_tensor(n, list(s), F32, kind="ExternalOutput").ap() for n, s in OUTPUT_SPECS}
    es = ExitStack()
    with es:
        P = Prog(nc, es)

        def sb(name, shape, dt):
            return es.enter_context(nc.sbuf_tensor(name, list(shape), dt))

        xT = sb('xT_sb', [128, 8, 1024], F32)
        hT = sb('hT', [128, 8, 1024], BF16)
        ring = [sb('ring%d' % i, [128, RING_ELEMS], BF16) for i in range(NSLOT)]
        actb = [sb('actb%d' % i, [128, 512], BF16) for i in range(8)]
        rstdN = sb('rstdN', [128, 512], F32)
        tmp = [sb('tmp%d' % i, [128, 512], F32) for i in range(NTMP)]
        sqb = [sb('sqb%d' % i, [128, 512], BF16) for i in range(3)]
        ps = [es.enter_context(nc.psum_tensor('ps%d' % i, [128, 512], F32)) for i in range(8)]
        identb = sb('identb_sb', [128, 128], BF16)
        onesb = sb('onesb_sb', [128, 128], BF16)
        blk64b = sb('blk64b_sb', [128, 128], BF16)
        ropeB = sb('ropeB_sb', [128, 2, 1024], F32)
        ropeA = sb('ropeA_sb', [128, 2, 1024], BF16)
        cs64 = sb('cs64_sb', [128, 256], BF16)
        epsT = sb('epsT', [128, 1], F32)
        gvec = sb('gvec_sb', [128, NL, NG], F32)
        gT = sb('gT_sb', [128, NL, 3, 8], F32)
        badaT = sb('badaT_sb', [128, NL, 72], F32)
        condf = sb('condf', [128, 8], F32)
        condb = sb('condb', [128, 8], BF16)
        ctxf = sb('ctxf', [128, 1], F32)
        modT = [sb('modT%d' % i, [128, 72], F32) for i in range(2)]
        modA = [sb('modA%d' % i, [128, 3, 8], F32) for i in range(2)]
        modG = [sb('modG%d' % i, [128, 3, 8], F32) for i in range(2)]
        sinkexp = sb('sinkexp', [128, 4], F32)
        rec = sb('rec', [128, 4], F32)
        QB = sb('QB', [128, 4, 1024], BF16)
        QD = sb('QD', [128, 4, 1024], BF16)
        QA = sb('QA', [128, 4, 1024], BF16)
        KB = sb('KB', [128, 1280], BF16)
        KD = sb('KD', [128, 2, 1280], BF16)
        KA = sb('KA', [128, 4, 1280], BF16)
        VA = sb('VA', [128, 10, 4, 65], BF16)
        VB = sb('VB', [128, 10, 2, 65], BF16)
        VD = sb('VD', [128, 10, 4, 65], BF16)
        PT = sb('PT', [128, NPT, 512], BF16)
        ABt = PT[:, 0:8, :].rearrange("p l (c x) -> p l c x", c=2)
        XcT = PT[:, 8:12, :].rearrange("p (c t) x -> p c (t x)", c=2)
        mtok = sb('mtok', [128, 2, 4, 128], BF16)
        cqn = sb('cqn', [128, 2, 512], BF16)
        ckvn = sb('ckvn', [128, 1024], BF16)
        sqkr = sb('sqkr', [128, 512], BF16)
        krr = rstdN
        wuq = sb('wuq', [128, 2, 384], BF16)
        wuqP = sb('wuqP', [128, 2, 384], BF16)
        wukv = sb('wukv', [128, 512], BF16)
        ckvc = sb('ckvc', [128, 256], BF16)

        st = dict(ps=0, tmp=0, sq=0, ring=0, pt=0, act=0, ld=0, out=0)
        out_ids = []

        def rot(name, n):
            i = st[name]
            st[name] = (i + 1) % n
            return i

        newps = lambda: rot('ps', 7)
        newtmp = lambda: rot('tmp', NTMP)
        newsq = lambda: rot('sq', 3)

        def MM(out, lhsT, rhs, start, stop, rd, wr):
            P.add('pe', lambda e: e.matmul(out, lhsT=lhsT, rhs=rhs, start=start, stop=stop), rd, wr)

        def ACT(out, in_, func, rd, wr, scale=None, bias=None):
            kw = {}
            if scale is not None:
                kw['scale'] = scale
            if bias is not None:
                kw['bias'] = bias
            P.add('act', lambda e: e.activation(out=out, in_=in_, func=func, **kw), rd, wr)

        def TT(out, in0, in1, op, rd, wr):
            P.add('dve', lambda e: e.tensor_tensor(out=out, in0=in0, in1=in1, op=op), rd, wr)

        def STT(out, in0, scalar, in1, op0, op1, rd, wr):
            P.add('dve', lambda e: e.scalar_tensor_tensor(out=out, in0=in0, scalar=scalar, in1=in1, op0=op0, op1=op1), rd, wr)

        def TS(out, in0, s1, op0, rd, wr):
            P.add('dve', lambda e: e.tensor_scalar(out=out, in0=in0, scalar1=s1, scalar2=None, op0=op0), rd, wr)

        def RECIP(out, in_, rd, wr):
            P.add('dve', lambda e: e.reciprocal(out=out, in_=in_), rd, wr)

        def RECIPF(out, in_, rd, wr):
            P.add('dve', lambda e: e.reciprocal_approx_fast(out=out, in_=in_), rd, wr)

        def DMA(q, out, in_, rd, wr, chan):
            return P.add(q, lambda e: e.dma_start(out=out, in_=in_), rd, wr, chan=chan)

        def LOAD(out, in_, keys, q='sp'):
            return DMA(q, out, in_, [], keys, 'ld%d' % rot('ld', 4))

        def STORE(out, in_, rd):
            out_ids.append(DMA('sp', out, in_, rd, [], 'st%d' % rot('out', 4)))

        def ring_get(src_ap, a, b):
            s = rot('ring', NSLOT)
            view = ring[s][:, 0:a * b].rearrange("p (a b) -> p a b", b=b)
            DMA('pool', view, src_ap, [], [('ring', s)], 'ring%d' % s)
            return view, ('ring', s)

        cur_layer = [0]

        def CK(name):
            if stop == name or stop == '%s@%d' % (name, cur_layer[0]):
                raise _Stop()

        def kcp(ap):
            return ap.rearrange("(kc p) n -> p kc n", p=128)

        C = 'consts'
        cl = []
        for name, tile in [('identb', identb), ('onesb', onesb), ('blk64b', blk64b),
                           ('ropeB', ropeB), ('ropeA', ropeA), ('cs64', cs64),
                           ('cond', condf), ('ctxflag', ctxf)]:
            cl.append(LOAD(tile[:], IN[name], []))
        cl.append(LOAD(gvec[:], IN['gvec'].rearrange("l p g -> p l g"), []))
        cl.append(LOAD(gT[:], IN['gT'].rearrange("l p k c -> p l k c"), []))
        cl.append(LOAD(badaT[:], IN['b_adaT'].rearrange("l p j -> p l j"), []))
        xkeys = [('x', c, t) for c in range(8) for t in range(2)]
        LOAD(xT[:], kcp(IN['xT']), xkeys)
        P.add('dve', lambda e: e.memset(VA[:].rearrange("p a b c -> p (a b c)"), 1.0), [], ['Vinit'])
        P.add('dve', lambda e: e.memset(VB[:].rearrange("p a b c -> p (a b c)"), 1.0), [], ['Vinit'])
        P.add('dve', lambda e: e.memset(VD[:].rearrange("p a b c -> p (a b c)"), 1.0), [], ['Vinit'])
        zi = []
        for tl in (QB, QD, QA, KA):
            zi.append(P.add('dve', lambda e, tl=tl: e.memset(tl[:].rearrange("p a b -> p (a b)"), 0.0), [], []))
        for h in range(4):
            cl.append(P.add('sp', lambda e, h=h: e.dma_start(out=KA[96:104, h, :], in_=IN['EA']), [], [], chan='ld%d' % rot('ld', 4), extra_deps=zi))
            cl.append(P.add('sp', lambda e, h=h: e.dma_start(out=QA[96:104, h, :], in_=IN['FA']), [], [], chan='ld%d' % rot('ld', 4), extra_deps=zi))
        P.add('dve', lambda e: e.memset(epsT[:], EPS), [], [C], extra_deps=cl + zi)
        ACT(condb[:], condf[:], AF.Silu, [C], ['condb'])
        for Vt, nh in ((VA, 4), (VB, 2), (VD, 4)):
            for kt in (8, 9):
                for h in range(nh):
                    ACT(Vt[:, kt, h, 64:65], ctxf[:, 0:1], AF.Copy, ['Vinit', C], ['Vinit'])

        def rstd_from(ps_ap, pskey, inv_d, p0=0, p1=128, n=512):
            ti = newtmp()
            t = tmp[ti][p0:p1, 0:n]
            ACT(t, ps_ap, AF.Ln, [pskey, C], [('tmp', ti)], scale=inv_d, bias=epsT[p0:p1, 0:1])
            ACT(t, t, AF.Exp, [('tmp', ti)], [('tmp', ti)], scale=-0.5)
            return ti

        def mod_steps(l):
            par = l % 2
            for k in range(3):
                for b6 in range(6):
                    blk = k * 6 + b6
                    wt, wk = ring_get(kcp(IN['w_ada'][l][:, blk * 512:(blk + 1) * 512]), 8, 512)
                    for j in range(4):
                        col = blk * 4 + j
                        for kc in range(8):
                            MM(ps[7][:, col:col + 1], wt[:, kc, j * 128:(j + 1) * 128], condb[:, kc:kc + 1], kc == 0, kc == 7,
                               [wk, 'condb'], ['psM'])
                    yield
                c0 = 24 * k
                TT(modT[par][:, c0:c0 + 24], ps[7][:, c0:c0 + 24], badaT[:, l, c0:c0 + 24], ALU.add, ['psM', C], [('mod', par, k)])
                STT(modA[par][:, k, :], modT[par][:, c0 + 8:c0 + 16], 1.0, gT[:, l, k, :], ALU.add, ALU.mult,
                    [('mod', par, k), C], [('modA', par, k)])
                TS(modG[par][:, k, :], modT[par][:, c0 + 16:c0 + 24], (1.0 if k == 1 else 0.5), ALU.mult, [('mod', par, k)], [('modG', par, k)])
            yield

        def norm_mod(l, k):
            par = l % 2
            for t in range(2):
                hs = slice(t * 512, (t + 1) * 512)
                pi = newps()
                for c in range(8):
                    si = newsq()
                    ACT(sqb[si][:, :], xT[:, c, hs], AF.Square, [('x', c, t)], [('sq', si)])
                    MM(ps[pi][:, :], onesb[:, :], sqb[si][:, :], c == 0, c == 7, [('sq', si), C], [('ps', pi)])
                ACT(rstdN[:, :], ps[pi][:, :], AF.Ln, [('ps', pi), C], ['rstdN'], scale=1.0 / 1024, bias=epsT[:, 0:1])
                ACT(rstdN[:, :], rstdN[:, :], AF.Exp, ['rstdN'], ['rstdN'], scale=-0.5)
                for c in range(8):
                    ti = newtmp()
                    STT(tmp[ti][:, :], xT[:, c, hs], modA[par][:, k, c:c + 1], rstdN[:, :], ALU.mult, ALU.mult,
                        [('x', c, t), ('modA', par, k), 'rstdN'], [('tmp', ti)])
                    ACT(hT[:, c, hs], tmp[ti][:, :], AF.Identity, [('tmp', ti), ('mod', par, k)], [('h', c, t)],
                        bias=modT[par][:, 3 * k * 8 + c:3 * k * 8 + c + 1])

        def ffn(l, which, modgen):
            par = l % 2
            wg = IN['w_gate%d' % which][l]
            wu = IN['w_up%d' % which][l]
            wd = IN['w_down%d' % which][l]
            gk = 0 if which == 1 else 2
            for blk in range(6):
                nch = 4 if blk < 5 else 2
                c0 = blk * 512
                ncol = nch * 128
                gw, gkey = ring_get(kcp(wg[:, c0:c0 + ncol]), 8, ncol)
                uw, ukey = ring_get(kcp(wu[:, c0:c0 + ncol]), 8, ncol)
                dw, dkey = ring_get(wd[c0:c0 + ncol, :].rearrange("(j p) n -> p j n", p=128), nch, 1024)
                for t in range(2):
                    hs = slice(t * 512, (t + 1) * 512)
                    aslots = []
                    for j in range(nch):
                        ai = rot('act', 8)
                        aslots.append(ai)
                        pg = newps()
                        for kc in range(8):
                            MM(ps[pg][:, :], gw[:, kc, j * 128:(j + 1) * 128], hT[:, kc, hs], kc == 0, kc == 7,
                               [gkey, ('h', kc, t)], [('ps', pg)])
                        pu = newps()
                        for kc in range(8):
                            MM(ps[pu][:, :], uw[:, kc, j * 128:(j + 1) * 128], hT[:, kc, hs], kc == 0, kc == 7,
                               [ukey, ('h', kc, t)], [('ps', pu)])
                        ti = newtmp()
                        ACT(tmp[ti][:, :], ps[pg][:, :], AF.Silu, [('ps', pg)], [('tmp', ti)])
                        TT(actb[ai][:, :], tmp[ti][:, :], ps[pu][:, :], ALU.mult, [('tmp', ti), ('ps', pu)], [('act', ai)])
                    for dc in range(8):
                        po = newps()
                        for j in range(nch):
                            MM(ps[po][:, :], dw[:, j, dc * 128:(dc + 1) * 128], actb[aslots[j]][:, :], j == 0, j == nch - 1,
                               [dkey, ('act', aslots[j])], [('ps', po)])
                        STT(xT[:, dc, hs], ps[po][:, :], modG[par][:, gk, dc:dc + 1], xT[:, dc, hs], ALU.mult, ALU.add,
                            [('ps', po), ('modG', par, gk), ('x', dc, t)], [('x', dc, t)])
                if modgen is not None:
                    modgen()
                    if l == 0:
                        modgen()

        def rope_norm(psraw, pskey, p0, p1, ones_lhsT, inv_d, gain, out16, outkeys, rope=None, store=None, n=512, split=None, psf=None, gainf=None):
            si = newsq()
            ACT(sqb[si][p0:p1, 0:n], psraw, AF.Square, [pskey], [('sq', si)])
            pss = newps()
            MM(ps[pss][p0:p1, 0:n], ones_lhsT, sqb[si][p0:p1, 0:n], True, True, [('sq', si), C], [('ps', pss)])
            ri = rstd_from(ps[pss][p0:p1, 0:n], ('ps', pss), inv_d, p0, p1, n)
            r = tmp[ri][p0:p1, 0:n]
            if rope is None and store is None and split is not None:
                for (a_, b_, ap_, ks_) in split:
                    STT(ap_, psf(a_, b_), gainf(a_, b_), tmp[ri][a_:b_, 0:n], ALU.mult, ALU.mult, [pskey, ('tmp', ri), C], ks_)
                return
            if rope is None and store is None:
                STT(out16, psraw, gain, r, ALU.mult, ALU.mult, [pskey, ('tmp', ri), C], outkeys)
                return
            if rope is None:
                xi = newtmp()
                xn = tmp[xi][p0:p1, 0:n]
                STT(xn, psraw, gain, r, ALU.mult, ALU.mult, [pskey, ('tmp', ri), C], [('tmp', xi)])
                STORE(store, xn, [('tmp', xi)])
                ACT(out16, xn, AF.Copy, [('tmp', xi)], outkeys)
                return
            psraw_p, pskey_p, gain_p, ropeC, ropeS = rope
            t1 = newtmp()
            STT(tmp[t1][p0:p1, 0:n], psraw, gain, ropeC, ALU.mult, ALU.mult, [pskey, C], [('tmp', t1)])
            t2 = newtmp()
            STT(tmp[t2][p0:p1, 0:n], psraw_p, gain_p, ropeS, ALU.mult, ALU.mult, [pskey_p, C], [('tmp', t2)])
            TT(tmp[t1][p0:p1, 0:n], tmp[t1][p0:p1, 0:n], tmp[t2][p0:p1, 0:n], ALU.add, [('tmp', t1), ('tmp', t2)], [('tmp', t1)])
            if split is not None:
                for (a_, b_, ap_, ks_) in split:
                    TT(ap_, tmp[t1][a_:b_, 0:n], tmp[ri][a_:b_, 0:n], ALU.mult, [('tmp', t1), ('tmp', ri)], ks_)
            elif store is None:
                TT(out16, tmp[t1][p0:p1, 0:n], r, ALU.mult, [('tmp', t1), ('tmp', ri)], outkeys)
            else:
                TT(tmp[t1][p0:p1, 0:n], tmp[t1][p0:p1, 0:n], r, ALU.mult, [('tmp', t1), ('tmp', ri)], [('tmp', t1)])
                STORE(store, tmp[t1][p0:p1, 0:n], [('tmp', t1)])
                ACT(out16, tmp[t1][p0:p1, 0:n], AF.Copy, [('tmp', t1)], outkeys)

        def mixer(l, modgen):
            par = l % 2

            def gv(col, p0=0, p1=128):
                return gvec[p0:p1, l, col:col + 1]

            def step():
                if modgen is not None:
                    modgen()

            DMA('pool', wuq[:], IN['w_uq'][l].rearrange("(c p) n -> p c n", p=128), [], ['wuq'], 'cx0')
            DMA('pool', wuqP[:], IN['w_uqP'][l].rearrange("(c p) n -> p c n", p=128), [], ['wuqP'], 'cx0')
            DMA('pool', wukv[:], IN['w_ukv'][l], [], ['wukv'], 'cx1')
            DMA('pool', ckvc[:], IN['ckvT_c'][l], [], ['ckvc'], 'cx2')
            DMA('pool', KB[:, 1024:1280], IN['winkT_c'][l], [], [('KB', 'c')], 'cx4')
            DMA('pool', KD[:, :, 1024:1280], IN['nakT_c'][l].rearrange("(c p) k -> p c k", p=128), [], [('KD', 0, 'c'), ('KD', 1, 'c')], 'cx5')
            for i in range(2):
                DMA('pool', VB[:, 8 + i, :, 0:64], IN['winv_c'][l][i * 128:(i + 1) * 128, :].rearrange("p (h d) -> p h d", d=64),
                    [], [('VB', 8 + i)], 'cx6')
                DMA('pool', VD[:, 8 + i, :, 0:64], IN['nav_c'][l][i * 128:(i + 1) * 128, :].rearrange("p (h d) -> p h d", d=64),
                    [], [('VD', 8 + i)], 'cx7')
            ACT(sinkexp[:, :], gvec[:, l, 9:13], AF.Exp, [C], ['sinkexp'])
            wukv_v = wukv[:, 0:512].rearrange("p (h x) -> p h x", x=128)[:, :, 64:128]

            b0, b0k = ring_get(kcp(IN['w_in'][l][:, 0:416]), 8, 416)
            bpa, bpak = ring_get(kcp(IN['w_inPa'][l]), 8, 32)
            for t in range(2):
                hs = slice(t * 512, (t + 1) * 512)
                pcs = []
                for c in range(2):
                    pi = newps()
                    pcs.append(pi)
                    for kc in range(8):
                        MM(ps[pi][:, :], b0[:, kc, c * 128:(c + 1) * 128], hT[:, kc, hs], kc == 0, kc == 7, [b0k, ('h', kc, t)], [('ps', pi)])
                pss = newps()
                for c in range(2):
                    si = newsq()
                    ACT(sqb[si][:, :], ps[pcs[c]][:, :], AF.Square, [('ps', pcs[c])], [('sq', si)])
                    MM(ps[pss][:, :], onesb[:, :], sqb[si][:, :], c == 0, c == 1, [('sq', si), C], [('ps', pss)])
                ri = rstd_from(ps[pss][:, :], ('ps', pss), 1.0 / 256)
                for c in range(2):
                    STT(cqn[:, c, :], ps[pcs[c]][:, :], gv(c), tmp[ri][:, :], ALU.mult, ALU.mult,
                        [('ps', pcs[c]), ('tmp', ri), C], [('cqn', c)])
                pk = newps()
                for kc in range(8):
                    MM(ps[pk][:, :], b0[:, kc, 256:384], hT[:, kc, hs], kc == 0, kc == 7, [b0k, ('h', kc, t)], [('ps', pk)])
                rope_norm(ps[pk][:, :], ('ps', pk), 0, 128, onesb[:, :], 1.0 / 128, gv(2), ckvn[:, hs], [('ckvn', t)],
                          rope=None, store=OUT['o_ckvT'][l][:, hs])
                pr = newps()
                for kc in range(8):
                    MM(ps[pr][64:96, :], b0[:, kc, 384:416], hT[:, kc, hs], kc == 0, kc == 7, [b0k, ('h', kc, t)], [('ps', pr)])
                ti = newtmp()
                ACT(tmp[ti][64:96, :], ps[pr][64:96, :], AF.Copy, [('ps', pr)], [('tmp', ti)])
                STORE(OUT['o_krT'][l][:, hs], tmp[ti][64:96, :], [('tmp', ti)])
                ACT(sqkr[64:96, :], ps[pr][64:96, :], AF.Square, [('ps', pr)], ['sqkr'])
                prp = newps()
                for kc in range(8):
                    MM(ps[prp][64:96, :], bpa[:, kc, 0:32], hT[:, kc, hs], kc == 0, kc == 7, [bpak, ('h', kc, t)], [('ps', prp)])
                t1 = newtmp()
                STT(tmp[t1][64:96, :], ps[prp][64:96, :], gv(14, 64, 96), ropeA[64:96, 1, hs], ALU.mult, ALU.mult, [('ps', prp), C], [('tmp', t1)])
                STT(krr[64:96, :], ps[pr][64:96, :], gv(4, 64, 96), ropeA[64:96, 0, hs], ALU.mult, ALU.mult, [('ps', pr), C], ['rstdN'])
                TT(krr[64:96, :], krr[64:96, :], tmp[t1][64:96, :], ALU.add, ['rstdN', ('tmp', t1)], ['rstdN'])
                for h in range(4):
                    pn = newps()
                    MM(ps[pn][0:64, :], wukv[:, h * 128:h * 128 + 64], ckvn[:, hs], True, True, ['wukv', ('ckvn', t)], [('ps', pn)])
                    si = newsq()
                    ACT(sqb[si][0:64, :], ps[pn][0:64, :], AF.Square, [('ps', pn)], [('sq', si)])
                    pss = newps()
                    MM(ps[pss][0:96, :], onesb[0:64, 0:96], sqb[si][0:64, :], True, False, [('sq', si), C], [('ps', pss)])
                    MM(ps[pss][0:96, :], onesb[64:96, 0:96], sqkr[64:96, :], False, True, ['sqkr', C], [('ps', pss)])
                    ri = rstd_from(ps[pss][0:96, :], ('ps', pss), 1.0 / 96, 0, 96)
                    STT(KA[0:64, h, hs], ps[pn][0:64, :], gv(4, 0, 64), tmp[ri][0:64, :], ALU.mult, ALU.mult,
                        [('ps', pn), ('tmp', ri), C], [('KA', h, t)])
                    TT(KA[64:96, h, hs], krr[64:96, :], tmp[ri][64:96, :], ALU.mult, ['rstdN', ('tmp', ri)], [('KA', h, t)])
                    pq = newps()
                    for c in range(2):
                        MM(ps[pq][0:96, :], wuq[:, c, h * 96:(h + 1) * 96], cqn[:, c, :], c == 0, c == 1, ['wuq', ('cqn', c)], [('ps', pq)])
                    pqp = newps()
                    for c in range(2):
                        MM(ps[pqp][0:96, :], wuqP[:, c, h * 96:(h + 1) * 96], cqn[:, c, :], c == 0, c == 1, ['wuqP', ('cqn', c)], [('ps', pqp)])
                    rope_norm(ps[pq][0:96, :], ('ps', pq), 0, 96, onesb[0:96, 0:96], 1.0 / 96, gv(3, 0, 96), QA[0:96, h, hs],
                              [('QA', h, t)], rope=(ps[pqp][0:96, :], ('ps', pqp), gv(13, 0, 96), ropeA[0:96, 0, hs], ropeA[0:96, 1, hs]))
                for tt in range(4):
                    kt = t * 4 + tt
                    pv = newps()
                    MM(ps[pv][:, 0:256], ckvn[:, kt * 128:(kt + 1) * 128], wukv_v, True, True, ['wukv', ('ckvn', t)], [('ps', pv)])
                    ACT(VA[:, kt, :, 0:64], ps[pv][:, 0:256].rearrange("p (h d) -> p h d", d=64), AF.Copy, [('ps', pv), 'Vinit'], [('VA', kt)])
            step()
            CK('projA')
            tkc = newtmp()
            krc_ap = tmp[tkc][64:96, 0:256]
            DMA('sp', krc_ap, IN['krT_c'][l], [], [('tmp', tkc)], 'cx3')
            si = newsq()
            ACT(sqb[si][64:96, 0:256], krc_ap, AF.Square, [('tmp', tkc)], [('sq', si)])
            sq_krc = si
            tgc = newtmp()
            TS(tmp[tgc][64:96, 0:256], krc_ap, gv(4, 64, 96), ALU.mult, [('tmp', tkc), C], [('tmp', tgc)])
            for h in range(4):
                pn = newps()
                MM(ps[pn][0:64, 0:256], wukv[:, h * 128:h * 128 + 64], ckvc[:, :], True, True, ['wukv', 'ckvc'], [('ps', pn)])
                si = newsq()
                if si == sq_krc:
                    si = newsq()
                ACT(sqb[si][0:64, 0:256], ps[pn][0:64, 0:256], AF.Square, [('ps', pn)], [('sq', si)])
                pss = newps()
                MM(ps[pss][0:96, 0:256], onesb[0:64, 0:96], sqb[si][0:64, 0:256], True, False, [('sq', si), C], [('ps', pss)])
                MM(ps[pss][0:96, 0:256], onesb[64:96, 0:96], sqb[sq_krc][64:96, 0:256], False, True, [('sq', sq_krc), C], [('ps', pss)])
                ri = rstd_from(ps[pss][0:96, 0:256], ('ps', pss), 1.0 / 96, 0, 96, 256)
                if ri == tgc:
                    raise RuntimeError("tmp rotation clash")
                STT(KA[0:64, h, 1024:1280], ps[pn][0:64, 0:256], gv(4, 0, 64), tmp[ri][0:64, 0:256], ALU.mult, ALU.mult,
                    [('ps', pn), ('tmp', ri), C], [('KA', h, 'c')])
                TT(KA[64:96, h, 1024:1280], tmp[tgc][64:96, 0:256], tmp[ri][64:96, 0:256], ALU.mult, [('tmp', tgc), ('tmp', ri)], [('KA', h, 'c')])
            for i in range(2):
                pv = newps()
                MM(ps[pv][:, 0:256], ckvc[:, i * 128:(i + 1) * 128], wukv_v, True, True, ['wukv', 'ckvc'], [('ps', pv)])
                ACT(VA[:, 8 + i, :, 0:64], ps[pv][:, 0:256].rearrange("p (h d) -> p h d", d=64), AF.Copy, [('ps', pv), 'Vinit'], [('VA', 8 + i)])
            step()
            CK('ctxA')

            b1, b1k = ring_get(kcp(IN['w_in'][l][:, 416:928]), 8, 512)
            b1p, b1pk = ring_get(kcp(IN['w_inPb'][l]), 8, 384)
            for t in range(2):
                hs = slice(t * 512, (t + 1) * 512)
                for ci, hpair in enumerate(((0, 2), (1, 3))):
                    pi = newps()
                    for hh, pb in zip(hpair, (0, 64)):
                        for kc in range(8):
                            MM(ps[pi][pb:pb + 64, :], b1[:, kc, hh * 64:(hh + 1) * 64], hT[:, kc, hs], kc == 0, kc == 7,
                               [b1k, ('h', kc, t)], [('ps', pi)])
                    pip = newps()
                    for hh, pb in zip(hpair, (0, 64)):
                        for kc in range(8):
                            MM(ps[pip][pb:pb + 64, :], b1p[:, kc, hh * 64:(hh + 1) * 64], hT[:, kc, hs], kc == 0, kc == 7,
                               [b1pk, ('h', kc, t)], [('ps', pip)])
                    rope_norm(ps[pi][:, :], ('ps', pi), 0, 128, blk64b[:, :], 1.0 / 64, gv(5), None, None,
                              rope=(ps[pip][:, :], ('ps', pip), gv(15), ropeB[:, 0, hs], ropeB[:, 1, hs]),
                              split=[(0, 64, QB[0:64, hpair[0], hs], [('QB', hpair[0], t)]),
                                     (64, 128, QB[64:128, hpair[1], hs], [('QB', hpair[1], t)])])
                    CK('qb')
                pi = newps()
                for kc in range(8):
                    MM(ps[pi][:, :], b1[:, kc, 256:384], hT[:, kc, hs], kc == 0, kc == 7, [b1k, ('h', kc, t)], [('ps', pi)])
                pip = newps()
                for kc in range(8):
                    MM(ps[pip][:, :], b1p[:, kc, 256:384], hT[:, kc, hs], kc == 0, kc == 7, [b1pk, ('h', kc, t)], [('ps', pip)])
                rope_norm(ps[pi][:, :], ('ps', pi), 0, 128, blk64b[:, :], 1.0 / 64, gv(6), KB[:, hs], [('KB', t)],
                          rope=(ps[pip][:, :], ('ps', pip), gv(16), ropeB[:, 0, hs], ropeB[:, 1, hs]), store=OUT['o_kbT'][l][:, hs])
                CK('kb')
                for tt in range(4):
                    kt = t * 4 + tt
                    pv = newps()
                    for kc in range(8):
                        MM(ps[pv][:, 0:128], hT[:, kc, kt * 128:(kt + 1) * 128], b1[:, kc, 384:512], kc == 0, kc == 7,
                           [b1k, ('h', kc, t)], [('ps', pv)])
                    ti = newtmp()
                    ACT(tmp[ti][:, 0:128], ps[pv][:, 0:128], AF.Copy, [('ps', pv)], [('tmp', ti)])
                    STORE(OUT['o_vb'][l][kt * 128:(kt + 1) * 128, :], tmp[ti][:, 0:128], [('tmp', ti)])
                    ACT(VB[:, kt, :, 0:64], ps[pv][:, 0:128].rearrange("p (h d) -> p h d", d=64), AF.Copy, [('ps', pv), 'Vinit'], [('VB', kt)])
            step()
            CK('projB')
            b2, b2k = ring_get(kcp(IN['w_in'][l][:, 928:1440]), 8, 512)
            for t in range(2):
                hs = slice(t * 512, (t + 1) * 512)
                for ch in range(2):
                    pi = newps()
                    for kc in range(8):
                        MM(ps[pi][:, :], b2[:, kc, ch * 128:(ch + 1) * 128], hT[:, kc, hs], kc == 0, kc == 7, [b2k, ('h', kc, t)], [('ps', pi)])
                    ACT(XcT[:, ch, hs], ps[pi][:, :], AF.Copy, [('ps', pi)], [('pt', 8 + 2 * ch + t)])
                for ch in range(2):
                    pi = newps()
                    for kc in range(8):
                        MM(ps[pi][:, :], b2[:, kc, 256 + ch * 128:256 + (ch + 1) * 128], hT[:, kc, hs], kc == 0, kc == 7,
                           [b2k, ('h', kc, t)], [('ps', pi)])
                    rope_norm(ps[pi][:, :], ('ps', pi), 0, 128, blk64b[:, :], 1.0 / 64, gv(7), None, None,
                              split=[(0, 64, QD[0:64, 2 * ch, hs], [('QD', 2 * ch, t)]),
                                     (64, 128, QD[64:128, 2 * ch + 1, hs], [('QD', 2 * ch + 1, t)])],
                              psf=lambda a_, b_, pi=pi: ps[pi][a_:b_, :], gainf=lambda a_, b_: gv(7, a_, b_))
            step()
            b3, b3k = ring_get(kcp(IN['w_in'][l][:, 1440:1952]), 8, 512)
            for t in range(2):
                hs = slice(t * 512, (t + 1) * 512)
                for ch in range(2):
                    pi = newps()
                    for kc in range(8):
                        MM(ps[pi][:, :], b3[:, kc, ch * 128:(ch + 1) * 128], hT[:, kc, hs], kc == 0, kc == 7, [b3k, ('h', kc, t)], [('ps', pi)])
                    rope_norm(ps[pi][:, :], ('ps', pi), 0, 128, blk64b[:, :], 1.0 / 64, gv(8), KD[:, ch, hs], [('KD', ch, t)],
                              rope=None, store=OUT['o_kdT'][l][ch * 128:(ch + 1) * 128, hs])
                for tt in range(4):
                    kt = t * 4 + tt
                    pv = newps()
                    for kc in range(8):
                        MM(ps[pv][:, 0:256], hT[:, kc, kt * 128:(kt + 1) * 128], b3[:, kc, 256:512], kc == 0, kc == 7,
                           [b3k, ('h', kc, t)], [('ps', pv)])
                    ti = newtmp()
                    ACT(tmp[ti][:, 0:256], ps[pv][:, 0:256], AF.Copy, [('ps', pv)], [('tmp', ti)])
                    STORE(OUT['o_vd'][l][kt * 128:(kt + 1) * 128, :], tmp[ti][:, 0:256], [('tmp', ti)])
                    ACT(VD[:, kt, :, 0:64], ps[pv][:, 0:256].rearrange("p (h d) -> p h d", d=64), AF.Copy, [('ps', pv), 'Vinit'], [('VD', kt)])
            step()
            CK('projD')
            for lt in range(8):
                for ch in range(2):
                    pi = newps()
                    MM(ps[pi][:, 0:256], XcT[:, ch, lt * 128:(lt + 1) * 128], cs64[:, :], True, True, [('pt', 8 + 2 * ch + lt // 4), C], [('ps', pi)])
                    P.add('dve', lambda e, lt=lt, ch=ch, pi=pi: e.tensor_copy(out=ABt[:, lt, ch, :], in_=ps[pi][:, 0:256]),
                          [('ps', pi)], [('ab', lt, ch)])
            for t in range(2):
                hs = slice(t * 512, (t + 1) * 512)
                cb, cbk = ring_get(IN['dftC'][:, hs].rearrange("(lt p) n -> p lt n", p=128), 8, 512)
                sbk_, sbkk = ring_get(IN['dftS'][:, hs].rearrange("(lt p) n -> p lt n", p=128), 8, 512)
                for ch in range(2):
                    pi = newps()
                    for lt in range(8):
                        MM(ps[pi][:, :], ABt[:, lt, ch, 0:128], cb[:, lt, :], lt == 0, False, [('ab', lt, ch), ('pt', lt), cbk], [('ps', pi)])
                    for lt in range(8):
                        MM(ps[pi][:, :], ABt[:, lt, ch, 128:256], sbk_[:, lt, :], False, lt == 7, [('ab', lt, ch), ('pt', lt), sbkk], [('ps', pi)])
                    ACT(hT[:, 4 + ch, hs], ps[pi][:, :], AF.Copy, [('ps', pi)], [('h', 4 + ch, t)])
            step()
            CK('fourier')

            shared = {}

            def unit_kts(kind, t):
                if kind == 'B':
                    return [kt for kt in range(4 * t - 1, 4 * t + 5) if 0 <= kt <= 7] + [8, 9]
                if kind == 'D':
                    return list(range(0, 6) if t == 0 else range(2, 8)) + [8, 9]
                return list(range(10))

            def prefetch(kind, h):
                if kind == 'B':
                    if 'maskB' not in shared:
                        shared['maskB'] = ring_get(IN['maskB'], 6, 512)
                elif kind == 'D':
                    if ('biasD', h) not in shared:
                        shared[('biasD', h)] = [ring_get(IN['biasD'][l][h][2 * g * 128:(2 * g + 6) * 128, g * 512:(g + 1) * 512]
                                                         .rearrange("(kt p) q -> p kt q", p=128), 6, 512) for g in range(2)]

            def attend_qk(kind, h, t):
                hs = slice(t * 512, (t + 1) * 512)
                kts = unit_kts(kind, t)
                prefetch(kind, h)
                slots = {}
                for kt in kts:
                    pi = newps()
                    ksl = slice(kt * 128, (kt + 1) * 128)
                    own = kt < 8
                    kk = (kt // 4) if own else 'c'
                    if kind == 'A':
                        MM(ps[pi][:, :], KA[:, h, ksl], QA[:, h, hs], True, True, [('KA', h, kk), ('QA', h, t), C], [('ps', pi)])
                        scale = 96.0 ** -0.5
                    elif kind == 'B':
                        MM(ps[pi][:, :], KB[:, ksl], QB[:, h, hs], True, not own, [('KB', kk), ('QB', h, t), C], [('ps', pi)])
                        if own:
                            mv, mk = shared['maskB']
                            MM(ps[pi][:, :], identb[:, :], mv[:, kt - 4 * t + 1, :], False, True, [C, mk], [('ps', pi)])
                        scale = 0.125
                    else:
                        ch = h // 2
                        MM(ps[pi][:, :], KD[:, ch, ksl], QD[:, h, hs], True, not own, [('KD', ch, kk), ('QD', h, t), C], [('ps', pi)])
                        if own:
                            bv, bk = shared[('biasD', h)][t]
                            MM(ps[pi][:, :], identb[:, :], bv[:, kt - 2 * t, :], False, True, [C, bk], [('ps', pi)])
                        scale = 0.125
                    s = rot('pt', NPT)
                    slots[kt] = s
                    ACT(PT[:, s, :], ps[pi][:, :], AF.Exp, [('ps', pi)], [('pt', s)], scale=scale)
                return (kind, h, t, kts, slots)

            def attend_pv(ctx):
                kind, h, t, kts, slots = ctx
                V, vh, vname = {'A': (VA, h, 'VA'), 'B': (VB, h // 2, 'VB'), 'D': (VD, h, 'VD')}[kind]
                po = newps()
                for qi in range(4):
                    for i, kt in enumerate(kts):
                        MM(ps[po][:, qi * 65:(qi + 1) * 65], PT[:, slots[kt], qi * 128:(qi + 1) * 128], V[:, kt, vh, 0:65],
                           i == 0, i == len(kts) - 1, [('pt', slots[kt]), (vname, kt), 'Vinit'], [('ps', po)])
                den = ps[po][:, 0:260].rearrange("p (q x) -> p q x", x=65)[:, :, 64:65]
                rec3 = rec[:, 0:4].rearrange("p (q x) -> p q x", x=1)
                if kind == 'B':
                    TS(rec3, den, sinkexp[:, h:h + 1], ALU.add, [('ps', po), 'sinkexp'], ['rec'])
                    RECIP(rec[:, 0:4], rec[:, 0:4], ['rec'], ['rec'])
                else:
                    RECIP(rec3, den, [('ps', po)], ['rec'])
                for qi in range(4):
                    hslot = h % 2
                    TS(mtok[:, t, qi, hslot * 64:(hslot + 1) * 64], ps[po][:, qi * 65:qi * 65 + 64], rec[:, qi:qi + 1], ALU.mult,
                       [('ps', po), 'rec'], [('mtok', t, qi, hslot)])

            def attend_tr(chunk):
                for t in range(2):
                    hs = slice(t * 512, (t + 1) * 512)
                    pt_ = newps()
                    for qi in range(4):
                        MM(ps[pt_][:, qi * 128:(qi + 1) * 128], mtok[:, t, qi, :], identb[:, :], True, True,
                           [('mtok', t, qi, 0), ('mtok', t, qi, 1), C], [('ps', pt_)])
                    P.add('dve', lambda e, hs=hs, pt_=pt_, cc=chunk: e.tensor_copy(out=hT[:, cc, hs], in_=ps[pt_][:, :]),
                          [('ps', pt_)], [('h', chunk, t)])

            units = []
            for kind, chunk0 in (('A', 0), ('B', 2), ('D', 6)):
                for pair in range(2):
                    for h in (2 * pair, 2 * pair + 1):
                        for t in range(2):
                            units.append((kind, h, t, chunk0 + pair, (h % 2 == 1 and t == 1)))
            prev = None

            pending = []

            def flush(pv_):
                for ch_ in pending:
                    attend_tr(ch_)
                    step()
                del pending[:]
                attend_pv(pv_[0])
                if pv_[2]:
                    pending.append(pv_[1])

            for ui, (kind, h, t, chunk, last) in enumerate(units):
                if prev is not None and len(prev[0][3]) + len(unit_kts(kind, t)) > NPT:
                    flush(prev)
                    prev = None
                ctx = attend_qk(kind, h, t)
                if ui + 1 < len(units):
                    prefetch(units[ui + 1][0], units[ui + 1][1])
                if prev is not None:
                    flush(prev)
                prev = (ctx, chunk, last)
            flush(prev)
            for ch_ in pending:
                attend_tr(ch_)
                step()
            CK('attn')

            for blk in range(2):
                wo, wok = ring_get(kcp(IN['w_o'][l][:, blk * 512:(blk + 1) * 512]), 8, 512)
                for dcc in range(4):
                    dc = blk * 4 + dcc
                    for t in range(2):
                        hs = slice(t * 512, (t + 1) * 512)
                        po = newps()
                        for c in range(8):
                            MM(ps[po][:, :], wo[:, c, dcc * 128:(dcc + 1) * 128], hT[:, c, hs], c == 0, c == 7, [wok, ('h', c, t)], [('ps', po)])
                        STT(xT[:, dc, hs], ps[po][:, :], modG[par][:, 1, dc:dc + 1], xT[:, dc, hs], ALU.mult, ALU.add,
                            [('ps', po), ('modG', par, 1), ('x', dc, t)], [('x', dc, t)])
            step()

        try:
            CK('load')
            gens = [mod_steps(0)]

            def mg():
                while gens:
                    try:
                        next(gens[0])
                        return
                    except StopIteration:
                        gens.pop(0)

            def drain(n_keep):
                while len(gens) > n_keep:
                    mg()

            for _ in range(7):
                mg()
            CK('mod')
            for l in range(nl):
                cur_layer[0] = l
                if l + 1 < nl:
                    gens.append(mod_steps(l + 1))
                norm_mod(l, 0)
                CK('norm0')
                ffn(l, 1, mg)
                CK('ffn1')
                if l == 0:
                    drain(1 if nl > 1 else 0)
                norm_mod(l, 1)
                mixer(l, mg)
                CK('mixer')
                norm_mod(l, 2)
                ffn(l, 2, mg)
                drain(0)
                CK('layer')
        except _Stop:
            pass
        STORE(kcp(OUT['yT']), xT[:], xkeys)
        P.add('sp', None, extra_deps=out_ids)
        P.emit()
        nc._prog_stats = (P.n_ops, P.sig_counts, P.chan_counts)
    return nc


def _rope_tables(r, sample):
    Cm = np.ones((r, T), np.float32)
    Sm = np.zeros((r, T), np.float32)
    if not sample:
        return Cm, Sm
    half = r // 2
    t = np.arange(T)
    inv = (10000.0 ** (-np.arange(0, half, 2, dtype=np.float32) / np.float32(half))).astype(np.float32)
    q = half // 2
    for part, pos in ((0, t // 64), (1, t % 64)):
        ang = pos.astype(np.float32)[:, None] * inv[None, :]
        cos = np.cos(ang).astype(np.float32).T
        sin = np.sin(ang).astype(np.float32).T
        b = part * half
        Cm[b:b + q] = cos
        Cm[b + q:b + half] = cos
        Sm[b:b + q] = -sin
        Sm[b + q:b + half] = sin
    return Cm, Sm


def _partner(r):
    half = r // 2
    q = half // 2
    return np.array([d + q if (d % half) < q else d - q for d in range(r)])


def _consts(sample):
    c = {}
    c['identb'] = np.eye(128, dtype=np.float32).astype(bf16)
    c['onesb'] = np.ones((128, 128), np.float32).astype(bf16)
    c['blk64b'] = np.kron(np.eye(2, dtype=np.float32), np.ones((64, 64), np.float32)).astype(bf16)
    Cb, Sb = _rope_tables(64, sample)
    rb = np.zeros((128, 2, T), np.float32)
    rb[0:64, 0], rb[64:128, 0] = Cb, Cb
    rb[0:64, 1], rb[64:128, 1] = Sb, Sb
    c['ropeB'] = rb
    Ca, Sa = _rope_tables(32, sample)
    ra = np.zeros((128, 2, T), np.float32)
    ra[:, 0] = 1.0
    ra[64:96, 0] = Ca
    ra[64:96, 1] = Sa
    c['ropeA'] = ra.astype(bf16)
    k = np.arange(64)
    ang = 2 * np.pi * np.outer(k, k) / 64.0
    C64 = np.cos(ang) / 8.0
    S64 = np.sin(ang) / 8.0
    cs = np.zeros((128, 256), np.float64)
    cs[0:64, 0:64] = C64
    cs[64:128, 64:128] = C64
    cs[0:64, 128:192] = S64
    cs[64:128, 192:256] = S64
    c['cs64'] = cs.astype(np.float32).astype(bf16)
    L = 1024 if sample else 256
    kk = np.arange(L)
    angL = 2 * np.pi * (np.outer(kk, kk) % L) / float(L)
    CL = np.cos(angL) / np.sqrt(L)
    SL = -np.sin(angL) / np.sqrt(L)
    if sample:
        dC, dS = CL, SL
    else:
        dC = np.kron(np.eye(4), CL)
        dS = np.kron(np.eye(4), SL)
    c['dftC'] = dC.astype(np.float32).astype(bf16)
    c['dftS'] = dS.astype(np.float32).astype(bf16)
    ea = np.zeros((8, 1280), np.float32)
    fa = np.zeros((8, 1024), np.float32)
    if not sample:
        for s in range(4):
            ea[s, s * 256:(s + 1) * 256] = 1.0
            fa[s, s * 256:(s + 1) * 256] = BIG
        ea[4, 0:1024] = 1.0
        fa[4, :] = -BIG
    c['EA'] = ea.astype(bf16)
    c['FA'] = fa.astype(bf16)
    mb = np.full((128, 6, 512), NEG, np.float32)
    kl = np.arange(128)[:, None]
    ql = np.arange(128)[None, :]
    for ki in range(6):
        ktp = ki - 1
        for qi in range(4):
            off = ktp - qi
            blkm = np.full((128, 128), NEG, np.float32)
            if sample:
                if off == 0:
                    blkm[:] = 0.0
                elif off == -1:
                    blkm = np.where(ql <= kl, 0.0, NEG).astype(np.float32)
                elif off == 1:
                    blkm = np.where(kl <= ql, 0.0, NEG).astype(np.float32)
            else:
                if off == 0 or (off == 1 and qi % 2 == 0) or (off == -1 and qi % 2 == 1):
                    blkm[:] = 0.0
            mb[:, ki, qi * 128:(qi + 1) * 128] = blkm
    c['maskB'] = mb.astype(bf16)
    return c


def _na_index():
    rows = 16
    r = np.arange(rows)
    row_start = np.clip(r - 4, 0, rows - 8)
    col = np.arange(64)
    col_start = np.clip(col - 8, 0, 64 - 16)
    q = np.arange(1024)
    qr, qc = q // 64, q % 64
    kr, kc = qr, qc
    KR, QR = kr[:, None], qr[None, :]
    KC, QC = kc[:, None], qc[None, :]
    valid = (KR >= row_start[QR]) & (KR < row_start[QR] + 8) & (KC >= col_start[QC]) & (KC < col_start[QC] + 16)
    dr = np.clip(KR - QR + 7, 0, 14)
    dc = np.clip(KC - QC + 15, 0, 30)
    return valid, dr, dc


def _vecT(v):
    return np.ascontiguousarray(v.reshape(-1, 128).T)


_PROG = {}


def _prep(x_prompt, x_sample, cache_mla_ckv, cache_mla_krope, cache_win_k, cache_win_v, cache_na_k, cache_na_v,
           c, c_ctx, w_ada, b_ada, g_ffn1, w_gate1, w_up1, w_down1, g_mix, w_in, g_qa, w_uq, g_kva, w_ukv,
           qn_a, kn_a, qn_b, kn_b, sink_b, qn_d, kn_d, rpb_d, w_o, g_ffn2, w_gate2, w_up2, w_down2):
    f = lambda a: np.ascontiguousarray(np.asarray(a, dtype=np.float32))
    x_prompt, x_sample, c, c_ctx = f(x_prompt), f(x_sample), f(c), f(c_ctx)
    shared = {n: f(v) for n, v in dict(w_ada=w_ada, w_gate1=w_gate1, w_up1=w_up1, w_down1=w_down1, w_gate2=w_gate2,
                                       w_up2=w_up2, w_down2=w_down2, w_in=w_in, w_o=w_o, w_uq=w_uq, w_ukv=w_ukv).items()}
    b_ada = f(b_ada)
    shared['b_adaT'] = np.ascontiguousarray(b_ada.reshape(NL, 72, 128).transpose(0, 2, 1))
    gs = np.stack([f(g_ffn1), f(g_mix), f(g_ffn2)], axis=1)
    shared['gT'] = np.ascontiguousarray(gs.reshape(NL, 3, 8, 128).transpose(0, 3, 1, 2))
    gv = np.zeros((NL, 128, NG), np.float32)
    gv[:, :, 0:2] = f(g_qa).reshape(NL, 2, 128).transpose(0, 2, 1)
    gv[:, :, 2] = f(g_kva)
    gv[:, 0:96, 3] = f(qn_a)
    gv[:, 0:96, 4] = f(kn_a)
    gv[:, :, 5] = np.tile(f(qn_b), (1, 2))
    gv[:, :, 6] = np.tile(f(kn_b), (1, 2))
    gv[:, :, 7] = np.tile(f(qn_d), (1, 2))
    gv[:, :, 8] = np.tile(f(kn_d), (1, 2))
    gv[:, :, 9:13] = f(sink_b)[:, None, :]
    p64, p32 = _partner(64), _partner(32)
    ia = np.concatenate([np.arange(64), 64 + p32])
    gv[:, 0:96, 13] = f(qn_a)[:, ia]
    gv[:, 0:96, 14] = f(kn_a)[:, ia]
    gv[:, :, 15] = np.tile(f(qn_b)[:, p64], (1, 2))
    gv[:, :, 16] = np.tile(f(kn_b)[:, p64], (1, 2))
    w_in_f, w_uq_f = shared['w_in'], shared['w_uq']
    shared['w_inPa'] = np.ascontiguousarray(w_in_f[:, :, 384 + p32])
    qb_cols = np.concatenate([416 + h * 64 + p64 for h in range(4)])
    kb_cols = np.concatenate([672 + h * 64 + p64 for h in range(2)])
    shared['w_inPb'] = np.ascontiguousarray(w_in_f[:, :, np.concatenate([qb_cols, kb_cols])])
    shared['w_uqP'] = np.ascontiguousarray(w_uq_f[:, :, np.concatenate([h * 96 + ia for h in range(4)])])
    shared['gvec'] = gv
    consts = {True: _consts(True), False: _consts(False)}
    valid, dr, dc = _na_index()
    rpb = f(rpb_d)
    bias_s = np.where(valid[None, None], rpb[:, :, dr, dc], np.float32(NEG)).astype(np.float32)
    seq = np.arange(1024) // 256
    bias_p1 = np.where(seq[:, None] == seq[None, :], np.float32(0.0), np.float32(NEG)).astype(np.float32)
    bias_p = np.ascontiguousarray(np.broadcast_to(bias_p1, (NL, 4, 1024, 1024)))
    cm_ckv, cm_kr = f(cache_mla_ckv), f(cache_mla_krope)
    cw_k, cw_v, cn_k, cn_v = f(cache_win_k), f(cache_win_v), f(cache_na_k), f(cache_na_v)
    in_maps = []
    for core in range(8):
        sample = core >= 4
        m = dict(shared)
        m.update(consts[sample])
        if sample:
            b = core - 4
            xs = x_sample[b]
            cond = c[b]
            m['ckvT_c'] = np.ascontiguousarray(cm_ckv[b].transpose(0, 2, 1))
            m['krT_c'] = np.ascontiguousarray(cm_kr[b].transpose(0, 2, 1))
            m['winkT_c'] = np.ascontiguousarray(cw_k[b].reshape(NL, 256, 128).transpose(0, 2, 1))
            m['winv_c'] = np.ascontiguousarray(cw_v[b].reshape(NL, 256, 128))
            m['nakT_c'] = np.ascontiguousarray(cn_k[b].reshape(NL, 256, 256).transpose(0, 2, 1))
            m['nav_c'] = np.ascontiguousarray(cn_v[b].reshape(NL, 256, 256))
            m['biasD'] = bias_s
            m['ctxflag'] = np.ones((128, 1), np.float32)
        else:
            xs = x_prompt[4 * core:4 * core + 4].reshape(1024, 1024)
            cond = c_ctx
            m['ckvT_c'] = np.zeros((NL, 128, 256), np.float32)
            m['krT_c'] = np.zeros((NL, 32, 256), np.float32)
            m['winkT_c'] = np.zeros((NL, 128, 256), np.float32)
            m['winv_c'] = np.zeros((NL, 256, 128), np.float32)
            m['nakT_c'] = np.zeros((NL, 256, 256), np.float32)
            m['nav_c'] = np.zeros((NL, 256, 256), np.float32)
            m['biasD'] = bias_p
            m['ctxflag'] = np.zeros((128, 1), np.float32)
        m['xT'] = np.ascontiguousarray(xs.T)
        m['cond'] = _vecT(cond)
        in_maps.append(m)
    return in_maps


def _assemble(R):
    y_prompt = np.concatenate([R[i]['yT'].T.reshape(4, 256, 1024) for i in range(4)], axis=0)
    y_sample = np.stack([R[4 + b]['yT'].T for b in range(4)], axis=0)

    def featmaj(name, feat):
        outs = []
        for i in range(4):
            a = R[i][name]
            a = a.reshape(NL, feat, 4, 256).transpose(2, 0, 3, 1)
            outs.append(a)
        return np.ascontiguousarray(np.concatenate(outs, axis=0))

    def tokmaj(name, feat):
        outs = []
        for i in range(4):
            a = R[i][name].reshape(NL, 4, 256, feat).transpose(1, 0, 2, 3)
            outs.append(a)
        return np.ascontiguousarray(np.concatenate(outs, axis=0))

    new_ckv = featmaj('o_ckvT', 128)
    new_kr = featmaj('o_krT', 32)
    new_wk = featmaj('o_kbT', 128).reshape(16, NL, 256, 2, 64)
    new_wv = tokmaj('o_vb', 128).reshape(16, NL, 256, 2, 64)
    new_nk = featmaj('o_kdT', 256).reshape(16, NL, 256, 4, 64)
    new_nv = tokmaj('o_vd', 256).reshape(16, NL, 256, 4, 64)
    return (np.ascontiguousarray(y_prompt.astype(np.float32)), np.ascontiguousarray(y_sample.astype(np.float32)),
            new_ckv, new_kr, new_wk, new_wv, new_nk, new_nv)


def kernel(**inputs):
    in_maps = _prep(**inputs)
    if 'nc' not in _PROG:
        _PROG['nc'] = build_program(NL)
    res = run_bass_kernel_spmd(_PROG['nc'], in_maps, core_ids=list(range(8)))
    return _assemble(res.results)
```
